# Optimizing a Trainium2 kernel written in Bass

```python
import math
import jax
import jax.numpy as jnp
from jax import lax
import numpy as np

D_MODEL = 1024
BATCH = 8
SEQ = 4096
DEPTH = 4

GRID_W = 64
CTX_LEN = 256
N_BRANCH = 4
BRANCH_W = 256
CHUNK = 64
DN_HEADS = 4
DN_DK = 64
DN_DV = 64
DN_CONV = 3
NA_HEADS = 4
NA_DH = 64
NA_WIN_ROWS = 8
NA_WIN_COLS = 16
GLA_HEADS = 4
GLA_DK = 32
GLA_DV = 64
GLA_RANK = 16
GLA_TAU = 16.0
DIFF_HEADS = 4
DIFF_DQK = 32
DIFF_DV = 64
Q_BLOCK = 128
ROPE_THETA = 10000.0
D_FF = 2816
FFN_CONV = 3
EPS = 1e-6

DN_QKV = 2 * DN_HEADS * DN_DK + DN_HEADS * DN_DV
NA_QKV = 3 * NA_HEADS * NA_DH
DIFF_QK = DIFF_HEADS * 2 * DIFF_DQK
DIFF_QKV = 2 * DIFF_QK + DIFF_HEADS * DIFF_DV
IN_WIDTHS = (DN_QKV, DN_HEADS * DN_DV, 2 * DN_HEADS, 2 * DN_HEADS,
             NA_QKV,
             GLA_HEADS * GLA_DK, GLA_HEADS * GLA_DK, GLA_HEADS * GLA_DV, GLA_HEADS * GLA_DV, 2 * GLA_RANK,
             DIFF_QKV,
             N_BRANCH * D_MODEL)
N_IN = sum(IN_WIDTHS)

kernel_name = 'hybrid_parallel_mixer_dit_block'


def rms_norm(x, g):
    xf = x.astype(jnp.float32)
    y = xf * lax.rsqrt(jnp.mean(xf * xf, axis=-1, keepdims=True) + EPS)
    return (y * g.astype(jnp.float32)).astype(x.dtype)


def l2_normalize(x):
    xf = x.astype(jnp.float32)
    return xf * lax.rsqrt(jnp.sum(xf * xf, axis=-1, keepdims=True) + EPS)


def dwconv_centred(x, w):
    width, ch = w.shape
    pad = width // 2
    return lax.conv_general_dilated(x, w[:, None, :].astype(x.dtype), window_strides=(1,),
                                    padding=[(pad, pad)], dimension_numbers=('NWC', 'WIO', 'NWC'),
                                    feature_group_count=ch)


def axial_rope_tables(length):
    t = jnp.arange(length)
    row = (t // GRID_W).astype(jnp.float32)
    col = (t % GRID_W).astype(jnp.float32)
    n_freq = DIFF_DQK // 4
    inv_freq = jnp.power(jnp.float32(ROPE_THETA), -jnp.arange(n_freq, dtype=jnp.float32) / n_freq)
    ang_r = row[:, None] * inv_freq
    ang_c = col[:, None] * inv_freq
    ang = jnp.concatenate([ang_r, ang_r, ang_c, ang_c], axis=-1)
    return jnp.cos(ang), jnp.sin(ang)


def apply_axial_rope(x, cos, sin):
    a, b, c, d = jnp.split(x, 4, axis=-1)
    rotated = jnp.concatenate([-b, a, -d, c], axis=-1)
    return x * cos.astype(x.dtype) + rotated * sin.astype(x.dtype)


def _flip_seq(t):
    return jnp.flip(t, axis=2)


def gated_delta_chunked(q, k, v, g, beta, s0):
    f32 = jnp.float32
    B, H, L, dk = q.shape
    dv = v.shape[-1]
    n, C = L // CHUNK, CHUNK
    q, k, v = [t.astype(f32).reshape(B, H, n, C, t.shape[-1]) for t in (q, k, v)]
    g = g.astype(f32).reshape(B, H, n, C)
    beta = beta.astype(f32).reshape(B, H, n, C)
    gc = jnp.cumsum(g, axis=-1)
    incl = jnp.tril(jnp.ones((C, C), dtype=bool))
    strict = jnp.tril(jnp.ones((C, C), dtype=bool), -1)
    decay = jnp.exp(jnp.where(incl, gc[..., :, None] - gc[..., None, :], -jnp.inf))
    kb = k * beta[..., None]
    n_mat = jnp.where(strict, jnp.einsum('bhncd,bhnjd->bhncj', kb, k) * decay, 0.0)
    rhs = jnp.concatenate([v * beta[..., None], kb * jnp.exp(gc)[..., None]], axis=-1)
    sol = lax.linalg.triangular_solve(jnp.eye(C, dtype=f32) + n_mat, rhs, left_side=True, lower=True)
    u, w = sol[..., :dv], sol[..., dv:]
    a_qk = jnp.einsum('bhncd,bhnjd->bhncj', q, k) * decay
    q_dec = q * jnp.exp(gc)[..., None]
    k_dec = k * jnp.exp(gc[..., -1:] - gc)[..., None]
    g_last = jnp.exp(gc[..., -1])

    def step(S, xs):
        u_i, w_i, a_i, qd_i, kd_i, gl_i = xs
        v_new = u_i - jnp.einsum('bhcd,bhdv->bhcv', w_i, S)
        o_i = jnp.einsum('bhcd,bhdv->bhcv', qd_i, S) + jnp.einsum('bhcj,bhjv->bhcv', a_i, v_new)
        S = S * gl_i[..., None, None] + jnp.einsum('bhcd,bhcv->bhdv', kd_i, v_new)
        return S, o_i

    xs = tuple(jnp.moveaxis(t, 2, 0) for t in (u, w, a_qk, q_dec, k_dec, g_last))
    s_fin, o = lax.scan(step, s0, xs)
    return jnp.moveaxis(o, 0, 2).reshape(B, H, L, dv), s_fin


def gla_chunked(q, k, v, log_a, s0):
    f32 = jnp.float32
    B, H, L, dk = q.shape
    dv = v.shape[-1]
    n, C = L // CHUNK, CHUNK
    q, k, v, log_a = [t.astype(f32).reshape(B, H, n, C, t.shape[-1]) for t in (q, k, v, log_a)]
    b = jnp.cumsum(log_a, axis=3)
    b_mid = b[:, :, :, C // 2 - 1:C // 2, :]
    a_intra = jnp.einsum('bhncd,bhnjd->bhncj', q * jnp.exp(b - b_mid), k * jnp.exp(b_mid - b))
    a_intra = jnp.where(jnp.tril(jnp.ones((C, C), dtype=bool)), a_intra, 0.0)
    o_intra = jnp.einsum('bhncj,bhnjv->bhncv', a_intra, v)
    b_last = b[:, :, :, -1, :]
    ds = jnp.einsum('bhncd,bhncv->bhndv', k * jnp.exp(b_last[:, :, :, None, :] - b), v)

    def step(S, xs):
        ds_i, dec_i = xs
        return dec_i[..., None] * S + ds_i, S

    s_fin, s_prev = lax.scan(step, s0, (jnp.moveaxis(ds, 2, 0), jnp.moveaxis(jnp.exp(b_last), 2, 0)))
    o_inter = jnp.einsum('bhncd,nbhdv->bhncv', q * jnp.exp(b), s_prev)
    return (o_intra + o_inter).reshape(B, H, L, dv), s_fin


def bidirectional_scan(chunk_fn, lat_shared, lat_dir, ctx_shared, ctx_dir, with_ctx):
    q_c, v_c = ctx_shared[0], ctx_shared[2]
    s0 = jnp.zeros(q_c.shape[:2] + (q_c.shape[-1], v_c.shape[-1]), jnp.float32)
    outs_l, outs_c = [], []
    for d in range(2):
        orient = _flip_seq if d == 1 else (lambda t: t)
        c_args = [orient(t) for t in ctx_shared] + [orient(t[d]) for t in ctx_dir]
        l_args = [orient(t) for t in lat_shared] + [orient(t[d]) for t in lat_dir]
        o_c, s_ctx = chunk_fn(*c_args, s0)
        o_l, _ = chunk_fn(*l_args, s_ctx)
        outs_l.append(orient(o_l))
        outs_c.append(orient(o_c))
    y_l = outs_l[0] + outs_l[1]
    y_c = outs_c[0] + outs_c[1] if with_ctx else None
    return y_l, y_c


def _gated_head_norm(o, gate, g):
    B, H, L, dv = o.shape
    on = rms_norm(jnp.transpose(o, (0, 2, 1, 3)), g)
    out = on * jax.nn.silu(gate.astype(jnp.float32)).reshape(B, L, H, dv)
    return out.reshape(B, L, H * dv).astype(gate.dtype)


def _dn_inputs(qkv, beta_raw, a_raw, conv_w, a_log, dt_bias):
    B, L, _ = qkv.shape
    q, k, v = jnp.split(jax.nn.silu(dwconv_centred(qkv, conv_w)), [DN_HEADS * DN_DK, 2 * DN_HEADS * DN_DK], axis=-1)
    heads = lambda t, d: jnp.transpose(t.reshape(B, L, DN_HEADS, d), (0, 2, 1, 3))
    q = l2_normalize(heads(q, DN_DK)) * DN_DK ** -0.5
    k = l2_normalize(heads(k, DN_DK))
    beta = jax.nn.sigmoid(beta_raw.astype(jnp.float32)).reshape(B, L, 2, DN_HEADS)
    g = -jnp.exp(a_log.astype(jnp.float32)) * jax.nn.softplus(
        a_raw.astype(jnp.float32).reshape(B, L, 2, DN_HEADS) + dt_bias.astype(jnp.float32))
    per_dir = lambda t: jnp.transpose(t, (2, 0, 3, 1))
    return (q, k, heads(v, DN_DV)), (per_dir(g), per_dir(beta))


def mixer_gated_deltanet(lat, ctx, conv_w, a_log, dt_bias, norm_g, with_ctx):
    ls, ld = _dn_inputs(lat[0], lat[2], lat[3], conv_w, a_log, dt_bias)
    cs, cd = _dn_inputs(ctx[0], ctx[2], ctx[3], conv_w, a_log, dt_bias)
    o_l, o_c = bidirectional_scan(gated_delta_chunked, ls, ld, cs, cd, with_ctx)
    y_l = _gated_head_norm(o_l, lat[1], norm_g)
    y_c = _gated_head_norm(o_c, ctx[1], norm_g) if with_ctx else None
    return y_l, y_c


def _gla_inputs(q, k, v, a1, w_a2, b_a):
    B, L, _ = q.shape
    heads = lambda t, d: jnp.transpose(t.reshape(B, L, GLA_HEADS, d), (0, 2, 1, 3))
    logit = jnp.einsum('blnr,nrk->blnk', a1.reshape(B, L, 2, GLA_RANK), w_a2) + b_a
    log_a = jax.nn.log_sigmoid(logit.astype(jnp.float32)) / GLA_TAU
    log_a = jnp.transpose(log_a.reshape(B, L, 2, GLA_HEADS, GLA_DK), (2, 0, 3, 1, 4))
    return (heads(q, GLA_DK) * GLA_DK ** -0.5, heads(k, GLA_DK), heads(v, GLA_DV)), (log_a,)


def mixer_gla(lat, ctx, w_a2, b_a, norm_g, with_ctx):
    ls, ld = _gla_inputs(lat[0], lat[1], lat[2], lat[4], w_a2, b_a)
    cs, cd = _gla_inputs(ctx[0], ctx[1], ctx[2], ctx[4], w_a2, b_a)
    o_l, o_c = bidirectional_scan(gla_chunked, ls, ld, cs, cd, with_ctx)
    y_l = _gated_head_norm(o_l, lat[3], norm_g)
    y_c = _gated_head_norm(o_c, ctx[3], norm_g) if with_ctx else None
    return y_l, y_c


def mixer_neighbourhood(qkv_l, qkv_c, q_norm, k_norm, rpb, with_ctx):
    def prep(qkv):
        B, L, _ = qkv.shape
        t = qkv.reshape(B, L, 3, NA_HEADS, NA_DH)
        return rms_norm(t[:, :, 0], q_norm), rms_norm(t[:, :, 1], k_norm), t[:, :, 2]

    ql, kl, vl = prep(qkv_l)
    qc, kc, vc = prep(qkv_c)
    B, L = ql.shape[:2]
    rows = L // GRID_W
    wr = min(NA_WIN_ROWS, rows)
    wc = NA_WIN_COLS
    scale = NA_DH ** -0.5
    row_idx = jnp.arange(rows)
    row_start = jnp.clip(row_idx - wr // 2, 0, rows - wr)
    col = jnp.arange(GRID_W)
    col_keys = jnp.clip(col - wc // 2, 0, GRID_W - wc)[:, None] + jnp.arange(wc)
    col_bias = col_keys - col[:, None] + (NA_WIN_COLS - 1)
    k_grid = kl.reshape(B, rows, GRID_W, NA_HEADS, NA_DH)
    v_grid = vl.reshape(B, rows, GRID_W, NA_HEADS, NA_DH)
    q_rows = jnp.moveaxis(ql.reshape(B, rows, GRID_W, NA_HEADS, NA_DH), 1, 0)

    def row_block(xs):
        q_r, rs, r = xs
        k_win = lax.dynamic_slice_in_dim(k_grid, rs, wr, axis=1)[:, :, col_keys]
        v_win = lax.dynamic_slice_in_dim(v_grid, rs, wr, axis=1)[:, :, col_keys]
        r_bias = rs + jnp.arange(wr) - r + (NA_WIN_ROWS - 1)
        bias = jnp.transpose(rpb[:, r_bias[:, None, None], col_bias[None, :, :]], (0, 2, 1, 3))
        s_win = jnp.einsum('bqhd,brqjhd->bhqrj', q_r, k_win).astype(jnp.float32) * scale + bias.astype(jnp.float32)
        s_ctx = jnp.einsum('bqhd,bkhd->bhqk', q_r, kc).astype(jnp.float32) * scale
        p = jax.nn.softmax(jnp.concatenate([s_win.reshape(B, NA_HEADS, GRID_W, wr * wc), s_ctx], axis=-1), axis=-1)
        p = p.astype(vl.dtype)
        p_win = p[..., :wr * wc].reshape(B, NA_HEADS, GRID_W, wr, wc)
        return (jnp.einsum('bhqrj,brqjhd->bqhd', p_win, v_win)
                + jnp.einsum('bhqk,bkhd->bqhd', p[..., wr * wc:], vc))

    o = lax.map(row_block, (q_rows, row_start, row_idx))
    y_l = jnp.moveaxis(o, 0, 1).reshape(B, L, NA_HEADS * NA_DH)
    y_c = None
    if with_ctx:
        s = jnp.einsum('bqhd,bkhd->bhqk', qc, kc).astype(jnp.float32) * scale
        p = jax.nn.softmax(s, axis=-1).astype(vc.dtype)
        y_c = jnp.einsum('bhqk,bkhd->bqhd', p, vc).reshape(B, qc.shape[1], NA_HEADS * NA_DH)
    return y_l, y_c


def mixer_differential(qkv_l, qkv_c, q_norm, k_norm, lam_params, norm_g, lam_init, cos, sin, with_ctx):
    def prep(qkv):
        B, L, _ = qkv.shape
        q = rms_norm(qkv[..., :DIFF_QK].reshape(B, L, DIFF_HEADS, 2, DIFF_DQK), q_norm)
        k = rms_norm(qkv[..., DIFF_QK:2 * DIFF_QK].reshape(B, L, DIFF_HEADS, 2, DIFF_DQK), k_norm)
        v = qkv[..., 2 * DIFF_QK:].reshape(B, L, DIFF_HEADS, DIFF_DV)
        return q, k, v

    ql, kl, vl = prep(qkv_l)
    qc, kc, vc = prep(qkv_c)
    B, L = ql.shape[:2]
    c5, s5 = cos[None, :, None, None, :], sin[None, :, None, None, :]
    ql = apply_axial_rope(ql, c5, s5)
    kl = apply_axial_rope(kl, c5, s5)
    lp = lam_params.astype(jnp.float32)
    lam = jnp.exp(jnp.sum(lp[0] * lp[1])) - jnp.exp(jnp.sum(lp[2] * lp[3])) + lam_init
    k_all = jnp.concatenate([kl, kc], axis=1)
    v_all = jnp.concatenate([vl, vc], axis=1)
    scale = DIFF_DQK ** -0.5

    def attend(qb, keys, vals):
        s = jnp.einsum('bqhtd,bkhtd->bhtqk', qb, keys).astype(jnp.float32) * scale
        p = jax.nn.softmax(s, axis=-1)
        w = p[:, :, 0] - lam * p[:, :, 1]
        return jnp.einsum('bhqk,bkhd->bqhd', w.astype(vals.dtype), vals)

    def post(o):
        return (rms_norm(o, norm_g) * (1.0 - lam_init)).reshape(o.shape[0], o.shape[1], DIFF_HEADS * DIFF_DV)

    nb = L // Q_BLOCK
    q_blocks = jnp.moveaxis(ql.reshape(B, nb, Q_BLOCK, DIFF_HEADS, 2, DIFF_DQK), 1, 0)
    o_l = lax.map(lambda qb: attend(qb, k_all, v_all), q_blocks)
    y_l = post(jnp.moveaxis(o_l, 0, 1).reshape(B, L, DIFF_HEADS, DIFF_DV))
    y_c = post(attend(qc, kc, vc)) if with_ctx else None
    return y_l, y_c


def merge_branches(branches, gate_raw, b_gate, w_branch, w_out):
    y = jnp.stack(branches, axis=2)
    B, L = y.shape[:2]
    proj = jnp.einsum('blgw,gwd->blgd', y, w_branch)
    gate = jax.nn.sigmoid(gate_raw.reshape(B, L, N_BRANCH, D_MODEL) + b_gate)
    return jnp.sum(gate * proj, axis=2) @ w_out


def conv_ffn(h, w_in, conv_w, conv_b, w_out):
    u, v = jnp.split(h @ w_in, 2, axis=-1)
    u = dwconv_centred(u, conv_w) + conv_b
    return (jax.nn.silu(u) * v) @ w_out


def setup_inputs(seed: int = 0) -> dict:
    key = jax.random.key(seed)
    ks = jax.random.split(key, 32)
    f32 = jnp.float32

    def nrm(i, shape, scale):
        return jax.random.normal(ks[i], shape, f32) * scale

    dt = jnp.exp(jax.random.uniform(ks[10], (DEPTH, 2, DN_HEADS), f32, minval=math.log(1e-3), maxval=math.log(1e-1)))
    return {
        'x': nrm(0, (BATCH, SEQ, D_MODEL), 1.0),
        'c': nrm(1, (BATCH, D_MODEL), 1.0),
        'ctx': nrm(2, (BATCH, CTX_LEN, D_MODEL), 1.0),
        'c_ctx': nrm(3, (D_MODEL,), 1.0),
        'w_mod': nrm(4, (DEPTH, D_MODEL, 6 * D_MODEL), 0.5 * D_MODEL ** -0.5),
        'b_mod': nrm(5, (DEPTH, 6 * D_MODEL), 0.02),
        'norm1_g': 1.0 + nrm(6, (DEPTH, D_MODEL), 0.02),
        'norm2_g': 1.0 + nrm(7, (DEPTH, D_MODEL), 0.02),
        'w_in': nrm(8, (DEPTH, D_MODEL, N_IN), D_MODEL ** -0.5),
        'b_gate': nrm(9, (DEPTH, N_BRANCH, D_MODEL), 0.02),
        'dn_conv': nrm(11, (DEPTH, DN_CONV, DN_QKV), DN_CONV ** -0.5),
        'dn_a_log': jnp.log(jax.random.uniform(ks[12], (DEPTH, 2, DN_HEADS), f32, minval=1.0, maxval=16.0)),
        'dn_dt_bias': dt + jnp.log(-jnp.expm1(-dt)),
        'dn_norm_g': 1.0 + nrm(13, (DEPTH, DN_DV), 0.02),
        'na_q_norm': 1.0 + nrm(14, (DEPTH, NA_DH), 0.02),
        'na_k_norm': 1.0 + nrm(15, (DEPTH, NA_DH), 0.02),
        'na_rpb': nrm(16, (DEPTH, NA_HEADS, 2 * NA_WIN_ROWS - 1, 2 * NA_WIN_COLS - 1), 0.1),
        'gla_w_a2': nrm(17, (DEPTH, 2, GLA_RANK, GLA_HEADS * GLA_DK), GLA_RANK ** -0.5),
        'gla_b_a': nrm(18, (DEPTH, 2, GLA_HEADS * GLA_DK), 0.1),
        'gla_norm_g': 1.0 + nrm(19, (DEPTH, GLA_DV), 0.02),
        'df_q_norm': 1.0 + nrm(20, (DEPTH, DIFF_DQK), 0.02),
        'df_k_norm': 1.0 + nrm(21, (DEPTH, DIFF_DQK), 0.02),
        'df_lambda': nrm(22, (DEPTH, 4, DIFF_DQK), 0.1),
        'df_norm_g': 1.0 + nrm(23, (DEPTH, DIFF_DV), 0.02),
        'w_branch': nrm(24, (DEPTH, N_BRANCH, BRANCH_W, D_MODEL), BRANCH_W ** -0.5),
        'w_out': nrm(25, (DEPTH, D_MODEL, D_MODEL), D_MODEL ** -0.5),
        'ffn_w_in': nrm(26, (DEPTH, D_MODEL, 2 * D_FF), D_MODEL ** -0.5),
        'ffn_conv_w': nrm(27, (DEPTH, FFN_CONV, D_FF), FFN_CONV ** -0.5),
        'ffn_conv_b': nrm(28, (DEPTH, D_FF), 0.02),
        'ffn_w_out': nrm(29, (DEPTH, D_FF, D_MODEL), D_FF ** -0.5),
    }


def reference(x, c, ctx, c_ctx, w_mod, b_mod, norm1_g, norm2_g, w_in, b_gate,
              dn_conv, dn_a_log, dn_dt_bias, dn_norm_g,
              na_q_norm, na_k_norm, na_rpb,
              gla_w_a2, gla_b_a, gla_norm_g,
              df_q_norm, df_k_norm, df_lambda, df_norm_g,
              w_branch, w_out, ffn_w_in, ffn_conv_w, ffn_conv_b, ffn_w_out):
    L = x.shape[1]
    cos, sin = axial_rope_tables(L)
    split_at = np.cumsum(IN_WIDTHS)[:-1].tolist()
    xc = ctx
    for li in range(DEPTH):
        with_ctx = li < DEPTH - 1
        lam_init = 0.8 - 0.6 * math.exp(-0.3 * li)
        mod = jax.nn.silu(c) @ w_mod[li] + b_mod[li]
        mod_c = jax.nn.silu(c_ctx) @ w_mod[li] + b_mod[li]
        sh1, sc1, g1, sh2, sc2, g2 = jnp.split(mod[:, None, :], 6, axis=-1)
        csh1, csc1, cg1, csh2, csc2, cg2 = jnp.split(mod_c[None, None, :], 6, axis=-1)
        h = rms_norm(x, norm1_g[li]) * (1.0 + sc1) + sh1
        hc = rms_norm(xc, norm1_g[li]) * (1.0 + csc1) + csh1
        pl = jnp.split(h @ w_in[li], split_at, axis=-1)
        pc = jnp.split(hc @ w_in[li], split_at, axis=-1)
        ya = mixer_gated_deltanet(pl[0:4], pc[0:4], dn_conv[li], dn_a_log[li], dn_dt_bias[li], dn_norm_g[li], with_ctx)
        yb = mixer_neighbourhood(pl[4], pc[4], na_q_norm[li], na_k_norm[li], na_rpb[li], with_ctx)
        yc = mixer_gla(pl[5:10], pc[5:10], gla_w_a2[li], gla_b_a[li], gla_norm_g[li], with_ctx)
        yd = mixer_differential(pl[10], pc[10], df_q_norm[li], df_k_norm[li], df_lambda[li], df_norm_g[li],
                                lam_init, cos, sin, with_ctx)
        x = x + g1 * merge_branches((ya[0], yb[0], yc[0], yd[0]), pl[11], b_gate[li], w_branch[li], w_out[li])
        h2 = rms_norm(x, norm2_g[li]) * (1.0 + sc2) + sh2
        x = x + g2 * conv_ffn(h2, ffn_w_in[li], ffn_conv_w[li], ffn_conv_b[li], ffn_w_out[li])
        if with_ctx:
            xc = xc + cg1 * merge_branches((ya[1], yb[1], yc[1], yd[1]), pc[11], b_gate[li], w_branch[li], w_out[li])
            hc2 = rms_norm(xc, norm2_g[li]) * (1.0 + csc2) + csh2
            xc = xc + cg2 * conv_ffn(hc2, ffn_w_in[li], ffn_conv_w[li], ffn_conv_b[li], ffn_w_out[li])
    return x
```

```python
import contextlib
import math
import numpy as np
import ml_dtypes
import concourse.bass as bass
import concourse.mybir as mybir
from concourse.bass_utils import run_bass_kernel_spmd

F32 = mybir.dt.float32
BF16 = mybir.dt.bfloat16
AF = mybir.ActivationFunctionType
ALU = mybir.AluOpType
AX = mybir.AxisListType

D = 1024
L = 4096
CL = 256
T = L + CL
DEPTH = 4
NIN = 7472
NMIX = 3376
DFF = 2816
EPS = 1e-6
NDMA_SEM = 8
SEQS = (("lat", 0, L), ("ctx", L, CL))


class Sched:
    ENGS = ("pe", "act", "dve", "pool", "sp")

    def __init__(self, nc, st):
        self.nc = nc
        self.ops = []
        self.last_w = {}
        self.readers = {}
        self.dma_count = {"sp": 0, "pool": 0}
        self.dma_hist = {"sp": [], "pool": []}
        self.emitted = 0
        self.cnt = {e: 0 for e in self.ENGS}
        self.sems = {}
        for e in self.ENGS:
            self.sems[e] = st.enter_context(nc.semaphore("s_" + e))
        for q in ("sp", "pool"):
            for i in range(NDMA_SEM):
                self.sems[(q, i)] = st.enter_context(nc.semaphore("d_%s%d" % (q, i)))

    def op(self, eng, fn, reads=(), writes=(), dma=False):
        oid = len(self.ops)
        deps = {}
        lo = self.emitted
        for k in reads:
            w = self.last_w.get(k)
            if w is not None and w >= lo:
                deps[w] = 2
        for k in writes:
            w = self.last_w.get(k)
            if w is not None and w >= lo:
                deps[w] = max(deps.get(w, 0), 1)
            for r in self.readers.get(k, ()):
                if r >= lo:
                    deps.setdefault(r, 0)
        for d in list(deps):
            do = self.ops[d]
            if do["eng"] == eng and not do["dma"] and not dma:
                if deps[d] == 0 or (deps[d] == 1 and eng == "pe"):
                    del deps[d]
        o = dict(eng=eng, fn=fn, dma=dma, deps=deps)
        if dma:
            n = self.dma_count[eng]
            self.dma_count[eng] = n + 1
            o["dma_i"] = n
            h = self.dma_hist[eng]
            if n >= NDMA_SEM and h[n - NDMA_SEM] >= lo:
                deps[h[n - NDMA_SEM]] = 2
            h.append(oid)
        self.ops.append(o)
        for k in reads:
            self.readers.setdefault(k, []).append(oid)
        for k in writes:
            self.last_w[k] = oid
            self.readers[k] = []
        return oid

    def flush(self):
        nc = self.nc
        allops = self.ops
        lo = self.emitted
        ops = allops[lo:]
        self.emitted = len(allops)
        if not ops:
            return
        for o in ops:
            o["sig"] = o["dma"]
        for o in ops:
            for d in o["deps"]:
                if not allops[d]["dma"]:
                    allops[d]["sig"] = True
        for o in ops:
            if o["dma"]:
                i = o["dma_i"]
                o["semkey"] = (o["eng"], i % NDMA_SEM)
                o["semval"] = 16 * (i // NDMA_SEM + 1)
            elif o["sig"]:
                self.cnt[o["eng"]] += 1
                o["semkey"] = o["eng"]
                o["semval"] = self.cnt[o["eng"]]
        known = {e: {} for e in self.ENGS}
        for o in ops:
            kn = known[o["eng"]]
            waits = []
            for d in sorted(o["deps"]):
                do = allops[d]
                sk, sv = do["semkey"], do["semval"]
                if kn.get(sk, 0) >= sv:
                    continue
                waits.append((sk, sv))
                kn[sk] = sv
                for k2, v2 in do["clock"].items():
                    if kn.get(k2, 0) < v2:
                        kn[k2] = v2
            o["waits"] = waits
            o["clock"] = dict(kn)
            if "semkey" in o and not o["dma"]:
                o["clock"][o["semkey"]] = o["semval"]
        sems = self.sems
        dma_count = dict(self.dma_count)

        def replay(ename):
            def body(eng):
                for o in ops:
                    if o["eng"] != ename:
                        continue
                    for sk, sv in o["waits"]:
                        eng.wait_ge(sems[sk], sv)
                    ins = o["fn"](eng)
                    if o["dma"]:
                        ins.then_inc(sems[o["semkey"]], 16)
                    elif o["sig"]:
                        ins.then_inc(sems[o["semkey"]], 1)
                if ename in ("sp", "pool"):
                    n = dma_count[ename]
                    for i in range(NDMA_SEM):
                        c = (n - i + NDMA_SEM - 1) // NDMA_SEM
                        if c > 0:
                            eng.wait_ge(sems[(ename, i)], 16 * c)
            return body

        with nc.Block() as block:
            block.tensor(replay("pe"))
            block.scalar(replay("act"))
            block.vector(replay("dve"))
            block.gpsimd(replay("pool"))
            block.sync(replay("sp"))
        for o in ops:
            o["fn"] = None
            o["clock"] = None


def vec_layout():
    off = {}
    n = 0

    def add(name, cols):
        nonlocal n
        off[name] = n
        n += cols

    for li in range(DEPTH):
        add(("bmod", li), 48)
        add(("n1g", li), 8)
        add(("n2g", li), 8)
        add(("bgate", li), 32)
        add(("fcw", li), 66)
        add(("fcb", li), 22)
        for nm in ("dfqn", "dfkn", "dfng", "naqn", "nakn", "glng", "dnng", "dndtb", "dnalog"):
            add((nm, li), 1)
        add(("dncw", li), 18)
    add("m1", 1)
    add("m2", 1)
    add("one", 1)
    add("hm", 4)
    add("dnsgn", 1)
    return off, n


def pmajor(v):
    v = np.asarray(v, np.float32)
    lead = int(np.prod(v.shape[:-1])) if v.ndim > 1 else 1
    n = v.shape[-1] // 128
    return np.ascontiguousarray(v.reshape(lead, n, 128).transpose(2, 0, 1).reshape(128, lead * n))


def pack_vecs(inp):
    off, n = vec_layout()
    V = np.zeros((128, n), np.float32)

    def put(name, arr):
        V[:, off[name]:off[name] + arr.shape[1]] = arr

    for li in range(DEPTH):
        put(("bmod", li), pmajor(inp["b_mod"][li]))
        put(("n1g", li), pmajor(inp["norm1_g"][li]))
        put(("n2g", li), pmajor(inp["norm2_g"][li]))
        put(("bgate", li), pmajor(inp["b_gate"][li]))
        put(("fcw", li), pmajor(inp["ffn_conv_w"][li]))
        put(("fcb", li), pmajor(inp["ffn_conv_b"][li]))
        put(("dfqn", li), np.tile(inp["df_q_norm"][li], 4)[:, None])
        put(("dfkn", li), np.tile(inp["df_k_norm"][li], 4)[:, None])
        put(("dfng", li), np.tile(inp["df_norm_g"][li], 2)[:, None])
        put(("naqn", li), np.tile(inp["na_q_norm"][li], 2)[:, None])
        put(("nakn", li), np.tile(inp["na_k_norm"][li], 2)[:, None])
        put(("glng", li), np.tile(inp["gla_norm_g"][li], 2)[:, None])
        put(("dnng", li), np.tile(inp["dn_norm_g"][li], 2)[:, None])
        z8 = np.zeros(8, np.float32)
        put(("dndtb", li), np.concatenate([z8, np.asarray(inp["dn_dt_bias"][li], np.float32).reshape(8), np.zeros(112, np.float32)])[:, None])
        put(("dnalog", li), np.concatenate([z8, np.asarray(inp["dn_a_log"][li], np.float32).reshape(8), np.zeros(112, np.float32)])[:, None])
        put(("dncw", li), pmajor(inp["dn_conv"][li]))
    p = np.arange(128)
    put("m1", ((p // 32) % 2 == 0).astype(np.float32)[:, None])
    put("m2", ((p // 32) % 2 == 1).astype(np.float32)[:, None])
    put("one", np.ones((128, 1), np.float32))
    put("hm", (p[:, None] // 32 == np.arange(4)[None, :]).astype(np.float32))
    put("dnsgn", np.where(p < 8, -1.0, 1.0).astype(np.float32)[:, None])
    return V


NEG = -30000.0


def const_mats():
    C = np.zeros((4, 128, 128), np.float32)
    p = np.arange(128)
    C[0] = (p[:, None] // 32 == p[None, :] // 32) / 32.0
    C[1] = (p[:, None] // 64 == p[None, :] // 64) / 64.0
    for m in range(128):
        g, d = m // 32, m % 32
        q = d // 8
        if q == 0:
            C[2, g * 32 + d + 8, m] = -1.0
        elif q == 1:
            C[2, g * 32 + d - 8, m] = 1.0
        elif q == 2:
            C[2, g * 32 + d + 8, m] = -1.0
        else:
            C[2, g * 32 + d - 8, m] = 1.0
    C[3] = 1.0
    return np.ascontiguousarray(C.transpose(1, 0, 2).reshape(128, 512))


def rope_tables():
    t = np.arange(L)
    row = (t // 64).astype(np.float32)
    col = (t % 64).astype(np.float32)
    nf = 8
    inv = np.power(np.float32(10000.0), -np.arange(nf, dtype=np.float32) / nf).astype(np.float32)
    ar = row[:, None] * inv
    ac = col[:, None] * inv
    ang = np.concatenate([ar, ar, ac, ac], -1)
    cs = np.stack([np.cos(ang), np.sin(ang)], 0).astype(np.float32)
    return np.ascontiguousarray(np.tile(cs.transpose(0, 2, 1), (1, 4, 1)))


SCM = {}
for _i, _n in enumerate(("I", "U", "NU", "MI", "MS", "MST", "CKD", "CQ", "M01")):
    SCM[_n] = _i
NSCM = len(SCM)


def scan_mats():
    t = np.arange(128)
    same = (t[:, None] // 64) == (t[None, :] // 64)
    out = np.zeros((2, NSCM, 128, 128), np.float32)
    for d in range(2):
        before = (t[:, None] <= t[None, :]) if d == 0 else (t[:, None] >= t[None, :])
        strict = (t[:, None] < t[None, :]) if d == 0 else (t[:, None] > t[None, :])
        U = (same & before).astype(np.float32)
        out[d, SCM["I"]] = np.eye(128, dtype=np.float32)
        out[d, SCM["U"]] = U
        out[d, SCM["NU"]] = -U
        out[d, SCM["MI"]] = np.where(same & before, 0.0, NEG)
        out[d, SCM["MS"]] = np.where(same & strict, 0.0, NEG)
        out[d, SCM["MST"]] = np.where(same & strict, 0.0, NEG).T
        out[d, SCM["CKD"]] = (same & strict.T).astype(np.float32)
        pos = t % 64
        midpos = 31 if d == 0 else 32
        umid = (same & ((pos[:, None] <= midpos) if d == 0 else (pos[:, None] >= midpos))).astype(np.float32)
        out[d, SCM["CQ"]] = U - umid
        out[d, SCM["M01"]] = (same & before).astype(np.float32)
    return np.ascontiguousarray(out.transpose(2, 0, 1, 3).reshape(128, 2 * NSCM * 128))


def na_chunks(rp):
    if rp in (0, 1):
        return [(kc, 5 + rp * 4 + kc) for kc in range(4)]
    if rp in (30, 31):
        return [(28 + j, 13 + (rp - 30) * 4 + j) for j in range(4)]
    return [(rp - 2 + j, j) for j in range(5)]


def na_bias_tables(rpb):
    out = np.full((4, 128, 21, 128), NEG, np.float32)
    kk = np.arange(128)
    for rp in [2, 0, 1, 30, 31]:
        for (kc, bi) in na_chunks(rp):
            qrow = 2 * rp + kk // 64
            qcol = kk % 64
            krow = 2 * kc + kk // 64
            kcol = kk % 64
            rs = np.clip(qrow - 4, 0, 56)
            cst = np.clip(qcol - 8, 0, 48)
            dr = krow[:, None] - qrow[None, :]
            dc = kcol[:, None] - qcol[None, :]
            valid = ((krow[:, None] >= rs[None, :]) & (krow[:, None] < rs[None, :] + 8)
                     & (kcol[:, None] >= cst[None, :]) & (kcol[:, None] < cst[None, :] + 16))
            ri = np.clip(dr + 7, 0, 14)
            ci = np.clip(dc + 15, 0, 30)
            for h in range(4):
                g = rpb[h][ri, ci]
                out[h, :, bi, :] = np.where(valid, g, np.float32(NEG))
    return out


PF_BLOCKS = ([(i * 128, 128) for i in range(8)] + [(1024, 16)] + [(1040 + i * 128, 128) for i in range(6)]
             + [(1808, 128), (1936, 128), (2064, 128), (2192, 128), (2320, 128), (2448, 128), (2576, 16), (2592, 16)]
             + [(2608 + i * 128, 128) for i in range(6)])
NPF = len(PF_BLOCKS)
PT_GROUPS = ((1552, 256, 0), (3120, 256, 256), (1936, 384, 512))
NPT = 896


def build(debug=None, nlayers=DEPTH):
    debug = debug or {}
    nc = bass.Bass("TRN2", target_bir_lowering=False)
    voff, NV = vec_layout()

    def din(name, shape, dt=F32):
        return nc.dram_tensor(name, list(shape), dt, kind="ExternalInput").ap()

    def dscr(name, shape, dt):
        kind = "ExternalOutput" if name in debug.get("dump", ()) else "Internal"
        return nc.dram_tensor(name, list(shape), dt, kind=kind).ap()

    xT_in = din("xT", [D, L])
    cT_in = din("ctxT", [D, CL])
    cs_in = din("cs", [128, 16])
    vecs_in = din("vecs", [128, NV])
    w_mod = din("w_mod", [DEPTH, D, 6 * D])
    w_in = din("w_in", [DEPTH, D, NIN])
    w_branch = din("w_branch", [DEPTH, D, D])
    w_out = din("w_out", [DEPTH, D, D])
    f_w_in = din("ffn_w_in", [DEPTH, D, 2 * DFF])
    f_w_out = din("ffn_w_out", [DEPTH, DFF, D])
    outT = nc.dram_tensor("outT", [D, L], F32, kind="ExternalOutput").ap()
    y_dbg = din("y_dbg", [D, T], BF16) if debug.get("y_in") else None
    mixers = debug.get("mixers", ("dn", "na", "gla", "df"))
    cmat_in = din("cmat", [128, 512])
    rope_in = din("rope", [2, 128, L])
    nab_in = din("nab", [DEPTH, 4, 128, 21, 128])
    dflam_in = din("dflam", [DEPTH, 128])
    scm_in = din("scm", [128, 2 * NSCM * 128])
    gla_w_in = din("gla_w", [DEPTH, 2, 17, 128])

    wi_b = dscr("wi_b", [DEPTH, D, NIN], BF16)
    wb_b = dscr("wb_b", [DEPTH, D, D], BF16)
    wo_b = dscr("wo_b", [DEPTH, D, D], BF16)
    f1_b = dscr("f1_b", [DEPTH, D, 2 * DFF], BF16)
    f2_b = dscr("f2_b", [DEPTH, DFF, D], BF16)
    xbuf = dscr("xbuf", [D, T], F32)
    x1buf = dscr("x1buf", [D, T], F32)
    hbuf = dscr("hbuf", [D, T], BF16)
    ybuf = dscr("ybuf", [D, T], BF16)
    PF = dscr("PF", [NPF * 128, T], F32)
    PT = dscr("PT", [T, NPT], F32)
    xd = nc.dram_tensor("xd", [DEPTH, D, T], F32, kind="ExternalOutput").ap() if debug.get("xdump") else None

    with contextlib.ExitStack() as top:
        S = Sched(nc, top)

        uid = [0]

        def sbuf(st, name, shape, dt):
            uid[0] += 1
            return st.enter_context(nc.sbuf_tensor("sb%d_%s" % (uid[0], name), list(shape), dt))

        PS = [top.enter_context(nc.psum_tensor("ps%d" % i, [128, 512], F32)) for i in range(8)]

        def MM(out, lhsT, rhs, st, sp, r, w):
            S.op("pe", lambda e: e.matmul(out, lhsT=lhsT, rhs=rhs, start=st, stop=sp), reads=r, writes=w)

        def ACT(out, in_, func, r, w, bias=0.0, scale=1.0):
            S.op("act", lambda e: e.activation(out=out, in_=in_, func=func, bias=bias, scale=scale), reads=r, writes=w)

        def TT(out, in0, in1, op, r, w, eng="dve"):
            S.op(eng, lambda e: e.tensor_tensor(out=out, in0=in0, in1=in1, op=op), reads=r, writes=w)

        def TS(out, in0, s1, s2, op0, op1, r, w, eng="dve"):
            S.op(eng, lambda e: e.tensor_scalar(out=out, in0=in0, scalar1=s1, scalar2=s2, op0=op0, op1=op1), reads=r, writes=w)

        def STT(out, in0, scalar, in1, op0, op1, r, w, eng="dve"):
            S.op(eng, lambda e: e.scalar_tensor_tensor(out=out, in0=in0, scalar=scalar, in1=in1, op0=op0, op1=op1), reads=r, writes=w)

        def CP(out, in_, r, w, eng="dve"):
            S.op(eng, lambda e: e.tensor_copy(out=out, in_=in_), reads=r, writes=w)

        def MSET(ap, val, w, eng="pool"):
            S.op(eng, lambda e: e.memset(ap, val), writes=w)

        def DMA(out, in_, r, w, q="sp"):
            S.op(q, lambda e: e.dma_start(out=out, in_=in_), reads=r, writes=w, dma=True)

        vecs = sbuf(top, "vecs", [128, NV], F32)
        modsb = sbuf(top, "modsb", [128, DEPTH, 48, 2], F32)
        gsc = sbuf(top, "gsc", [128, DEPTH, 2, 8, 2], F32)
        ones_b = sbuf(top, "ones_b", [128, 128], BF16)
        DMA(vecs[:], vecs_in[:], [], ["vecs"])
        MSET(ones_b[:], 1.0, ["ones_b"])

        cmat = sbuf(top, "cmat", [128, 512], F32)
        DMA(cmat[:], cmat_in[:], [], ["cmat"])
        BD32 = cmat[:, 0:128]
        BD64 = cmat[:, 128:256]
        RM = cmat[:, 256:384]
        ONESF = cmat[:, 384:512]

        scm = sbuf(top, "scm", [128, 2, NSCM, 128], F32)
        DMA(scm[:].rearrange("p a b c -> p (a b c)"), scm_in[:], [], ["scm"])

        def CM(d, name):
            return scm[:, d, SCM[name], :]

        def V(name, j=0, n=1):
            o = voff[name] + j
            return vecs[:, o:o + n]

        def cast2d(dst, src, rows, cols, key):
            for r0 in range(0, rows, 1024):
                r1 = min(rows, r0 + 1024)
                for c0 in range(0, cols, 2048):
                    c1 = min(cols, c0 + 2048)
                    DMA(dst[r0:r1, c0:c1], src[r0:r1, c0:c1], [], [key], q="pool")

        layer_list = list(debug.get("layers", range(nlayers)))
        for li in layer_list:
            cast2d(wi_b[li], w_in[li], D, NIN, ("wi_b", li))
            cast2d(wb_b[li], w_branch[li], D, D, ("wb_b", li))
            cast2d(wo_b[li], w_out[li], D, D, ("wo_b", li))
            cast2d(f1_b[li], f_w_in[li], D, 2 * DFF, ("f1_b", li))
            cast2d(f2_b[li], f_w_out[li], DFF, D, ("f2_b", li))

        with contextlib.ExitStack() as ph:
            scs = sbuf(ph, "scs", [128, 16], F32)
            wm = [sbuf(ph, "wm%d" % i, [128, 8, 768], F32) for i in range(2)]
            DMA(scs[:], cs_in[:], [], ["scs"])
            ACT(scs[:], scs[:], AF.Silu, ["scs"], ["scs"])
            nb = 0
            for li in layer_list:
                wv = w_mod[li].rearrange("(k p) c -> p k c", p=128)
                for cb in range(8):
                    wt = wm[nb % 2]
                    wk = ("wm", nb % 2)
                    nb += 1
                    DMA(wt[:], wv[:, :, cb * 768:(cb + 1) * 768], [], [wk])
                    for jj in range(6):
                        j = cb * 6 + jj
                        for k in range(8):
                            MM(PS[0][:, 2 * j:2 * j + 2], wt[:, k, jj * 128:(jj + 1) * 128], scs[:, 2 * k:2 * k + 2],
                               k == 0, k == 7, [wk, "scs"], ["ps0"])
                TT(modsb[:, li], PS[0][:, 0:96].rearrange("p (j s) -> p j s", s=2),
                   V(("bmod", li), 0, 48).rearrange("p (j o) -> p j o", o=1).to_broadcast([128, 48, 2]), ALU.add,
                   ["ps0", "vecs"], ["modsb"])
                for which, (so, gname) in enumerate(((8, "n1g"), (32, "n2g"))):
                    TS(gsc[:, li, which], modsb[:, li, so:so + 8, :], 1.0, 0.0, ALU.add, ALU.add, ["modsb"], ["gsc"])
                    TT(gsc[:, li, which], gsc[:, li, which],
                       V((gname, li), 0, 8).rearrange("p (j o) -> p j o", o=1).to_broadcast([128, 8, 2]), ALU.mult,
                       ["gsc", "vecs"], ["gsc"])
            S.flush()

        def modv(li, what, k, s):
            base = {"sh1": 0, "sc1": 8, "g1": 16, "sh2": 24, "sc2": 32, "g2": 40}[what]
            return modsb[:, li, base + k, s:s + 1]

        def norm_tile(xt, ht, W, li, which, s, tmp_sq, tmp_r, tmp_f, psb, kx, kh, tag):
            shn = "sh1" if which == 0 else "sh2"
            for k in range(8):
                ACT(tmp_sq[:, 0:W], xt[:, k, 0:W], AF.Square, [kx], [tag + "sq"])
                MM(PS[psb][:, 0:W], ones_b[:], tmp_sq[:, 0:W], k == 0, k == 7, ["ones_b", tag + "sq"], ["ps%d" % psb])
            ACT(tmp_r[:, 0:W], PS[psb][:, 0:W], AF.Ln, ["ps%d" % psb], [tag + "r"], bias=EPS, scale=1.0 / D)
            ACT(tmp_r[:, 0:W], tmp_r[:, 0:W], AF.Exp, [tag + "r"], [tag + "r"], scale=-0.5)
            for k in range(8):
                STT(tmp_f[:, 0:W], xt[:, k, 0:W], gsc[:, li, which, k, s:s + 1], tmp_r[:, 0:W], ALU.mult, ALU.mult,
                    [kx, "gsc", tag + "r"], [tag + "f"])
                ACT(ht[:, k, 0:W], tmp_f[:, 0:W], AF.Identity, [tag + "f", "modsb"], [kh], bias=modv(li, shn, k, s))

        for li in layer_list:
            with_ctx = li < DEPTH - 1
            with contextlib.ExitStack() as ph:
                wi = sbuf(ph, "wi", [128, 8, NMIX], BF16)
                xt = [sbuf(ph, "xt%d" % i, [128, 8, 512], F32) for i in range(2)]
                ht = [sbuf(ph, "ht%d" % i, [128, 8, 512], BF16) for i in range(2)]
                sq = sbuf(ph, "sq", [128, 512], BF16)
                rr = sbuf(ph, "rr", [128, 512], F32)
                ff = sbuf(ph, "ff", [128, 512], F32)
                stg = [sbuf(ph, "stg%d" % i, [128, 512], F32) for i in range(4)]
                for k in range(8):
                    DMA(wi[:, k, :], wi_b[li][k * 128:(k + 1) * 128, 0:NMIX], [("wi_b", li)], [("wi", k)])
                wik = [("wi", k) for k in range(8)]
                it = 0
                nst = 0
                for (sname, s0, slen) in SEQS:
                    s = 0 if sname == "lat" else 1
                    for t0 in range(s0, s0 + slen, 512):
                        W = min(512, s0 + slen - t0)
                        b = it % 2
                        it += 1
                        if li == layer_list[0]:
                            src = xT_in[:, t0:t0 + W] if s == 0 else cT_in[:, t0 - L:t0 - L + W]
                        else:
                            src = xbuf[:, t0:t0 + W]
                        DMA(xt[b][:, :, 0:W], src.rearrange("(k p) t -> p k t", p=128), ["xbuf"], [("xt", b)])
                        norm_tile(xt[b], ht[b], W, li, 0, s, sq, rr, ff, 0, ("xt", b), ("ht", b), "n1")
                        DMA(hbuf[:, t0:t0 + W].rearrange("(k p) t -> p k t", p=128), ht[b][:, :, 0:W], [("ht", b)], ["hbuf"])
                        for bi, (c0, ncol) in enumerate(PF_BLOCKS):
                            pb = 1 + (bi % 4)
                            for k in range(8):
                                MM(PS[pb][0:ncol, 0:W], wi[:, k, c0:c0 + ncol], ht[b][:, k, 0:W], k == 0, k == 7,
                                   [("wi", k), ("ht", b)], ["ps%d" % pb])
                            sg = nst % 4
                            nst += 1
                            if bi % 2 == 0:
                                ACT(stg[sg][0:ncol, 0:W], PS[pb][0:ncol, 0:W], AF.Copy, ["ps%d" % pb], [("stg", sg)])
                            else:
                                CP(stg[sg][0:ncol, 0:W], PS[pb][0:ncol, 0:W], ["ps%d" % pb], [("stg", sg)])
                            DMA(PF[bi * 128:bi * 128 + ncol, t0:t0 + W], stg[sg][0:ncol, 0:W], [("stg", sg)], [("PF", bi)])
                        for q in range(W // 128):
                            for gi, (c0, ncol, p0) in enumerate(PT_GROUPS):
                                pb = 5 if gi < 2 else 7
                                pcol = 256 if gi == 1 else 0
                                for k in range(8):
                                    MM(PS[pb][:, pcol:pcol + ncol], ht[b][:, k, q * 128:(q + 1) * 128], wi[:, k, c0:c0 + ncol],
                                       k == 0, k == 7, [("wi", k), ("ht", b)], ["ps%d" % pb])
                            for pb, p0, ncol in ((5, 0, 512), (7, 512, 384)):
                                sg = nst % 4
                                nst += 1
                                CP(stg[sg][:, 0:ncol], PS[pb][:, 0:ncol], ["ps%d" % pb], [("stg", sg)])
                                DMA(PT[t0 + q * 128:t0 + (q + 1) * 128, p0:p0 + ncol], stg[sg][:, 0:ncol], [("stg", sg)], [("PT", p0)])
                S.flush()

            if y_dbg is not None:
                with contextlib.ExitStack() as ph:
                    yt = sbuf(ph, "ycp", [128, 8, 512], BF16)
                    for t0 in range(0, T, 512):
                        W = min(512, T - t0)
                        DMA(yt[:, :, 0:W], y_dbg[:, t0:t0 + W].rearrange("(k p) t -> p k t", p=128), [], ["ycp"])
                        DMA(ybuf[:, t0:t0 + W].rearrange("(k p) t -> p k t", p=128), yt[:, :, 0:W], ["ycp"], ["ybuf"])
                    S.flush()


            BLK = {"dfq": 23, "dfk": 25, "naq": 9, "nak": 11}

            def qk_prep(ph, specs):
                px = [sbuf(ph, "px%d" % i, [128, 512], F32) for i in range(2)]
                psq = sbuf(ph, "psq", [128, 512], F32)
                prs = sbuf(ph, "prs", [128, 512], F32)
                pxn = sbuf(ph, "pxn", [128, 512], F32)
                pa = sbuf(ph, "ppa", [128, 512], F32)
                pb_ = sbuf(ph, "ppb", [128, 512], F32)
                rc = sbuf(ph, "rc", [128, 2, 512], F32)
                n = 0
                for t0 in range(0, T, 512):
                    W = min(512, T - t0)
                    lat = t0 < L
                    if lat and any(sp[3] for sp in specs):
                        DMA(rc[:, 0, :], rope_in[0, :, t0:t0 + 512], [], ["rc"])
                        DMA(rc[:, 1, :], rope_in[1, :, t0:t0 + 512], [], ["rc"])
                    for (blk, gm, gcol, rope, outs) in specs:
                        b = n % 2
                        n += 1
                        DMA(px[b][:, 0:W], PF[blk * 128:(blk + 1) * 128, t0:t0 + W], [], [("px", b)])
                        ACT(psq[:, 0:W], px[b][:, 0:W], AF.Square, [("px", b)], ["psq"])
                        MM(PS[0][:, 0:W], gm, psq[:, 0:W], True, True, ["cmat", "psq"], ["ps0"])
                        ACT(prs[:, 0:W], PS[0][:, 0:W], AF.Ln, ["ps0"], ["prs"], bias=EPS)
                        ACT(prs[:, 0:W], prs[:, 0:W], AF.Exp, ["prs"], ["prs"], scale=-0.5)
                        STT(pxn[:, 0:W], px[b][:, 0:W], gcol, prs[:, 0:W], ALU.mult, ALU.mult, [("px", b), "vecs", "prs"], ["pxn"])
                        val = pxn
                        vk = "pxn"
                        if rope and lat:
                            MM(PS[1][:, 0:W], RM, pxn[:, 0:W], True, True, ["cmat", "pxn"], ["ps1"])
                            TT(pa[:, 0:W], pxn[:, 0:W], rc[:, 0, 0:W], ALU.mult, ["pxn", "rc"], ["ppa"], eng="pool")
                            TT(pb_[:, 0:W], PS[1][:, 0:W], rc[:, 1, 0:W], ALU.mult, ["ps1", "rc"], ["ppb"])
                            TT(pa[:, 0:W], pa[:, 0:W], pb_[:, 0:W], ALU.add, ["ppa", "ppb"], ["ppa"], eng="pool")
                            val = pa
                            vk = "ppa"
                        for (dst, dk, mcol) in outs:
                            TS(dst[:, t0:t0 + W], val[:, 0:W], mcol, 0.0, ALU.mult, ALU.add, [vk, "vecs"], [dk])

            def load_vaug(ph, name, pcol):
                va = sbuf(ph, name, [128, 34, 4, 65], BF16)
                vst = [sbuf(ph, name + "st%d" % i, [128, 256], F32) for i in range(2)]
                MSET(va[:, :, :, 64:65], 1.0, [name])
                for kc in range(34):
                    b = kc % 2
                    DMA(vst[b][:], PT[kc * 128:(kc + 1) * 128, pcol:pcol + 256], [], [(name + "st", b)])
                    CP(va[:, kc, :, 0:64], vst[b][:].rearrange("p (h d) -> p h d", h=4), [(name + "st", b)], [name],
                       eng=("dve" if kc % 2 else "pool"))
                return va

            def attn_norm(ph_tiles, Obank, W, okey):
                osb, rrow, onrm = ph_tiles
                ACT(osb[0:65, 0:W], PS[Obank][0:65, 0:W], AF.Copy, ["ps%d" % Obank], ["osb"])
                S.op("dve", lambda e: e.reciprocal(out=rrow[64:65, 0:W], in_=osb[64:65, 0:W]), reads=["osb"], writes=["rrow"])
                MM(PS[7][0:64, 0:W], ONESF[64:65, 0:64], rrow[64:65, 0:W], True, True, ["cmat", "rrow"], ["ps7"])
                TT(onrm[0:64, 0:W], osb[0:64, 0:W], PS[7][0:64, 0:W], ALU.mult, ["osb", "ps7"], [okey])


            def head_norm_gate(ph, osum, okey, gate_blk, gcol, yrow0):
                gx = [sbuf(ph, "hg_x%d" % i, [128, 512], F32) for i in range(2)]
                gs = sbuf(ph, "hg_s", [128, 512], F32)
                gr = sbuf(ph, "hg_r", [128, 512], F32)
                gy = [sbuf(ph, "hg_y%d" % i, [128, 512], BF16) for i in range(2)]
                n = 0
                for t0 in range(0, T, 512):
                    W = min(512, T - t0)
                    for hp in range(2):
                        b = n % 2
                        n += 1
                        DMA(gx[b][:, 0:W], PF[(gate_blk + hp) * 128:(gate_blk + hp + 1) * 128, t0:t0 + W], [], [("hgx", b)])
                        ACT(gx[b][:, 0:W], gx[b][:, 0:W], AF.Silu, [("hgx", b)], [("hgx", b)])
                        ACT(gs[:, 0:W], osum[:, hp, t0:t0 + W], AF.Square, [okey], ["hgs"])
                        MM(PS[0][:, 0:W], BD64, gs[:, 0:W], True, True, ["cmat", "hgs"], ["ps0"])
                        ACT(gr[:, 0:W], PS[0][:, 0:W], AF.Ln, ["ps0"], ["hgr"], bias=EPS)
                        ACT(gr[:, 0:W], gr[:, 0:W], AF.Exp, ["hgr"], ["hgr"], scale=-0.5)
                        STT(gs[:, 0:W], osum[:, hp, t0:t0 + W], gcol, gr[:, 0:W], ALU.mult, ALU.mult, [okey, "vecs", "hgr"], ["hgs"])
                        TT(gy[b][:, 0:W], gs[:, 0:W], gx[b][:, 0:W], ALU.mult, ["hgs", ("hgx", b)], [("hgy", b)])
                        DMA(ybuf[yrow0 + hp * 128:yrow0 + (hp + 1) * 128, t0:t0 + W], gy[b][:, 0:W], [("hgy", b)], [("ybuf", yrow0)])

            def scan_blocks(d):
                cb = [L + 0, L + 128] if d == 0 else [L + 128, L + 0]
                lb_ = list(range(0, L, 128)) if d == 0 else list(range(L - 128, -1, -128))
                return cb + lb_


            if "dn" in mixers:
                with contextlib.ExitStack() as ph:
                    osum = sbuf(ph, "dno", [128, 2, T], F32)
                    qT = sbuf(ph, "dnq", [128, 2, T], BF16)
                    kT = sbuf(ph, "dnk", [128, 2, T], BF16)
                    vT = sbuf(ph, "dnv", [128, 2, T], BF16)
                    rowsT = sbuf(ph, "dnrows", [16, T], F32)
                    coef = sbuf(ph, "dncoef", [16, 1], F32)
                    I16 = sbuf(ph, "dnI16", [128, 128], BF16)
                    CP(I16[:], CM(0, "I"), ["scm"], ["dnI16"])
                    ACT(coef[:], V(("dnalog", li))[0:16, :], AF.Exp, ["vecs"], ["dncoef"])
                    TS(coef[:], coef[:], -1.0, 0.0, ALU.mult, ALU.add, ["dncoef"], ["dncoef"])
                    DMA(rowsT[:], PF[8 * 128:8 * 128 + 16, :], [], ["dnrows"])
                    for c0 in range(0, T, 1088):
                        sl = rowsT[:, c0:c0 + 1088]
                        ACT(sl, sl, AF.Exp, ["dnrows", "vecs"], ["dnrows"], bias=V(("dndtb", li))[0:16, :], scale=V("dnsgn")[0:16, :])
                        ACT(sl, sl, AF.Ln, ["dnrows"], ["dnrows"], bias=1.0)
                        TS(sl, sl, coef[:, 0:1], 0.0, ALU.mult, ALU.add, ["dnrows", "dncoef"], ["dnrows"])
                    with contextlib.ExitStack() as ph2:
                        xs = [sbuf(ph2, "dnxs%d" % i, [128, 514], F32) for i in range(2)]
                        ca = sbuf(ph2, "dnca", [128, 512], F32)
                        cb2 = sbuf(ph2, "dncb", [128, 512], F32)
                        sq2 = sbuf(ph2, "dnsq", [128, 512], F32)
                        n = 0
                        cw = voff[("dncw", li)]
                        for (sname, s0, slen) in SEQS:
                            for t0 in range(s0, s0 + slen, 512):
                                W = min(512, s0 + slen - t0)
                                a0 = max(t0 - 1, s0)
                                a1_ = min(t0 + W + 1, s0 + slen)
                                for blk in range(6):
                                    b = n % 2
                                    n += 1
                                    off0 = a0 - t0 + 1
                                    DMA(xs[b][:, off0:off0 + (a1_ - a0)], PF[blk * 128:(blk + 1) * 128, a0:a1_], [], [("dnxs", b)])
                                    if t0 == s0:
                                        MSET(xs[b][:, 0:1], 0.0, [("dnxs", b)])
                                    if t0 + W == s0 + slen:
                                        MSET(xs[b][:, W + 1:W + 2], 0.0, [("dnxs", b)])
                                    TS(ca[:, 0:W], xs[b][:, 1:1 + W], vecs[:, cw + 6 + blk:cw + 7 + blk], 0.0, ALU.mult, ALU.add, [("dnxs", b), "vecs"], ["dnca"])
                                    STT(cb2[:, 0:W], xs[b][:, 0:W], vecs[:, cw + blk:cw + blk + 1], ca[:, 0:W], ALU.mult, ALU.add, [("dnxs", b), "vecs", "dnca"], ["dncb"])
                                    STT(ca[:, 0:W], xs[b][:, 2:2 + W], vecs[:, cw + 12 + blk:cw + 13 + blk], cb2[:, 0:W], ALU.mult, ALU.add, [("dnxs", b), "vecs", "dncb"], ["dnca"])
                                    ACT(cb2[:, 0:W], ca[:, 0:W], AF.Silu, ["dnca"], ["dncb"])
                                    if blk >= 4:
                                        CP(vT[:, blk - 4, t0:t0 + W], cb2[:, 0:W], ["dncb"], [("dnv", blk - 4)], eng="pool")
                                        continue
                                    ACT(sq2[:, 0:W], cb2[:, 0:W], AF.Square, ["dncb"], ["dnsq"])
                                    MM(PS[0][:, 0:W], BD64, sq2[:, 0:W], True, True, ["cmat", "dnsq"], ["ps0"])
                                    ACT(sq2[:, 0:W], PS[0][:, 0:W], AF.Ln, ["ps0"], ["dnsq"], bias=EPS / 64)
                                    ACT(sq2[:, 0:W], sq2[:, 0:W], AF.Exp, ["dnsq"], ["dnsq"], scale=-0.5)
                                    if blk < 2:
                                        STT(qT[:, blk, t0:t0 + W], cb2[:, 0:W], 1.0 / 64, sq2[:, 0:W], ALU.mult, ALU.mult, ["dncb", "dnsq"], [("dnq", blk)])
                                    else:
                                        STT(kT[:, blk - 2, t0:t0 + W], cb2[:, 0:W], 1.0 / 8, sq2[:, 0:W], ALU.mult, ALU.mult, ["dncb", "dnsq"], [("dnk", blk - 2)])
                        S.flush()
                    rt = sbuf(ph, "dnrt", [128, 16], F32)
                    gBs = sbuf(ph, "dngB", [128, 128], F32)
                    lBs = sbuf(ph, "dnlB", [128, 128], F32)
                    ekd = sbuf(ph, "dnekd", [128, 16], F32)
                    ebt = sbuf(ph, "dnebt", [128, 8], F32)
                    kv = sbuf(ph, "dnkv", [128, 512], F32)
                    E5 = sbuf(ph, "dnE5", [128, 5, 128], F32)
                    NPQ = debug.get("dn_npq", 2)
                    Pb = [sbuf(ph, "dnP%d" % i, [128, 128], F32) for i in range(NPQ)]
                    Qb = [sbuf(ph, "dnQ%d" % i, [128, 128], F32) for i in range(NPQ)]
                    rF = [sbuf(ph, "dnrF%d" % i, [128, 64], F32) for i in range(2)]
                    for i in range(2):
                        MSET(rF[i][:], 0.0, [("dnrF", i)])
                    X = sbuf(ph, "dnX", [128, 128], F32)
                    X16 = sbuf(ph, "dnX16", [128, 128], BF16)
                    aqk = sbuf(ph, "dnaqk", [128, 128], BF16)
                    kbe = sbuf(ph, "dnkbe", [128, 128], BF16)
                    qd = sbuf(ph, "dnqd", [128, 128], BF16)
                    vb = sbuf(ph, "dnvb", [128, 64], F32)
                    kdc = sbuf(ph, "dnkdc", [128, 64], BF16)
                    r16 = sbuf(ph, "dnr16", [128, 64], BF16)
                    vn16 = sbuf(ph, "dnvn", [128, 64], BF16)
                    Sf = [sbuf(ph, "dnS%d" % h, [128, 64], F32) for h in range(4)]
                    S16 = [sbuf(ph, "dnS16_%d" % h, [128, 64], BF16) for h in range(4)]
                    for d in range(2):
                        for h in range(4):
                            MSET(Sf[h][:], 0.0, [("dnS", h)])
                            MSET(S16[h][:], 0.0, [("dnS16", h)])
                        for t0 in scan_blocks(d)[:debug.get("dn_nblk", 100)]:
                            MM(PS[0][:, 0:16], rowsT[:, t0:t0 + 128], CM(0, "I")[0:16, 0:16], True, True, ["dnrows", "scm"], ["ps0"])
                            CP(rt[:], PS[0][:, 0:16], ["ps0"], ["dnrt"])
                            MM(PS[0][:, 16:32], CM(d, "CKD"), rt[:], True, True, ["scm", "dnrt"], ["ps0"])
                            ACT(ekd[:], PS[0][:, 16:32], AF.Exp, ["ps0"], ["dnekd"])
                            ACT(ebt[:], rt[:, 0:8], AF.Exp, ["dnrt"], ["dnebt"])
                            for i2 in range(2):
                                MM(PS[1][:, i2 * 128:(i2 + 1) * 128], kT[:, i2, t0:t0 + 128], I16[:], True, True, [("dnk", i2), "dnI16"], ["ps1"])
                                MM(PS[1][:, 256 + i2 * 128:256 + (i2 + 1) * 128], vT[:, i2, t0:t0 + 128], I16[:], True, True, [("dnv", i2), "dnI16"], ["ps1"])
                            CP(kv[:], PS[1][:], ["ps1"], ["dnkv"])
                            order = (0, 1) if d == 0 else (1, 0)
                            for h in range(4):
                                dh = d * 4 + h
                                blk = h // 2
                                r0 = (h % 2) * 64
                                CP(gBs[:], rt[:, 8 + dh:9 + dh].to_broadcast([128, 128]), ["dnrt"], ["dngB"])
                                CP(lBs[:], rt[:, dh:dh + 1].to_broadcast([128, 128]), ["dnrt"], ["dnlB"], eng="pool")
                                gB = gBs[:]
                                lB = lBs[:]
                                kr = ["dngB", "dnlB", "scm"]
                                dstage = debug.get("dn_stage", 99)
                                if dstage < 1:
                                    continue
                                MM(PS[2][:, 0:128], gB, CM(d, "U"), True, False, kr, ["ps2"])
                                MM(PS[2][:, 0:128], CM(d, "NU"), gB, False, False, kr, ["ps2"])
                                MM(PS[2][:, 0:128], CM(d, "I"), CM(d, "MI"), False, True, kr, ["ps2"])
                                MM(PS[2][:, 128:256], gB, CM(d, "U"), True, False, kr, ["ps2"])
                                MM(PS[2][:, 128:256], CM(d, "NU"), gB, False, False, kr, ["ps2"])
                                MM(PS[2][:, 128:256], lB, CM(d, "I"), False, False, kr, ["ps2"])
                                MM(PS[2][:, 128:256], CM(d, "I"), CM(d, "MS"), False, True, kr, ["ps2"])
                                MM(PS[2][:, 256:384], CM(d, "U"), gB, True, False, kr, ["ps2"])
                                MM(PS[2][:, 256:384], gB, CM(d, "NU"), False, False, kr, ["ps2"])
                                MM(PS[2][:, 256:384], CM(d, "I"), lB, False, False, kr, ["ps2"])
                                MM(PS[2][:, 256:384], CM(d, "I"), CM(d, "MST"), False, True, kr, ["ps2"])
                                MM(PS[2][:, 384:512], gB, CM(d, "U"), True, True, kr, ["ps2"])
                                MM(PS[3][:, 0:128], gB, CM(d, "U"), True, False, kr, ["ps3"])
                                MM(PS[3][:, 0:128], lB, CM(d, "I"), False, True, kr, ["ps3"])
                                ACT(E5[:, 0:4, :].rearrange("p a c -> p (a c)"), PS[2][:], AF.Exp, ["ps2"], ["dnE5"])
                                ACT(E5[:, 4, :], PS[3][:, 0:128], AF.Exp, ["ps3"], ["dnE5"])
                                kTh = kT[r0:r0 + 64, blk, t0:t0 + 128]
                                if dstage < 2:
                                    continue
                                MM(PS[3][:, 128:256], kTh, kTh, True, True, [("dnk", blk)], ["ps3"])
                                MM(PS[3][:, 256:384], kTh, qT[r0:r0 + 64, blk, t0:t0 + 128], True, True, [("dnk", blk), ("dnq", blk)], ["ps3"])
                                if debug.get("dn_sub", 9) < 1:
                                    continue
                                STT(Qb[0][:], PS[3][:, 128:256], -1.0, E5[:, 1, :], ALU.mult, ALU.mult, ["ps3", "dnE5"], [("dnQ", 0)])
                                STT(Pb[0][:], PS[3][:, 128:256], -1.0, E5[:, 2, :], ALU.mult, ALU.mult, ["ps3", "dnE5"], [("dnP", 0)])
                                TT(aqk[:], PS[3][:, 256:384], E5[:, 0, :], ALU.mult, ["ps3", "dnE5"], ["dnaqk"])
                                if debug.get("dn_sub", 9) < 2:
                                    continue
                                TT(X[:], Qb[0][:], CM(0, "I"), ALU.add, [("dnQ", 0), "scm"], ["dnX"])
                                if dstage < 3:
                                    continue
                                for lvl in range(debug.get("dn_lvl", 5)):
                                    a, bn = lvl % NPQ, (lvl + 1) % NPQ
                                    MM(PS[4][:, 0:128], Qb[a][:], Pb[a][:], True, True, [("dnQ", a), ("dnP", a)], ["ps4"])
                                    ACT(Pb[bn][:], PS[4][:, 0:128], AF.Copy, ["ps4"], [("dnP", bn)])
                                    if lvl < 4:
                                        MM(PS[0][:, 128:256], Pb[a][:], Qb[a][:], True, True, [("dnQ", a), ("dnP", a)], ["ps0"])
                                        CP(Qb[bn][:], PS[0][:, 128:256], ["ps0"], [("dnQ", bn)])
                                    MM(PS[1][:, 256:384], Pb[bn][:], X[:], True, True, [("dnP", bn), "dnX"], ["ps1"])
                                    TT(X[:], X[:], PS[1][:, 256:384], ALU.add, ["dnX", "ps1"], ["dnX"])
                                if dstage < 4:
                                    continue
                                TT(kbe[r0:r0 + 64, :], kTh, E5[r0:r0 + 64, 4, :], ALU.mult, [("dnk", blk), "dnE5"], ["dnkbe"])
                                TT(qd[r0:r0 + 64, :], qT[r0:r0 + 64, blk, t0:t0 + 128], E5[r0:r0 + 64, 3, :], ALU.mult, [("dnq", blk), "dnE5"], ["dnqd"])
                                TS(vb[:], kv[:, 256 + h * 64:256 + (h + 1) * 64], ebt[:, dh:dh + 1], 0.0, ALU.mult, ALU.add, ["dnkv", "dnebt"], ["dnvb"])
                                TS(kdc[:], kv[:, h * 64:(h + 1) * 64], ekd[:, 8 + dh:9 + dh], 0.0, ALU.mult, ALU.add, ["dnkv", "dnekd"], ["dnkdc"])
                                orow = (h % 2) * 64
                                pO = 6 + h // 2
                                if dstage < 5:
                                    continue
                                for i in order:
                                    c0 = i * 64
                                    MM(PS[5][c0:c0 + 64, 0:64], kbe[r0:r0 + 64, c0:c0 + 64], S16[h][r0:r0 + 64, :], True, True, ["dnkbe", ("dnS16", h)], ["ps5"])
                                    TT(rF[i][c0:c0 + 64, :], vb[c0:c0 + 64, :], PS[5][c0:c0 + 64, 0:64], ALU.subtract, ["dnvb", "ps5"], [("dnrF", i)])
                                    MM(PS[5][:, 64:128], X[:], rF[i][:], True, True, ["dnX", ("dnrF", i)], ["ps5"])
                                    ACT(vn16[c0:c0 + 64, :], PS[5][c0:c0 + 64, 64:128], AF.Copy, ["ps5"], ["dnvn"])
                                    MM(PS[pO][orow:orow + 64, c0:c0 + 64], S16[h][r0:r0 + 64, :], qd[r0:r0 + 64, c0:c0 + 64], True, False,
                                       [("dnS16", h), "dnqd"], ["ps%d" % pO])
                                    MM(PS[pO][orow:orow + 64, c0:c0 + 64], vn16[c0:c0 + 64, :], aqk[c0:c0 + 64, c0:c0 + 64], False, True,
                                       ["dnvn", "dnaqk"], ["ps%d" % pO])
                                    MM(PS[5][r0:r0 + 64, 128:192], kdc[c0:c0 + 64, :], vn16[c0:c0 + 64, :], True, True, ["dnkdc", "dnvn"], ["ps5"])
                                    last = c0 + 63 if d == 0 else c0
                                    STT(Sf[h][r0:r0 + 64, :], Sf[h][r0:r0 + 64, :], E5[r0:r0 + 64, 3, last:last + 1], PS[5][r0:r0 + 64, 128:192],
                                        ALU.mult, ALU.add, [("dnS", h), "dnE5", "ps5"], [("dnS", h)])
                                    ACT(S16[h][r0:r0 + 64, :], Sf[h][r0:r0 + 64, :], AF.Copy, [("dnS", h)], [("dnS16", h)])
                            if debug.get("dn_stage", 99) < 5:
                                continue
                            for hp in range(2):
                                if d == 0:
                                    ACT(osum[:, hp, t0:t0 + 128], PS[6 + hp][:, 0:128], AF.Copy, ["ps%d" % (6 + hp)], ["dno"])
                                else:
                                    TT(osum[:, hp, t0:t0 + 128], osum[:, hp, t0:t0 + 128], PS[6 + hp][:, 0:128], ALU.add, ["dno", "ps%d" % (6 + hp)], ["dno"])
                    head_norm_gate(ph, osum, "dno", 6, V(("dnng", li)), 0)
                    S.flush()

            if "gla" in mixers:
                with contextlib.ExitStack() as ph:
                    osum = sbuf(ph, "glo", [128, 2, T], F32)
                    qT = sbuf(ph, "glq", [128, T], F32)
                    kT = sbuf(ph, "glk", [128, T], F32)
                    a1 = [sbuf(ph, "gla1_%d" % i, [17, T], F32) for i in range(2)]
                    wa = sbuf(ph, "glwa", [17, 2, 128], F32)
                    DMA(qT[:], PF[15 * 128:16 * 128, :], [], ["glq"])
                    DMA(kT[:], PF[16 * 128:17 * 128, :], [], ["glk"])
                    for d in range(2):
                        MSET(a1[d][:], 1.0, [("gla1", d)])
                        DMA(a1[d][0:16, :], PF[(21 + d) * 128:(21 + d) * 128 + 16, :], [], [("gla1", d)])
                        DMA(wa[:, d, :], gla_w_in[li, d], [], ["glwa"])
                    ktok = [sbuf(ph, "glkt%d" % i, [128, 384], F32) for i in range(2)]
                    vb16 = [sbuf(ph, "glvb%d" % i, [128, 256], BF16) for i in range(2)]
                    ln_ = sbuf(ph, "glln", [128, 128], F32)
                    eq = sbuf(ph, "gleq", [128, 128], F32)
                    ek = sbuf(ph, "glek", [128, 128], F32)
                    eb = sbuf(ph, "gleb", [128, 128], F32)
                    ekd = sbuf(ph, "glekd", [128, 128], F32)
                    qt_ = sbuf(ph, "glqt", [128, 128], F32)
                    ktl = sbuf(ph, "glktl", [128, 128], BF16)
                    qb = sbuf(ph, "glqb", [128, 128], F32)
                    qth = sbuf(ph, "glqth", [128, 4, 128], BF16)
                    qbh = sbuf(ph, "glqbh", [128, 4, 128], BF16)
                    kdec = sbuf(ph, "glkdec", [128, 128], BF16)
                    S16 = sbuf(ph, "glS16", [128, 64], BF16)
                    Ah = sbuf(ph, "glA", [128, 4, 128], BF16)
                    dsm = sbuf(ph, "gldsm", [128, 4, 64], F32)
                    dsr = sbuf(ph, "gldsr", [128, 64], F32)
                    Sst = sbuf(ph, "glS", [128, 64], F32)
                    osb = sbuf(ph, "glosb", [128, 2, 128], F32)
                    sc = 32 ** -0.5
                    nb = 0
                    for d in range(2):
                        MSET(Sst[:], 0.0, ["glS"])
                        for t0 in scan_blocks(d)[:debug.get("gla_nblk", 100)]:
                            b = nb % 2
                            nb += 1
                            stage = debug.get("gla_stage", 99)
                            DMA(ktok[b][:], PT[t0:t0 + 128, 512:896], [], [("glkt", b)])
                            CP(vb16[b][:], ktok[b][:, 128:384], [("glkt", b)], [("glvb", b)], eng="pool")
                            MM(PS[0][:, 0:128], a1[d][:, t0:t0 + 128], wa[:, d, :], True, True, [("gla1", d), "glwa"], ["ps0"])
                            ACT(ln_[:], PS[0][:, 0:128], AF.Exp, ["ps0"], ["glln"], scale=-1.0)
                            ACT(ln_[:], ln_[:], AF.Ln, ["glln"], ["glln"], bias=1.0)
                            if stage < 1:
                                continue
                            MM(PS[1][:, 0:128], ln_[:], CM(d, "CQ"), True, True, ["glln", "scm"], ["ps1"])
                            MM(PS[1][:, 128:256], ln_[:], CM(d, "U"), True, True, ["glln", "scm"], ["ps1"])
                            MM(PS[1][:, 256:384], CM(d, "CKD"), ln_[:], True, True, ["glln", "scm"], ["ps1"])
                            ACT(eq[:], PS[1][:, 0:128], AF.Exp, ["ps1"], ["gleq"], scale=-1.0 / 16)
                            ACT(ek[:], PS[1][:, 0:128], AF.Exp, ["ps1"], ["glek"], scale=1.0 / 16)
                            ACT(eb[:], PS[1][:, 128:256], AF.Exp, ["ps1"], ["gleb"], scale=-1.0 / 16)
                            ACT(ekd[:], PS[1][:, 256:384], AF.Exp, ["ps1"], ["glekd"], scale=-1.0 / 16)
                            STT(qt_[:], qT[:, t0:t0 + 128], sc, eq[:], ALU.mult, ALU.mult, ["glq", "gleq"], ["glqt"])
                            TT(ktl[:], kT[:, t0:t0 + 128], ek[:], ALU.mult, ["glk", "glek"], ["glktl"])
                            STT(qb[:], qT[:, t0:t0 + 128], sc, eb[:], ALU.mult, ALU.mult, ["glq", "gleb"], ["glqb"])
                            TT(kdec[:], ktok[b][:, 0:128], ekd[:], ALU.mult, [("glkt", b), "glekd"], ["glkdec"], eng="pool")
                            if stage < 2:
                                continue
                            for h in range(4):
                                TS(qth[:, h, :], qt_[:], V("hm", h), 0.0, ALU.mult, ALU.add, ["glqt", "vecs"], ["glqth"])
                                TS(qbh[:, h, :], qb[:], V("hm", h), 0.0, ALU.mult, ALU.add, ["glqb", "vecs"], ["glqbh"])
                            if stage < 3:
                                continue
                            for h in range(4):
                                MM(PS[2][:, h * 128:(h + 1) * 128], ktl[:], qth[:, h, :], True, True, ["glktl", "glqth"], ["ps2"])
                            TT(Ah[:], PS[2][:].rearrange("p (h c) -> p h c", h=4),
                               CM(d, "M01").rearrange("p (o c) -> p o c", o=1).to_broadcast([128, 4, 128]), ALU.mult, ["ps2", "scm"], ["glA"])
                            order = (0, 1) if d == 0 else (1, 0)
                            if stage < 4:
                                continue
                            for h in range(4):
                                orow = (h % 2) * 64
                                pO = 3 + h // 2
                                MM(PS[pO][orow:orow + 64, 0:128], vb16[b][:, h * 64:(h + 1) * 64], Ah[:, h, :], True, False,
                                   [("glvb", b), "glA"], ["ps%d" % pO])
                            for ii, i in enumerate(order):
                                c0 = i * 64
                                CP(S16[:], Sst[:], ["glS"], ["glS16"], eng="pool")
                                for h in range(4):
                                    orow = (h % 2) * 64
                                    pO = 3 + h // 2
                                    MM(PS[pO][orow:orow + 64, c0:c0 + 64], S16[:, :], qbh[:, h, c0:c0 + 64], False, ii == 1,
                                       ["glS16", "glqbh"], ["ps%d" % pO])
                                MM(PS[5][:, 0:256], kdec[c0:c0 + 64, :], vb16[b][c0:c0 + 64, :], True, True, ["glkdec", ("glvb", b)], ["ps5"])
                                TT(dsm[:], PS[5][:, 0:256].rearrange("p (h v) -> p h v", h=4),
                                   V("hm", 0, 4).rearrange("p (h o) -> p h o", o=1).to_broadcast([128, 4, 64]), ALU.mult, ["ps5", "vecs"], ["gldsm"])
                                S.op("dve", lambda e: e.reduce_sum(out=dsr[:], in_=dsm[:].rearrange("p h v -> p v h"), axis=AX.X),
                                     reads=["gldsm"], writes=["gldsr"])
                                last = c0 + 63 if d == 0 else c0
                                STT(Sst[:], Sst[:], eb[:, last:last + 1], dsr[:], ALU.mult, ALU.add, ["glS", "gleb", "gldsr"], ["glS"])
                            for hp in range(2):
                                if d == 0:
                                    ACT(osum[:, hp, t0:t0 + 128], PS[3 + hp][:, 0:128], AF.Copy, ["ps%d" % (3 + hp)], ["glo"])
                                else:
                                    TT(osum[:, hp, t0:t0 + 128], osum[:, hp, t0:t0 + 128], PS[3 + hp][:, 0:128], ALU.add, ["glo", "ps%d" % (3 + hp)], ["glo"])
                    if debug.get("gla_stage", 99) >= 99:
                        head_norm_gate(ph, osum, "glo", 19, V(("glng", li)), 512)
                    S.flush()

            if "df" in mixers:
                with contextlib.ExitStack() as ph:
                    qr = sbuf(ph, "dfqr", [128, 2, T], BF16)
                    k1z = sbuf(ph, "dfk1", [128, 2, T], BF16)
                    k2z = sbuf(ph, "dfk2", [128, 2, T], BF16)
                    with contextlib.ExitStack() as ph2:
                        specs = []
                        for i in range(2):
                            specs.append((BLK["dfq"] + i, BD32, V(("dfqn", li)), True, [(qr[:, i, :], ("dfqr", i), V("one"))]))
                            specs.append((BLK["dfk"] + i, BD32, V(("dfkn", li)), True,
                                          [(k1z[:, i, :], ("dfk1", i), V("m1")), (k2z[:, i, :], ("dfk2", i), V("m2"))]))
                        qk_prep(ph2, specs)
                        S.flush()
                    va = load_vaug(ph, "dfva", 256)
                    lp = sbuf(ph, "lp", [128, 2, 2, 32], F32)
                    lpp = sbuf(ph, "lpp", [128, 2, 32], F32)
                    lps = sbuf(ph, "lps", [128, 2], F32)
                    nlam = sbuf(ph, "nlam", [128, 1], F32)
                    lam_init = 0.8 - 0.6 * math.exp(-0.3 * li)
                    DMA(lp[:].rearrange("p a b d -> p (a b d)"), dflam_in[li:li + 1, :].partition_broadcast(128), [], ["lp"])
                    TT(lpp[:], lp[:, :, 0, :], lp[:, :, 1, :], ALU.mult, ["lp"], ["lpp"])
                    S.op("dve", lambda e: e.reduce_sum(out=lps[:], in_=lpp[:], axis=AX.X), reads=["lpp"], writes=["lps"])
                    ACT(lps[:], lps[:], AF.Exp, ["lps"], ["lps"])
                    TT(nlam[:], lps[:, 1:2], lps[:, 0:1], ALU.subtract, ["lps"], ["nlam"])
                    TS(nlam[:], nlam[:], -lam_init, 0.0, ALU.add, ALU.add, ["nlam"], ["nlam"])
                    E = [sbuf(ph, "dfE%d" % i, [128, 512], BF16) for i in range(4)]
                    osb = sbuf(ph, "osb", [128, 512], F32)
                    rrow = sbuf(ph, "rrow", [128, 512], F32)
                    o1 = sbuf(ph, "o1n", [128, 512], F32)
                    o2 = sbuf(ph, "o2n", [128, 512], F32)
                    dsq = sbuf(ph, "dsq", [128, 512], F32)
                    yo = [sbuf(ph, "dfy%d" % i, [128, 512], BF16) for i in range(2)]
                    sc = 32 ** -0.5
                    ne = 0
                    ny = 0
                    qtiles = [(t0, 512, list(range(34))) for t0 in range(0, L, 512)]
                    if with_ctx:
                        qtiles.append((L, CL, [32, 33]))
                    for (t0, W, kcs) in qtiles:
                        for h in range(4):
                            blk = h // 2
                            r0 = (h % 2) * 64
                            for ci, kc in enumerate(kcs):
                                for t, kz in enumerate((k1z, k2z)):
                                    sb_ = 2 + (ne % 4)
                                    eb = ne % 4
                                    ne += 1
                                    MM(PS[sb_][:, 0:W], kz[r0:r0 + 64, blk, kc * 128:(kc + 1) * 128], qr[r0:r0 + 64, blk, t0:t0 + W],
                                       True, True, [("dfk%d" % (t + 1), blk), ("dfqr", blk)], ["ps%d" % sb_])
                                    ACT(E[eb][:, 0:W], PS[sb_][:, 0:W], AF.Exp, ["ps%d" % sb_], [("dfE", eb)], scale=sc)
                                    MM(PS[t][0:65, 0:W], va[:, kc, h, :], E[eb][:, 0:W], ci == 0, ci == len(kcs) - 1,
                                       ["dfva", ("dfE", eb)], ["ps%d" % t])
                            attn_norm((osb, rrow, o1), 0, W, "o1n")
                            attn_norm((osb, rrow, o2), 1, W, "o2n")
                            STT(o1[0:64, 0:W], o2[0:64, 0:W], nlam[0:64, 0:1], o1[0:64, 0:W], ALU.mult, ALU.add,
                                ["o1n", "o2n", "nlam"], ["o1n"])
                            ACT(dsq[0:64, 0:W], o1[0:64, 0:W], AF.Square, ["o1n"], ["dsq"])
                            MM(PS[7][0:64, 0:W], BD64[0:64, 0:64], dsq[0:64, 0:W], True, True, ["cmat", "dsq"], ["ps7"])
                            ACT(dsq[0:64, 0:W], PS[7][0:64, 0:W], AF.Ln, ["ps7"], ["dsq"], bias=EPS)
                            ACT(dsq[0:64, 0:W], dsq[0:64, 0:W], AF.Exp, ["dsq"], ["dsq"], scale=-0.5)
                            yb_ = ny % 2
                            ny += 1
                            STT(dsq[0:64, 0:W], o1[0:64, 0:W], V(("dfng", li))[0:64, :], dsq[0:64, 0:W], ALU.mult, ALU.mult,
                                ["o1n", "vecs", "dsq"], ["dsq"])
                            TS(yo[yb_][0:64, 0:W], dsq[0:64, 0:W], 1.0 - lam_init, 0.0, ALU.mult, ALU.add, ["dsq"], [("dfy", yb_)])
                            DMA(ybuf[768 + h * 64:768 + (h + 1) * 64, t0:t0 + W], yo[yb_][0:64, 0:W], [("dfy", yb_)], [("ybuf", "df")])
                    S.flush()

            if "na" in mixers:
                with contextlib.ExitStack() as ph:
                    qn = sbuf(ph, "naq", [128, 2, T], BF16)
                    kn = sbuf(ph, "nak", [128, 2, T], BF16)
                    with contextlib.ExitStack() as ph2:
                        specs = []
                        for i in range(2):
                            specs.append((BLK["naq"] + i, BD64, V(("naqn", li)), False, [(qn[:, i, :], ("naq", i), V("one"))]))
                            specs.append((BLK["nak"] + i, BD64, V(("nakn", li)), False, [(kn[:, i, :], ("nak", i), V("one"))]))
                        qk_prep(ph2, specs)
                        S.flush()
                    va = load_vaug(ph, "nava", 0)
                    bias = [sbuf(ph, "nab%d" % i, [128, 21, 128], F32) for i in range(2)]
                    sbt = [sbuf(ph, "nasb%d" % i, [128, 640], F32) for i in range(2)]
                    E = [sbuf(ph, "naE%d" % i, [128, 896], BF16) for i in range(2)]
                    osb = sbuf(ph, "osb", [128, 512], F32)
                    rrow = sbuf(ph, "rrow", [128, 512], F32)
                    o1 = sbuf(ph, "o1n", [128, 512], F32)
                    yo = [sbuf(ph, "nay%d" % i, [128, 512], BF16) for i in range(2)]
                    sc = 64 ** -0.5
                    n = 0
                    ny = 0
                    for h in range(4):
                        blk = h // 2
                        r0 = (h % 2) * 64
                        hb = h % 2
                        DMA(bias[hb][:], nab_in[li, h], [], [("nab", hb)])
                        for rg in range(8):
                            for rr_ in range(4):
                                rp = rg * 4 + rr_
                                chunks = na_chunks(rp)
                                b = n % 2
                                n += 1
                                pA = 2 + 2 * b
                                pB = pA + 1
                                q_ap = qn[r0:r0 + 64, blk, rp * 128:(rp + 1) * 128]
                                nw = len(chunks)
                                for j, (kc, bi) in enumerate(chunks):
                                    pbk, pc = (pA, j * 128) if j < 4 else (pB, 0)
                                    MM(PS[pbk][:, pc:pc + 128], kn[r0:r0 + 64, blk, kc * 128:(kc + 1) * 128], q_ap, True, True,
                                       [("nak", blk), ("naq", blk)], ["ps%d" % pbk])
                                for j2 in range(2):
                                    pc = 128 + j2 * 128
                                    MM(PS[pB][:, pc:pc + 128], kn[r0:r0 + 64, blk, L + j2 * 128:L + (j2 + 1) * 128], q_ap, True, True,
                                       [("nak", blk), ("naq", blk)], ["ps%d" % pB])
                                bi0 = chunks[0][1]
                                STT(sbt[b][:, 0:512], PS[pA][:, 0:512], sc, bias[hb][:, bi0:bi0 + 4, :].rearrange("p a q -> p (a q)"),
                                    ALU.mult, ALU.add, ["ps%d" % pA, ("nab", hb)], [("nasb", b)])
                                if nw == 5:
                                    STT(sbt[b][:, 512:640], PS[pB][:, 0:128], sc, bias[hb][:, 4, :], ALU.mult, ALU.add,
                                        ["ps%d" % pB, ("nab", hb)], [("nasb", b)])
                                ACT(E[b][:, 0:nw * 128], sbt[b][:, 0:nw * 128], AF.Exp, [("nasb", b)], [("naE", b)])
                                ACT(E[b][:, 640:896], PS[pB][:, 128:384], AF.Exp, ["ps%d" % pB], [("naE", b)], scale=sc)
                                ecols = [(kc, j * 128) for j, (kc, bi) in enumerate(chunks)] + [(32, 640), (33, 768)]
                                for ci, (kc, ec) in enumerate(ecols):
                                    MM(PS[0][0:65, rr_ * 128:(rr_ + 1) * 128], va[:, kc, h, :], E[b][:, ec:ec + 128], ci == 0, ci == len(ecols) - 1,
                                       ["nava", ("naE", b)], ["ps0"])
                            attn_norm((osb, rrow, o1), 0, 512, "o1n")
                            yb_ = ny % 2
                            ny += 1
                            CP(yo[yb_][0:64, :], o1[0:64, :], ["o1n"], [("nay", yb_)], eng="pool")
                            DMA(ybuf[256 + h * 64:256 + (h + 1) * 64, rg * 512:(rg + 1) * 512], yo[yb_][0:64, :], [("nay", yb_)], [("ybuf", "na")])
                        if with_ctx:
                            q_ap = qn[r0:r0 + 64, blk, L:L + CL]
                            for j2 in range(2):
                                MM(PS[1][:, j2 * 256:(j2 + 1) * 256], kn[r0:r0 + 64, blk, L + j2 * 128:L + (j2 + 1) * 128], q_ap, True, True,
                                   [("nak", blk), ("naq", blk)], ["ps1"])
                            ACT(E[0][:, 0:512], PS[1][:, 0:512], AF.Exp, ["ps1"], [("naE", 0)], scale=sc)
                            for j2 in range(2):
                                MM(PS[0][0:65, 0:256], va[:, 32 + j2, h, :], E[0][:, j2 * 256:(j2 + 1) * 256], j2 == 0, j2 == 1,
                                   ["nava", ("naE", 0)], ["ps0"])
                            attn_norm((osb, rrow, o1), 0, 256, "o1n")
                            yb_ = ny % 2
                            ny += 1
                            CP(yo[yb_][0:64, 0:256], o1[0:64, 0:256], ["o1n"], [("nay", yb_)], eng="pool")
                            DMA(ybuf[256 + h * 64:256 + (h + 1) * 64, L:L + CL], yo[yb_][0:64, 0:256], [("nay", yb_)], [("ybuf", "na")])
                    S.flush()

            with contextlib.ExitStack() as ph:
                wg = sbuf(ph, "wg", [128, 8, 4096], BF16)
                wbr = sbuf(ph, "wbr", [128, 8, D], BF16)
                wo = sbuf(ph, "wo", [128, 8, D], BF16)
                xt = [sbuf(ph, "xt%d" % i, [128, 8, 512], F32) for i in range(2)]
                ht = [sbuf(ph, "ht%d" % i, [128, 8, 512], BF16) for i in range(2)]
                yt = [sbuf(ph, "yt%d" % i, [128, 8, 512], BF16) for i in range(2)]
                mg = sbuf(ph, "mg", [128, 8, 512], BF16)
                sig = [sbuf(ph, "sig%d" % i, [128, 512], F32) for i in range(2)]
                acc = sbuf(ph, "acc", [128, 512], F32)
                tmp = sbuf(ph, "tmp", [128, 512], F32)
                for k in range(8):
                    DMA(wg[:, k, :], wi_b[li][k * 128:(k + 1) * 128, NMIX:NIN], [("wi_b", li)], [("wg", k)])
                    DMA(wbr[:, k, :], wb_b[li][k * 128:(k + 1) * 128, :], [("wb_b", li)], [("wbr", k)])
                    DMA(wo[:, k, :], wo_b[li][k * 128:(k + 1) * 128, :], [("wo_b", li)], [("wo", k)])
                it = 0
                ng = 0
                for (sname, s0, slen) in SEQS:
                    s = 0 if sname == "lat" else 1
                    if s == 1 and not with_ctx:
                        continue
                    for t0 in range(s0, s0 + slen, 512):
                        W = min(512, s0 + slen - t0)
                        b = it % 2
                        it += 1
                        if li == layer_list[0]:
                            src = xT_in[:, t0:t0 + W] if s == 0 else cT_in[:, t0 - L:t0 - L + W]
                        else:
                            src = xbuf[:, t0:t0 + W]
                        DMA(xt[b][:, :, 0:W], src.rearrange("(k p) t -> p k t", p=128), ["xbuf"], [("xt", b)])
                        DMA(ht[b][:, :, 0:W], hbuf[:, t0:t0 + W].rearrange("(k p) t -> p k t", p=128), ["hbuf"], [("ht", b)])
                        DMA(yt[b][:, :, 0:W], ybuf[:, t0:t0 + W].rearrange("(k p) t -> p k t", p=128), ["ybuf"], [("yt", b)])
                        for dc in range(8):
                            for g in range(4):
                                pa = 2 * (ng % 2)
                                pbk = pa + 1
                                sgi = ng % 2
                                ng += 1
                                cg = g * D + dc * 128
                                for k in range(8):
                                    MM(PS[pa][:, 0:W], wg[:, k, cg:cg + 128], ht[b][:, k, 0:W], k == 0, k == 7,
                                       [("wg", k), ("ht", b)], ["ps%d" % pa])
                                for k2 in range(2):
                                    MM(PS[pbk][:, 0:W], wbr[:, 2 * g + k2, dc * 128:(dc + 1) * 128], yt[b][:, 2 * g + k2, 0:W],
                                       k2 == 0, k2 == 1, [("wbr", 2 * g + k2), ("yt", b)], ["ps%d" % pbk])
                                ACT(sig[sgi][:, 0:W], PS[pa][:, 0:W], AF.Sigmoid, ["ps%d" % pa, "vecs"], [("sig", sgi)],
                                    bias=V(("bgate", li), g * 8 + dc))
                                if g == 0:
                                    TT(acc[:, 0:W], sig[sgi][:, 0:W], PS[pbk][:, 0:W], ALU.mult, [("sig", sgi), "ps%d" % pbk], ["acc"])
                                else:
                                    TT(tmp[:, 0:W], sig[sgi][:, 0:W], PS[pbk][:, 0:W], ALU.mult, [("sig", sgi), "ps%d" % pbk], ["tmp"])
                                    if g < 3:
                                        TT(acc[:, 0:W], acc[:, 0:W], tmp[:, 0:W], ALU.add, ["acc", "tmp"], ["acc"], eng="pool")
                                    else:
                                        TT(mg[:, dc, 0:W], acc[:, 0:W], tmp[:, 0:W], ALU.add, ["acc", "tmp"], [("mg", dc)], eng="pool")
                        for dc in range(8):
                            pb = 4 + dc % 2
                            for k in range(8):
                                MM(PS[pb][:, 0:W], wo[:, k, dc * 128:(dc + 1) * 128], mg[:, k, 0:W], k == 0, k == 7,
                                   [("wo", k), ("mg", k)], ["ps%d" % pb])
                            STT(xt[b][:, dc, 0:W], PS[pb][:, 0:W], modv(li, "g1", dc, s), xt[b][:, dc, 0:W], ALU.mult, ALU.add,
                                ["ps%d" % pb, "modsb", ("xt", b)], [("xt", b)])
                        DMA(x1buf[:, t0:t0 + W].rearrange("(k p) t -> p k t", p=128), xt[b][:, :, 0:W], [("xt", b)], ["x1buf"])
                S.flush()

            with contextlib.ExitStack() as ph:
                FT = 456
                xt = [sbuf(ph, "xt%d" % i, [128, 8, 512], F32) for i in range(2)]
                ht = sbuf(ph, "ht", [128, 8, 512], BF16)
                gt = sbuf(ph, "gt", [128, 22, 512], BF16)
                sq = sbuf(ph, "sq", [128, 512], BF16)
                rr = sbuf(ph, "rr", [128, 512], F32)
                ff = sbuf(ph, "ff", [128, 512], F32)
                ust = [sbuf(ph, "ust%d" % i, [128, 516], F32) for i in range(2)]
                ca = sbuf(ph, "ca", [128, 512], F32)
                cb_ = sbuf(ph, "cb", [128, 512], F32)
                w1 = [sbuf(ph, "w1_%d" % i, [128, 8, 256], BF16) for i in range(3)]
                w2f = sbuf(ph, "w2f", [128, 22, D], BF16)
                for j in range(22):
                    DMA(w2f[:, j, :], f2_b[li][j * 128:(j + 1) * 128, :], [("f2_b", li)], [("w2f", j)])
                it = 0
                nw = 0
                for (sname, s0, slen) in SEQS:
                    s = 0 if sname == "lat" else 1
                    if s == 1 and not with_ctx:
                        continue
                    for t0 in range(s0, s0 + slen, FT):
                        t1 = min(t0 + FT, s0 + slen)
                        a0 = max(t0 - 1, s0)
                        a1 = min(t1 + 1, s0 + slen)
                        W = a1 - a0
                        WI = t1 - t0
                        io = t0 - a0
                        b = it % 2
                        it += 1
                        DMA(xt[b][:, :, 0:W], x1buf[:, a0:a1].rearrange("(k p) t -> p k t", p=128), ["x1buf"], [("xt", b)])
                        norm_tile(xt[b], ht, W, li, 1, s, sq, rr, ff, 0, ("xt", b), "ht", "n2")
                        for j in range(22):
                            wb = nw % 3
                            nw += 1
                            DMA(w1[wb][:, :, 0:128], f1_b[li][:, j * 128:(j + 1) * 128].rearrange("(k p) c -> p k c", p=128),
                                [("f1_b", li)], [("w1", wb)])
                            DMA(w1[wb][:, :, 128:256], f1_b[li][:, DFF + j * 128:DFF + (j + 1) * 128].rearrange("(k p) c -> p k c", p=128),
                                [("f1_b", li)], [("w1", wb)])
                            pu = 1 + 2 * (j % 2)
                            pv = pu + 1
                            ub = j % 2
                            for k in range(8):
                                MM(PS[pu][:, 0:W], w1[wb][:, k, 0:128], ht[:, k, 0:W], k == 0, k == 7, [("w1", wb), "ht"], ["ps%d" % pu])
                            for k in range(8):
                                MM(PS[pv][:, 0:W], w1[wb][:, k, 128:256], ht[:, k, 0:W], k == 0, k == 7, [("w1", wb), "ht"], ["ps%d" % pv])
                            c_in = 1 - io
                            ACT(ust[ub][:, c_in + 0:c_in + W], PS[pu][:, 0:W], AF.Copy, ["ps%d" % pu], [("ust", ub)])
                            if io == 0:
                                MSET(ust[ub][:, 0:1], 0.0, [("ust", ub)])
                            if a1 == t1:
                                MSET(ust[ub][:, WI + 1:WI + 2], 0.0, [("ust", ub)])
                            fo = voff[("fcw", li)]
                            TS(ca[:, 0:WI], ust[ub][:, 1:1 + WI], vecs[:, fo + 22 + j:fo + 23 + j], V(("fcb", li), j), ALU.mult, ALU.add,
                               [("ust", ub), "vecs"], ["ca"])
                            STT(cb_[:, 0:WI], ust[ub][:, 0:WI], vecs[:, fo + j:fo + j + 1], ca[:, 0:WI], ALU.mult, ALU.add,
                                [("ust", ub), "vecs", "ca"], ["cb"])
                            STT(ca[:, 0:WI], ust[ub][:, 2:2 + WI], vecs[:, fo + 44 + j:fo + 45 + j], cb_[:, 0:WI], ALU.mult, ALU.add,
                                [("ust", ub), "vecs", "cb"], ["ca"])
                            ACT(cb_[:, 0:WI], ca[:, 0:WI], AF.Silu, ["ca"], ["cb"])
                            TT(gt[:, j, 0:WI], cb_[:, 0:WI], PS[pv][:, io:io + WI], ALU.mult, ["cb", "ps%d" % pv], [("gt", j)])
                        for dc in range(8):
                            pb = 5 + dc % 2
                            for j in range(22):
                                MM(PS[pb][:, 0:WI], w2f[:, j, dc * 128:(dc + 1) * 128], gt[:, j, 0:WI], j == 0, j == 21,
                                   [("w2f", j), ("gt", j)], ["ps%d" % pb])
                            STT(xt[b][:, dc, io:io + WI], PS[pb][:, 0:WI], modv(li, "g2", dc, s), xt[b][:, dc, io:io + WI], ALU.mult, ALU.add,
                                ["ps%d" % pb, "modsb", ("xt", b)], [("xt", b)])
                        dst = outT[:, t0:t1] if (li == DEPTH - 1 and s == 0) else xbuf[:, t0:t1]
                        DMA(dst.rearrange("(k p) t -> p k t", p=128), xt[b][:, :, io:io + WI], [("xt", b)], ["xbuf"])
                        if xd is not None:
                            DMA(xd[li][:, t0:t1].rearrange("(k p) t -> p k t", p=128), xt[b][:, :, io:io + WI], [("xt", b)], ["xd"])
                S.flush()
    return nc


def host_inputs(inp, b):
    m = {}
    m["xT"] = np.ascontiguousarray(np.asarray(inp["x"][b], np.float32).T)
    m["ctxT"] = np.ascontiguousarray(np.asarray(inp["ctx"][b], np.float32).T)
    cs = np.stack([np.asarray(inp["c"][b], np.float32), np.asarray(inp["c_ctx"], np.float32)], axis=-1)
    m["cs"] = np.ascontiguousarray(cs.reshape(8, 128, 2).transpose(1, 0, 2).reshape(128, 16))
    m["vecs"] = pack_vecs(inp)
    m["cmat"] = const_mats()
    m["rope"] = rope_tables()
    m["nab"] = np.stack([na_bias_tables(np.asarray(inp["na_rpb"][li], np.float32)) for li in range(DEPTH)], 0)
    m["scm"] = scan_mats()
    m["gla_w"] = np.ascontiguousarray(np.concatenate([np.asarray(inp["gla_w_a2"], np.float32),
                                                      np.asarray(inp["gla_b_a"], np.float32)[:, :, None, :]], axis=2))
    m["dflam"] = np.ascontiguousarray(np.asarray(inp["df_lambda"], np.float32).reshape(DEPTH, 128))
    for k in ("w_mod", "w_in", "w_out", "ffn_w_in", "ffn_w_out"):
        m[k] = np.ascontiguousarray(np.asarray(inp[k], np.float32))
    m["w_branch"] = np.ascontiguousarray(np.asarray(inp["w_branch"], np.float32).reshape(DEPTH, D, D))
    return m


def kernel(**inp):
    nc = build()
    shared = None
    in_maps = []
    for b in range(8):
        m = host_inputs(inp, b)
        if shared is None:
            shared = {k: m[k] for k in ("vecs", "cmat", "rope", "nab", "dflam", "scm", "gla_w", "w_mod", "w_in", "w_out", "ffn_w_in", "ffn_w_out", "w_branch")}
        else:
            m.update(shared)
        in_maps.append(m)
    res = run_bass_kernel_spmd(nc, in_maps, core_ids=list(range(8)))
    out = np.stack([np.ascontiguousarray(r["outT"].T) for r in res.results], axis=0)
    return out.astype(np.float32)
```

```python
import contextlib
import math
import numpy as np
import ml_dtypes
import concourse.bass as bass
import concourse.mybir as mybir
from concourse.bass_utils import run_bass_kernel_spmd

F32 = mybir.dt.float32
BF16 = mybir.dt.bfloat16
AF = mybir.ActivationFunctionType
ALU = mybir.AluOpType
AX = mybir.AxisListType

D = 1024
L = 4096
CL = 256
T = L + CL
DEPTH = 4
NIN = 7472
NMIX = 3376
DFF = 2816
EPS = 1e-6
NDMA_SEM = 8
SEQS = (("lat", 0, L), ("ctx", L, CL))


class Sched:
    ENGS = ("pe", "act", "dve", "pool", "sp")

    def __init__(self, nc, st):
        self.nc = nc
        self.ops = []
        self.last_w = {}
        self.readers = {}
        self.dma_count = {"sp": 0, "pool": 0}
        self.dma_hist = {"sp": [], "pool": []}
        self.emitted = 0
        self.cnt = {e: 0 for e in self.ENGS}
        self.sems = {}
        for e in self.ENGS:
            self.sems[e] = st.enter_context(nc.semaphore("s_" + e))
        for q in ("sp", "pool"):
            for i in range(NDMA_SEM):
                self.sems[(q, i)] = st.enter_context(nc.semaphore("d_%s%d" % (q, i)))

    def op(self, eng, fn, reads=(), writes=(), dma=False):
        oid = len(self.ops)
        deps = {}
        lo = self.emitted
        for k in reads:
            w = self.last_w.get(k)
            if w is not None and w >= lo:
                deps[w] = 2
        for k in writes:
            w = self.last_w.get(k)
            if w is not None and w >= lo:
                deps[w] = max(deps.get(w, 0), 1)
            for r in self.readers.get(k, ()):
                if r >= lo:
                    deps.setdefault(r, 0)
        for d in list(deps):
            do = self.ops[d]
            if do["eng"] == eng and not do["dma"] and not dma:
                if deps[d] == 0 or (deps[d] == 1 and eng == "pe"):
                    del deps[d]
        o = dict(eng=eng, fn=fn, dma=dma, deps=deps)
        if dma:
            n = self.dma_count[eng]
            self.dma_count[eng] = n + 1
            o["dma_i"] = n
            h = self.dma_hist[eng]
            if n >= NDMA_SEM and h[n - NDMA_SEM] >= lo:
                deps[h[n - NDMA_SEM]] = 2
            h.append(oid)
        self.ops.append(o)
        for k in reads:
            self.readers.setdefault(k, []).append(oid)
        for k in writes:
            self.last_w[k] = oid
            self.readers[k] = []
        return oid

    def flush(self):
        nc = self.nc
        allops = self.ops
        lo = self.emitted
        ops = allops[lo:]
        self.emitted = len(allops)
        if not ops:
            return
        for o in ops:
            o["sig"] = o["dma"]
        for o in ops:
            for d in o["deps"]:
                if not allops[d]["dma"]:
                    allops[d]["sig"] = True
        for o in ops:
            if o["dma"]:
                i = o["dma_i"]
                o["semkey"] = (o["eng"], i % NDMA_SEM)
                o["semval"] = 16 * (i // NDMA_SEM + 1)
            elif o["sig"]:
                self.cnt[o["eng"]] += 1
                o["semkey"] = o["eng"]
                o["semval"] = self.cnt[o["eng"]]
        known = {e: {} for e in self.ENGS}
        for o in ops:
            kn = known[o["eng"]]
            waits = []
            for d in sorted(o["deps"]):
                do = allops[d]
                sk, sv = do["semkey"], do["semval"]
                if kn.get(sk, 0) >= sv:
                    continue
                waits.append((sk, sv))
                kn[sk] = sv
                for k2, v2 in do["clock"].items():
                    if kn.get(k2, 0) < v2:
                        kn[k2] = v2
            o["waits"] = waits
            o["clock"] = dict(kn)
            if "semkey" in o and not o["dma"]:
                o["clock"][o["semkey"]] = o["semval"]
        sems = self.sems
        dma_count = dict(self.dma_count)

        def replay(ename):
            def body(eng):
                for o in ops:
                    if o["eng"] != ename:
                        continue
                    for sk, sv in o["waits"]:
                        eng.wait_ge(sems[sk], sv)
                    ins = o["fn"](eng)
                    if o["dma"]:
                        ins.then_inc(sems[o["semkey"]], 16)
                    elif o["sig"]:
                        ins.then_inc(sems[o["semkey"]], 1)
                if ename in ("sp", "pool"):
                    n = dma_count[ename]
                    for i in range(NDMA_SEM):
                        c = (n - i + NDMA_SEM - 1) // NDMA_SEM
                        if c > 0:
                            eng.wait_ge(sems[(ename, i)], 16 * c)
            return body

        with nc.Block() as block:
            block.tensor(replay("pe"))
            block.scalar(replay("act"))
            block.vector(replay("dve"))
            block.gpsimd(replay("pool"))
            block.sync(replay("sp"))
        for o in ops:
            o["fn"] = None
            o["clock"] = None


def vec_layout():
    off = {}
    n = 0

    def add(name, cols):
        nonlocal n
        off[name] = n
        n += cols

    for li in range(DEPTH):
        add(("bmod", li), 48)
        add(("n1g", li), 8)
        add(("n2g", li), 8)
        add(("bgate", li), 32)
        add(("fcw", li), 66)
        add(("fcb", li), 22)
        for nm in ("dfqn", "dfkn", "dfng", "naqn", "nakn", "glng", "dnng", "dndtb", "dnalog"):
            add((nm, li), 1)
        add(("dncw", li), 18)
    add("m1", 1)
    add("m2", 1)
    add("one", 1)
    add("hm", 4)
    add("dnsgn", 1)
    return off, n


def pmajor(v):
    v = np.asarray(v, np.float32)
    lead = int(np.prod(v.shape[:-1])) if v.ndim > 1 else 1
    n = v.shape[-1] // 128
    return np.ascontiguousarray(v.reshape(lead, n, 128).transpose(2, 0, 1).reshape(128, lead * n))


def pack_vecs(inp):
    off, n = vec_layout()
    V = np.zeros((128, n), np.float32)

    def put(name, arr):
        V[:, off[name]:off[name] + arr.shape[1]] = arr

    for li in range(DEPTH):
        put(("bmod", li), pmajor(inp["b_mod"][li]))
        put(("n1g", li), pmajor(inp["norm1_g"][li]))
        put(("n2g", li), pmajor(inp["norm2_g"][li]))
        put(("bgate", li), pmajor(inp["b_gate"][li]))
        put(("fcw", li), pmajor(inp["ffn_conv_w"][li]))
        put(("fcb", li), pmajor(inp["ffn_conv_b"][li]))
        put(("dfqn", li), np.tile(inp["df_q_norm"][li], 4)[:, None])
        put(("dfkn", li), np.tile(inp["df_k_norm"][li], 4)[:, None])
        put(("dfng", li), np.tile(inp["df_norm_g"][li], 2)[:, None])
        put(("naqn", li), np.tile(inp["na_q_norm"][li], 2)[:, None])
        put(("nakn", li), np.tile(inp["na_k_norm"][li], 2)[:, None])
        put(("glng", li), np.tile(inp["gla_norm_g"][li], 2)[:, None])
        put(("dnng", li), np.tile(inp["dn_norm_g"][li], 2)[:, None])
        z8 = np.zeros(8, np.float32)
        put(("dndtb", li), np.concatenate([z8, np.asarray(inp["dn_dt_bias"][li], np.float32).reshape(8), np.zeros(112, np.float32)])[:, None])
        put(("dnalog", li), np.concatenate([z8, np.asarray(inp["dn_a_log"][li], np.float32).reshape(8), np.zeros(112, np.float32)])[:, None])
        put(("dncw", li), pmajor(inp["dn_conv"][li]))
    p = np.arange(128)
    put("m1", ((p // 32) % 2 == 0).astype(np.float32)[:, None])
    put("m2", ((p // 32) % 2 == 1).astype(np.float32)[:, None])
    put("one", np.ones((128, 1), np.float32))
    put("hm", (p[:, None] // 32 == np.arange(4)[None, :]).astype(np.float32))
    put("dnsgn", np.where(p < 8, -1.0, 1.0).astype(np.float32)[:, None])
    return V


NEG = -30000.0


def const_mats():
    C = np.zeros((4, 128, 128), np.float32)
    p = np.arange(128)
    C[0] = (p[:, None] // 32 == p[None, :] // 32) / 32.0
    C[1] = (p[:, None] // 64 == p[None, :] // 64) / 64.0
    for m in range(128):
        g, d = m // 32, m % 32
        q = d // 8
        if q == 0:
            C[2, g * 32 + d + 8, m] = -1.0
        elif q == 1:
            C[2, g * 32 + d - 8, m] = 1.0
        elif q == 2:
            C[2, g * 32 + d + 8, m] = -1.0
        else:
            C[2, g * 32 + d - 8, m] = 1.0
    C[3] = 1.0
    return np.ascontiguousarray(C.transpose(1, 0, 2).reshape(128, 512))


def rope_tables():
    t = np.arange(L)
    row = (t // 64).astype(np.float32)
    col = (t % 64).astype(np.float32)
    nf = 8
    inv = np.power(np.float32(10000.0), -np.arange(nf, dtype=np.float32) / nf).astype(np.float32)
    ar = row[:, None] * inv
    ac = col[:, None] * inv
    ang = np.concatenate([ar, ar, ac, ac], -1)
    cs = np.stack([np.cos(ang), np.sin(ang)], 0).astype(np.float32)
    return np.ascontiguousarray(np.tile(cs.transpose(0, 2, 1), (1, 4, 1)))


SCM = {}
for _i, _n in enumerate(("I", "U", "NU", "MI", "MS", "MST", "CKD", "CQ", "M01")):
    SCM[_n] = _i
NSCM = len(SCM)


def scan_mats():
    t = np.arange(128)
    same = (t[:, None] // 64) == (t[None, :] // 64)
    out = np.zeros((2, NSCM, 128, 128), np.float32)
    for d in range(2):
        before = (t[:, None] <= t[None, :]) if d == 0 else (t[:, None] >= t[None, :])
        strict = (t[:, None] < t[None, :]) if d == 0 else (t[:, None] > t[None, :])
        U = (same & before).astype(np.float32)
        out[d, SCM["I"]] = np.eye(128, dtype=np.float32)
        out[d, SCM["U"]] = U
        out[d, SCM["NU"]] = -U
        out[d, SCM["MI"]] = np.where(same & before, 0.0, NEG)
        out[d, SCM["MS"]] = np.where(same & strict, 0.0, NEG)
        out[d, SCM["MST"]] = np.where(same & strict, 0.0, NEG).T
        out[d, SCM["CKD"]] = (same & strict.T).astype(np.float32)
        pos = t % 64
        midpos = 31 if d == 0 else 32
        umid = (same & ((pos[:, None] <= midpos) if d == 0 else (pos[:, None] >= midpos))).astype(np.float32)
        out[d, SCM["CQ"]] = U - umid
        out[d, SCM["M01"]] = (same & before).astype(np.float32)
    return np.ascontiguousarray(out.transpose(2, 0, 1, 3).reshape(128, 2 * NSCM * 128))


def na_chunks(rp):
    if rp in (0, 1):
        return [(kc, 5 + rp * 4 + kc) for kc in range(4)]
    if rp in (30, 31):
        return [(28 + j, 13 + (rp - 30) * 4 + j) for j in range(4)]
    return [(rp - 2 + j, j) for j in range(5)]


def na_bias_tables(rpb):
    out = np.full((4, 128, 21, 128), NEG, np.float32)
    kk = np.arange(128)
    for rp in [2, 0, 1, 30, 31]:
        for (kc, bi) in na_chunks(rp):
            qrow = 2 * rp + kk // 64
            qcol = kk % 64
            krow = 2 * kc + kk // 64
            kcol = kk % 64
            rs = np.clip(qrow - 4, 0, 56)
            cst = np.clip(qcol - 8, 0, 48)
            dr = krow[:, None] - qrow[None, :]
            dc = kcol[:, None] - qcol[None, :]
            valid = ((krow[:, None] >= rs[None, :]) & (krow[:, None] < rs[None, :] + 8)
                     & (kcol[:, None] >= cst[None, :]) & (kcol[:, None] < cst[None, :] + 16))
            ri = np.clip(dr + 7, 0, 14)
            ci = np.clip(dc + 15, 0, 30)
            for h in range(4):
                g = rpb[h][ri, ci]
                out[h, :, bi, :] = np.where(valid, g, np.float32(NEG))
    return out


PF_BLOCKS = ([(i * 128, 128) for i in range(8)] + [(1024, 16)] + [(1040 + i * 128, 128) for i in range(6)]
             + [(1808, 128), (1936, 128), (2064, 128), (2192, 128), (2320, 128), (2448, 128), (2576, 16), (2592, 16)]
             + [(2608 + i * 128, 128) for i in range(6)])
NPF = len(PF_BLOCKS)
PT_GROUPS = ((1552, 256, 0), (3120, 256, 256), (1936, 384, 512))
NPT = 896


def build(debug=None, nlayers=DEPTH):
    debug = debug or {}
    nc = bass.Bass("TRN2", target_bir_lowering=False)
    voff, NV = vec_layout()

    def din(name, shape, dt=F32):
        return nc.dram_tensor(name, list(shape), dt, kind="ExternalInput").ap()

    def dscr(name, shape, dt):
        kind = "ExternalOutput" if name in debug.get("dump", ()) else "Internal"
        return nc.dram_tensor(name, list(shape), dt, kind=kind).ap()

    xT_in = din("xT", [D, L])
    cT_in = din("ctxT", [D, CL])
    cs_in = din("cs", [128, 16])
    vecs_in = din("vecs", [128, NV])
    w_mod = din("w_mod", [DEPTH, D, 6 * D])
    w_in = din("w_in", [DEPTH, D, NIN])
    w_branch = din("w_branch", [DEPTH, D, D])
    w_out = din("w_out", [DEPTH, D, D])
    f_w_in = din("ffn_w_in", [DEPTH, D, 2 * DFF])
    f_w_out = din("ffn_w_out", [DEPTH, DFF, D])
    outT = nc.dram_tensor("outT", [D, L], F32, kind="ExternalOutput").ap()
    y_dbg = din("y_dbg", [D, T], BF16) if debug.get("y_in") else None
    mixers = debug.get("mixers", ("dn", "na", "gla", "df"))
    cmat_in = din("cmat", [128, 512])
    rope_in = din("rope", [2, 128, L])
    nab_in = din("nab", [DEPTH, 4, 128, 21, 128])
    dflam_in = din("dflam", [DEPTH, 128])
    scm_in = din("scm", [128, 2 * NSCM * 128])
    gla_w_in = din("gla_w", [DEPTH, 2, 17, 128])

    wi_b = dscr("wi_b", [DEPTH, D, NIN], BF16)
    wb_b = dscr("wb_b", [DEPTH, D, D], BF16)
    wo_b = dscr("wo_b", [DEPTH, D, D], BF16)
    f1_b = dscr("f1_b", [DEPTH, D, 2 * DFF], BF16)
    f2_b = dscr("f2_b", [DEPTH, DFF, D], BF16)
    xbuf = dscr("xbuf", [D, T], F32)
    x1buf = dscr("x1buf", [D, T], F32)
    hbuf = dscr("hbuf", [D, T], BF16)
    ybuf = dscr("ybuf", [D, T], BF16)
    PF = dscr("PF", [NPF * 128, T], F32)
    PT = dscr("PT", [T, NPT], F32)
    xd = nc.dram_tensor("xd", [DEPTH, D, T], F32, kind="ExternalOutput").ap() if debug.get("xdump") else None

    with contextlib.ExitStack() as top:
        S = Sched(nc, top)

        uid = [0]

        def sbuf(st, name, shape, dt):
            uid[0] += 1
            return st.enter_context(nc.sbuf_tensor("sb%d_%s" % (uid[0], name), list(shape), dt))

        PS = [top.enter_context(nc.psum_tensor("ps%d" % i, [128, 512], F32)) for i in range(8)]

        def MM(out, lhsT, rhs, st, sp, r, w):
            S.op("pe", lambda e: e.matmul(out, lhsT=lhsT, rhs=rhs, start=st, stop=sp), reads=r, writes=w)

        def ACT(out, in_, func, r, w, bias=0.0, scale=1.0):
            S.op("act", lambda e: e.activation(out=out, in_=in_, func=func, bias=bias, scale=scale), reads=r, writes=w)

        def TT(out, in0, in1, op, r, w, eng="dve"):
            S.op(eng, lambda e: e.tensor_tensor(out=out, in0=in0, in1=in1, op=op), reads=r, writes=w)

        def TS(out, in0, s1, s2, op0, op1, r, w, eng="dve"):
            S.op(eng, lambda e: e.tensor_scalar(out=out, in0=in0, scalar1=s1, scalar2=s2, op0=op0, op1=op1), reads=r, writes=w)

        def STT(out, in0, scalar, in1, op0, op1, r, w, eng="dve"):
            S.op(eng, lambda e: e.scalar_tensor_tensor(out=out, in0=in0, scalar=scalar, in1=in1, op0=op0, op1=op1), reads=r, writes=w)

        def CP(out, in_, r, w, eng="dve"):
            S.op(eng, lambda e: e.tensor_copy(out=out, in_=in_), reads=r, writes=w)

        def MSET(ap, val, w, eng="pool"):
            S.op(eng, lambda e: e.memset(ap, val), writes=w)

        def DMA(out, in_, r, w, q="sp"):
            S.op(q, lambda e: e.dma_start(out=out, in_=in_), reads=r, writes=w, dma=True)

        vecs = sbuf(top, "vecs", [128, NV], F32)
        modsb = sbuf(top, "modsb", [128, DEPTH, 48, 2], F32)
        gsc = sbuf(top, "gsc", [128, DEPTH, 2, 8, 2], F32)
        ones_b = sbuf(top, "ones_b", [128, 128], BF16)
        DMA(vecs[:], vecs_in[:], [], ["vecs"])
        MSET(ones_b[:], 1.0, ["ones_b"])

        cmat = sbuf(top, "cmat", [128, 512], F32)
        DMA(cmat[:], cmat_in[:], [], ["cmat"])
        BD32 = cmat[:, 0:128]
        BD64 = cmat[:, 128:256]
        RM = cmat[:, 256:384]
        ONESF = cmat[:, 384:512]

        scm = sbuf(top, "scm", [128, 2, NSCM, 128], F32)
        DMA(scm[:].rearrange("p a b c -> p (a b c)"), scm_in[:], [], ["scm"])

        def CM(d, name):
            return scm[:, d, SCM[name], :]

        def V(name, j=0, n=1):
            o = voff[name] + j
            return vecs[:, o:o + n]

        def cast2d(dst, src, rows, cols, key):
            for r0 in range(0, rows, 1024):
                r1 = min(rows, r0 + 1024)
                for c0 in range(0, cols, 2048):
                    c1 = min(cols, c0 + 2048)
                    DMA(dst[r0:r1, c0:c1], src[r0:r1, c0:c1], [], [key], q="pool")

        layer_list = list(debug.get("layers", range(nlayers)))
        for li in layer_list:
            cast2d(wi_b[li], w_in[li], D, NIN, ("wi_b", li))
            cast2d(wb_b[li], w_branch[li], D, D, ("wb_b", li))
            cast2d(wo_b[li], w_out[li], D, D, ("wo_b", li))
            cast2d(f1_b[li], f_w_in[li], D, 2 * DFF, ("f1_b", li))
            cast2d(f2_b[li], f_w_out[li], DFF, D, ("f2_b", li))

        with contextlib.ExitStack() as ph:
            scs = sbuf(ph, "scs", [128, 16], F32)
            wm = [sbuf(ph, "wm%d" % i, [128, 8, 768], F32) for i in range(2)]
            DMA(scs[:], cs_in[:], [], ["scs"])
            ACT(scs[:], scs[:], AF.Silu, ["scs"], ["scs"])
            nb = 0
            for li in layer_list:
                wv = w_mod[li].rearrange("(k p) c -> p k c", p=128)
                for cb in range(8):
                    wt = wm[nb % 2]
                    wk = ("wm", nb % 2)
                    nb += 1
                    DMA(wt[:], wv[:, :, cb * 768:(cb + 1) * 768], [], [wk])
                    for jj in range(6):
                        j = cb * 6 + jj
                        for k in range(8):
                            MM(PS[0][:, 2 * j:2 * j + 2], wt[:, k, jj * 128:(jj + 1) * 128], scs[:, 2 * k:2 * k + 2],
                               k == 0, k == 7, [wk, "scs"], ["ps0"])
                TT(modsb[:, li], PS[0][:, 0:96].rearrange("p (j s) -> p j s", s=2),
                   V(("bmod", li), 0, 48).rearrange("p (j o) -> p j o", o=1).to_broadcast([128, 48, 2]), ALU.add,
                   ["ps0", "vecs"], ["modsb"])
                for which, (so, gname) in enumerate(((8, "n1g"), (32, "n2g"))):
                    TS(gsc[:, li, which], modsb[:, li, so:so + 8, :], 1.0, 0.0, ALU.add, ALU.add, ["modsb"], ["gsc"])
                    TT(gsc[:, li, which], gsc[:, li, which],
                       V((gname, li), 0, 8).rearrange("p (j o) -> p j o", o=1).to_broadcast([128, 8, 2]), ALU.mult,
                       ["gsc", "vecs"], ["gsc"])
            S.flush()

        def modv(li, what, k, s):
            base = {"sh1": 0, "sc1": 8, "g1": 16, "sh2": 24, "sc2": 32, "g2": 40}[what]
            return modsb[:, li, base + k, s:s + 1]

        def norm_tile(xt, ht, W, li, which, s, tmp_sq, tmp_r, tmp_f, psb, kx, kh, tag):
            shn = "sh1" if which == 0 else "sh2"
            for k in range(8):
                ACT(tmp_sq[:, 0:W], xt[:, k, 0:W], AF.Square, [kx], [tag + "sq"])
                MM(PS[psb][:, 0:W], ones_b[:], tmp_sq[:, 0:W], k == 0, k == 7, ["ones_b", tag + "sq"], ["ps%d" % psb])
            ACT(tmp_r[:, 0:W], PS[psb][:, 0:W], AF.Ln, ["ps%d" % psb], [tag + "r"], bias=EPS, scale=1.0 / D)
            ACT(tmp_r[:, 0:W], tmp_r[:, 0:W], AF.Exp, [tag + "r"], [tag + "r"], scale=-0.5)
            for k in range(8):
                STT(tmp_f[:, 0:W], xt[:, k, 0:W], gsc[:, li, which, k, s:s + 1], tmp_r[:, 0:W], ALU.mult, ALU.mult,
                    [kx, "gsc", tag + "r"], [tag + "f"])
                ACT(ht[:, k, 0:W], tmp_f[:, 0:W], AF.Identity, [tag + "f", "modsb"], [kh], bias=modv(li, shn, k, s))

        for li in layer_list:
            with_ctx = li < DEPTH - 1
            with contextlib.ExitStack() as ph:
                wi = sbuf(ph, "wi", [128, 8, NMIX], BF16)
                xt = [sbuf(ph, "xt%d" % i, [128, 8, 512], F32) for i in range(2)]
                ht = [sbuf(ph, "ht%d" % i, [128, 8, 512], BF16) for i in range(2)]
                sq = sbuf(ph, "sq", [128, 512], BF16)
                rr = sbuf(ph, "rr", [128, 512], F32)
                ff = sbuf(ph, "ff", [128, 512], F32)
                stg = [sbuf(ph, "stg%d" % i, [128, 512], F32) for i in range(4)]
                for k in range(8):
                    DMA(wi[:, k, :], wi_b[li][k * 128:(k + 1) * 128, 0:NMIX], [("wi_b", li)], [("wi", k)])
                wik = [("wi", k) for k in range(8)]
                it = 0
                nst = 0
                for (sname, s0, slen) in SEQS:
                    s = 0 if sname == "lat" else 1
                    for t0 in range(s0, s0 + slen, 512):
                        W = min(512, s0 + slen - t0)
                        b = it % 2
                        it += 1
                        if li == layer_list[0]:
                            src = xT_in[:, t0:t0 + W] if s == 0 else cT_in[:, t0 - L:t0 - L + W]
                        else:
                            src = xbuf[:, t0:t0 + W]
                        DMA(xt[b][:, :, 0:W], src.rearrange("(k p) t -> p k t", p=128), ["xbuf"], [("xt", b)])
                        norm_tile(xt[b], ht[b], W, li, 0, s, sq, rr, ff, 0, ("xt", b), ("ht", b), "n1")
                        DMA(hbuf[:, t0:t0 + W].rearrange("(k p) t -> p k t", p=128), ht[b][:, :, 0:W], [("ht", b)], ["hbuf"])
                        for bi, (c0, ncol) in enumerate(PF_BLOCKS):
                            pb = 1 + (bi % 4)
                            for k in range(8):
                                MM(PS[pb][0:ncol, 0:W], wi[:, k, c0:c0 + ncol], ht[b][:, k, 0:W], k == 0, k == 7,
                                   [("wi", k), ("ht", b)], ["ps%d" % pb])
                            sg = nst % 4
                            nst += 1
                            if bi % 2 == 0:
                                ACT(stg[sg][0:ncol, 0:W], PS[pb][0:ncol, 0:W], AF.Copy, ["ps%d" % pb], [("stg", sg)])
                            else:
                                CP(stg[sg][0:ncol, 0:W], PS[pb][0:ncol, 0:W], ["ps%d" % pb], [("stg", sg)])
                            DMA(PF[bi * 128:bi * 128 + ncol, t0:t0 + W], stg[sg][0:ncol, 0:W], [("stg", sg)], [("PF", bi)])
                        for q in range(W // 128):
                            for gi, (c0, ncol, p0) in enumerate(PT_GROUPS):
                                pb = 5 if gi < 2 else 7
                                pcol = 256 if gi == 1 else 0
                                for k in range(8):
                                    MM(PS[pb][:, pcol:pcol + ncol], ht[b][:, k, q * 128:(q + 1) * 128], wi[:, k, c0:c0 + ncol],
                                       k == 0, k == 7, [("wi", k), ("ht", b)], ["ps%d" % pb])
                            for pb, p0, ncol in ((5, 0, 512), (7, 512, 384)):
                                sg = nst % 4
                                nst += 1
                                CP(stg[sg][:, 0:ncol], PS[pb][:, 0:ncol], ["ps%d" % pb], [("stg", sg)])
                                DMA(PT[t0 + q * 128:t0 + (q + 1) * 128, p0:p0 + ncol], stg[sg][:, 0:ncol], [("stg", sg)], [("PT", p0)])
                S.flush()

            if y_dbg is not None:
                with contextlib.ExitStack() as ph:
                    yt = sbuf(ph, "ycp", [128, 8, 512], BF16)
                    for t0 in range(0, T, 512):
                        W = min(512, T - t0)
                        DMA(yt[:, :, 0:W], y_dbg[:, t0:t0 + W].rearrange("(k p) t -> p k t", p=128), [], ["ycp"])
                        DMA(ybuf[:, t0:t0 + W].rearrange("(k p) t -> p k t", p=128), yt[:, :, 0:W], ["ycp"], ["ybuf"])
                    S.flush()


            BLK = {"dfq": 23, "dfk": 25, "naq": 9, "nak": 11}

            def qk_prep(ph, specs):
                px = [sbuf(ph, "px%d" % i, [128, 512], F32) for i in range(2)]
                psq = sbuf(ph, "psq", [128, 512], F32)
                prs = sbuf(ph, "prs", [128, 512], F32)
                pxn = sbuf(ph, "pxn", [128, 512], F32)
                pa = sbuf(ph, "ppa", [128, 512], F32)
                pb_ = sbuf(ph, "ppb", [128, 512], F32)
                rc = sbuf(ph, "rc", [128, 2, 512], F32)
                n = 0
                for t0 in range(0, T, 512):
                    W = min(512, T - t0)
                    lat = t0 < L
                    if lat and any(sp[3] for sp in specs):
                        DMA(rc[:, 0, :], rope_in[0, :, t0:t0 + 512], [], ["rc"])
                        DMA(rc[:, 1, :], rope_in[1, :, t0:t0 + 512], [], ["rc"])
                    for (blk, gm, gcol, rope, outs) in specs:
                        b = n % 2
                        n += 1
                        DMA(px[b][:, 0:W], PF[blk * 128:(blk + 1) * 128, t0:t0 + W], [], [("px", b)])
                        ACT(psq[:, 0:W], px[b][:, 0:W], AF.Square, [("px", b)], ["psq"])
                        MM(PS[0][:, 0:W], gm, psq[:, 0:W], True, True, ["cmat", "psq"], ["ps0"])
                        ACT(prs[:, 0:W], PS[0][:, 0:W], AF.Ln, ["ps0"], ["prs"], bias=EPS)
                        ACT(prs[:, 0:W], prs[:, 0:W], AF.Exp, ["prs"], ["prs"], scale=-0.5)
                        STT(pxn[:, 0:W], px[b][:, 0:W], gcol, prs[:, 0:W], ALU.mult, ALU.mult, [("px", b), "vecs", "prs"], ["pxn"])
                        val = pxn
                        vk = "pxn"
                        if rope and lat:
                            MM(PS[1][:, 0:W], RM, pxn[:, 0:W], True, True, ["cmat", "pxn"], ["ps1"])
                            TT(pa[:, 0:W], pxn[:, 0:W], rc[:, 0, 0:W], ALU.mult, ["pxn", "rc"], ["ppa"], eng="pool")
                            TT(pb_[:, 0:W], PS[1][:, 0:W], rc[:, 1, 0:W], ALU.mult, ["ps1", "rc"], ["ppb"])
                            TT(pa[:, 0:W], pa[:, 0:W], pb_[:, 0:W], ALU.add, ["ppa", "ppb"], ["ppa"], eng="pool")
                            val = pa
                            vk = "ppa"
                        for (dst, dk, mcol) in outs:
                            TS(dst[:, t0:t0 + W], val[:, 0:W], mcol, 0.0, ALU.mult, ALU.add, [vk, "vecs"], [dk])

            def load_vaug(ph, name, pcol):
                va = sbuf(ph, name, [128, 34, 4, 65], BF16)
                vst = [sbuf(ph, name + "st%d" % i, [128, 256], F32) for i in range(2)]
                MSET(va[:, :, :, 64:65], 1.0, [name])
                for kc in range(34):
                    b = kc % 2
                    DMA(vst[b][:], PT[kc * 128:(kc + 1) * 128, pcol:pcol + 256], [], [(name + "st", b)])
                    CP(va[:, kc, :, 0:64], vst[b][:].rearrange("p (h d) -> p h d", h=4), [(name + "st", b)], [name],
                       eng=("dve" if kc % 2 else "pool"))
                return va

            def attn_norm(ph_tiles, Obank, W, okey):
                osb, rrow, onrm = ph_tiles
                ACT(osb[0:65, 0:W], PS[Obank][0:65, 0:W], AF.Copy, ["ps%d" % Obank], ["osb"])
                S.op("dve", lambda e: e.reciprocal(out=rrow[64:65, 0:W], in_=osb[64:65, 0:W]), reads=["osb"], writes=["rrow"])
                MM(PS[7][0:64, 0:W], ONESF[64:65, 0:64], rrow[64:65, 0:W], True, True, ["cmat", "rrow"], ["ps7"])
                TT(onrm[0:64, 0:W], osb[0:64, 0:W], PS[7][0:64, 0:W], ALU.mult, ["osb", "ps7"], [okey])


            def head_norm_gate(ph, osum, okey, gate_blk, gcol, yrow0):
                gx = [sbuf(ph, "hg_x%d" % i, [128, 512], F32) for i in range(2)]
                gs = sbuf(ph, "hg_s", [128, 512], F32)
                gr = sbuf(ph, "hg_r", [128, 512], F32)
                gy = [sbuf(ph, "hg_y%d" % i, [128, 512], BF16) for i in range(2)]
                n = 0
                for t0 in range(0, T, 512):
                    W = min(512, T - t0)
                    for hp in range(2):
                        b = n % 2
                        n += 1
                        DMA(gx[b][:, 0:W], PF[(gate_blk + hp) * 128:(gate_blk + hp + 1) * 128, t0:t0 + W], [], [("hgx", b)])
                        ACT(gx[b][:, 0:W], gx[b][:, 0:W], AF.Silu, [("hgx", b)], [("hgx", b)])
                        ACT(gs[:, 0:W], osum[:, hp, t0:t0 + W], AF.Square, [okey], ["hgs"])
                        MM(PS[0][:, 0:W], BD64, gs[:, 0:W], True, True, ["cmat", "hgs"], ["ps0"])
                        ACT(gr[:, 0:W], PS[0][:, 0:W], AF.Ln, ["ps0"], ["hgr"], bias=EPS)
                        ACT(gr[:, 0:W], gr[:, 0:W], AF.Exp, ["hgr"], ["hgr"], scale=-0.5)
                        STT(gs[:, 0:W], osum[:, hp, t0:t0 + W], gcol, gr[:, 0:W], ALU.mult, ALU.mult, [okey, "vecs", "hgr"], ["hgs"])
                        TT(gy[b][:, 0:W], gs[:, 0:W], gx[b][:, 0:W], ALU.mult, ["hgs", ("hgx", b)], [("hgy", b)])
                        DMA(ybuf[yrow0 + hp * 128:yrow0 + (hp + 1) * 128, t0:t0 + W], gy[b][:, 0:W], [("hgy", b)], [("ybuf", yrow0)])

            def scan_blocks(d):
                cb = [L + 0, L + 128] if d == 0 else [L + 128, L + 0]
                lb_ = list(range(0, L, 128)) if d == 0 else list(range(L - 128, -1, -128))
                return cb + lb_


            if "dn" in mixers:
                with contextlib.ExitStack() as ph:
                    osum = sbuf(ph, "dno", [128, 2, T], F32)
                    qT = sbuf(ph, "dnq", [128, 2, T], BF16)
                    kT = sbuf(ph, "dnk", [128, 2, T], BF16)
                    vT = sbuf(ph, "dnv", [128, 2, T], BF16)
                    rowsT = sbuf(ph, "dnrows", [16, T], F32)
                    coef = sbuf(ph, "dncoef", [16, 1], F32)
                    I16 = sbuf(ph, "dnI16", [128, 128], BF16)
                    CP(I16[:], CM(0, "I"), ["scm"], ["dnI16"])
                    ACT(coef[:], V(("dnalog", li))[0:16, :], AF.Exp, ["vecs"], ["dncoef"])
                    TS(coef[:], coef[:], -1.0, 0.0, ALU.mult, ALU.add, ["dncoef"], ["dncoef"])
                    DMA(rowsT[:], PF[8 * 128:8 * 128 + 16, :], [], ["dnrows"])
                    for c0 in range(0, T, 1088):
                        sl = rowsT[:, c0:c0 + 1088]
                        ACT(sl, sl, AF.Exp, ["dnrows", "vecs"], ["dnrows"], bias=V(("dndtb", li))[0:16, :], scale=V("dnsgn")[0:16, :])
                        ACT(sl, sl, AF.Ln, ["dnrows"], ["dnrows"], bias=1.0)
                        TS(sl, sl, coef[:, 0:1], 0.0, ALU.mult, ALU.add, ["dnrows", "dncoef"], ["dnrows"])
                    with contextlib.ExitStack() as ph2:
                        xs = [sbuf(ph2, "dnxs%d" % i, [128, 514], F32) for i in range(2)]
                        ca = sbuf(ph2, "dnca", [128, 512], F32)
                        cb2 = sbuf(ph2, "dncb", [128, 512], F32)
                        sq2 = sbuf(ph2, "dnsq", [128, 512], F32)
                        n = 0
                        cw = voff[("dncw", li)]
                        for (sname, s0, slen) in SEQS:
                            for t0 in range(s0, s0 + slen, 512):
                                W = min(512, s0 + slen - t0)
                                a0 = max(t0 - 1, s0)
                                a1_ = min(t0 + W + 1, s0 + slen)
                                for blk in range(6):
                                    b = n % 2
                                    n += 1
                                    off0 = a0 - t0 + 1
                                    DMA(xs[b][:, off0:off0 + (a1_ - a0)], PF[blk * 128:(blk + 1) * 128, a0:a1_], [], [("dnxs", b)])
                                    if t0 == s0:
                                        MSET(xs[b][:, 0:1], 0.0, [("dnxs", b)])
                                    if t0 + W == s0 + slen:
                                        MSET(xs[b][:, W + 1:W + 2], 0.0, [("dnxs", b)])
                                    TS(ca[:, 0:W], xs[b][:, 1:1 + W], vecs[:, cw + 6 + blk:cw + 7 + blk], 0.0, ALU.mult, ALU.add, [("dnxs", b), "vecs"], ["dnca"])
                                    STT(cb2[:, 0:W], xs[b][:, 0:W], vecs[:, cw + blk:cw + blk + 1], ca[:, 0:W], ALU.mult, ALU.add, [("dnxs", b), "vecs", "dnca"], ["dncb"])
                                    STT(ca[:, 0:W], xs[b][:, 2:2 + W], vecs[:, cw + 12 + blk:cw + 13 + blk], cb2[:, 0:W], ALU.mult, ALU.add, [("dnxs", b), "vecs", "dncb"], ["dnca"])
                                    ACT(cb2[:, 0:W], ca[:, 0:W], AF.Silu, ["dnca"], ["dncb"])
                                    if blk >= 4:
                                        CP(vT[:, blk - 4, t0:t0 + W], cb2[:, 0:W], ["dncb"], [("dnv", blk - 4)], eng="pool")
                                        continue
                                    ACT(sq2[:, 0:W], cb2[:, 0:W], AF.Square, ["dncb"], ["dnsq"])
                                    MM(PS[0][:, 0:W], BD64, sq2[:, 0:W], True, True, ["cmat", "dnsq"], ["ps0"])
                                    ACT(sq2[:, 0:W], PS[0][:, 0:W], AF.Ln, ["ps0"], ["dnsq"], bias=EPS / 64)
                                    ACT(sq2[:, 0:W], sq2[:, 0:W], AF.Exp, ["dnsq"], ["dnsq"], scale=-0.5)
                                    if blk < 2:
                                        STT(qT[:, blk, t0:t0 + W], cb2[:, 0:W], 1.0 / 64, sq2[:, 0:W], ALU.mult, ALU.mult, ["dncb", "dnsq"], [("dnq", blk)])
                                    else:
                                        STT(kT[:, blk - 2, t0:t0 + W], cb2[:, 0:W], 1.0 / 8, sq2[:, 0:W], ALU.mult, ALU.mult, ["dncb", "dnsq"], [("dnk", blk - 2)])
                        S.flush()
                    rt = sbuf(ph, "dnrt", [128, 16], F32)
                    gBs = sbuf(ph, "dngB", [128, 128], F32)
                    lBs = sbuf(ph, "dnlB", [128, 128], F32)
                    ekd = sbuf(ph, "dnekd", [128, 16], F32)
                    ebt = sbuf(ph, "dnebt", [128, 8], F32)
                    kv = sbuf(ph, "dnkv", [128, 512], F32)
                    E5 = sbuf(ph, "dnE5", [128, 5, 128], F32)
                    NPQ = debug.get("dn_npq", 2)
                    Pb = [sbuf(ph, "dnP%d" % i, [128, 128], F32) for i in range(NPQ)]
                    Qb = [sbuf(ph, "dnQ%d" % i, [128, 128], F32) for i in range(NPQ)]
                    rF = [sbuf(ph, "dnrF%d" % i, [128, 64], F32) for i in range(2)]
                    for i in range(2):
                        MSET(rF[i][:], 0.0, [("dnrF", i)])
                    X = sbuf(ph, "dnX", [128, 128], F32)
                    X16 = sbuf(ph, "dnX16", [128, 128], BF16)
                    aqk = sbuf(ph, "dnaqk", [128, 128], BF16)
                    kbe = sbuf(ph, "dnkbe", [128, 128], BF16)
                    qd = sbuf(ph, "dnqd", [128, 128], BF16)
                    vb = sbuf(ph, "dnvb", [128, 64], F32)
                    kdc = sbuf(ph, "dnkdc", [128, 64], BF16)
                    r16 = sbuf(ph, "dnr16", [128, 64], BF16)
                    vn16 = sbuf(ph, "dnvn", [128, 64], BF16)
                    Sf = [sbuf(ph, "dnS%d" % h, [128, 64], F32) for h in range(4)]
                    S16 = [sbuf(ph, "dnS16_%d" % h, [128, 64], BF16) for h in range(4)]
                    for d in range(2):
                        for h in range(4):
                            MSET(Sf[h][:], 0.0, [("dnS", h)])
                            MSET(S16[h][:], 0.0, [("dnS16", h)])
                        for t0 in scan_blocks(d)[:debug.get("dn_nblk", 100)]:
                            MM(PS[0][:, 0:16], rowsT[:, t0:t0 + 128], CM(0, "I")[0:16, 0:16], True, True, ["dnrows", "scm"], ["ps0"])
                            CP(rt[:], PS[0][:, 0:16], ["ps0"], ["dnrt"])
                            MM(PS[0][:, 16:32], CM(d, "CKD"), rt[:], True, True, ["scm", "dnrt"], ["ps0"])
                            ACT(ekd[:], PS[0][:, 16:32], AF.Exp, ["ps0"], ["dnekd"])
                            ACT(ebt[:], rt[:, 0:8], AF.Exp, ["dnrt"], ["dnebt"])
                            for i2 in range(2):
                                MM(PS[1][:, i2 * 128:(i2 + 1) * 128], kT[:, i2, t0:t0 + 128], I16[:], True, True, [("dnk", i2), "dnI16"], ["ps1"])
                                MM(PS[1][:, 256 + i2 * 128:256 + (i2 + 1) * 128], vT[:, i2, t0:t0 + 128], I16[:], True, True, [("dnv", i2), "dnI16"], ["ps1"])
                            CP(kv[:], PS[1][:], ["ps1"], ["dnkv"])
                            order = (0, 1) if d == 0 else (1, 0)
                            for h in range(4):
                                dh = d * 4 + h
                                blk = h // 2
                                r0 = (h % 2) * 64
                                CP(gBs[:], rt[:, 8 + dh:9 + dh].to_broadcast([128, 128]), ["dnrt"], ["dngB"])
                                CP(lBs[:], rt[:, dh:dh + 1].to_broadcast([128, 128]), ["dnrt"], ["dnlB"], eng="pool")
                                gB = gBs[:]
                                lB = lBs[:]
                                kr = ["dngB", "dnlB", "scm"]
                                dstage = debug.get("dn_stage", 99)
                                if dstage < 1:
                                    continue
                                MM(PS[2][:, 0:128], gB, CM(d, "U"), True, False, kr, ["ps2"])
                                MM(PS[2][:, 0:128], CM(d, "NU"), gB, False, False, kr, ["ps2"])
                                MM(PS[2][:, 0:128], CM(d, "I"), CM(d, "MI"), False, True, kr, ["ps2"])
                                MM(PS[2][:, 128:256], gB, CM(d, "U"), True, False, kr, ["ps2"])
                                MM(PS[2][:, 128:256], CM(d, "NU"), gB, False, False, kr, ["ps2"])
                                MM(PS[2][:, 128:256], lB, CM(d, "I"), False, False, kr, ["ps2"])
                                MM(PS[2][:, 128:256], CM(d, "I"), CM(d, "MS"), False, True, kr, ["ps2"])
                                MM(PS[2][:, 256:384], CM(d, "U"), gB, True, False, kr, ["ps2"])
                                MM(PS[2][:, 256:384], gB, CM(d, "NU"), False, False, kr, ["ps2"])
                                MM(PS[2][:, 256:384], CM(d, "I"), lB, False, False, kr, ["ps2"])
                                MM(PS[2][:, 256:384], CM(d, "I"), CM(d, "MST"), False, True, kr, ["ps2"])
                                MM(PS[2][:, 384:512], gB, CM(d, "U"), True, True, kr, ["ps2"])
                                MM(PS[3][:, 0:128], gB, CM(d, "U"), True, False, kr, ["ps3"])
                                MM(PS[3][:, 0:128], lB, CM(d, "I"), False, True, kr, ["ps3"])
                                ACT(E5[:, 0:4, :].rearrange("p a c -> p (a c)"), PS[2][:], AF.Exp, ["ps2"], ["dnE5"])
                                ACT(E5[:, 4, :], PS[3][:, 0:128], AF.Exp, ["ps3"], ["dnE5"])
                                kTh = kT[r0:r0 + 64, blk, t0:t0 + 128]
                                if dstage < 2:
                                    continue
                                MM(PS[3][:, 128:256], kTh, kTh, True, True, [("dnk", blk)], ["ps3"])
                                MM(PS[3][:, 256:384], kTh, qT[r0:r0 + 64, blk, t0:t0 + 128], True, True, [("dnk", blk), ("dnq", blk)], ["ps3"])
                                if debug.get("dn_sub", 9) < 1:
                                    continue
                                STT(Qb[0][:], PS[3][:, 128:256], -1.0, E5[:, 1, :], ALU.mult, ALU.mult, ["ps3", "dnE5"], [("dnQ", 0)])
                                STT(Pb[0][:], PS[3][:, 128:256], -1.0, E5[:, 2, :], ALU.mult, ALU.mult, ["ps3", "dnE5"], [("dnP", 0)])
                                TT(aqk[:], PS[3][:, 256:384], E5[:, 0, :], ALU.mult, ["ps3", "dnE5"], ["dnaqk"])
                                if debug.get("dn_sub", 9) < 2:
                                    continue
                                TT(X[:], Qb[0][:], CM(0, "I"), ALU.add, [("dnQ", 0), "scm"], ["dnX"])
                                if dstage < 3:
                                    continue
                                for lvl in range(debug.get("dn_lvl", 5)):
                                    a, bn = lvl % NPQ, (lvl + 1) % NPQ
                                    MM(PS[4][:, 0:128], Qb[a][:], Pb[a][:], True, True, [("dnQ", a), ("dnP", a)], ["ps4"])
                                    ACT(Pb[bn][:], PS[4][:, 0:128], AF.Copy, ["ps4"], [("dnP", bn)])
                                    if lvl < 4:
                                        MM(PS[0][:, 128:256], Pb[a][:], Qb[a][:], True, True, [("dnQ", a), ("dnP", a)], ["ps0"])
                                        CP(Qb[bn][:], PS[0][:, 128:256], ["ps0"], [("dnQ", bn)])
                                    MM(PS[1][:, 256:384], Pb[bn][:], X[:], True, True, [("dnP", bn), "dnX"], ["ps1"])
                                    TT(X[:], X[:], PS[1][:, 256:384], ALU.add, ["dnX", "ps1"], ["dnX"])
                                if dstage < 4:
                                    continue
                                TT(kbe[r0:r0 + 64, :], kTh, E5[r0:r0 + 64, 4, :], ALU.mult, [("dnk", blk), "dnE5"], ["dnkbe"])
                                TT(qd[r0:r0 + 64, :], qT[r0:r0 + 64, blk, t0:t0 + 128], E5[r0:r0 + 64, 3, :], ALU.mult, [("dnq", blk), "dnE5"], ["dnqd"])
                                TS(vb[:], kv[:, 256 + h * 64:256 + (h + 1) * 64], ebt[:, dh:dh + 1], 0.0, ALU.mult, ALU.add, ["dnkv", "dnebt"], ["dnvb"])
                                TS(kdc[:], kv[:, h * 64:(h + 1) * 64], ekd[:, 8 + dh:9 + dh], 0.0, ALU.mult, ALU.add, ["dnkv", "dnekd"], ["dnkdc"])
                                orow = (h % 2) * 64
                                pO = 6 + h // 2
                                if dstage < 5:
                                    continue
                                for i in order:
                                    c0 = i * 64
                                    MM(PS[5][c0:c0 + 64, 0:64], kbe[r0:r0 + 64, c0:c0 + 64], S16[h][r0:r0 + 64, :], True, True, ["dnkbe", ("dnS16", h)], ["ps5"])
                                    TT(rF[i][c0:c0 + 64, :], vb[c0:c0 + 64, :], PS[5][c0:c0 + 64, 0:64], ALU.subtract, ["dnvb", "ps5"], [("dnrF", i)])
                                    MM(PS[5][:, 64:128], X[:], rF[i][:], True, True, ["dnX", ("dnrF", i)], ["ps5"])
                                    ACT(vn16[c0:c0 + 64, :], PS[5][c0:c0 + 64, 64:128], AF.Copy, ["ps5"], ["dnvn"])
                                    MM(PS[pO][orow:orow + 64, c0:c0 + 64], S16[h][r0:r0 + 64, :], qd[r0:r0 + 64, c0:c0 + 64], True, False,
                                       [("dnS16", h), "dnqd"], ["ps%d" % pO])
                                    MM(PS[pO][orow:orow + 64, c0:c0 + 64], vn16[c0:c0 + 64, :], aqk[c0:c0 + 64, c0:c0 + 64], False, True,
                                       ["dnvn", "dnaqk"], ["ps%d" % pO])
                                    MM(PS[5][r0:r0 + 64, 128:192], kdc[c0:c0 + 64, :], vn16[c0:c0 + 64, :], True, True, ["dnkdc", "dnvn"], ["ps5"])
                                    last = c0 + 63 if d == 0 else c0
                                    STT(Sf[h][r0:r0 + 64, :], Sf[h][r0:r0 + 64, :], E5[r0:r0 + 64, 3, last:last + 1], PS[5][r0:r0 + 64, 128:192],
                                        ALU.mult, ALU.add, [("dnS", h), "dnE5", "ps5"], [("dnS", h)])
                                    ACT(S16[h][r0:r0 + 64, :], Sf[h][r0:r0 + 64, :], AF.Copy, [("dnS", h)], [("dnS16", h)])
                            if debug.get("dn_stage", 99) < 5:
                                continue
                            for hp in range(2):
                                if d == 0:
                                    ACT(osum[:, hp, t0:t0 + 128], PS[6 + hp][:, 0:128], AF.Copy, ["ps%d" % (6 + hp)], ["dno"])
                                else:
                                    TT(osum[:, hp, t0:t0 + 128], osum[:, hp, t0:t0 + 128], PS[6 + hp][:, 0:128], ALU.add, ["dno", "ps%d" % (6 + hp)], ["dno"])
                    head_norm_gate(ph, osum, "dno", 6, V(("dnng", li)), 0)
                    S.flush()

            if "gla" in mixers:
                with contextlib.ExitStack() as ph:
                    osum = sbuf(ph, "glo", [128, 2, T], F32)
                    qT = sbuf(ph, "glq", [128, T], F32)
                    kT = sbuf(ph, "glk", [128, T], F32)
                    a1 = [sbuf(ph, "gla1_%d" % i, [17, T], F32) for i in range(2)]
                    wa = sbuf(ph, "glwa", [17, 2, 128], F32)
                    DMA(qT[:], PF[15 * 128:16 * 128, :], [], ["glq"])
                    DMA(kT[:], PF[16 * 128:17 * 128, :], [], ["glk"])
                    for d in range(2):
                        MSET(a1[d][:], 1.0, [("gla1", d)])
                        DMA(a1[d][0:16, :], PF[(21 + d) * 128:(21 + d) * 128 + 16, :], [], [("gla1", d)])
                        DMA(wa[:, d, :], gla_w_in[li, d], [], ["glwa"])
                    ktok = [sbuf(ph, "glkt%d" % i, [128, 384], F32) for i in range(2)]
                    vb16 = [sbuf(ph, "glvb%d" % i, [128, 256], BF16) for i in range(2)]
                    ln_ = sbuf(ph, "glln", [128, 128], F32)
                    eq = sbuf(ph, "gleq", [128, 128], F32)
                    ek = sbuf(ph, "glek", [128, 128], F32)
                    eb = sbuf(ph, "gleb", [128, 128], F32)
                    ekd = sbuf(ph, "glekd", [128, 128], F32)
                    qt_ = sbuf(ph, "glqt", [128, 128], F32)
                    ktl = sbuf(ph, "glktl", [128, 128], BF16)
                    qb = sbuf(ph, "glqb", [128, 128], F32)
                    qth = sbuf(ph, "glqth", [128, 4, 128], BF16)
                    qbh = sbuf(ph, "glqbh", [128, 4, 128], BF16)
                    kdec = sbuf(ph, "glkdec", [128, 128], BF16)
                    S16 = sbuf(ph, "glS16", [128, 64], BF16)
                    Ah = sbuf(ph, "glA", [128, 4, 128], BF16)
                    dsm = sbuf(ph, "gldsm", [128, 4, 64], F32)
                    dsr = sbuf(ph, "gldsr", [128, 64], F32)
                    Sst = sbuf(ph, "glS", [128, 64], F32)
                    osb = sbuf(ph, "glosb", [128, 2, 128], F32)
                    sc = 32 ** -0.5
                    nb = 0
                    for d in range(2):
                        MSET(Sst[:], 0.0, ["glS"])
                        for t0 in scan_blocks(d)[:debug.get("gla_nblk", 100)]:
                            b = nb % 2
                            nb += 1
                            stage = debug.get("gla_stage", 99)
                            DMA(ktok[b][:], PT[t0:t0 + 128, 512:896], [], [("glkt", b)])
                            CP(vb16[b][:], ktok[b][:, 128:384], [("glkt", b)], [("glvb", b)], eng="pool")
                            MM(PS[0][:, 0:128], a1[d][:, t0:t0 + 128], wa[:, d, :], True, True, [("gla1", d), "glwa"], ["ps0"])
                            ACT(ln_[:], PS[0][:, 0:128], AF.Exp, ["ps0"], ["glln"], scale=-1.0)
                            ACT(ln_[:], ln_[:], AF.Ln, ["glln"], ["glln"], bias=1.0)
                            if stage < 1:
                                continue
                            MM(PS[1][:, 0:128], ln_[:], CM(d, "CQ"), True, True, ["glln", "scm"], ["ps1"])
                            MM(PS[1][:, 128:256], ln_[:], CM(d, "U"), True, True, ["glln", "scm"], ["ps1"])
                            MM(PS[1][:, 256:384], CM(d, "CKD"), ln_[:], True, True, ["glln", "scm"], ["ps1"])
                            ACT(eq[:], PS[1][:, 0:128], AF.Exp, ["ps1"], ["gleq"], scale=-1.0 / 16)
                            ACT(ek[:], PS[1][:, 0:128], AF.Exp, ["ps1"], ["glek"], scale=1.0 / 16)
                            ACT(eb[:], PS[1][:, 128:256], AF.Exp, ["ps1"], ["gleb"], scale=-1.0 / 16)
                            ACT(ekd[:], PS[1][:, 256:384], AF.Exp, ["ps1"], ["glekd"], scale=-1.0 / 16)
                            STT(qt_[:], qT[:, t0:t0 + 128], sc, eq[:], ALU.mult, ALU.mult, ["glq", "gleq"], ["glqt"])
                            TT(ktl[:], kT[:, t0:t0 + 128], ek[:], ALU.mult, ["glk", "glek"], ["glktl"])
                            STT(qb[:], qT[:, t0:t0 + 128], sc, eb[:], ALU.mult, ALU.mult, ["glq", "gleb"], ["glqb"])
                            TT(kdec[:], ktok[b][:, 0:128], ekd[:], ALU.mult, [("glkt", b), "glekd"], ["glkdec"], eng="pool")
                            if stage < 2:
                                continue
                            for h in range(4):
                                TS(qth[:, h, :], qt_[:], V("hm", h), 0.0, ALU.mult, ALU.add, ["glqt", "vecs"], ["glqth"])
                                TS(qbh[:, h, :], qb[:], V("hm", h), 0.0, ALU.mult, ALU.add, ["glqb", "vecs"], ["glqbh"])
                            if stage < 3:
                                continue
                            for h in range(4):
                                MM(PS[2][:, h * 128:(h + 1) * 128], ktl[:], qth[:, h, :], True, True, ["glktl", "glqth"], ["ps2"])
                            TT(Ah[:], PS[2][:].rearrange("p (h c) -> p h c", h=4),
                               CM(d, "M01").rearrange("p (o c) -> p o c", o=1).to_broadcast([128, 4, 128]), ALU.mult, ["ps2", "scm"], ["glA"])
                            order = (0, 1) if d == 0 else (1, 0)
                            if stage < 4:
                                continue
                            for h in range(4):
                                orow = (h % 2) * 64
                                pO = 3 + h // 2
                                MM(PS[pO][orow:orow + 64, 0:128], vb16[b][:, h * 64:(h + 1) * 64], Ah[:, h, :], True, False,
                                   [("glvb", b), "glA"], ["ps%d" % pO])
                            for ii, i in enumerate(order):
                                c0 = i * 64
                                CP(S16[:], Sst[:], ["glS"], ["glS16"], eng="pool")
                                for h in range(4):
                                    orow = (h % 2) * 64
                                    pO = 3 + h // 2
                                    MM(PS[pO][orow:orow + 64, c0:c0 + 64], S16[:, :], qbh[:, h, c0:c0 + 64], False, ii == 1,
                                       ["glS16", "glqbh"], ["ps%d" % pO])
                                MM(PS[5][:, 0:256], kdec[c0:c0 + 64, :], vb16[b][c0:c0 + 64, :], True, True, ["glkdec", ("glvb", b)], ["ps5"])
                                TT(dsm[:], PS[5][:, 0:256].rearrange("p (h v) -> p h v", h=4),
                                   V("hm", 0, 4).rearrange("p (h o) -> p h o", o=1).to_broadcast([128, 4, 64]), ALU.mult, ["ps5", "vecs"], ["gldsm"])
                                S.op("dve", lambda e: e.reduce_sum(out=dsr[:], in_=dsm[:].rearrange("p h v -> p v h"), axis=AX.X),
                                     reads=["gldsm"], writes=["gldsr"])
                                last = c0 + 63 if d == 0 else c0
                                STT(Sst[:], Sst[:], eb[:, last:last + 1], dsr[:], ALU.mult, ALU.add, ["glS", "gleb", "gldsr"], ["glS"])
                            for hp in range(2):
                                if d == 0:
                                    ACT(osum[:, hp, t0:t0 + 128], PS[3 + hp][:, 0:128], AF.Copy, ["ps%d" % (3 + hp)], ["glo"])
                                else:
                                    TT(osum[:, hp, t0:t0 + 128], osum[:, hp, t0:t0 + 128], PS[3 + hp][:, 0:128], ALU.add, ["glo", "ps%d" % (3 + hp)], ["glo"])
                    if debug.get("gla_stage", 99) >= 99:
                        head_norm_gate(ph, osum, "glo", 19, V(("glng", li)), 512)
                    S.flush()

            if "df" in mixers:
                with contextlib.ExitStack() as ph:
                    qr = sbuf(ph, "dfqr", [128, 2, T], BF16)
                    k1z = sbuf(ph, "dfk1", [128, 2, T], BF16)
                    k2z = sbuf(ph, "dfk2", [128, 2, T], BF16)
                    with contextlib.ExitStack() as ph2:
                        specs = []
                        for i in range(2):
                            specs.append((BLK["dfq"] + i, BD32, V(("dfqn", li)), True, [(qr[:, i, :], ("dfqr", i), V("one"))]))
                            specs.append((BLK["dfk"] + i, BD32, V(("dfkn", li)), True,
                                          [(k1z[:, i, :], ("dfk1", i), V("m1")), (k2z[:, i, :], ("dfk2", i), V("m2"))]))
                        qk_prep(ph2, specs)
                        S.flush()
                    va = load_vaug(ph, "dfva", 256)
                    lp = sbuf(ph, "lp", [128, 2, 2, 32], F32)
                    lpp = sbuf(ph, "lpp", [128, 2, 32], F32)
                    lps = sbuf(ph, "lps", [128, 2], F32)
                    nlam = sbuf(ph, "nlam", [128, 1], F32)
                    lam_init = 0.8 - 0.6 * math.exp(-0.3 * li)
                    DMA(lp[:].rearrange("p a b d -> p (a b d)"), dflam_in[li:li + 1, :].partition_broadcast(128), [], ["lp"])
                    TT(lpp[:], lp[:, :, 0, :], lp[:, :, 1, :], ALU.mult, ["lp"], ["lpp"])
                    S.op("dve", lambda e: e.reduce_sum(out=lps[:], in_=lpp[:], axis=AX.X), reads=["lpp"], writes=["lps"])
                    ACT(lps[:], lps[:], AF.Exp, ["lps"], ["lps"])
                    TT(nlam[:], lps[:, 1:2], lps[:, 0:1], ALU.subtract, ["lps"], ["nlam"])
                    TS(nlam[:], nlam[:], -lam_init, 0.0, ALU.add, ALU.add, ["nlam"], ["nlam"])
                    E = [sbuf(ph, "dfE%d" % i, [128, 512], BF16) for i in range(4)]
                    osb = sbuf(ph, "osb", [128, 512], F32)
                    rrow = sbuf(ph, "rrow", [128, 512], F32)
                    o1 = sbuf(ph, "o1n", [128, 512], F32)
                    o2 = sbuf(ph, "o2n", [128, 512], F32)
                    dsq = sbuf(ph, "dsq", [128, 512], F32)
                    yo = [sbuf(ph, "dfy%d" % i, [128, 512], BF16) for i in range(2)]
                    sc = 32 ** -0.5
                    ne = 0
                    ny = 0
                    qtiles = [(t0, 512, list(range(34))) for t0 in range(0, L, 512)]
                    if with_ctx:
                        qtiles.append((L, CL, [32, 33]))
                    steps = []
                    for (t0, W, kcs) in qtiles:
                        for h in range(4):
                            for ci, kc in enumerate(kcs):
                                for t in range(2):
                                    steps.append((t0, W, h, kc, t, ci == 0, ci == len(kcs) - 1))
                    PIPE = 2
                    kzs = (k1z, k2z)

                    def emit_score(i):
                        (t0, W, h, kc, t, first, last) = steps[i]
                        blk = h // 2
                        r0 = (h % 2) * 64
                        sb_ = 2 + (i % 4)
                        MM(PS[sb_][:, 0:W], kzs[t][r0:r0 + 64, blk, kc * 128:(kc + 1) * 128], qr[r0:r0 + 64, blk, t0:t0 + W],
                           True, True, [("dfk%d" % (t + 1), blk), ("dfqr", blk)], ["ps%d" % sb_])

                    def emit_rest(i):
                        nonlocal ny
                        (t0, W, h, kc, t, first, last) = steps[i]
                        sb_ = 2 + (i % 4)
                        eb = i % 4
                        ACT(E[eb][:, 0:W], PS[sb_][:, 0:W], AF.Exp, ["ps%d" % sb_], [("dfE", eb)], scale=sc)
                        MM(PS[t][0:65, 0:W], va[:, kc, h, :], E[eb][:, 0:W], first, last, ["dfva", ("dfE", eb)], ["ps%d" % t])
                        if not (last and t == 1):
                            return
                        attn_norm((osb, rrow, o1), 0, W, "o1n")
                        attn_norm((osb, rrow, o2), 1, W, "o2n")
                        STT(o1[0:64, 0:W], o2[0:64, 0:W], nlam[0:64, 0:1], o1[0:64, 0:W], ALU.mult, ALU.add,
                            ["o1n", "o2n", "nlam"], ["o1n"])
                        ACT(dsq[0:64, 0:W], o1[0:64, 0:W], AF.Square, ["o1n"], ["dsq"])
                        MM(PS[7][0:64, 0:W], BD64[0:64, 0:64], dsq[0:64, 0:W], True, True, ["cmat", "dsq"], ["ps7"])
                        ACT(dsq[0:64, 0:W], PS[7][0:64, 0:W], AF.Ln, ["ps7"], ["dsq"], bias=EPS)
                        ACT(dsq[0:64, 0:W], dsq[0:64, 0:W], AF.Exp, ["dsq"], ["dsq"], scale=-0.5)
                        yb_ = ny % 2
                        ny += 1
                        STT(dsq[0:64, 0:W], o1[0:64, 0:W], V(("dfng", li))[0:64, :], dsq[0:64, 0:W], ALU.mult, ALU.mult,
                            ["o1n", "vecs", "dsq"], ["dsq"])
                        TS(yo[yb_][0:64, 0:W], dsq[0:64, 0:W], 1.0 - lam_init, 0.0, ALU.mult, ALU.add, ["dsq"], [("dfy", yb_)])
                        DMA(ybuf[768 + h * 64:768 + (h + 1) * 64, t0:t0 + W], yo[yb_][0:64, 0:W], [("dfy", yb_)], [("ybuf", "df")])

                    for i in range(len(steps) + PIPE):
                        if i < len(steps):
                            emit_score(i)
                        if i >= PIPE:
                            emit_rest(i - PIPE)
                    S.flush()

            if "na" in mixers:
                with contextlib.ExitStack() as ph:
                    qn = sbuf(ph, "naq", [128, 2, T], BF16)
                    kn = sbuf(ph, "nak", [128, 2, T], BF16)
                    with contextlib.ExitStack() as ph2:
                        specs = []
                        for i in range(2):
                            specs.append((BLK["naq"] + i, BD64, V(("naqn", li)), False, [(qn[:, i, :], ("naq", i), V("one"))]))
                            specs.append((BLK["nak"] + i, BD64, V(("nakn", li)), False, [(kn[:, i, :], ("nak", i), V("one"))]))
                        qk_prep(ph2, specs)
                        S.flush()
                    va = load_vaug(ph, "nava", 0)
                    bias = [sbuf(ph, "nab%d" % i, [128, 21, 128], F32) for i in range(2)]
                    sbt = [sbuf(ph, "nasb%d" % i, [128, 640], F32) for i in range(2)]
                    E = [sbuf(ph, "naE%d" % i, [128, 896], BF16) for i in range(2)]
                    osb = sbuf(ph, "osb", [128, 512], F32)
                    rrow = sbuf(ph, "rrow", [128, 512], F32)
                    o1 = sbuf(ph, "o1n", [128, 512], F32)
                    yo = [sbuf(ph, "nay%d" % i, [128, 512], BF16) for i in range(2)]
                    sc = 64 ** -0.5
                    n = 0
                    ny = 0
                    for h in range(4):
                        blk = h // 2
                        r0 = (h % 2) * 64
                        hb = h % 2
                        DMA(bias[hb][:], nab_in[li, h], [], [("nab", hb)])
                        for rg in range(8):
                            for rr_ in range(4):
                                rp = rg * 4 + rr_
                                chunks = na_chunks(rp)
                                b = n % 2
                                n += 1
                                pA = 2 + 2 * b
                                pB = pA + 1
                                q_ap = qn[r0:r0 + 64, blk, rp * 128:(rp + 1) * 128]
                                nw = len(chunks)
                                for j, (kc, bi) in enumerate(chunks):
                                    pbk, pc = (pA, j * 128) if j < 4 else (pB, 0)
                                    MM(PS[pbk][:, pc:pc + 128], kn[r0:r0 + 64, blk, kc * 128:(kc + 1) * 128], q_ap, True, True,
                                       [("nak", blk), ("naq", blk)], ["ps%d" % pbk])
                                for j2 in range(2):
                                    pc = 128 + j2 * 128
                                    MM(PS[pB][:, pc:pc + 128], kn[r0:r0 + 64, blk, L + j2 * 128:L + (j2 + 1) * 128], q_ap, True, True,
                                       [("nak", blk), ("naq", blk)], ["ps%d" % pB])
                                bi0 = chunks[0][1]
                                STT(sbt[b][:, 0:512], PS[pA][:, 0:512], sc, bias[hb][:, bi0:bi0 + 4, :].rearrange("p a q -> p (a q)"),
                                    ALU.mult, ALU.add, ["ps%d" % pA, ("nab", hb)], [("nasb", b)])
                                if nw == 5:
                                    STT(sbt[b][:, 512:640], PS[pB][:, 0:128], sc, bias[hb][:, 4, :], ALU.mult, ALU.add,
                                        ["ps%d" % pB, ("nab", hb)], [("nasb", b)])
                                ACT(E[b][:, 0:nw * 128], sbt[b][:, 0:nw * 128], AF.Exp, [("nasb", b)], [("naE", b)])
                                ACT(E[b][:, 640:896], PS[pB][:, 128:384], AF.Exp, ["ps%d" % pB], [("naE", b)], scale=sc)
                                ecols = [(kc, j * 128) for j, (kc, bi) in enumerate(chunks)] + [(32, 640), (33, 768)]
                                for ci, (kc, ec) in enumerate(ecols):
                                    MM(PS[0][0:65, rr_ * 128:(rr_ + 1) * 128], va[:, kc, h, :], E[b][:, ec:ec + 128], ci == 0, ci == len(ecols) - 1,
                                       ["nava", ("naE", b)], ["ps0"])
                            attn_norm((osb, rrow, o1), 0, 512, "o1n")
                            yb_ = ny % 2
                            ny += 1
                            CP(yo[yb_][0:64, :], o1[0:64, :], ["o1n"], [("nay", yb_)], eng="pool")
                            DMA(ybuf[256 + h * 64:256 + (h + 1) * 64, rg * 512:(rg + 1) * 512], yo[yb_][0:64, :], [("nay", yb_)], [("ybuf", "na")])
                        if with_ctx:
                            q_ap = qn[r0:r0 + 64, blk, L:L + CL]
                            for j2 in range(2):
                                MM(PS[1][:, j2 * 256:(j2 + 1) * 256], kn[r0:r0 + 64, blk, L + j2 * 128:L + (j2 + 1) * 128], q_ap, True, True,
                                   [("nak", blk), ("naq", blk)], ["ps1"])
                            ACT(E[0][:, 0:512], PS[1][:, 0:512], AF.Exp, ["ps1"], [("naE", 0)], scale=sc)
                            for j2 in range(2):
                                MM(PS[0][0:65, 0:256], va[:, 32 + j2, h, :], E[0][:, j2 * 256:(j2 + 1) * 256], j2 == 0, j2 == 1,
                                   ["nava", ("naE", 0)], ["ps0"])
                            attn_norm((osb, rrow, o1), 0, 256, "o1n")
                            yb_ = ny % 2
                            ny += 1
                            CP(yo[yb_][0:64, 0:256], o1[0:64, 0:256], ["o1n"], [("nay", yb_)], eng="pool")
                            DMA(ybuf[256 + h * 64:256 + (h + 1) * 64, L:L + CL], yo[yb_][0:64, 0:256], [("nay", yb_)], [("ybuf", "na")])
                    S.flush()

            with contextlib.ExitStack() as ph:
                wg = sbuf(ph, "wg", [128, 8, 4096], BF16)
                wbr = sbuf(ph, "wbr", [128, 8, D], BF16)
                wo = sbuf(ph, "wo", [128, 8, D], BF16)
                xt = [sbuf(ph, "xt%d" % i, [128, 8, 512], F32) for i in range(2)]
                ht = [sbuf(ph, "ht%d" % i, [128, 8, 512], BF16) for i in range(2)]
                yt = [sbuf(ph, "yt%d" % i, [128, 8, 512], BF16) for i in range(2)]
                mg = sbuf(ph, "mg", [128, 8, 512], BF16)
                sig = [sbuf(ph, "sig%d" % i, [128, 512], F32) for i in range(2)]
                acc = sbuf(ph, "acc", [128, 512], F32)
                tmp = sbuf(ph, "tmp", [128, 512], F32)
                for k in range(8):
                    DMA(wg[:, k, :], wi_b[li][k * 128:(k + 1) * 128, NMIX:NIN], [("wi_b", li)], [("wg", k)])
                    DMA(wbr[:, k, :], wb_b[li][k * 128:(k + 1) * 128, :], [("wb_b", li)], [("wbr", k)])
                    DMA(wo[:, k, :], wo_b[li][k * 128:(k + 1) * 128, :], [("wo_b", li)], [("wo", k)])
                it = 0
                ng = 0
                for (sname, s0, slen) in SEQS:
                    s = 0 if sname == "lat" else 1
                    if s == 1 and not with_ctx:
                        continue
                    for t0 in range(s0, s0 + slen, 512):
                        W = min(512, s0 + slen - t0)
                        b = it % 2
                        it += 1
                        if li == layer_list[0]:
                            src = xT_in[:, t0:t0 + W] if s == 0 else cT_in[:, t0 - L:t0 - L + W]
                        else:
                            src = xbuf[:, t0:t0 + W]
                        DMA(xt[b][:, :, 0:W], src.rearrange("(k p) t -> p k t", p=128), ["xbuf"], [("xt", b)])
                        DMA(ht[b][:, :, 0:W], hbuf[:, t0:t0 + W].rearrange("(k p) t -> p k t", p=128), ["hbuf"], [("ht", b)])
                        DMA(yt[b][:, :, 0:W], ybuf[:, t0:t0 + W].rearrange("(k p) t -> p k t", p=128), ["ybuf"], [("yt", b)])
                        for dc in range(8):
                            for g in range(4):
                                pa = 2 * (ng % 2)
                                pbk = pa + 1
                                sgi = ng % 2
                                ng += 1
                                cg = g * D + dc * 128
                                for k in range(8):
                                    MM(PS[pa][:, 0:W], wg[:, k, cg:cg + 128], ht[b][:, k, 0:W], k == 0, k == 7,
                                       [("wg", k), ("ht", b)], ["ps%d" % pa])
                                for k2 in range(2):
                                    MM(PS[pbk][:, 0:W], wbr[:, 2 * g + k2, dc * 128:(dc + 1) * 128], yt[b][:, 2 * g + k2, 0:W],
                                       k2 == 0, k2 == 1, [("wbr", 2 * g + k2), ("yt", b)], ["ps%d" % pbk])
                                ACT(sig[sgi][:, 0:W], PS[pa][:, 0:W], AF.Sigmoid, ["ps%d" % pa, "vecs"], [("sig", sgi)],
                                    bias=V(("bgate", li), g * 8 + dc))
                                if g == 0:
                                    TT(acc[:, 0:W], sig[sgi][:, 0:W], PS[pbk][:, 0:W], ALU.mult, [("sig", sgi), "ps%d" % pbk], ["acc"])
                                else:
                                    TT(tmp[:, 0:W], sig[sgi][:, 0:W], PS[pbk][:, 0:W], ALU.mult, [("sig", sgi), "ps%d" % pbk], ["tmp"])
                                    if g < 3:
                                        TT(acc[:, 0:W], acc[:, 0:W], tmp[:, 0:W], ALU.add, ["acc", "tmp"], ["acc"], eng="pool")
                                    else:
                                        TT(mg[:, dc, 0:W], acc[:, 0:W], tmp[:, 0:W], ALU.add, ["acc", "tmp"], [("mg", dc)], eng="pool")
                        for dc in range(8):
                            pb = 4 + dc % 2
                            for k in range(8):
                                MM(PS[pb][:, 0:W], wo[:, k, dc * 128:(dc + 1) * 128], mg[:, k, 0:W], k == 0, k == 7,
                                   [("wo", k), ("mg", k)], ["ps%d" % pb])
                            STT(xt[b][:, dc, 0:W], PS[pb][:, 0:W], modv(li, "g1", dc, s), xt[b][:, dc, 0:W], ALU.mult, ALU.add,
                                ["ps%d" % pb, "modsb", ("xt", b)], [("xt", b)])
                        DMA(x1buf[:, t0:t0 + W].rearrange("(k p) t -> p k t", p=128), xt[b][:, :, 0:W], [("xt", b)], ["x1buf"])
                S.flush()

            with contextlib.ExitStack() as ph:
                FT = 456
                xt = [sbuf(ph, "xt%d" % i, [128, 8, 512], F32) for i in range(2)]
                ht = sbuf(ph, "ht", [128, 8, 512], BF16)
                gt = sbuf(ph, "gt", [128, 22, 512], BF16)
                sq = sbuf(ph, "sq", [128, 512], BF16)
                rr = sbuf(ph, "rr", [128, 512], F32)
                ff = sbuf(ph, "ff", [128, 512], F32)
                ust = [sbuf(ph, "ust%d" % i, [128, 516], F32) for i in range(2)]
                ca = sbuf(ph, "ca", [128, 512], F32)
                cb_ = sbuf(ph, "cb", [128, 512], F32)
                w1 = [sbuf(ph, "w1_%d" % i, [128, 8, 256], BF16) for i in range(3)]
                w2f = sbuf(ph, "w2f", [128, 22, D], BF16)
                for j in range(22):
                    DMA(w2f[:, j, :], f2_b[li][j * 128:(j + 1) * 128, :], [("f2_b", li)], [("w2f", j)])
                it = 0
                nw = 0
                for (sname, s0, slen) in SEQS:
                    s = 0 if sname == "lat" else 1
                    if s == 1 and not with_ctx:
                        continue
                    for t0 in range(s0, s0 + slen, FT):
                        t1 = min(t0 + FT, s0 + slen)
                        a0 = max(t0 - 1, s0)
                        a1 = min(t1 + 1, s0 + slen)
                        W = a1 - a0
                        WI = t1 - t0
                        io = t0 - a0
                        b = it % 2
                        it += 1
                        DMA(xt[b][:, :, 0:W], x1buf[:, a0:a1].rearrange("(k p) t -> p k t", p=128), ["x1buf"], [("xt", b)])
                        norm_tile(xt[b], ht, W, li, 1, s, sq, rr, ff, 0, ("xt", b), "ht", "n2")
                        for j in range(22):
                            wb = nw % 3
                            nw += 1
                            DMA(w1[wb][:, :, 0:128], f1_b[li][:, j * 128:(j + 1) * 128].rearrange("(k p) c -> p k c", p=128),
                                [("f1_b", li)], [("w1", wb)])
                            DMA(w1[wb][:, :, 128:256], f1_b[li][:, DFF + j * 128:DFF + (j + 1) * 128].rearrange("(k p) c -> p k c", p=128),
                                [("f1_b", li)], [("w1", wb)])
                            pu = 1 + 2 * (j % 2)
                            pv = pu + 1
                            ub = j % 2
                            for k in range(8):
                                MM(PS[pu][:, 0:W], w1[wb][:, k, 0:128], ht[:, k, 0:W], k == 0, k == 7, [("w1", wb), "ht"], ["ps%d" % pu])
                            for k in range(8):
                                MM(PS[pv][:, 0:W], w1[wb][:, k, 128:256], ht[:, k, 0:W], k == 0, k == 7, [("w1", wb), "ht"], ["ps%d" % pv])
                            c_in = 1 - io
                            ACT(ust[ub][:, c_in + 0:c_in + W], PS[pu][:, 0:W], AF.Copy, ["ps%d" % pu], [("ust", ub)])
                            if io == 0:
                                MSET(ust[ub][:, 0:1], 0.0, [("ust", ub)])
                            if a1 == t1:
                                MSET(ust[ub][:, WI + 1:WI + 2], 0.0, [("ust", ub)])
                            fo = voff[("fcw", li)]
                            TS(ca[:, 0:WI], ust[ub][:, 1:1 + WI], vecs[:, fo + 22 + j:fo + 23 + j], V(("fcb", li), j), ALU.mult, ALU.add,
                               [("ust", ub), "vecs"], ["ca"])
                            STT(cb_[:, 0:WI], ust[ub][:, 0:WI], vecs[:, fo + j:fo + j + 1], ca[:, 0:WI], ALU.mult, ALU.add,
                                [("ust", ub), "vecs", "ca"], ["cb"])
                            STT(ca[:, 0:WI], ust[ub][:, 2:2 + WI], vecs[:, fo + 44 + j:fo + 45 + j], cb_[:, 0:WI], ALU.mult, ALU.add,
                                [("ust", ub), "vecs", "cb"], ["ca"])
                            ACT(cb_[:, 0:WI], ca[:, 0:WI], AF.Silu, ["ca"], ["cb"])
                            TT(gt[:, j, 0:WI], cb_[:, 0:WI], PS[pv][:, io:io + WI], ALU.mult, ["cb", "ps%d" % pv], [("gt", j)])
                        for dc in range(8):
                            pb = 5 + dc % 2
                            for j in range(22):
                                MM(PS[pb][:, 0:WI], w2f[:, j, dc * 128:(dc + 1) * 128], gt[:, j, 0:WI], j == 0, j == 21,
                                   [("w2f", j), ("gt", j)], ["ps%d" % pb])
                            STT(xt[b][:, dc, io:io + WI], PS[pb][:, 0:WI], modv(li, "g2", dc, s), xt[b][:, dc, io:io + WI], ALU.mult, ALU.add,
                                ["ps%d" % pb, "modsb", ("xt", b)], [("xt", b)])
                        dst = outT[:, t0:t1] if (li == DEPTH - 1 and s == 0) else xbuf[:, t0:t1]
                        DMA(dst.rearrange("(k p) t -> p k t", p=128), xt[b][:, :, io:io + WI], [("xt", b)], ["xbuf"])
                        if xd is not None:
                            DMA(xd[li][:, t0:t1].rearrange("(k p) t -> p k t", p=128), xt[b][:, :, io:io + WI], [("xt", b)], ["xd"])
                S.flush()
    return nc


def host_inputs(inp, b):
    m = {}
    m["xT"] = np.ascontiguousarray(np.asarray(inp["x"][b], np.float32).T)
    m["ctxT"] = np.ascontiguousarray(np.asarray(inp["ctx"][b], np.float32).T)
    cs = np.stack([np.asarray(inp["c"][b], np.float32), np.asarray(inp["c_ctx"], np.float32)], axis=-1)
    m["cs"] = np.ascontiguousarray(cs.reshape(8, 128, 2).transpose(1, 0, 2).reshape(128, 16))
    m["vecs"] = pack_vecs(inp)
    m["cmat"] = const_mats()
    m["rope"] = rope_tables()
    m["nab"] = np.stack([na_bias_tables(np.asarray(inp["na_rpb"][li], np.float32)) for li in range(DEPTH)], 0)
    m["scm"] = scan_mats()
    m["gla_w"] = np.ascontiguousarray(np.concatenate([np.asarray(inp["gla_w_a2"], np.float32),
                                                      np.asarray(inp["gla_b_a"], np.float32)[:, :, None, :]], axis=2))
    m["dflam"] = np.ascontiguousarray(np.asarray(inp["df_lambda"], np.float32).reshape(DEPTH, 128))
    for k in ("w_mod", "w_in", "w_out", "ffn_w_in", "ffn_w_out"):
        m[k] = np.ascontiguousarray(np.asarray(inp[k], np.float32))
    m["w_branch"] = np.ascontiguousarray(np.asarray(inp["w_branch"], np.float32).reshape(DEPTH, D, D))
    return m


def kernel(**inp):
    nc = build()
    shared = None
    in_maps = []
    for b in range(8):
        m = host_inputs(inp, b)
        if shared is None:
            shared = {k: m[k] for k in ("vecs", "cmat", "rope", "nab", "dflam", "scm", "gla_w", "w_mod", "w_in", "w_out", "ffn_w_in", "ffn_w_out", "w_branch")}
        else:
            m.update(shared)
        in_maps.append(m)
    res = run_bass_kernel_spmd(nc, in_maps, core_ids=list(range(8)))
    out = np.stack([np.ascontiguousarray(r["outT"].T) for r in res.results], axis=0)
    return out.astype(np.float32)
```

```python
import contextlib
import math
import numpy as np
import ml_dtypes
import concourse.bass as bass
import concourse.mybir as mybir
from concourse.bass_utils import run_bass_kernel_spmd

F32 = mybir.dt.float32
BF16 = mybir.dt.bfloat16
AF = mybir.ActivationFunctionType
ALU = mybir.AluOpType
AX = mybir.AxisListType

D = 1024
L = 4096
CL = 256
T = L + CL
DEPTH = 4
NIN = 7472
NMIX = 3376
DFF = 2816
EPS = 1e-6
NDMA_SEM = 8
SEQS = (("lat", 0, L), ("ctx", L, CL))


class Sched:
    ENGS = ("pe", "act", "dve", "pool", "sp")

    def __init__(self, nc, st):
        self.nc = nc
        self.ops = []
        self.last_w = {}
        self.readers = {}
        self.dma_count = {"sp": 0, "pool": 0}
        self.dma_hist = {"sp": [], "pool": []}
        self.emitted = 0
        self.cnt = {e: 0 for e in self.ENGS}
        self.sems = {}
        for e in self.ENGS:
            self.sems[e] = st.enter_context(nc.semaphore("s_" + e))
        for q in ("sp", "pool"):
            for i in range(NDMA_SEM):
                self.sems[(q, i)] = st.enter_context(nc.semaphore("d_%s%d" % (q, i)))

    def op(self, eng, fn, reads=(), writes=(), dma=False):
        oid = len(self.ops)
        deps = {}
        lo = self.emitted
        for k in reads:
            w = self.last_w.get(k)
            if w is not None and w >= lo:
                deps[w] = 2
        for k in writes:
            w = self.last_w.get(k)
            if w is not None and w >= lo:
                deps[w] = max(deps.get(w, 0), 1)
            for r in self.readers.get(k, ()):
                if r >= lo:
                    deps.setdefault(r, 0)
        for d in list(deps):
            do = self.ops[d]
            if do["eng"] == eng and not do["dma"] and not dma:
                if deps[d] == 0 or (deps[d] == 1 and eng == "pe"):
                    del deps[d]
        o = dict(eng=eng, fn=fn, dma=dma, deps=deps)
        if dma:
            n = self.dma_count[eng]
            self.dma_count[eng] = n + 1
            o["dma_i"] = n
            h = self.dma_hist[eng]
            if n >= NDMA_SEM and h[n - NDMA_SEM] >= lo:
                deps[h[n - NDMA_SEM]] = 2
            h.append(oid)
        self.ops.append(o)
        for k in reads:
            self.readers.setdefault(k, []).append(oid)
        for k in writes:
            self.last_w[k] = oid
            self.readers[k] = []
        return oid

    def flush(self):
        nc = self.nc
        allops = self.ops
        lo = self.emitted
        ops = allops[lo:]
        self.emitted = len(allops)
        if not ops:
            return
        for o in ops:
            o["sig"] = o["dma"]
        for o in ops:
            for d in o["deps"]:
                if not allops[d]["dma"]:
                    allops[d]["sig"] = True
        for o in ops:
            if o["dma"]:
                i = o["dma_i"]
                o["semkey"] = (o["eng"], i % NDMA_SEM)
                o["semval"] = 16 * (i // NDMA_SEM + 1)
            elif o["sig"]:
                self.cnt[o["eng"]] += 1
                o["semkey"] = o["eng"]
                o["semval"] = self.cnt[o["eng"]]
        known = {e: {} for e in self.ENGS}
        for o in ops:
            kn = known[o["eng"]]
            waits = []
            for d in sorted(o["deps"]):
                do = allops[d]
                sk, sv = do["semkey"], do["semval"]
                if kn.get(sk, 0) >= sv:
                    continue
                waits.append((sk, sv))
                kn[sk] = sv
                for k2, v2 in do["clock"].items():
                    if kn.get(k2, 0) < v2:
                        kn[k2] = v2
            o["waits"] = waits
            o["clock"] = dict(kn)
            if "semkey" in o and not o["dma"]:
                o["clock"][o["semkey"]] = o["semval"]
        sems = self.sems
        dma_count = dict(self.dma_count)

        def replay(ename):
            def body(eng):
                for o in ops:
                    if o["eng"] != ename:
                        continue
                    for sk, sv in o["waits"]:
                        eng.wait_ge(sems[sk], sv)
                    ins = o["fn"](eng)
                    if o["dma"]:
                        ins.then_inc(sems[o["semkey"]], 16)
                    elif o["sig"]:
                        ins.then_inc(sems[o["semkey"]], 1)
                if ename in ("sp", "pool"):
                    n = dma_count[ename]
                    for i in range(NDMA_SEM):
                        c = (n - i + NDMA_SEM - 1) // NDMA_SEM
                        if c > 0:
                            eng.wait_ge(sems[(ename, i)], 16 * c)
            return body

        with nc.Block() as block:
            block.tensor(replay("pe"))
            block.scalar(replay("act"))
            block.vector(replay("dve"))
            block.gpsimd(replay("pool"))
            block.sync(replay("sp"))
        for o in ops:
            o["fn"] = None
            o["clock"] = None


def vec_layout():
    off = {}
    n = 0

    def add(name, cols):
        nonlocal n
        off[name] = n
        n += cols

    for li in range(DEPTH):
        add(("bmod", li), 48)
        add(("n1g", li), 8)
        add(("n2g", li), 8)
        add(("bgate", li), 32)
        add(("fcw", li), 66)
        add(("fcb", li), 22)
        for nm in ("dfqn", "dfkn", "dfng", "naqn", "nakn", "glng", "dnng", "dndtb", "dnalog"):
            add((nm, li), 1)
        add(("dncw", li), 18)
    add("m1", 1)
    add("m2", 1)
    add("one", 1)
    add("hm", 4)
    add("dnsgn", 1)
    return off, n


def pmajor(v):
    v = np.asarray(v, np.float32)
    lead = int(np.prod(v.shape[:-1])) if v.ndim > 1 else 1
    n = v.shape[-1] // 128
    return np.ascontiguousarray(v.reshape(lead, n, 128).transpose(2, 0, 1).reshape(128, lead * n))


def pack_vecs(inp):
    off, n = vec_layout()
    V = np.zeros((128, n), np.float32)

    def put(name, arr):
        V[:, off[name]:off[name] + arr.shape[1]] = arr

    for li in range(DEPTH):
        put(("bmod", li), pmajor(inp["b_mod"][li]))
        put(("n1g", li), pmajor(inp["norm1_g"][li]))
        put(("n2g", li), pmajor(inp["norm2_g"][li]))
        put(("bgate", li), pmajor(inp["b_gate"][li]))
        put(("fcw", li), pmajor(inp["ffn_conv_w"][li]))
        put(("fcb", li), pmajor(inp["ffn_conv_b"][li]))
        put(("dfqn", li), np.tile(inp["df_q_norm"][li], 4)[:, None])
        put(("dfkn", li), np.tile(inp["df_k_norm"][li], 4)[:, None])
        put(("dfng", li), np.tile(inp["df_norm_g"][li], 2)[:, None])
        put(("naqn", li), np.tile(inp["na_q_norm"][li], 2)[:, None])
        put(("nakn", li), np.tile(inp["na_k_norm"][li], 2)[:, None])
        put(("glng", li), np.tile(inp["gla_norm_g"][li], 2)[:, None])
        put(("dnng", li), np.tile(inp["dn_norm_g"][li], 2)[:, None])
        z8 = np.zeros(8, np.float32)
        put(("dndtb", li), np.concatenate([z8, np.asarray(inp["dn_dt_bias"][li], np.float32).reshape(8), np.zeros(112, np.float32)])[:, None])
        put(("dnalog", li), np.concatenate([z8, np.asarray(inp["dn_a_log"][li], np.float32).reshape(8), np.zeros(112, np.float32)])[:, None])
        put(("dncw", li), pmajor(inp["dn_conv"][li]))
    p = np.arange(128)
    put("m1", ((p // 32) % 2 == 0).astype(np.float32)[:, None])
    put("m2", ((p // 32) % 2 == 1).astype(np.float32)[:, None])
    put("one", np.ones((128, 1), np.float32))
    put("hm", (p[:, None] // 32 == np.arange(4)[None, :]).astype(np.float32))
    put("dnsgn", np.where(p < 8, -1.0, 1.0).astype(np.float32)[:, None])
    return V


NEG = -30000.0


def const_mats():
    C = np.zeros((4, 128, 128), np.float32)
    p = np.arange(128)
    C[0] = (p[:, None] // 32 == p[None, :] // 32) / 32.0
    C[1] = (p[:, None] // 64 == p[None, :] // 64) / 64.0
    for m in range(128):
        g, d = m // 32, m % 32
        q = d // 8
        if q == 0:
            C[2, g * 32 + d + 8, m] = -1.0
        elif q == 1:
            C[2, g * 32 + d - 8, m] = 1.0
        elif q == 2:
            C[2, g * 32 + d + 8, m] = -1.0
        else:
            C[2, g * 32 + d - 8, m] = 1.0
    C[3] = 1.0
    return np.ascontiguousarray(C.transpose(1, 0, 2).reshape(128, 512))


def rope_tables():
    t = np.arange(L)
    row = (t // 64).astype(np.float32)
    col = (t % 64).astype(np.float32)
    nf = 8
    inv = np.power(np.float32(10000.0), -np.arange(nf, dtype=np.float32) / nf).astype(np.float32)
    ar = row[:, None] * inv
    ac = col[:, None] * inv
    ang = np.concatenate([ar, ar, ac, ac], -1)
    cs = np.stack([np.cos(ang), np.sin(ang)], 0).astype(np.float32)
    return np.ascontiguousarray(np.tile(cs.transpose(0, 2, 1), (1, 4, 1)))


SCM = {}
for _i, _n in enumerate(("I", "U", "NU", "MI", "MS", "MST", "CKD", "CQ", "M01")):
    SCM[_n] = _i
NSCM = len(SCM)


def scan_mats():
    t = np.arange(128)
    same = (t[:, None] // 64) == (t[None, :] // 64)
    out = np.zeros((2, NSCM, 128, 128), np.float32)
    for d in range(2):
        before = (t[:, None] <= t[None, :]) if d == 0 else (t[:, None] >= t[None, :])
        strict = (t[:, None] < t[None, :]) if d == 0 else (t[:, None] > t[None, :])
        U = (same & before).astype(np.float32)
        out[d, SCM["I"]] = np.eye(128, dtype=np.float32)
        out[d, SCM["U"]] = U
        out[d, SCM["NU"]] = -U
        out[d, SCM["MI"]] = np.where(same & before, 0.0, NEG)
        out[d, SCM["MS"]] = np.where(same & strict, 0.0, NEG)
        out[d, SCM["MST"]] = np.where(same & strict, 0.0, NEG).T
        out[d, SCM["CKD"]] = (same & strict.T).astype(np.float32)
        pos = t % 64
        midpos = 31 if d == 0 else 32
        umid = (same & ((pos[:, None] <= midpos) if d == 0 else (pos[:, None] >= midpos))).astype(np.float32)
        out[d, SCM["CQ"]] = U - umid
        out[d, SCM["M01"]] = (same & before).astype(np.float32)
    return np.ascontiguousarray(out.transpose(2, 0, 1, 3).reshape(128, 2 * NSCM * 128))


def na_chunks(rp):
    if rp in (0, 1):
        return [(kc, 5 + rp * 4 + kc) for kc in range(4)]
    if rp in (30, 31):
        return [(28 + j, 13 + (rp - 30) * 4 + j) for j in range(4)]
    return [(rp - 2 + j, j) for j in range(5)]


def na_bias_tables(rpb):
    out = np.full((4, 128, 21, 128), NEG, np.float32)
    kk = np.arange(128)
    for rp in [2, 0, 1, 30, 31]:
        for (kc, bi) in na_chunks(rp):
            qrow = 2 * rp + kk // 64
            qcol = kk % 64
            krow = 2 * kc + kk // 64
            kcol = kk % 64
            rs = np.clip(qrow - 4, 0, 56)
            cst = np.clip(qcol - 8, 0, 48)
            dr = krow[:, None] - qrow[None, :]
            dc = kcol[:, None] - qcol[None, :]
            valid = ((krow[:, None] >= rs[None, :]) & (krow[:, None] < rs[None, :] + 8)
                     & (kcol[:, None] >= cst[None, :]) & (kcol[:, None] < cst[None, :] + 16))
            ri = np.clip(dr + 7, 0, 14)
            ci = np.clip(dc + 15, 0, 30)
            for h in range(4):
                g = rpb[h][ri, ci]
                out[h, :, bi, :] = np.where(valid, g, np.float32(NEG))
    return out


PF_BLOCKS = ([(i * 128, 128) for i in range(8)] + [(1024, 16)] + [(1040 + i * 128, 128) for i in range(6)]
             + [(1808, 128), (1936, 128), (2064, 128), (2192, 128), (2320, 128), (2448, 128), (2576, 16), (2592, 16)]
             + [(2608 + i * 128, 128) for i in range(6)])
NPF = len(PF_BLOCKS)
PT_GROUPS = ((1552, 256, 0), (3120, 256, 256), (1936, 384, 512))
NPT = 896


def build(debug=None, nlayers=DEPTH):
    debug = debug or {}
    nc = bass.Bass("TRN2", target_bir_lowering=False)
    voff, NV = vec_layout()

    def din(name, shape, dt=F32):
        return nc.dram_tensor(name, list(shape), dt, kind="ExternalInput").ap()

    def dscr(name, shape, dt):
        kind = "ExternalOutput" if name in debug.get("dump", ()) else "Internal"
        return nc.dram_tensor(name, list(shape), dt, kind=kind).ap()

    xT_in = din("xT", [D, L])
    cT_in = din("ctxT", [D, CL])
    cs_in = din("cs", [128, 16])
    vecs_in = din("vecs", [128, NV])
    w_mod = din("w_mod", [DEPTH, D, 6 * D])
    w_in = din("w_in", [DEPTH, D, NIN])
    w_branch = din("w_branch", [DEPTH, D, D])
    w_out = din("w_out", [DEPTH, D, D])
    f_w_in = din("ffn_w_in", [DEPTH, D, 2 * DFF])
    f_w_out = din("ffn_w_out", [DEPTH, DFF, D])
    outT = nc.dram_tensor("outT", [D, L], F32, kind="ExternalOutput").ap()
    y_dbg = din("y_dbg", [D, T], BF16) if debug.get("y_in") else None
    mixers = debug.get("mixers", ("dn", "na", "gla", "df"))
    cmat_in = din("cmat", [128, 512])
    rope_in = din("rope", [2, 128, L])
    nab_in = din("nab", [DEPTH, 4, 128, 21, 128])
    dflam_in = din("dflam", [DEPTH, 128])
    scm_in = din("scm", [128, 2 * NSCM * 128])
    gla_w_in = din("gla_w", [DEPTH, 2, 17, 128])

    wi_b = dscr("wi_b", [DEPTH, D, NIN], BF16)
    wb_b = dscr("wb_b", [DEPTH, D, D], BF16)
    wo_b = dscr("wo_b", [DEPTH, D, D], BF16)
    f1_b = dscr("f1_b", [DEPTH, D, 2 * DFF], BF16)
    f2_b = dscr("f2_b", [DEPTH, DFF, D], BF16)
    xbuf = dscr("xbuf", [D, T], F32)
    x1buf = dscr("x1buf", [D, T], F32)
    hbuf = dscr("hbuf", [D, T], BF16)
    ybuf = dscr("ybuf", [D, T], BF16)
    PF = dscr("PF", [NPF * 128, T], F32)
    PT = dscr("PT", [T, NPT], F32)
    xd = nc.dram_tensor("xd", [DEPTH, D, T], F32, kind="ExternalOutput").ap() if debug.get("xdump") else None

    with contextlib.ExitStack() as top:
        S = Sched(nc, top)

        uid = [0]

        def sbuf(st, name, shape, dt):
            uid[0] += 1
            return st.enter_context(nc.sbuf_tensor("sb%d_%s" % (uid[0], name), list(shape), dt))

        PS = [top.enter_context(nc.psum_tensor("ps%d" % i, [128, 512], F32)) for i in range(8)]

        def MM(out, lhsT, rhs, st, sp, r, w):
            S.op("pe", lambda e: e.matmul(out, lhsT=lhsT, rhs=rhs, start=st, stop=sp), reads=r, writes=w)

        def ACT(out, in_, func, r, w, bias=0.0, scale=1.0):
            S.op("act", lambda e: e.activation(out=out, in_=in_, func=func, bias=bias, scale=scale), reads=r, writes=w)

        def TT(out, in0, in1, op, r, w, eng="dve"):
            S.op(eng, lambda e: e.tensor_tensor(out=out, in0=in0, in1=in1, op=op), reads=r, writes=w)

        def TS(out, in0, s1, s2, op0, op1, r, w, eng="dve"):
            S.op(eng, lambda e: e.tensor_scalar(out=out, in0=in0, scalar1=s1, scalar2=s2, op0=op0, op1=op1), reads=r, writes=w)

        def STT(out, in0, scalar, in1, op0, op1, r, w, eng="dve"):
            S.op(eng, lambda e: e.scalar_tensor_tensor(out=out, in0=in0, scalar=scalar, in1=in1, op0=op0, op1=op1), reads=r, writes=w)

        def CP(out, in_, r, w, eng="dve"):
            S.op(eng, lambda e: e.tensor_copy(out=out, in_=in_), reads=r, writes=w)

        def MSET(ap, val, w, eng="pool"):
            S.op(eng, lambda e: e.memset(ap, val), writes=w)

        def DMA(out, in_, r, w, q="sp"):
            S.op(q, lambda e: e.dma_start(out=out, in_=in_), reads=r, writes=w, dma=True)

        vecs = sbuf(top, "vecs", [128, NV], F32)
        modsb = sbuf(top, "modsb", [128, DEPTH, 48, 2], F32)
        gsc = sbuf(top, "gsc", [128, DEPTH, 2, 8, 2], F32)
        ones_b = sbuf(top, "ones_b", [128, 128], BF16)
        DMA(vecs[:], vecs_in[:], [], ["vecs"])
        MSET(ones_b[:], 1.0, ["ones_b"])

        cmat = sbuf(top, "cmat", [128, 512], F32)
        DMA(cmat[:], cmat_in[:], [], ["cmat"])
        BD32 = cmat[:, 0:128]
        BD64 = cmat[:, 128:256]
        RM = cmat[:, 256:384]
        ONESF = cmat[:, 384:512]

        scm = sbuf(top, "scm", [128, 2, NSCM, 128], F32)
        DMA(scm[:].rearrange("p a b c -> p (a b c)"), scm_in[:], [], ["scm"])

        def CM(d, name):
            return scm[:, d, SCM[name], :]

        def V(name, j=0, n=1):
            o = voff[name] + j
            return vecs[:, o:o + n]

        def cast2d(dst, src, rows, cols, key):
            for r0 in range(0, rows, 1024):
                r1 = min(rows, r0 + 1024)
                for c0 in range(0, cols, 2048):
                    c1 = min(cols, c0 + 2048)
                    DMA(dst[r0:r1, c0:c1], src[r0:r1, c0:c1], [], [key], q="pool")

        layer_list = list(debug.get("layers", range(nlayers)))
        for li in layer_list:
            cast2d(wi_b[li], w_in[li], D, NIN, ("wi_b", li))
            cast2d(wb_b[li], w_branch[li], D, D, ("wb_b", li))
            cast2d(wo_b[li], w_out[li], D, D, ("wo_b", li))
            cast2d(f1_b[li], f_w_in[li], D, 2 * DFF, ("f1_b", li))
            cast2d(f2_b[li], f_w_out[li], DFF, D, ("f2_b", li))

        with contextlib.ExitStack() as ph:
            scs = sbuf(ph, "scs", [128, 16], F32)
            wm = [sbuf(ph, "wm%d" % i, [128, 8, 768], F32) for i in range(2)]
            DMA(scs[:], cs_in[:], [], ["scs"])
            ACT(scs[:], scs[:], AF.Silu, ["scs"], ["scs"])
            nb = 0
            for li in layer_list:
                wv = w_mod[li].rearrange("(k p) c -> p k c", p=128)
                for cb in range(8):
                    wt = wm[nb % 2]
                    wk = ("wm", nb % 2)
                    nb += 1
                    DMA(wt[:], wv[:, :, cb * 768:(cb + 1) * 768], [], [wk])
                    for jj in range(6):
                        j = cb * 6 + jj
                        for k in range(8):
                            MM(PS[0][:, 2 * j:2 * j + 2], wt[:, k, jj * 128:(jj + 1) * 128], scs[:, 2 * k:2 * k + 2],
                               k == 0, k == 7, [wk, "scs"], ["ps0"])
                TT(modsb[:, li], PS[0][:, 0:96].rearrange("p (j s) -> p j s", s=2),
                   V(("bmod", li), 0, 48).rearrange("p (j o) -> p j o", o=1).to_broadcast([128, 48, 2]), ALU.add,
                   ["ps0", "vecs"], ["modsb"])
                for which, (so, gname) in enumerate(((8, "n1g"), (32, "n2g"))):
                    TS(gsc[:, li, which], modsb[:, li, so:so + 8, :], 1.0, 0.0, ALU.add, ALU.add, ["modsb"], ["gsc"])
                    TT(gsc[:, li, which], gsc[:, li, which],
                       V((gname, li), 0, 8).rearrange("p (j o) -> p j o", o=1).to_broadcast([128, 8, 2]), ALU.mult,
                       ["gsc", "vecs"], ["gsc"])
            S.flush()

        def modv(li, what, k, s):
            base = {"sh1": 0, "sc1": 8, "g1": 16, "sh2": 24, "sc2": 32, "g2": 40}[what]
            return modsb[:, li, base + k, s:s + 1]

        def norm_tile(xt, ht, W, li, which, s, tmp_sq, tmp_r, tmp_f, psb, kx, kh, tag):
            shn = "sh1" if which == 0 else "sh2"
            for k in range(8):
                ACT(tmp_sq[:, 0:W], xt[:, k, 0:W], AF.Square, [kx], [tag + "sq"])
                MM(PS[psb][:, 0:W], ones_b[:], tmp_sq[:, 0:W], k == 0, k == 7, ["ones_b", tag + "sq"], ["ps%d" % psb])
            ACT(tmp_r[:, 0:W], PS[psb][:, 0:W], AF.Ln, ["ps%d" % psb], [tag + "r"], bias=EPS, scale=1.0 / D)
            ACT(tmp_r[:, 0:W], tmp_r[:, 0:W], AF.Exp, [tag + "r"], [tag + "r"], scale=-0.5)
            for k in range(8):
                STT(tmp_f[:, 0:W], xt[:, k, 0:W], gsc[:, li, which, k, s:s + 1], tmp_r[:, 0:W], ALU.mult, ALU.mult,
                    [kx, "gsc", tag + "r"], [tag + "f"])
                ACT(ht[:, k, 0:W], tmp_f[:, 0:W], AF.Identity, [tag + "f", "modsb"], [kh], bias=modv(li, shn, k, s))

        for li in layer_list:
            with_ctx = li < DEPTH - 1
            with contextlib.ExitStack() as ph:
                wi = sbuf(ph, "wi", [128, 8, NMIX], BF16)
                xt = [sbuf(ph, "xt%d" % i, [128, 8, 512], F32) for i in range(2)]
                ht = [sbuf(ph, "ht%d" % i, [128, 8, 512], BF16) for i in range(2)]
                sq = sbuf(ph, "sq", [128, 512], BF16)
                rr = sbuf(ph, "rr", [128, 512], F32)
                ff = sbuf(ph, "ff", [128, 512], F32)
                stg = [sbuf(ph, "stg%d" % i, [128, 512], F32) for i in range(4)]
                for k in range(8):
                    DMA(wi[:, k, :], wi_b[li][k * 128:(k + 1) * 128, 0:NMIX], [("wi_b", li)], [("wi", k)])
                wik = [("wi", k) for k in range(8)]
                it = 0
                nst = 0
                for (sname, s0, slen) in SEQS:
                    s = 0 if sname == "lat" else 1
                    for t0 in range(s0, s0 + slen, 512):
                        W = min(512, s0 + slen - t0)
                        b = it % 2
                        it += 1
                        if li == layer_list[0]:
                            src = xT_in[:, t0:t0 + W] if s == 0 else cT_in[:, t0 - L:t0 - L + W]
                        else:
                            src = xbuf[:, t0:t0 + W]
                        DMA(xt[b][:, :, 0:W], src.rearrange("(k p) t -> p k t", p=128), ["xbuf"], [("xt", b)])
                        norm_tile(xt[b], ht[b], W, li, 0, s, sq, rr, ff, 0, ("xt", b), ("ht", b), "n1")
                        DMA(hbuf[:, t0:t0 + W].rearrange("(k p) t -> p k t", p=128), ht[b][:, :, 0:W], [("ht", b)], ["hbuf"])
                        for bi, (c0, ncol) in enumerate(PF_BLOCKS):
                            pb = 1 + (bi % 4)
                            for k in range(8):
                                MM(PS[pb][0:ncol, 0:W], wi[:, k, c0:c0 + ncol], ht[b][:, k, 0:W], k == 0, k == 7,
                                   [("wi", k), ("ht", b)], ["ps%d" % pb])
                            sg = nst % 4
                            nst += 1
                            if bi % 2 == 0:
                                ACT(stg[sg][0:ncol, 0:W], PS[pb][0:ncol, 0:W], AF.Copy, ["ps%d" % pb], [("stg", sg)])
                            else:
                                CP(stg[sg][0:ncol, 0:W], PS[pb][0:ncol, 0:W], ["ps%d" % pb], [("stg", sg)])
                            DMA(PF[bi * 128:bi * 128 + ncol, t0:t0 + W], stg[sg][0:ncol, 0:W], [("stg", sg)], [("PF", bi)])
                        for q in range(W // 128):
                            for gi, (c0, ncol, p0) in enumerate(PT_GROUPS):
                                pb = 5 if gi < 2 else 7
                                pcol = 256 if gi == 1 else 0
                                for k in range(8):
                                    MM(PS[pb][:, pcol:pcol + ncol], ht[b][:, k, q * 128:(q + 1) * 128], wi[:, k, c0:c0 + ncol],
                                       k == 0, k == 7, [("wi", k), ("ht", b)], ["ps%d" % pb])
                            for pb, p0, ncol in ((5, 0, 512), (7, 512, 384)):
                                sg = nst % 4
                                nst += 1
                                CP(stg[sg][:, 0:ncol], PS[pb][:, 0:ncol], ["ps%d" % pb], [("stg", sg)])
                                DMA(PT[t0 + q * 128:t0 + (q + 1) * 128, p0:p0 + ncol], stg[sg][:, 0:ncol], [("stg", sg)], [("PT", p0)])
                S.flush()

            if y_dbg is not None:
                with contextlib.ExitStack() as ph:
                    yt = sbuf(ph, "ycp", [128, 8, 512], BF16)
                    for t0 in range(0, T, 512):
                        W = min(512, T - t0)
                        DMA(yt[:, :, 0:W], y_dbg[:, t0:t0 + W].rearrange("(k p) t -> p k t", p=128), [], ["ycp"])
                        DMA(ybuf[:, t0:t0 + W].rearrange("(k p) t -> p k t", p=128), yt[:, :, 0:W], ["ycp"], ["ybuf"])
                    S.flush()


            BLK = {"dfq": 23, "dfk": 25, "naq": 9, "nak": 11}

            def qk_prep(ph, specs):
                px = [sbuf(ph, "px%d" % i, [128, 512], F32) for i in range(2)]
                psq = sbuf(ph, "psq", [128, 512], F32)
                prs = sbuf(ph, "prs", [128, 512], F32)
                pxn = sbuf(ph, "pxn", [128, 512], F32)
                pa = sbuf(ph, "ppa", [128, 512], F32)
                pb_ = sbuf(ph, "ppb", [128, 512], F32)
                rc = sbuf(ph, "rc", [128, 2, 512], F32)
                n = 0
                for t0 in range(0, T, 512):
                    W = min(512, T - t0)
                    lat = t0 < L
                    if lat and any(sp[3] for sp in specs):
                        DMA(rc[:, 0, :], rope_in[0, :, t0:t0 + 512], [], ["rc"])
                        DMA(rc[:, 1, :], rope_in[1, :, t0:t0 + 512], [], ["rc"])
                    for (blk, gm, gcol, rope, outs) in specs:
                        b = n % 2
                        n += 1
                        DMA(px[b][:, 0:W], PF[blk * 128:(blk + 1) * 128, t0:t0 + W], [], [("px", b)])
                        ACT(psq[:, 0:W], px[b][:, 0:W], AF.Square, [("px", b)], ["psq"])
                        MM(PS[0][:, 0:W], gm, psq[:, 0:W], True, True, ["cmat", "psq"], ["ps0"])
                        ACT(prs[:, 0:W], PS[0][:, 0:W], AF.Ln, ["ps0"], ["prs"], bias=EPS)
                        ACT(prs[:, 0:W], prs[:, 0:W], AF.Exp, ["prs"], ["prs"], scale=-0.5)
                        STT(pxn[:, 0:W], px[b][:, 0:W], gcol, prs[:, 0:W], ALU.mult, ALU.mult, [("px", b), "vecs", "prs"], ["pxn"])
                        val = pxn
                        vk = "pxn"
                        if rope and lat:
                            MM(PS[1][:, 0:W], RM, pxn[:, 0:W], True, True, ["cmat", "pxn"], ["ps1"])
                            TT(pa[:, 0:W], pxn[:, 0:W], rc[:, 0, 0:W], ALU.mult, ["pxn", "rc"], ["ppa"], eng="pool")
                            TT(pb_[:, 0:W], PS[1][:, 0:W], rc[:, 1, 0:W], ALU.mult, ["ps1", "rc"], ["ppb"])
                            TT(pa[:, 0:W], pa[:, 0:W], pb_[:, 0:W], ALU.add, ["ppa", "ppb"], ["ppa"], eng="pool")
                            val = pa
                            vk = "ppa"
                        for (dst, dk, mcol) in outs:
                            TS(dst[:, t0:t0 + W], val[:, 0:W], mcol, 0.0, ALU.mult, ALU.add, [vk, "vecs"], [dk])

            def load_vaug(ph, name, pcol):
                va = sbuf(ph, name, [128, 34, 4, 65], BF16)
                vst = [sbuf(ph, name + "st%d" % i, [128, 256], F32) for i in range(2)]
                MSET(va[:, :, :, 64:65], 1.0, [name])
                for kc in range(34):
                    b = kc % 2
                    DMA(vst[b][:], PT[kc * 128:(kc + 1) * 128, pcol:pcol + 256], [], [(name + "st", b)])
                    CP(va[:, kc, :, 0:64], vst[b][:].rearrange("p (h d) -> p h d", h=4), [(name + "st", b)], [name],
                       eng=("dve" if kc % 2 else "pool"))
                return va

            def attn_norm(ph_tiles, Obank, W, okey):
                osb, rrow, onrm = ph_tiles
                ACT(osb[0:65, 0:W], PS[Obank][0:65, 0:W], AF.Copy, ["ps%d" % Obank], ["osb"])
                S.op("dve", lambda e: e.reciprocal(out=rrow[64:65, 0:W], in_=osb[64:65, 0:W]), reads=["osb"], writes=["rrow"])
                MM(PS[7][0:64, 0:W], ONESF[64:65, 0:64], rrow[64:65, 0:W], True, True, ["cmat", "rrow"], ["ps7"])
                TT(onrm[0:64, 0:W], osb[0:64, 0:W], PS[7][0:64, 0:W], ALU.mult, ["osb", "ps7"], [okey])


            def head_norm_gate(ph, osum, okey, gate_blk, gcol, yrow0, perm=False):
                gx = [sbuf(ph, "hg_x%d" % i, [128, 512], F32) for i in range(2)]
                gs = sbuf(ph, "hg_s", [128, 512], F32)
                gr = sbuf(ph, "hg_r", [128, 512], F32)
                gy = [sbuf(ph, "hg_y%d" % i, [128, 512], BF16) for i in range(2)]
                n = 0
                for t0 in range(0, T, 512):
                    W = min(512, T - t0)
                    for hp in range(2):
                        b = n % 2
                        n += 1
                        if perm:
                            for j in range(2):
                                g0 = (gate_blk + j) * 128 + hp * 64
                                DMA(gx[b][j * 64:(j + 1) * 64, 0:W], PF[g0:g0 + 64, t0:t0 + W], [], [("hgx", b)])
                        else:
                            DMA(gx[b][:, 0:W], PF[(gate_blk + hp) * 128:(gate_blk + hp + 1) * 128, t0:t0 + W], [], [("hgx", b)])
                        ACT(gx[b][:, 0:W], gx[b][:, 0:W], AF.Silu, [("hgx", b)], [("hgx", b)])
                        ACT(gs[:, 0:W], osum[:, hp, t0:t0 + W], AF.Square, [okey], ["hgs"])
                        MM(PS[0][:, 0:W], BD64, gs[:, 0:W], True, True, ["cmat", "hgs"], ["ps0"])
                        ACT(gr[:, 0:W], PS[0][:, 0:W], AF.Ln, ["ps0"], ["hgr"], bias=EPS)
                        ACT(gr[:, 0:W], gr[:, 0:W], AF.Exp, ["hgr"], ["hgr"], scale=-0.5)
                        STT(gs[:, 0:W], osum[:, hp, t0:t0 + W], gcol, gr[:, 0:W], ALU.mult, ALU.mult, [okey, "vecs", "hgr"], ["hgs"])
                        TT(gy[b][:, 0:W], gs[:, 0:W], gx[b][:, 0:W], ALU.mult, ["hgs", ("hgx", b)], [("hgy", b)])
                        if perm:
                            for j in range(2):
                                y0 = yrow0 + (hp + 2 * j) * 64
                                DMA(ybuf[y0:y0 + 64, t0:t0 + W], gy[b][j * 64:(j + 1) * 64, 0:W], [("hgy", b)], [("ybuf", yrow0, j)])
                        else:
                            DMA(ybuf[yrow0 + hp * 128:yrow0 + (hp + 1) * 128, t0:t0 + W], gy[b][:, 0:W], [("hgy", b)], [("ybuf", yrow0)])

            def scan_blocks(d):
                cb = [L + 0, L + 128] if d == 0 else [L + 128, L + 0]
                lb_ = list(range(0, L, 128)) if d == 0 else list(range(L - 128, -1, -128))
                return cb + lb_


            if "dn" in mixers:
                with contextlib.ExitStack() as ph:
                    osum = sbuf(ph, "dno", [128, 2, T], F32)
                    qT = sbuf(ph, "dnq", [128, 2, T], BF16)
                    kT = sbuf(ph, "dnk", [128, 2, T], BF16)
                    vT = sbuf(ph, "dnv", [128, 2, T], BF16)
                    rowsT = sbuf(ph, "dnrows", [16, T], F32)
                    coef = sbuf(ph, "dncoef", [16, 1], F32)
                    I16 = sbuf(ph, "dnI16", [128, 128], BF16)
                    CP(I16[:], CM(0, "I"), ["scm"], ["dnI16"])
                    ACT(coef[:], V(("dnalog", li))[0:16, :], AF.Exp, ["vecs"], ["dncoef"])
                    TS(coef[:], coef[:], -1.0, 0.0, ALU.mult, ALU.add, ["dncoef"], ["dncoef"])
                    DMA(rowsT[:], PF[8 * 128:8 * 128 + 16, :], [], ["dnrows"])
                    for c0 in range(0, T, 1088):
                        sl = rowsT[:, c0:c0 + 1088]
                        ACT(sl, sl, AF.Exp, ["dnrows", "vecs"], ["dnrows"], bias=V(("dndtb", li))[0:16, :], scale=V("dnsgn")[0:16, :])
                        ACT(sl, sl, AF.Ln, ["dnrows"], ["dnrows"], bias=1.0)
                        TS(sl, sl, coef[:, 0:1], 0.0, ALU.mult, ALU.add, ["dnrows", "dncoef"], ["dnrows"])
                    with contextlib.ExitStack() as ph2:
                        xs = [sbuf(ph2, "dnxs%d" % i, [128, 514], F32) for i in range(2)]
                        ca = sbuf(ph2, "dnca", [128, 512], F32)
                        cb2 = sbuf(ph2, "dncb", [128, 512], F32)
                        sq2 = sbuf(ph2, "dnsq", [128, 512], F32)
                        n = 0
                        cw = voff[("dncw", li)]
                        for (sname, s0, slen) in SEQS:
                            for t0 in range(s0, s0 + slen, 512):
                                W = min(512, s0 + slen - t0)
                                a0 = max(t0 - 1, s0)
                                a1_ = min(t0 + W + 1, s0 + slen)
                                for blk in range(6):
                                    b = n % 2
                                    n += 1
                                    off0 = a0 - t0 + 1
                                    DMA(xs[b][:, off0:off0 + (a1_ - a0)], PF[blk * 128:(blk + 1) * 128, a0:a1_], [], [("dnxs", b)])
                                    if t0 == s0:
                                        MSET(xs[b][:, 0:1], 0.0, [("dnxs", b)])
                                    if t0 + W == s0 + slen:
                                        MSET(xs[b][:, W + 1:W + 2], 0.0, [("dnxs", b)])
                                    TS(ca[:, 0:W], xs[b][:, 1:1 + W], vecs[:, cw + 6 + blk:cw + 7 + blk], 0.0, ALU.mult, ALU.add, [("dnxs", b), "vecs"], ["dnca"])
                                    STT(cb2[:, 0:W], xs[b][:, 0:W], vecs[:, cw + blk:cw + blk + 1], ca[:, 0:W], ALU.mult, ALU.add, [("dnxs", b), "vecs", "dnca"], ["dncb"])
                                    STT(ca[:, 0:W], xs[b][:, 2:2 + W], vecs[:, cw + 12 + blk:cw + 13 + blk], cb2[:, 0:W], ALU.mult, ALU.add, [("dnxs", b), "vecs", "dncb"], ["dnca"])
                                    ACT(cb2[:, 0:W], ca[:, 0:W], AF.Silu, ["dnca"], ["dncb"])
                                    if blk >= 4:
                                        CP(vT[:, blk - 4, t0:t0 + W], cb2[:, 0:W], ["dncb"], [("dnv", blk - 4)], eng="pool")
                                        continue
                                    ACT(sq2[:, 0:W], cb2[:, 0:W], AF.Square, ["dncb"], ["dnsq"])
                                    MM(PS[0][:, 0:W], BD64, sq2[:, 0:W], True, True, ["cmat", "dnsq"], ["ps0"])
                                    ACT(sq2[:, 0:W], PS[0][:, 0:W], AF.Ln, ["ps0"], ["dnsq"], bias=EPS / 64)
                                    ACT(sq2[:, 0:W], sq2[:, 0:W], AF.Exp, ["dnsq"], ["dnsq"], scale=-0.5)
                                    if blk < 2:
                                        STT(qT[:, blk, t0:t0 + W], cb2[:, 0:W], 1.0 / 64, sq2[:, 0:W], ALU.mult, ALU.mult, ["dncb", "dnsq"], [("dnq", blk)])
                                    else:
                                        STT(kT[:, blk - 2, t0:t0 + W], cb2[:, 0:W], 1.0 / 8, sq2[:, 0:W], ALU.mult, ALU.mult, ["dncb", "dnsq"], [("dnk", blk - 2)])
                        S.flush()
                    rt = sbuf(ph, "dnrt", [128, 16], F32)
                    gB4 = sbuf(ph, "dngB", [128, 4, 128], F32)
                    lB4 = sbuf(ph, "dnlB", [128, 4, 128], F32)
                    ekd = sbuf(ph, "dnekd", [128, 16], F32)
                    ebt = sbuf(ph, "dnebt", [128, 8], F32)
                    kv = sbuf(ph, "dnkv", [128, 512], F32)
                    E5 = sbuf(ph, "dnE5", [128, 5, 4, 128], F32)
                    Pb = [sbuf(ph, "dnP%d" % i, [128, 4, 128], F32) for i in range(2)]
                    Qb = [sbuf(ph, "dnQ%d" % i, [128, 4, 128], F32) for i in range(2)]
                    X = sbuf(ph, "dnX", [128, 4, 128], F32)
                    aqk = sbuf(ph, "dnaqk", [128, 4, 128], BF16)
                    kbe = sbuf(ph, "dnkbe", [128, 2, 128], BF16)
                    qd = sbuf(ph, "dnqd", [128, 2, 128], BF16)
                    vb = sbuf(ph, "dnvb", [128, 4, 64], F32)
                    kdc = sbuf(ph, "dnkdc", [128, 4, 64], BF16)
                    vnZ = [sbuf(ph, "dnvn%d" % i, [128, 4, 64], BF16) for i in range(2)]
                    for i in range(2):
                        MSET(vnZ[i][:], 0.0, [("dnvn", i)])
                    rF = [sbuf(ph, "dnrF%d" % i, [128, 4, 64], F32) for i in range(2)]
                    Sf = sbuf(ph, "dnS", [128, 4, 64], F32)
                    S16 = sbuf(ph, "dnS16", [128, 4, 64], BF16)
                    for i in range(2):
                        MSET(rF[i][:], 0.0, [("dnrF", i)])
                    Ibc = CM(0, "I").rearrange("p (o c) -> p o c", o=1).to_broadcast([128, 4, 128])

                    def flat(ap):
                        return ap.rearrange("p h c -> p (h c)")

                    for d in range(2):
                        MSET(Sf[:], 0.0, ["dnS"])
                        MSET(S16[:], 0.0, ["dnS16"])
                        for t0 in scan_blocks(d)[:debug.get("dn_nblk", 100)]:
                            MM(PS[0][:, 0:16], rowsT[:, t0:t0 + 128], CM(0, "I")[0:16, 0:16], True, True, ["dnrows", "scm"], ["ps0"])
                            CP(rt[:], PS[0][:, 0:16], ["ps0"], ["dnrt"])
                            MM(PS[0][:, 16:32], CM(d, "CKD"), rt[:], True, True, ["scm", "dnrt"], ["ps0"])
                            ACT(ekd[:], PS[0][:, 16:32], AF.Exp, ["ps0"], ["dnekd"])
                            ACT(ebt[:], rt[:, 0:8], AF.Exp, ["dnrt"], ["dnebt"])
                            for i2 in range(2):
                                MM(PS[1][:, i2 * 128:(i2 + 1) * 128], kT[:, i2, t0:t0 + 128], I16[:], True, True, [("dnk", i2), "dnI16"], ["ps1"])
                                MM(PS[1][:, 256 + i2 * 128:256 + (i2 + 1) * 128], vT[:, i2, t0:t0 + 128], I16[:], True, True, [("dnv", i2), "dnI16"], ["ps1"])
                            CP(kv[:], PS[1][:], ["ps1"], ["dnkv"])
                            HP = [0, 2, 1, 3]
                            for par in range(2):
                                gsrc = rt[:, 8 + d * 4:12 + d * 4].rearrange("p (j q) -> p q j", q=2)[:, par, :]
                                lsrc = rt[:, d * 4:d * 4 + 4].rearrange("p (j q) -> p q j", q=2)[:, par, :]
                                CP(gB4[:, 2 * par:2 * par + 2, :], gsrc.rearrange("p (j o) -> p j o", o=1).to_broadcast([128, 2, 128]), ["dnrt"], ["dngB"])
                                CP(lB4[:, 2 * par:2 * par + 2, :], lsrc.rearrange("p (j o) -> p j o", o=1).to_broadcast([128, 2, 128]), ["dnrt"], ["dnlB"], eng="pool")
                            kr = ["dngB", "dnlB", "scm"]
                            order = (0, 1) if d == 0 else (1, 0)
                            for ty in range(5):
                                pb = 2 + ty % 2
                                pk = "ps%d" % pb
                                for pos in range(4):
                                    o_ = PS[pb][:, pos * 128:(pos + 1) * 128]
                                    gB = gB4[:, pos, :]
                                    lB = lB4[:, pos, :]
                                    if ty == 0:
                                        seq = [(gB, CM(d, "U")), (CM(d, "NU"), gB), (CM(d, "I"), CM(d, "MI"))]
                                    elif ty == 1:
                                        seq = [(gB, CM(d, "U")), (CM(d, "NU"), gB), (lB, CM(d, "I")), (CM(d, "I"), CM(d, "MS"))]
                                    elif ty == 2:
                                        seq = [(CM(d, "U"), gB), (gB, CM(d, "NU")), (CM(d, "I"), lB), (CM(d, "I"), CM(d, "MST"))]
                                    elif ty == 3:
                                        seq = [(gB, CM(d, "U"))]
                                    else:
                                        seq = [(gB, CM(d, "U")), (lB, CM(d, "I"))]
                                    for si, (la_, ra_) in enumerate(seq):
                                        MM(o_, la_, ra_, si == 0, si == len(seq) - 1, kr, [pk])
                                ACT(flat(E5[:, ty]), PS[pb][:], AF.Exp, [pk], [("dnE5", ty)])
                            for par in range(2):
                                r0 = par * 64
                                for j in range(2):
                                    kTh = kT[r0:r0 + 64, j, t0:t0 + 128]
                                    MM(PS[par][:, j * 128:(j + 1) * 128], kTh, kTh, True, True, [("dnk", j)], ["ps%d" % par])
                                    MM(PS[par][:, 256 + j * 128:256 + (j + 1) * 128], kTh, qT[r0:r0 + 64, j, t0:t0 + 128], True, True,
                                       [("dnk", j), ("dnq", j)], ["ps%d" % par])
                            for par in range(2):
                                sl = slice(2 * par, 2 * par + 2)
                                pk = "ps%d" % par
                                STT(flat(Qb[0][:, sl, :]), PS[par][:, 0:256], -1.0, flat(E5[:, 1, sl, :]), ALU.mult, ALU.mult, [pk, ("dnE5", 1)], [("dnQ", 0)])
                                STT(flat(Pb[0][:, sl, :]), PS[par][:, 0:256], -1.0, flat(E5[:, 2, sl, :]), ALU.mult, ALU.mult, [pk, ("dnE5", 2)], [("dnP", 0)])
                                TT(flat(aqk[:, sl, :]), PS[par][:, 256:512], flat(E5[:, 0, sl, :]), ALU.mult, [pk, ("dnE5", 0)], ["dnaqk"])
                            TT(X[:], Qb[0][:], Ibc, ALU.add, [("dnQ", 0), "scm"], ["dnX"])
                            for lvl in range(5):
                                a_, bn = lvl % 2, (lvl + 1) % 2
                                for pos in range(4):
                                    MM(PS[4][:, pos * 128:(pos + 1) * 128], Qb[a_][:, pos, :], Pb[a_][:, pos, :], True, True, [("dnQ", a_), ("dnP", a_)], ["ps4"])
                                ACT(flat(Pb[bn][:]), PS[4][:], AF.Copy, ["ps4"], [("dnP", bn)])
                                if lvl < 4:
                                    for pos in range(4):
                                        MM(PS[0][:, pos * 128:(pos + 1) * 128], Pb[a_][:, pos, :], Qb[a_][:, pos, :], True, True, [("dnQ", a_), ("dnP", a_)], ["ps0"])
                                    CP(flat(Qb[bn][:]), PS[0][:], ["ps0"], [("dnQ", bn)])
                                for pos in range(4):
                                    MM(PS[1][:, pos * 128:(pos + 1) * 128], Pb[bn][:, pos, :], X[:, pos, :], True, True, [("dnP", bn), "dnX"], ["ps1"])
                                TT(flat(X[:]), flat(X[:]), PS[1][:], ALU.add, ["dnX", "ps1"], ["dnX"])
                            for pos in range(4):
                                par, j = pos // 2, pos % 2
                                r0 = par * 64
                                TT(kbe[r0:r0 + 64, j, :], kT[r0:r0 + 64, j, t0:t0 + 128], E5[r0:r0 + 64, 4, pos, :], ALU.mult,
                                   [("dnk", j), ("dnE5", 4)], ["dnkbe"])
                                TT(qd[r0:r0 + 64, j, :], qT[r0:r0 + 64, j, t0:t0 + 128], E5[r0:r0 + 64, 3, pos, :], ALU.mult,
                                   [("dnq", j), ("dnE5", 3)], ["dnqd"])
                            for par in range(2):
                                sl = slice(2 * par, 2 * par + 2)
                                bsrc = ebt[:, d * 4:d * 4 + 4].rearrange("p (j q) -> p q j", q=2)[:, par, :]
                                esrc = ekd[:, 8 + d * 4:12 + d * 4].rearrange("p (j q) -> p q j", q=2)[:, par, :]
                                vsrc = kv[:, 256:512].rearrange("p (j q v) -> p q j v", q=2, v=64)[:, par]
                                ksrc = kv[:, 0:256].rearrange("p (j q v) -> p q j v", q=2, v=64)[:, par]
                                TT(vb[:, sl, :], vsrc, bsrc.rearrange("p (j o) -> p j o", o=1).to_broadcast([128, 2, 64]), ALU.mult, ["dnkv", "dnebt"], ["dnvb"])
                                TT(kdc[:, sl, :], ksrc, esrc.rearrange("p (j o) -> p j o", o=1).to_broadcast([128, 2, 64]), ALU.mult, ["dnkv", "dnekd"], ["dnkdc"])
                            for i in order:
                                c0 = i * 64
                                for pos in range(4):
                                    par, j = pos // 2, pos % 2
                                    r0 = par * 64
                                    pbk = 5 - par
                                    MM(PS[pbk][c0:c0 + 64, j * 64:(j + 1) * 64], kbe[r0:r0 + 64, j, c0:c0 + 64], S16[r0:r0 + 64, pos, :], True, True,
                                       ["dnkbe", "dnS16"], ["ps%d" % pbk])
                                for par in range(2):
                                    sl = slice(2 * par, 2 * par + 2)
                                    pbk = 5 - par
                                    TT(rF[i][c0:c0 + 64, sl, :], vb[c0:c0 + 64, sl, :], PS[pbk][c0:c0 + 64, 0:128].rearrange("p (h v) -> p h v", h=2),
                                       ALU.subtract, ["dnvb", "ps%d" % pbk], [("dnrF", i)])
                                for pos in range(4):
                                    MM(PS[2][:, pos * 64:(pos + 1) * 64], X[:, pos, :], rF[i][:, pos, :], True, True, ["dnX", ("dnrF", i)], ["ps2"])
                                ACT(vnZ[i][c0:c0 + 64, :, :], PS[2][c0:c0 + 64, 0:256].rearrange("p (h v) -> p h v", h=4), AF.Copy, ["ps2"], [("dnvn", i)])
                                for pos in range(4):
                                    par, j = pos // 2, pos % 2
                                    r0 = par * 64
                                    pO = 6 + par
                                    MM(PS[pO][j * 64:(j + 1) * 64, c0:c0 + 64], S16[r0:r0 + 64, pos, :], qd[r0:r0 + 64, j, c0:c0 + 64], True, False,
                                       ["dnS16", "dnqd"], ["ps%d" % pO])
                                    MM(PS[pO][j * 64:(j + 1) * 64, c0:c0 + 64], vnZ[i][:, pos, :], aqk[:, pos, c0:c0 + 64], False, True,
                                       [("dnvn", i), "dnaqk"], ["ps%d" % pO])
                                for pos in range(4):
                                    r0 = (pos // 2) * 64
                                    MM(PS[3][r0:r0 + 64, pos * 64:(pos + 1) * 64], kdc[c0:c0 + 64, pos, :], vnZ[i][c0:c0 + 64, pos, :], True, True,
                                       ["dnkdc", ("dnvn", i)], ["ps3"])
                                last = c0 + 63 if d == 0 else c0
                                TT(Sf[:], Sf[:], E5[:, 3, :, last:last + 1].to_broadcast([128, 4, 64]), ALU.mult, ["dnS", ("dnE5", 3)], ["dnS"])
                                TT(Sf[:], Sf[:], PS[3][:, 0:256].rearrange("p (h v) -> p h v", h=4), ALU.add, ["dnS", "ps3"], ["dnS"])
                                ACT(S16[:], Sf[:], AF.Copy, ["dnS"], ["dnS16"])
                            for hp in range(2):
                                if d == 0:
                                    ACT(osum[:, hp, t0:t0 + 128], PS[6 + hp][:, 0:128], AF.Copy, ["ps%d" % (6 + hp)], ["dno"])
                                else:
                                    TT(osum[:, hp, t0:t0 + 128], osum[:, hp, t0:t0 + 128], PS[6 + hp][:, 0:128], ALU.add, ["dno", "ps%d" % (6 + hp)], ["dno"])
                    head_norm_gate(ph, osum, "dno", 6, V(("dnng", li)), 0, perm=True)
                    S.flush()

            if "gla" in mixers:
                with contextlib.ExitStack() as ph:
                    osum = sbuf(ph, "glo", [128, 2, T], F32)
                    qT = sbuf(ph, "glq", [128, T], F32)
                    kT = sbuf(ph, "glk", [128, T], F32)
                    a1 = [sbuf(ph, "gla1_%d" % i, [17, T], F32) for i in range(2)]
                    wa = sbuf(ph, "glwa", [17, 2, 128], F32)
                    DMA(qT[:], PF[15 * 128:16 * 128, :], [], ["glq"])
                    DMA(kT[:], PF[16 * 128:17 * 128, :], [], ["glk"])
                    for d in range(2):
                        MSET(a1[d][:], 1.0, [("gla1", d)])
                        DMA(a1[d][0:16, :], PF[(21 + d) * 128:(21 + d) * 128 + 16, :], [], [("gla1", d)])
                        DMA(wa[:, d, :], gla_w_in[li, d], [], ["glwa"])
                    ktok = [sbuf(ph, "glkt%d" % i, [128, 384], F32) for i in range(2)]
                    vb16 = [sbuf(ph, "glvb%d" % i, [128, 256], BF16) for i in range(2)]
                    ln_ = sbuf(ph, "glln", [128, 128], F32)
                    eq = sbuf(ph, "gleq", [128, 128], F32)
                    ek = sbuf(ph, "glek", [128, 128], F32)
                    eb = sbuf(ph, "gleb", [128, 128], F32)
                    ekd = sbuf(ph, "glekd", [128, 128], F32)
                    qt_ = sbuf(ph, "glqt", [128, 128], F32)
                    ktl = sbuf(ph, "glktl", [128, 128], BF16)
                    qb = sbuf(ph, "glqb", [128, 128], F32)
                    qth = sbuf(ph, "glqth", [128, 4, 128], BF16)
                    qbh = sbuf(ph, "glqbh", [128, 4, 128], BF16)
                    kdec = sbuf(ph, "glkdec", [128, 128], BF16)
                    S16 = sbuf(ph, "glS16", [128, 64], BF16)
                    Ah = sbuf(ph, "glA", [128, 4, 128], BF16)
                    dsm = sbuf(ph, "gldsm", [128, 4, 64], F32)
                    dsr = sbuf(ph, "gldsr", [128, 64], F32)
                    Sst = sbuf(ph, "glS", [128, 64], F32)
                    osb = sbuf(ph, "glosb", [128, 2, 128], F32)
                    sc = 32 ** -0.5
                    nb = 0
                    for d in range(2):
                        MSET(Sst[:], 0.0, ["glS"])
                        for t0 in scan_blocks(d)[:debug.get("gla_nblk", 100)]:
                            b = nb % 2
                            nb += 1
                            stage = debug.get("gla_stage", 99)
                            DMA(ktok[b][:], PT[t0:t0 + 128, 512:896], [], [("glkt", b)])
                            CP(vb16[b][:], ktok[b][:, 128:384], [("glkt", b)], [("glvb", b)], eng="pool")
                            MM(PS[0][:, 0:128], a1[d][:, t0:t0 + 128], wa[:, d, :], True, True, [("gla1", d), "glwa"], ["ps0"])
                            ACT(ln_[:], PS[0][:, 0:128], AF.Exp, ["ps0"], ["glln"], scale=-1.0)
                            ACT(ln_[:], ln_[:], AF.Ln, ["glln"], ["glln"], bias=1.0)
                            if stage < 1:
                                continue
                            MM(PS[1][:, 0:128], ln_[:], CM(d, "CQ"), True, True, ["glln", "scm"], ["ps1"])
                            MM(PS[1][:, 128:256], ln_[:], CM(d, "U"), True, True, ["glln", "scm"], ["ps1"])
                            MM(PS[1][:, 256:384], CM(d, "CKD"), ln_[:], True, True, ["glln", "scm"], ["ps1"])
                            ACT(eq[:], PS[1][:, 0:128], AF.Exp, ["ps1"], ["gleq"], scale=-1.0 / 16)
                            ACT(ek[:], PS[1][:, 0:128], AF.Exp, ["ps1"], ["glek"], scale=1.0 / 16)
                            ACT(eb[:], PS[1][:, 128:256], AF.Exp, ["ps1"], ["gleb"], scale=-1.0 / 16)
                            ACT(ekd[:], PS[1][:, 256:384], AF.Exp, ["ps1"], ["glekd"], scale=-1.0 / 16)
                            STT(qt_[:], qT[:, t0:t0 + 128], sc, eq[:], ALU.mult, ALU.mult, ["glq", "gleq"], ["glqt"])
                            TT(ktl[:], kT[:, t0:t0 + 128], ek[:], ALU.mult, ["glk", "glek"], ["glktl"])
                            STT(qb[:], qT[:, t0:t0 + 128], sc, eb[:], ALU.mult, ALU.mult, ["glq", "gleb"], ["glqb"])
                            TT(kdec[:], ktok[b][:, 0:128], ekd[:], ALU.mult, [("glkt", b), "glekd"], ["glkdec"], eng="pool")
                            if stage < 2:
                                continue
                            for h in range(4):
                                TS(qth[:, h, :], qt_[:], V("hm", h), 0.0, ALU.mult, ALU.add, ["glqt", "vecs"], ["glqth"])
                                TS(qbh[:, h, :], qb[:], V("hm", h), 0.0, ALU.mult, ALU.add, ["glqb", "vecs"], ["glqbh"])
                            if stage < 3:
                                continue
                            for h in range(4):
                                MM(PS[2][:, h * 128:(h + 1) * 128], ktl[:], qth[:, h, :], True, True, ["glktl", "glqth"], ["ps2"])
                            TT(Ah[:], PS[2][:].rearrange("p (h c) -> p h c", h=4),
                               CM(d, "M01").rearrange("p (o c) -> p o c", o=1).to_broadcast([128, 4, 128]), ALU.mult, ["ps2", "scm"], ["glA"])
                            order = (0, 1) if d == 0 else (1, 0)
                            if stage < 4:
                                continue
                            for h in range(4):
                                orow = (h % 2) * 64
                                pO = 3 + h // 2
                                MM(PS[pO][orow:orow + 64, 0:128], vb16[b][:, h * 64:(h + 1) * 64], Ah[:, h, :], True, False,
                                   [("glvb", b), "glA"], ["ps%d" % pO])
                            for ii, i in enumerate(order):
                                c0 = i * 64
                                CP(S16[:], Sst[:], ["glS"], ["glS16"], eng="pool")
                                for h in range(4):
                                    orow = (h % 2) * 64
                                    pO = 3 + h // 2
                                    MM(PS[pO][orow:orow + 64, c0:c0 + 64], S16[:, :], qbh[:, h, c0:c0 + 64], False, ii == 1,
                                       ["glS16", "glqbh"], ["ps%d" % pO])
                                MM(PS[5][:, 0:256], kdec[c0:c0 + 64, :], vb16[b][c0:c0 + 64, :], True, True, ["glkdec", ("glvb", b)], ["ps5"])
                                TT(dsm[:], PS[5][:, 0:256].rearrange("p (h v) -> p h v", h=4),
                                   V("hm", 0, 4).rearrange("p (h o) -> p h o", o=1).to_broadcast([128, 4, 64]), ALU.mult, ["ps5", "vecs"], ["gldsm"])
                                S.op("dve", lambda e: e.reduce_sum(out=dsr[:], in_=dsm[:].rearrange("p h v -> p v h"), axis=AX.X),
                                     reads=["gldsm"], writes=["gldsr"])
                                last = c0 + 63 if d == 0 else c0
                                STT(Sst[:], Sst[:], eb[:, last:last + 1], dsr[:], ALU.mult, ALU.add, ["glS", "gleb", "gldsr"], ["glS"])
                            for hp in range(2):
                                if d == 0:
                                    ACT(osum[:, hp, t0:t0 + 128], PS[3 + hp][:, 0:128], AF.Copy, ["ps%d" % (3 + hp)], ["glo"])
                                else:
                                    TT(osum[:, hp, t0:t0 + 128], osum[:, hp, t0:t0 + 128], PS[3 + hp][:, 0:128], ALU.add, ["glo", "ps%d" % (3 + hp)], ["glo"])
                    if debug.get("gla_stage", 99) >= 99:
                        head_norm_gate(ph, osum, "glo", 19, V(("glng", li)), 512)
                    S.flush()

            if "df" in mixers:
                with contextlib.ExitStack() as ph:
                    qr = sbuf(ph, "dfqr", [128, 2, T], BF16)
                    k1z = sbuf(ph, "dfk1", [128, 2, T], BF16)
                    k2z = sbuf(ph, "dfk2", [128, 2, T], BF16)
                    with contextlib.ExitStack() as ph2:
                        specs = []
                        for i in range(2):
                            specs.append((BLK["dfq"] + i, BD32, V(("dfqn", li)), True, [(qr[:, i, :], ("dfqr", i), V("one"))]))
                            specs.append((BLK["dfk"] + i, BD32, V(("dfkn", li)), True,
                                          [(k1z[:, i, :], ("dfk1", i), V("m1")), (k2z[:, i, :], ("dfk2", i), V("m2"))]))
                        qk_prep(ph2, specs)
                        S.flush()
                    va = load_vaug(ph, "dfva", 256)
                    lp = sbuf(ph, "lp", [128, 2, 2, 32], F32)
                    lpp = sbuf(ph, "lpp", [128, 2, 32], F32)
                    lps = sbuf(ph, "lps", [128, 2], F32)
                    nlam = sbuf(ph, "nlam", [128, 1], F32)
                    lam_init = 0.8 - 0.6 * math.exp(-0.3 * li)
                    DMA(lp[:].rearrange("p a b d -> p (a b d)"), dflam_in[li:li + 1, :].partition_broadcast(128), [], ["lp"])
                    TT(lpp[:], lp[:, :, 0, :], lp[:, :, 1, :], ALU.mult, ["lp"], ["lpp"])
                    S.op("dve", lambda e: e.reduce_sum(out=lps[:], in_=lpp[:], axis=AX.X), reads=["lpp"], writes=["lps"])
                    ACT(lps[:], lps[:], AF.Exp, ["lps"], ["lps"])
                    TT(nlam[:], lps[:, 1:2], lps[:, 0:1], ALU.subtract, ["lps"], ["nlam"])
                    TS(nlam[:], nlam[:], -lam_init, 0.0, ALU.add, ALU.add, ["nlam"], ["nlam"])
                    E = [sbuf(ph, "dfE%d" % i, [128, 512], BF16) for i in range(4)]
                    osb = sbuf(ph, "osb", [128, 512], F32)
                    rrow = sbuf(ph, "rrow", [128, 512], F32)
                    o1 = sbuf(ph, "o1n", [128, 512], F32)
                    o2 = sbuf(ph, "o2n", [128, 512], F32)
                    dsq = sbuf(ph, "dsq", [128, 512], F32)
                    yo = [sbuf(ph, "dfy%d" % i, [128, 512], BF16) for i in range(2)]
                    sc = 32 ** -0.5
                    ne = 0
                    ny = 0
                    qtiles = [(t0, 512, list(range(34))) for t0 in range(0, L, 512)]
                    if with_ctx:
                        qtiles.append((L, CL, [32, 33]))
                    steps = []
                    for (t0, W, kcs) in qtiles:
                        for h in range(4):
                            for ci, kc in enumerate(kcs):
                                for t in range(2):
                                    steps.append((t0, W, h, kc, t, ci == 0, ci == len(kcs) - 1))
                    PIPE = 2
                    kzs = (k1z, k2z)

                    def emit_score(i):
                        (t0, W, h, kc, t, first, last) = steps[i]
                        blk = h // 2
                        r0 = (h % 2) * 64
                        sb_ = 2 + (i % 4)
                        MM(PS[sb_][:, 0:W], kzs[t][r0:r0 + 64, blk, kc * 128:(kc + 1) * 128], qr[r0:r0 + 64, blk, t0:t0 + W],
                           True, True, [("dfk%d" % (t + 1), blk), ("dfqr", blk)], ["ps%d" % sb_])

                    def emit_rest(i):
                        nonlocal ny
                        (t0, W, h, kc, t, first, last) = steps[i]
                        sb_ = 2 + (i % 4)
                        eb = i % 4
                        ACT(E[eb][:, 0:W], PS[sb_][:, 0:W], AF.Exp, ["ps%d" % sb_], [("dfE", eb)], scale=sc)
                        MM(PS[t][0:65, 0:W], va[:, kc, h, :], E[eb][:, 0:W], first, last, ["dfva", ("dfE", eb)], ["ps%d" % t])
                        if not (last and t == 1):
                            return
                        attn_norm((osb, rrow, o1), 0, W, "o1n")
                        attn_norm((osb, rrow, o2), 1, W, "o2n")
                        STT(o1[0:64, 0:W], o2[0:64, 0:W], nlam[0:64, 0:1], o1[0:64, 0:W], ALU.mult, ALU.add,
                            ["o1n", "o2n", "nlam"], ["o1n"])
                        ACT(dsq[0:64, 0:W], o1[0:64, 0:W], AF.Square, ["o1n"], ["dsq"])
                        MM(PS[7][0:64, 0:W], BD64[0:64, 0:64], dsq[0:64, 0:W], True, True, ["cmat", "dsq"], ["ps7"])
                        ACT(dsq[0:64, 0:W], PS[7][0:64, 0:W], AF.Ln, ["ps7"], ["dsq"], bias=EPS)
                        ACT(dsq[0:64, 0:W], dsq[0:64, 0:W], AF.Exp, ["dsq"], ["dsq"], scale=-0.5)
                        yb_ = ny % 2
                        ny += 1
                        STT(dsq[0:64, 0:W], o1[0:64, 0:W], V(("dfng", li))[0:64, :], dsq[0:64, 0:W], ALU.mult, ALU.mult,
                            ["o1n", "vecs", "dsq"], ["dsq"])
                        TS(yo[yb_][0:64, 0:W], dsq[0:64, 0:W], 1.0 - lam_init, 0.0, ALU.mult, ALU.add, ["dsq"], [("dfy", yb_)])
                        DMA(ybuf[768 + h * 64:768 + (h + 1) * 64, t0:t0 + W], yo[yb_][0:64, 0:W], [("dfy", yb_)], [("ybuf", "df")])

                    for i in range(len(steps) + PIPE):
                        if i < len(steps):
                            emit_score(i)
                        if i >= PIPE:
                            emit_rest(i - PIPE)
                    S.flush()

            if "na" in mixers:
                with contextlib.ExitStack() as ph:
                    qn = sbuf(ph, "naq", [128, 2, T], BF16)
                    kn = sbuf(ph, "nak", [128, 2, T], BF16)
                    with contextlib.ExitStack() as ph2:
                        specs = []
                        for i in range(2):
                            specs.append((BLK["naq"] + i, BD64, V(("naqn", li)), False, [(qn[:, i, :], ("naq", i), V("one"))]))
                            specs.append((BLK["nak"] + i, BD64, V(("nakn", li)), False, [(kn[:, i, :], ("nak", i), V("one"))]))
                        qk_prep(ph2, specs)
                        S.flush()
                    va = load_vaug(ph, "nava", 0)
                    bias = [sbuf(ph, "nab%d" % i, [128, 21, 128], F32) for i in range(2)]
                    sbt = [sbuf(ph, "nasb%d" % i, [128, 640], F32) for i in range(2)]
                    E = [sbuf(ph, "naE%d" % i, [128, 896], BF16) for i in range(2)]
                    osb = sbuf(ph, "osb", [128, 512], F32)
                    rrow = sbuf(ph, "rrow", [128, 512], F32)
                    o1 = sbuf(ph, "o1n", [128, 512], F32)
                    yo = [sbuf(ph, "nay%d" % i, [128, 512], BF16) for i in range(2)]
                    sc = 64 ** -0.5
                    n = 0
                    ny = 0
                    for h in range(4):
                        blk = h // 2
                        r0 = (h % 2) * 64
                        hb = h % 2
                        DMA(bias[hb][:], nab_in[li, h], [], [("nab", hb)])
                        for rg in range(8):
                            for rr_ in range(4):
                                rp = rg * 4 + rr_
                                chunks = na_chunks(rp)
                                b = n % 2
                                n += 1
                                pA = 2 + 2 * b
                                pB = pA + 1
                                q_ap = qn[r0:r0 + 64, blk, rp * 128:(rp + 1) * 128]
                                nw = len(chunks)
                                for j, (kc, bi) in enumerate(chunks):
                                    pbk, pc = (pA, j * 128) if j < 4 else (pB, 0)
                                    MM(PS[pbk][:, pc:pc + 128], kn[r0:r0 + 64, blk, kc * 128:(kc + 1) * 128], q_ap, True, True,
                                       [("nak", blk), ("naq", blk)], ["ps%d" % pbk])
                                for j2 in range(2):
                                    pc = 128 + j2 * 128
                                    MM(PS[pB][:, pc:pc + 128], kn[r0:r0 + 64, blk, L + j2 * 128:L + (j2 + 1) * 128], q_ap, True, True,
                                       [("nak", blk), ("naq", blk)], ["ps%d" % pB])
                                bi0 = chunks[0][1]
                                STT(sbt[b][:, 0:512], PS[pA][:, 0:512], sc, bias[hb][:, bi0:bi0 + 4, :].rearrange("p a q -> p (a q)"),
                                    ALU.mult, ALU.add, ["ps%d" % pA, ("nab", hb)], [("nasb", b)])
                                if nw == 5:
                                    STT(sbt[b][:, 512:640], PS[pB][:, 0:128], sc, bias[hb][:, 4, :], ALU.mult, ALU.add,
                                        ["ps%d" % pB, ("nab", hb)], [("nasb", b)])
                                ACT(E[b][:, 0:nw * 128], sbt[b][:, 0:nw * 128], AF.Exp, [("nasb", b)], [("naE", b)])
                                ACT(E[b][:, 640:896], PS[pB][:, 128:384], AF.Exp, ["ps%d" % pB], [("naE", b)], scale=sc)
                                ecols = [(kc, j * 128) for j, (kc, bi) in enumerate(chunks)] + [(32, 640), (33, 768)]
                                for ci, (kc, ec) in enumerate(ecols):
                                    MM(PS[0][0:65, rr_ * 128:(rr_ + 1) * 128], va[:, kc, h, :], E[b][:, ec:ec + 128], ci == 0, ci == len(ecols) - 1,
                                       ["nava", ("naE", b)], ["ps0"])
                            attn_norm((osb, rrow, o1), 0, 512, "o1n")
                            yb_ = ny % 2
                            ny += 1
                            CP(yo[yb_][0:64, :], o1[0:64, :], ["o1n"], [("nay", yb_)], eng="pool")
                            DMA(ybuf[256 + h * 64:256 + (h + 1) * 64, rg * 512:(rg + 1) * 512], yo[yb_][0:64, :], [("nay", yb_)], [("ybuf", "na")])
                        if with_ctx:
                            q_ap = qn[r0:r0 + 64, blk, L:L + CL]
                            for j2 in range(2):
                                MM(PS[1][:, j2 * 256:(j2 + 1) * 256], kn[r0:r0 + 64, blk, L + j2 * 128:L + (j2 + 1) * 128], q_ap, True, True,
                                   [("nak", blk), ("naq", blk)], ["ps1"])
                            ACT(E[0][:, 0:512], PS[1][:, 0:512], AF.Exp, ["ps1"], [("naE", 0)], scale=sc)
                            for j2 in range(2):
                                MM(PS[0][0:65, 0:256], va[:, 32 + j2, h, :], E[0][:, j2 * 256:(j2 + 1) * 256], j2 == 0, j2 == 1,
                                   ["nava", ("naE", 0)], ["ps0"])
                            attn_norm((osb, rrow, o1), 0, 256, "o1n")
                            yb_ = ny % 2
                            ny += 1
                            CP(yo[yb_][0:64, 0:256], o1[0:64, 0:256], ["o1n"], [("nay", yb_)], eng="pool")
                            DMA(ybuf[256 + h * 64:256 + (h + 1) * 64, L:L + CL], yo[yb_][0:64, 0:256], [("nay", yb_)], [("ybuf", "na")])
                    S.flush()

            with contextlib.ExitStack() as ph:
                wg = sbuf(ph, "wg", [128, 8, 4096], BF16)
                wbr = sbuf(ph, "wbr", [128, 8, D], BF16)
                wo = sbuf(ph, "wo", [128, 8, D], BF16)
                xt = [sbuf(ph, "xt%d" % i, [128, 8, 512], F32) for i in range(2)]
                ht = [sbuf(ph, "ht%d" % i, [128, 8, 512], BF16) for i in range(2)]
                yt = [sbuf(ph, "yt%d" % i, [128, 8, 512], BF16) for i in range(2)]
                mg = sbuf(ph, "mg", [128, 8, 512], BF16)
                sig = [sbuf(ph, "sig%d" % i, [128, 512], F32) for i in range(2)]
                acc = sbuf(ph, "acc", [128, 512], F32)
                tmp = sbuf(ph, "tmp", [128, 512], F32)
                for k in range(8):
                    DMA(wg[:, k, :], wi_b[li][k * 128:(k + 1) * 128, NMIX:NIN], [("wi_b", li)], [("wg", k)])
                    DMA(wbr[:, k, :], wb_b[li][k * 128:(k + 1) * 128, :], [("wb_b", li)], [("wbr", k)])
                    DMA(wo[:, k, :], wo_b[li][k * 128:(k + 1) * 128, :], [("wo_b", li)], [("wo", k)])
                it = 0
                ng = 0
                for (sname, s0, slen) in SEQS:
                    s = 0 if sname == "lat" else 1
                    if s == 1 and not with_ctx:
                        continue
                    for t0 in range(s0, s0 + slen, 512):
                        W = min(512, s0 + slen - t0)
                        b = it % 2
                        it += 1
                        if li == layer_list[0]:
                            src = xT_in[:, t0:t0 + W] if s == 0 else cT_in[:, t0 - L:t0 - L + W]
                        else:
                            src = xbuf[:, t0:t0 + W]
                        DMA(xt[b][:, :, 0:W], src.rearrange("(k p) t -> p k t", p=128), ["xbuf"], [("xt", b)])
                        DMA(ht[b][:, :, 0:W], hbuf[:, t0:t0 + W].rearrange("(k p) t -> p k t", p=128), ["hbuf"], [("ht", b)])
                        DMA(yt[b][:, :, 0:W], ybuf[:, t0:t0 + W].rearrange("(k p) t -> p k t", p=128), ["ybuf"], [("yt", b)])
                        for dc in range(8):
                            for g in range(4):
                                pa = 2 * (ng % 2)
                                pbk = pa + 1
                                sgi = ng % 2
                                ng += 1
                                cg = g * D + dc * 128
                                for k in range(8):
                                    MM(PS[pa][:, 0:W], wg[:, k, cg:cg + 128], ht[b][:, k, 0:W], k == 0, k == 7,
                                       [("wg", k), ("ht", b)], ["ps%d" % pa])
                                for k2 in range(2):
                                    MM(PS[pbk][:, 0:W], wbr[:, 2 * g + k2, dc * 128:(dc + 1) * 128], yt[b][:, 2 * g + k2, 0:W],
                                       k2 == 0, k2 == 1, [("wbr", 2 * g + k2), ("yt", b)], ["ps%d" % pbk])
                                ACT(sig[sgi][:, 0:W], PS[pa][:, 0:W], AF.Sigmoid, ["ps%d" % pa, "vecs"], [("sig", sgi)],
                                    bias=V(("bgate", li), g * 8 + dc))
                                if g == 0:
                                    TT(acc[:, 0:W], sig[sgi][:, 0:W], PS[pbk][:, 0:W], ALU.mult, [("sig", sgi), "ps%d" % pbk], ["acc"])
                                else:
                                    TT(tmp[:, 0:W], sig[sgi][:, 0:W], PS[pbk][:, 0:W], ALU.mult, [("sig", sgi), "ps%d" % pbk], ["tmp"])
                                    if g < 3:
                                        TT(acc[:, 0:W], acc[:, 0:W], tmp[:, 0:W], ALU.add, ["acc", "tmp"], ["acc"], eng="pool")
                                    else:
                                        TT(mg[:, dc, 0:W], acc[:, 0:W], tmp[:, 0:W], ALU.add, ["acc", "tmp"], [("mg", dc)], eng="pool")
                        for dc in range(8):
                            pb = 4 + dc % 2
                            for k in range(8):
                                MM(PS[pb][:, 0:W], wo[:, k, dc * 128:(dc + 1) * 128], mg[:, k, 0:W], k == 0, k == 7,
                                   [("wo", k), ("mg", k)], ["ps%d" % pb])
                            STT(xt[b][:, dc, 0:W], PS[pb][:, 0:W], modv(li, "g1", dc, s), xt[b][:, dc, 0:W], ALU.mult, ALU.add,
                                ["ps%d" % pb, "modsb", ("xt", b)], [("xt", b)])
                        DMA(x1buf[:, t0:t0 + W].rearrange("(k p) t -> p k t", p=128), xt[b][:, :, 0:W], [("xt", b)], ["x1buf"])
                S.flush()

            with contextlib.ExitStack() as ph:
                FT = 456
                xt = [sbuf(ph, "xt%d" % i, [128, 8, 512], F32) for i in range(2)]
                ht = sbuf(ph, "ht", [128, 8, 512], BF16)
                gt = sbuf(ph, "gt", [128, 22, 512], BF16)
                sq = sbuf(ph, "sq", [128, 512], BF16)
                rr = sbuf(ph, "rr", [128, 512], F32)
                ff = sbuf(ph, "ff", [128, 512], F32)
                ust = [sbuf(ph, "ust%d" % i, [128, 516], F32) for i in range(2)]
                ca = sbuf(ph, "ca", [128, 512], F32)
                cb_ = sbuf(ph, "cb", [128, 512], F32)
                w1 = [sbuf(ph, "w1_%d" % i, [128, 8, 256], BF16) for i in range(3)]
                w2f = sbuf(ph, "w2f", [128, 22, D], BF16)
                for j in range(22):
                    DMA(w2f[:, j, :], f2_b[li][j * 128:(j + 1) * 128, :], [("f2_b", li)], [("w2f", j)])
                it = 0
                nw = 0
                for (sname, s0, slen) in SEQS:
                    s = 0 if sname == "lat" else 1
                    if s == 1 and not with_ctx:
                        continue
                    for t0 in range(s0, s0 + slen, FT):
                        t1 = min(t0 + FT, s0 + slen)
                        a0 = max(t0 - 1, s0)
                        a1 = min(t1 + 1, s0 + slen)
                        W = a1 - a0
                        WI = t1 - t0
                        io = t0 - a0
                        b = it % 2
                        it += 1
                        DMA(xt[b][:, :, 0:W], x1buf[:, a0:a1].rearrange("(k p) t -> p k t", p=128), ["x1buf"], [("xt", b)])
                        norm_tile(xt[b], ht, W, li, 1, s, sq, rr, ff, 0, ("xt", b), "ht", "n2")
                        for j in range(22):
                            wb = nw % 3
                            nw += 1
                            DMA(w1[wb][:, :, 0:128], f1_b[li][:, j * 128:(j + 1) * 128].rearrange("(k p) c -> p k c", p=128),
                                [("f1_b", li)], [("w1", wb)])
                            DMA(w1[wb][:, :, 128:256], f1_b[li][:, DFF + j * 128:DFF + (j + 1) * 128].rearrange("(k p) c -> p k c", p=128),
                                [("f1_b", li)], [("w1", wb)])
                            pu = 1 + 2 * (j % 2)
                            pv = pu + 1
                            ub = j % 2
                            for k in range(8):
                                MM(PS[pu][:, 0:W], w1[wb][:, k, 0:128], ht[:, k, 0:W], k == 0, k == 7, [("w1", wb), "ht"], ["ps%d" % pu])
                            for k in range(8):
                                MM(PS[pv][:, 0:W], w1[wb][:, k, 128:256], ht[:, k, 0:W], k == 0, k == 7, [("w1", wb), "ht"], ["ps%d" % pv])
                            c_in = 1 - io
                            ACT(ust[ub][:, c_in + 0:c_in + W], PS[pu][:, 0:W], AF.Copy, ["ps%d" % pu], [("ust", ub)])
                            if io == 0:
                                MSET(ust[ub][:, 0:1], 0.0, [("ust", ub)])
                            if a1 == t1:
                                MSET(ust[ub][:, WI + 1:WI + 2], 0.0, [("ust", ub)])
                            fo = voff[("fcw", li)]
                            TS(ca[:, 0:WI], ust[ub][:, 1:1 + WI], vecs[:, fo + 22 + j:fo + 23 + j], V(("fcb", li), j), ALU.mult, ALU.add,
                               [("ust", ub), "vecs"], ["ca"])
                            STT(cb_[:, 0:WI], ust[ub][:, 0:WI], vecs[:, fo + j:fo + j + 1], ca[:, 0:WI], ALU.mult, ALU.add,
                                [("ust", ub), "vecs", "ca"], ["cb"])
                            STT(ca[:, 0:WI], ust[ub][:, 2:2 + WI], vecs[:, fo + 44 + j:fo + 45 + j], cb_[:, 0:WI], ALU.mult, ALU.add,
                                [("ust", ub), "vecs", "cb"], ["ca"])
                            ACT(cb_[:, 0:WI], ca[:, 0:WI], AF.Silu, ["ca"], ["cb"])
                            TT(gt[:, j, 0:WI], cb_[:, 0:WI], PS[pv][:, io:io + WI], ALU.mult, ["cb", "ps%d" % pv], [("gt", j)])
                        for dc in range(8):
                            pb = 5 + dc % 2
                            for j in range(22):
                                MM(PS[pb][:, 0:WI], w2f[:, j, dc * 128:(dc + 1) * 128], gt[:, j, 0:WI], j == 0, j == 21,
                                   [("w2f", j), ("gt", j)], ["ps%d" % pb])
                            STT(xt[b][:, dc, io:io + WI], PS[pb][:, 0:WI], modv(li, "g2", dc, s), xt[b][:, dc, io:io + WI], ALU.mult, ALU.add,
                                ["ps%d" % pb, "modsb", ("xt", b)], [("xt", b)])
                        dst = outT[:, t0:t1] if (li == DEPTH - 1 and s == 0) else xbuf[:, t0:t1]
                        DMA(dst.rearrange("(k p) t -> p k t", p=128), xt[b][:, :, io:io + WI], [("xt", b)], ["xbuf"])
                        if xd is not None:
                            DMA(xd[li][:, t0:t1].rearrange("(k p) t -> p k t", p=128), xt[b][:, :, io:io + WI], [("xt", b)], ["xd"])
                S.flush()
    return nc


def host_inputs(inp, b):
    m = {}
    m["xT"] = np.ascontiguousarray(np.asarray(inp["x"][b], np.float32).T)
    m["ctxT"] = np.ascontiguousarray(np.asarray(inp["ctx"][b], np.float32).T)
    cs = np.stack([np.asarray(inp["c"][b], np.float32), np.asarray(inp["c_ctx"], np.float32)], axis=-1)
    m["cs"] = np.ascontiguousarray(cs.reshape(8, 128, 2).transpose(1, 0, 2).reshape(128, 16))
    m["vecs"] = pack_vecs(inp)
    m["cmat"] = const_mats()
    m["rope"] = rope_tables()
    m["nab"] = np.stack([na_bias_tables(np.asarray(inp["na_rpb"][li], np.float32)) for li in range(DEPTH)], 0)
    m["scm"] = scan_mats()
    m["gla_w"] = np.ascontiguousarray(np.concatenate([np.asarray(inp["gla_w_a2"], np.float32),
                                                      np.asarray(inp["gla_b_a"], np.float32)[:, :, None, :]], axis=2))
    m["dflam"] = np.ascontiguousarray(np.asarray(inp["df_lambda"], np.float32).reshape(DEPTH, 128))
    for k in ("w_mod", "w_in", "w_out", "ffn_w_in", "ffn_w_out"):
        m[k] = np.ascontiguousarray(np.asarray(inp[k], np.float32))
    m["w_branch"] = np.ascontiguousarray(np.asarray(inp["w_branch"], np.float32).reshape(DEPTH, D, D))
    return m


def kernel(**inp):
    nc = build()
    shared = None
    in_maps = []
    for b in range(8):
        m = host_inputs(inp, b)
        if shared is None:
            shared = {k: m[k] for k in ("vecs", "cmat", "rope", "nab", "dflam", "scm", "gla_w", "w_mod", "w_in", "w_out", "ffn_w_in", "ffn_w_out", "w_branch")}
        else:
            m.update(shared)
        in_maps.append(m)
    res = run_bass_kernel_spmd(nc, in_maps, core_ids=list(range(8)))
    out = np.stack([np.ascontiguousarray(r["outT"].T) for r in res.results], axis=0)
    return out.astype(np.float32)
```

```python
import contextlib
import math
import numpy as np
import ml_dtypes
import concourse.bass as bass
import concourse.mybir as mybir
from concourse.bass_utils import run_bass_kernel_spmd

F32 = mybir.dt.float32
BF16 = mybir.dt.bfloat16
AF = mybir.ActivationFunctionType
ALU = mybir.AluOpType
AX = mybir.AxisListType

D = 1024
L = 4096
CL = 256
T = L + CL
DEPTH = 4
NIN = 7472
NMIX = 3376
DFF = 2816
EPS = 1e-6
NDMA_SEM = 8
SEQS = (("lat", 0, L), ("ctx", L, CL))


class Sched:
    ENGS = ("pe", "act", "dve", "pool", "sp")

    def __init__(self, nc, st):
        self.nc = nc
        self.ops = []
        self.last_w = {}
        self.readers = {}
        self.dma_count = {"sp": 0, "pool": 0}
        self.dma_hist = {"sp": [], "pool": []}
        self.emitted = 0
        self.cnt = {e: 0 for e in self.ENGS}
        self.sems = {}
        for e in self.ENGS:
            self.sems[e] = st.enter_context(nc.semaphore("s_" + e))
        for q in ("sp", "pool"):
            for i in range(NDMA_SEM):
                self.sems[(q, i)] = st.enter_context(nc.semaphore("d_%s%d" % (q, i)))

    def op(self, eng, fn, reads=(), writes=(), dma=False):
        oid = len(self.ops)
        deps = {}
        lo = self.emitted
        for k in reads:
            w = self.last_w.get(k)
            if w is not None and w >= lo:
                deps[w] = 2
        for k in writes:
            w = self.last_w.get(k)
            if w is not None and w >= lo:
                deps[w] = max(deps.get(w, 0), 1)
            for r in self.readers.get(k, ()):
                if r >= lo:
                    deps.setdefault(r, 0)
        for d in list(deps):
            do = self.ops[d]
            if do["eng"] == eng and not do["dma"] and not dma:
                if deps[d] == 0 or (deps[d] == 1 and eng == "pe"):
                    del deps[d]
        o = dict(eng=eng, fn=fn, dma=dma, deps=deps)
        if dma:
            n = self.dma_count[eng]
            self.dma_count[eng] = n + 1
            o["dma_i"] = n
            h = self.dma_hist[eng]
            if n >= NDMA_SEM and h[n - NDMA_SEM] >= lo:
                deps[h[n - NDMA_SEM]] = 2
            h.append(oid)
        self.ops.append(o)
        for k in reads:
            self.readers.setdefault(k, []).append(oid)
        for k in writes:
            self.last_w[k] = oid
            self.readers[k] = []
        return oid

    def flush(self):
        nc = self.nc
        allops = self.ops
        lo = self.emitted
        ops = allops[lo:]
        self.emitted = len(allops)
        if not ops:
            return
        for o in ops:
            o["sig"] = o["dma"]
        for o in ops:
            for d in o["deps"]:
                if not allops[d]["dma"]:
                    allops[d]["sig"] = True
        for o in ops:
            if o["dma"]:
                i = o["dma_i"]
                o["semkey"] = (o["eng"], i % NDMA_SEM)
                o["semval"] = 16 * (i // NDMA_SEM + 1)
            elif o["sig"]:
                self.cnt[o["eng"]] += 1
                o["semkey"] = o["eng"]
                o["semval"] = self.cnt[o["eng"]]
        known = {e: {} for e in self.ENGS}
        for o in ops:
            kn = known[o["eng"]]
            waits = []
            for d in sorted(o["deps"]):
                do = allops[d]
                sk, sv = do["semkey"], do["semval"]
                if kn.get(sk, 0) >= sv:
                    continue
                waits.append((sk, sv))
                kn[sk] = sv
                for k2, v2 in do["clock"].items():
                    if kn.get(k2, 0) < v2:
                        kn[k2] = v2
            o["waits"] = waits
            o["clock"] = dict(kn)
            if "semkey" in o and not o["dma"]:
                o["clock"][o["semkey"]] = o["semval"]
        sems = self.sems
        dma_count = dict(self.dma_count)

        def replay(ename):
            def body(eng):
                for o in ops:
                    if o["eng"] != ename:
                        continue
                    for sk, sv in o["waits"]:
                        eng.wait_ge(sems[sk], sv)
                    ins = o["fn"](eng)
                    if o["dma"]:
                        ins.then_inc(sems[o["semkey"]], 16)
                    elif o["sig"]:
                        ins.then_inc(sems[o["semkey"]], 1)
                if ename in ("sp", "pool"):
                    n = dma_count[ename]
                    for i in range(NDMA_SEM):
                        c = (n - i + NDMA_SEM - 1) // NDMA_SEM
                        if c > 0:
                            eng.wait_ge(sems[(ename, i)], 16 * c)
            return body

        with nc.Block() as block:
            block.tensor(replay("pe"))
            block.scalar(replay("act"))
            block.vector(replay("dve"))
            block.gpsimd(replay("pool"))
            block.sync(replay("sp"))
        for o in ops:
            o["fn"] = None
            o["clock"] = None


def vec_layout():
    off = {}
    n = 0

    def add(name, cols):
        nonlocal n
        off[name] = n
        n += cols

    for li in range(DEPTH):
        add(("bmod", li), 48)
        add(("n1g", li), 8)
        add(("n2g", li), 8)
        add(("bgate", li), 32)
        add(("fcw", li), 66)
        add(("fcb", li), 22)
        for nm in ("dfqn", "dfkn", "dfng", "naqn", "nakn", "glng", "dnng", "dndtb", "dnalog"):
            add((nm, li), 1)
        add(("dncw", li), 18)
    add("m1", 1)
    add("m2", 1)
    add("one", 1)
    add("hm", 4)
    add("dnsgn", 1)
    return off, n


def pmajor(v):
    v = np.asarray(v, np.float32)
    lead = int(np.prod(v.shape[:-1])) if v.ndim > 1 else 1
    n = v.shape[-1] // 128
    return np.ascontiguousarray(v.reshape(lead, n, 128).transpose(2, 0, 1).reshape(128, lead * n))


def pack_vecs(inp):
    off, n = vec_layout()
    V = np.zeros((128, n), np.float32)

    def put(name, arr):
        V[:, off[name]:off[name] + arr.shape[1]] = arr

    for li in range(DEPTH):
        put(("bmod", li), pmajor(inp["b_mod"][li]))
        put(("n1g", li), pmajor(inp["norm1_g"][li]))
        put(("n2g", li), pmajor(inp["norm2_g"][li]))
        put(("bgate", li), pmajor(inp["b_gate"][li]))
        put(("fcw", li), pmajor(inp["ffn_conv_w"][li]))
        put(("fcb", li), pmajor(inp["ffn_conv_b"][li]))
        put(("dfqn", li), np.tile(inp["df_q_norm"][li], 4)[:, None])
        put(("dfkn", li), np.tile(inp["df_k_norm"][li], 4)[:, None])
        put(("dfng", li), np.tile(inp["df_norm_g"][li], 2)[:, None])
        put(("naqn", li), np.tile(inp["na_q_norm"][li], 2)[:, None])
        put(("nakn", li), np.tile(inp["na_k_norm"][li], 2)[:, None])
        put(("glng", li), np.tile(inp["gla_norm_g"][li], 2)[:, None])
        put(("dnng", li), np.tile(inp["dn_norm_g"][li], 2)[:, None])
        z8 = np.zeros(8, np.float32)
        put(("dndtb", li), np.concatenate([z8, np.asarray(inp["dn_dt_bias"][li], np.float32).reshape(8), np.zeros(112, np.float32)])[:, None])
        put(("dnalog", li), np.concatenate([z8, np.asarray(inp["dn_a_log"][li], np.float32).reshape(8), np.zeros(112, np.float32)])[:, None])
        put(("dncw", li), pmajor(inp["dn_conv"][li]))
    p = np.arange(128)
    put("m1", ((p // 32) % 2 == 0).astype(np.float32)[:, None])
    put("m2", ((p // 32) % 2 == 1).astype(np.float32)[:, None])
    put("one", np.ones((128, 1), np.float32))
    put("hm", (p[:, None] // 32 == np.arange(4)[None, :]).astype(np.float32))
    put("dnsgn", np.where(p < 8, -1.0, 1.0).astype(np.float32)[:, None])
    return V


NEG = -30000.0


def const_mats():
    C = np.zeros((4, 128, 128), np.float32)
    p = np.arange(128)
    C[0] = (p[:, None] // 32 == p[None, :] // 32) / 32.0
    C[1] = (p[:, None] // 64 == p[None, :] // 64) / 64.0
    for m in range(128):
        g, d = m // 32, m % 32
        q = d // 8
        if q == 0:
            C[2, g * 32 + d + 8, m] = -1.0
        elif q == 1:
            C[2, g * 32 + d - 8, m] = 1.0
        elif q == 2:
            C[2, g * 32 + d + 8, m] = -1.0
        else:
            C[2, g * 32 + d - 8, m] = 1.0
    C[3] = 1.0
    return np.ascontiguousarray(C.transpose(1, 0, 2).reshape(128, 512))


def rope_tables():
    t = np.arange(L)
    row = (t // 64).astype(np.float32)
    col = (t % 64).astype(np.float32)
    nf = 8
    inv = np.power(np.float32(10000.0), -np.arange(nf, dtype=np.float32) / nf).astype(np.float32)
    ar = row[:, None] * inv
    ac = col[:, None] * inv
    ang = np.concatenate([ar, ar, ac, ac], -1)
    cs = np.stack([np.cos(ang), np.sin(ang)], 0).astype(np.float32)
    return np.ascontiguousarray(np.tile(cs.transpose(0, 2, 1), (1, 4, 1)))


SCM = {}
for _i, _n in enumerate(("I", "U", "NU", "MI", "MS", "MST", "CKD", "CQ", "M01")):
    SCM[_n] = _i
NSCM = len(SCM)


def scan_mats():
    t = np.arange(128)
    same = (t[:, None] // 64) == (t[None, :] // 64)
    out = np.zeros((2, NSCM, 128, 128), np.float32)
    for d in range(2):
        before = (t[:, None] <= t[None, :]) if d == 0 else (t[:, None] >= t[None, :])
        strict = (t[:, None] < t[None, :]) if d == 0 else (t[:, None] > t[None, :])
        U = (same & before).astype(np.float32)
        out[d, SCM["I"]] = np.eye(128, dtype=np.float32)
        out[d, SCM["U"]] = U
        out[d, SCM["NU"]] = -U
        out[d, SCM["MI"]] = np.where(same & before, 0.0, NEG)
        out[d, SCM["MS"]] = np.where(same & strict, 0.0, NEG)
        out[d, SCM["MST"]] = np.where(same & strict, 0.0, NEG).T
        out[d, SCM["CKD"]] = (same & strict.T).astype(np.float32)
        pos = t % 64
        midpos = 31 if d == 0 else 32
        umid = (same & ((pos[:, None] <= midpos) if d == 0 else (pos[:, None] >= midpos))).astype(np.float32)
        out[d, SCM["CQ"]] = U - umid
        out[d, SCM["M01"]] = (same & before).astype(np.float32)
    return np.ascontiguousarray(out.transpose(2, 0, 1, 3).reshape(128, 2 * NSCM * 128))


def na_chunks(rp):
    if rp in (0, 1):
        return [(kc, 5 + rp * 4 + kc) for kc in range(4)]
    if rp in (30, 31):
        return [(28 + j, 13 + (rp - 30) * 4 + j) for j in range(4)]
    return [(rp - 2 + j, j) for j in range(5)]


def na_bias_tables(rpb):
    out = np.full((4, 128, 21, 128), NEG, np.float32)
    kk = np.arange(128)
    for rp in [2, 0, 1, 30, 31]:
        for (kc, bi) in na_chunks(rp):
            qrow = 2 * rp + kk // 64
            qcol = kk % 64
            krow = 2 * kc + kk // 64
            kcol = kk % 64
            rs = np.clip(qrow - 4, 0, 56)
            cst = np.clip(qcol - 8, 0, 48)
            dr = krow[:, None] - qrow[None, :]
            dc = kcol[:, None] - qcol[None, :]
            valid = ((krow[:, None] >= rs[None, :]) & (krow[:, None] < rs[None, :] + 8)
                     & (kcol[:, None] >= cst[None, :]) & (kcol[:, None] < cst[None, :] + 16))
            ri = np.clip(dr + 7, 0, 14)
            ci = np.clip(dc + 15, 0, 30)
            for h in range(4):
                g = rpb[h][ri, ci]
                out[h, :, bi, :] = np.where(valid, g, np.float32(NEG))
    return out


PF_BLOCKS = ([(i * 128, 128) for i in range(8)] + [(1024, 16)] + [(1040 + i * 128, 128) for i in range(6)]
             + [(1808, 128), (1936, 128), (2064, 128), (2192, 128), (2320, 128), (2448, 128), (2576, 16), (2592, 16)]
             + [(2608 + i * 128, 128) for i in range(6)])
NPF = len(PF_BLOCKS)
PT_GROUPS = ((1552, 256, 0), (3120, 256, 256), (1936, 384, 512))
NPT = 896


def build(debug=None, nlayers=DEPTH):
    debug = debug or {}
    nc = bass.Bass("TRN2", target_bir_lowering=False)
    voff, NV = vec_layout()

    def din(name, shape, dt=F32):
        return nc.dram_tensor(name, list(shape), dt, kind="ExternalInput").ap()

    def dscr(name, shape, dt):
        kind = "ExternalOutput" if name in debug.get("dump", ()) else "Internal"
        return nc.dram_tensor(name, list(shape), dt, kind=kind).ap()

    xT_in = din("xT", [D, L])
    cT_in = din("ctxT", [D, CL])
    cs_in = din("cs", [128, 16])
    vecs_in = din("vecs", [128, NV])
    w_mod = din("w_mod", [DEPTH, D, 6 * D])
    w_in = din("w_in", [DEPTH, D, NIN])
    w_branch = din("w_branch", [DEPTH, D, D])
    w_out = din("w_out", [DEPTH, D, D])
    f_w_in = din("ffn_w_in", [DEPTH, D, 2 * DFF])
    f_w_out = din("ffn_w_out", [DEPTH, DFF, D])
    outT = nc.dram_tensor("outT", [D, L], F32, kind="ExternalOutput").ap()
    y_dbg = din("y_dbg", [D, T], BF16) if debug.get("y_in") else None
    mixers = debug.get("mixers", ("dn", "na", "gla", "df"))
    cmat_in = din("cmat", [128, 512])
    rope_in = din("rope", [2, 128, L])
    nab_in = din("nab", [DEPTH, 4, 128, 21, 128])
    dflam_in = din("dflam", [DEPTH, 128])
    scm_in = din("scm", [128, 2 * NSCM * 128])
    gla_w_in = din("gla_w", [DEPTH, 2, 17, 128])

    wi_b = dscr("wi_b", [DEPTH, D, NIN], BF16)
    wb_b = dscr("wb_b", [DEPTH, D, D], BF16)
    wo_b = dscr("wo_b", [DEPTH, D, D], BF16)
    f1_b = dscr("f1_b", [DEPTH, D, 2 * DFF], BF16)
    f2_b = dscr("f2_b", [DEPTH, DFF, D], BF16)
    xbuf = dscr("xbuf", [D, T], F32)
    x1buf = dscr("x1buf", [D, T], F32)
    hbuf = dscr("hbuf", [D, T], BF16)
    ybuf = dscr("ybuf", [D, T], BF16)
    PF = dscr("PF", [NPF * 128, T], F32)
    PT = dscr("PT", [T, NPT], F32)
    xd = nc.dram_tensor("xd", [DEPTH, D, T], F32, kind="ExternalOutput").ap() if debug.get("xdump") else None

    with contextlib.ExitStack() as top:
        S = Sched(nc, top)

        uid = [0]

        def sbuf(st, name, shape, dt):
            uid[0] += 1
            return st.enter_context(nc.sbuf_tensor("sb%d_%s" % (uid[0], name), list(shape), dt))

        PS = [top.enter_context(nc.psum_tensor("ps%d" % i, [128, 512], F32)) for i in range(8)]

        def MM(out, lhsT, rhs, st, sp, r, w):
            S.op("pe", lambda e: e.matmul(out, lhsT=lhsT, rhs=rhs, start=st, stop=sp), reads=r, writes=w)

        def ACT(out, in_, func, r, w, bias=0.0, scale=1.0):
            S.op("act", lambda e: e.activation(out=out, in_=in_, func=func, bias=bias, scale=scale), reads=r, writes=w)

        def TT(out, in0, in1, op, r, w, eng="dve"):
            S.op(eng, lambda e: e.tensor_tensor(out=out, in0=in0, in1=in1, op=op), reads=r, writes=w)

        def TS(out, in0, s1, s2, op0, op1, r, w, eng="dve"):
            S.op(eng, lambda e: e.tensor_scalar(out=out, in0=in0, scalar1=s1, scalar2=s2, op0=op0, op1=op1), reads=r, writes=w)

        def STT(out, in0, scalar, in1, op0, op1, r, w, eng="dve"):
            S.op(eng, lambda e: e.scalar_tensor_tensor(out=out, in0=in0, scalar=scalar, in1=in1, op0=op0, op1=op1), reads=r, writes=w)

        def CP(out, in_, r, w, eng="dve"):
            S.op(eng, lambda e: e.tensor_copy(out=out, in_=in_), reads=r, writes=w)

        def MSET(ap, val, w, eng="pool"):
            S.op(eng, lambda e: e.memset(ap, val), writes=w)

        def DMA(out, in_, r, w, q="sp"):
            S.op(q, lambda e: e.dma_start(out=out, in_=in_), reads=r, writes=w, dma=True)

        vecs = sbuf(top, "vecs", [128, NV], F32)
        modsb = sbuf(top, "modsb", [128, DEPTH, 48, 2], F32)
        gsc = sbuf(top, "gsc", [128, DEPTH, 2, 8, 2], F32)
        ones_b = sbuf(top, "ones_b", [128, 128], BF16)
        DMA(vecs[:], vecs_in[:], [], ["vecs"])
        MSET(ones_b[:], 1.0, ["ones_b"])

        cmat = sbuf(top, "cmat", [128, 512], F32)
        DMA(cmat[:], cmat_in[:], [], ["cmat"])
        BD32 = cmat[:, 0:128]
        BD64 = cmat[:, 128:256]
        RM = cmat[:, 256:384]
        ONESF = cmat[:, 384:512]

        scm = sbuf(top, "scm", [128, 2, NSCM, 128], F32)
        DMA(scm[:].rearrange("p a b c -> p (a b c)"), scm_in[:], [], ["scm"])

        def CM(d, name):
            return scm[:, d, SCM[name], :]

        def V(name, j=0, n=1):
            o = voff[name] + j
            return vecs[:, o:o + n]

        def cast2d(dst, src, rows, cols, key):
            for r0 in range(0, rows, 1024):
                r1 = min(rows, r0 + 1024)
                for c0 in range(0, cols, 2048):
                    c1 = min(cols, c0 + 2048)
                    DMA(dst[r0:r1, c0:c1], src[r0:r1, c0:c1], [], [key], q="pool")

        layer_list = list(debug.get("layers", range(nlayers)))

        def cast_layer(li):
            cast2d(wi_b[li], w_in[li], D, NIN, ("wi_b", li))
            cast2d(wb_b[li], w_branch[li], D, D, ("wb_b", li))
            cast2d(wo_b[li], w_out[li], D, D, ("wo_b", li))
            cast2d(f1_b[li], f_w_in[li], D, 2 * DFF, ("f1_b", li))
            cast2d(f2_b[li], f_w_out[li], DFF, D, ("f2_b", li))

        cast_layer(layer_list[0])

        with contextlib.ExitStack() as ph:
            scs = sbuf(ph, "scs", [128, 16], F32)
            wm = [sbuf(ph, "wm%d" % i, [128, 8, 768], F32) for i in range(2)]
            DMA(scs[:], cs_in[:], [], ["scs"])
            ACT(scs[:], scs[:], AF.Silu, ["scs"], ["scs"])
            nb = 0
            for li in layer_list:
                wv = w_mod[li].rearrange("(k p) c -> p k c", p=128)
                for cb in range(8):
                    wt = wm[nb % 2]
                    wk = ("wm", nb % 2)
                    nb += 1
                    DMA(wt[:], wv[:, :, cb * 768:(cb + 1) * 768], [], [wk])
                    for jj in range(6):
                        j = cb * 6 + jj
                        for k in range(8):
                            MM(PS[0][:, 2 * j:2 * j + 2], wt[:, k, jj * 128:(jj + 1) * 128], scs[:, 2 * k:2 * k + 2],
                               k == 0, k == 7, [wk, "scs"], ["ps0"])
                TT(modsb[:, li], PS[0][:, 0:96].rearrange("p (j s) -> p j s", s=2),
                   V(("bmod", li), 0, 48).rearrange("p (j o) -> p j o", o=1).to_broadcast([128, 48, 2]), ALU.add,
                   ["ps0", "vecs"], ["modsb"])
                for which, (so, gname) in enumerate(((8, "n1g"), (32, "n2g"))):
                    TS(gsc[:, li, which], modsb[:, li, so:so + 8, :], 1.0, 0.0, ALU.add, ALU.add, ["modsb"], ["gsc"])
                    TT(gsc[:, li, which], gsc[:, li, which],
                       V((gname, li), 0, 8).rearrange("p (j o) -> p j o", o=1).to_broadcast([128, 8, 2]), ALU.mult,
                       ["gsc", "vecs"], ["gsc"])
            S.flush()

        def modv(li, what, k, s):
            base = {"sh1": 0, "sc1": 8, "g1": 16, "sh2": 24, "sc2": 32, "g2": 40}[what]
            return modsb[:, li, base + k, s:s + 1]

        def norm_tile(xt, ht, W, li, which, s, tmp_sq, tmp_r, tmp_f, psb, kx, kh, tag):
            shn = "sh1" if which == 0 else "sh2"
            for k in range(8):
                ACT(tmp_sq[:, 0:W], xt[:, k, 0:W], AF.Square, [kx], [tag + "sq"])
                MM(PS[psb][:, 0:W], ones_b[:], tmp_sq[:, 0:W], k == 0, k == 7, ["ones_b", tag + "sq"], ["ps%d" % psb])
            ACT(tmp_r[:, 0:W], PS[psb][:, 0:W], AF.Ln, ["ps%d" % psb], [tag + "r"], bias=EPS, scale=1.0 / D)
            ACT(tmp_r[:, 0:W], tmp_r[:, 0:W], AF.Exp, [tag + "r"], [tag + "r"], scale=-0.5)
            for k in range(8):
                STT(tmp_f[:, 0:W], xt[:, k, 0:W], gsc[:, li, which, k, s:s + 1], tmp_r[:, 0:W], ALU.mult, ALU.mult,
                    [kx, "gsc", tag + "r"], [tag + "f"])
                ACT(ht[:, k, 0:W], tmp_f[:, 0:W], AF.Identity, [tag + "f", "modsb"], [kh], bias=modv(li, shn, k, s))

        for li in layer_list:
            with_ctx = li < DEPTH - 1
            with contextlib.ExitStack() as ph:
                wi = sbuf(ph, "wi", [128, 8, NMIX], BF16)
                xt = [sbuf(ph, "xt%d" % i, [128, 8, 512], F32) for i in range(2)]
                ht = [sbuf(ph, "ht%d" % i, [128, 8, 512], BF16) for i in range(2)]
                sq = sbuf(ph, "sq", [128, 512], BF16)
                rr = sbuf(ph, "rr", [128, 512], F32)
                ff = sbuf(ph, "ff", [128, 512], F32)
                stg = [sbuf(ph, "stg%d" % i, [128, 512], F32) for i in range(4)]
                for k in range(8):
                    DMA(wi[:, k, :], wi_b[li][k * 128:(k + 1) * 128, 0:NMIX], [("wi_b", li)], [("wi", k)])
                wik = [("wi", k) for k in range(8)]
                nst = 0
                tiles1 = []
                for (sname, s0, slen) in SEQS:
                    for t0 in range(s0, s0 + slen, 512):
                        tiles1.append((0 if sname == "lat" else 1, t0, min(512, s0 + slen - t0)))

                def p1_norm(i):
                    (s, t0, W) = tiles1[i]
                    b = i % 2
                    if li == layer_list[0]:
                        src = xT_in[:, t0:t0 + W] if s == 0 else cT_in[:, t0 - L:t0 - L + W]
                    else:
                        src = xbuf[:, t0:t0 + W]
                    DMA(xt[b][:, :, 0:W], src.rearrange("(k p) t -> p k t", p=128), ["xbuf"], [("xt", b)])
                    norm_tile(xt[b], ht[b], W, li, 0, s, sq, rr, ff, 0, ("xt", b), ("ht", b), "n1")
                    DMA(hbuf[:, t0:t0 + W].rearrange("(k p) t -> p k t", p=128), ht[b][:, :, 0:W], [("ht", b)], ["hbuf"])

                p1_norm(0)
                for ti in range(len(tiles1)):
                    if True:
                        (s, t0, W) = tiles1[ti]
                        b = ti % 2
                        if ti + 1 < len(tiles1):
                            p1_norm(ti + 1)
                        for bi, (c0, ncol) in enumerate(PF_BLOCKS):
                            pb = 1 + (bi % 4)
                            for k in range(8):
                                MM(PS[pb][0:ncol, 0:W], wi[:, k, c0:c0 + ncol], ht[b][:, k, 0:W], k == 0, k == 7,
                                   [("wi", k), ("ht", b)], ["ps%d" % pb])
                            sg = nst % 4
                            nst += 1
                            if bi % 2 == 0:
                                ACT(stg[sg][0:ncol, 0:W], PS[pb][0:ncol, 0:W], AF.Copy, ["ps%d" % pb], [("stg", sg)])
                            else:
                                CP(stg[sg][0:ncol, 0:W], PS[pb][0:ncol, 0:W], ["ps%d" % pb], [("stg", sg)])
                            DMA(PF[bi * 128:bi * 128 + ncol, t0:t0 + W], stg[sg][0:ncol, 0:W], [("stg", sg)], [("PF", bi)])
                        for q in range(W // 128):
                            for gi, (c0, ncol, p0) in enumerate(PT_GROUPS):
                                pb = 5 if gi < 2 else 7
                                pcol = 256 if gi == 1 else 0
                                for k in range(8):
                                    MM(PS[pb][:, pcol:pcol + ncol], ht[b][:, k, q * 128:(q + 1) * 128], wi[:, k, c0:c0 + ncol],
                                       k == 0, k == 7, [("wi", k), ("ht", b)], ["ps%d" % pb])
                            for pb, p0, ncol in ((5, 0, 512), (7, 512, 384)):
                                sg = nst % 4
                                nst += 1
                                CP(stg[sg][:, 0:ncol], PS[pb][:, 0:ncol], ["ps%d" % pb], [("stg", sg)])
                                DMA(PT[t0 + q * 128:t0 + (q + 1) * 128, p0:p0 + ncol], stg[sg][:, 0:ncol], [("stg", sg)], [("PT", p0)])
                S.flush()

            if y_dbg is not None:
                with contextlib.ExitStack() as ph:
                    yt = sbuf(ph, "ycp", [128, 8, 512], BF16)
                    for t0 in range(0, T, 512):
                        W = min(512, T - t0)
                        DMA(yt[:, :, 0:W], y_dbg[:, t0:t0 + W].rearrange("(k p) t -> p k t", p=128), [], ["ycp"])
                        DMA(ybuf[:, t0:t0 + W].rearrange("(k p) t -> p k t", p=128), yt[:, :, 0:W], ["ycp"], ["ybuf"])
                    S.flush()


            BLK = {"dfq": 23, "dfk": 25, "naq": 9, "nak": 11}

            def qk_prep(ph, specs):
                px = [sbuf(ph, "px%d" % i, [128, 512], F32) for i in range(2)]
                psq = sbuf(ph, "psq", [128, 512], F32)
                prs = sbuf(ph, "prs", [128, 512], F32)
                pxn = sbuf(ph, "pxn", [128, 512], F32)
                pa = sbuf(ph, "ppa", [128, 512], F32)
                pb_ = sbuf(ph, "ppb", [128, 512], F32)
                rc = sbuf(ph, "rc", [128, 2, 512], F32)
                n = 0
                for t0 in range(0, T, 512):
                    W = min(512, T - t0)
                    lat = t0 < L
                    if lat and any(sp[3] for sp in specs):
                        DMA(rc[:, 0, :], rope_in[0, :, t0:t0 + 512], [], ["rc"])
                        DMA(rc[:, 1, :], rope_in[1, :, t0:t0 + 512], [], ["rc"])
                    for (blk, gm, gcol, rope, outs) in specs:
                        b = n % 2
                        n += 1
                        DMA(px[b][:, 0:W], PF[blk * 128:(blk + 1) * 128, t0:t0 + W], [], [("px", b)])
                        ACT(psq[:, 0:W], px[b][:, 0:W], AF.Square, [("px", b)], ["psq"])
                        MM(PS[0][:, 0:W], gm, psq[:, 0:W], True, True, ["cmat", "psq"], ["ps0"])
                        ACT(prs[:, 0:W], PS[0][:, 0:W], AF.Ln, ["ps0"], ["prs"], bias=EPS)
                        ACT(prs[:, 0:W], prs[:, 0:W], AF.Exp, ["prs"], ["prs"], scale=-0.5)
                        STT(pxn[:, 0:W], px[b][:, 0:W], gcol, prs[:, 0:W], ALU.mult, ALU.mult, [("px", b), "vecs", "prs"], ["pxn"])
                        val = pxn
                        vk = "pxn"
                        if rope and lat:
                            MM(PS[1][:, 0:W], RM, pxn[:, 0:W], True, True, ["cmat", "pxn"], ["ps1"])
                            TT(pa[:, 0:W], pxn[:, 0:W], rc[:, 0, 0:W], ALU.mult, ["pxn", "rc"], ["ppa"], eng="pool")
                            TT(pb_[:, 0:W], PS[1][:, 0:W], rc[:, 1, 0:W], ALU.mult, ["ps1", "rc"], ["ppb"])
                            TT(pa[:, 0:W], pa[:, 0:W], pb_[:, 0:W], ALU.add, ["ppa", "ppb"], ["ppa"], eng="pool")
                            val = pa
                            vk = "ppa"
                        for (dst, dk, mcol) in outs:
                            TS(dst[:, t0:t0 + W], val[:, 0:W], mcol, 0.0, ALU.mult, ALU.add, [vk, "vecs"], [dk])

            def load_vaug(ph, name, pcol):
                va = sbuf(ph, name, [128, 34, 4, 65], BF16)
                vst = [sbuf(ph, name + "st%d" % i, [128, 256], F32) for i in range(2)]
                MSET(va[:, :, :, 64:65], 1.0, [name])
                for kc in range(34):
                    b = kc % 2
                    DMA(vst[b][:], PT[kc * 128:(kc + 1) * 128, pcol:pcol + 256], [], [(name + "st", b)])
                    CP(va[:, kc, :, 0:64], vst[b][:].rearrange("p (h d) -> p h d", h=4), [(name + "st", b)], [name],
                       eng=("dve" if kc % 2 else "pool"))
                return va

            def attn_norm(ph_tiles, Obank, W, okey):
                osb, rrow, onrm = ph_tiles
                ACT(osb[0:65, 0:W], PS[Obank][0:65, 0:W], AF.Copy, ["ps%d" % Obank], ["osb"])
                S.op("dve", lambda e: e.reciprocal(out=rrow[64:65, 0:W], in_=osb[64:65, 0:W]), reads=["osb"], writes=["rrow"])
                MM(PS[7][0:64, 0:W], ONESF[64:65, 0:64], rrow[64:65, 0:W], True, True, ["cmat", "rrow"], ["ps7"])
                TT(onrm[0:64, 0:W], osb[0:64, 0:W], PS[7][0:64, 0:W], ALU.mult, ["osb", "ps7"], [okey])


            def head_norm_gate(ph, osum, okey, gate_blk, gcol, yrow0, perm=False):
                gx = [sbuf(ph, "hg_x%d" % i, [128, 512], F32) for i in range(2)]
                gs = sbuf(ph, "hg_s", [128, 512], F32)
                gr = sbuf(ph, "hg_r", [128, 512], F32)
                gy = [sbuf(ph, "hg_y%d" % i, [128, 512], BF16) for i in range(2)]
                n = 0
                for t0 in range(0, T, 512):
                    W = min(512, T - t0)
                    for hp in range(2):
                        b = n % 2
                        n += 1
                        if perm:
                            for j in range(2):
                                g0 = (gate_blk + j) * 128 + hp * 64
                                DMA(gx[b][j * 64:(j + 1) * 64, 0:W], PF[g0:g0 + 64, t0:t0 + W], [], [("hgx", b)])
                        else:
                            DMA(gx[b][:, 0:W], PF[(gate_blk + hp) * 128:(gate_blk + hp + 1) * 128, t0:t0 + W], [], [("hgx", b)])
                        ACT(gx[b][:, 0:W], gx[b][:, 0:W], AF.Silu, [("hgx", b)], [("hgx", b)])
                        ACT(gs[:, 0:W], osum[:, hp, t0:t0 + W], AF.Square, [okey], ["hgs"])
                        MM(PS[0][:, 0:W], BD64, gs[:, 0:W], True, True, ["cmat", "hgs"], ["ps0"])
                        ACT(gr[:, 0:W], PS[0][:, 0:W], AF.Ln, ["ps0"], ["hgr"], bias=EPS)
                        ACT(gr[:, 0:W], gr[:, 0:W], AF.Exp, ["hgr"], ["hgr"], scale=-0.5)
                        STT(gs[:, 0:W], osum[:, hp, t0:t0 + W], gcol, gr[:, 0:W], ALU.mult, ALU.mult, [okey, "vecs", "hgr"], ["hgs"])
                        TT(gy[b][:, 0:W], gs[:, 0:W], gx[b][:, 0:W], ALU.mult, ["hgs", ("hgx", b)], [("hgy", b)])
                        if perm:
                            for j in range(2):
                                y0 = yrow0 + (hp + 2 * j) * 64
                                DMA(ybuf[y0:y0 + 64, t0:t0 + W], gy[b][j * 64:(j + 1) * 64, 0:W], [("hgy", b)], [("ybuf", yrow0, j)])
                        else:
                            DMA(ybuf[yrow0 + hp * 128:yrow0 + (hp + 1) * 128, t0:t0 + W], gy[b][:, 0:W], [("hgy", b)], [("ybuf", yrow0)])

            def scan_blocks(d):
                cb = [L + 0, L + 128] if d == 0 else [L + 128, L + 0]
                lb_ = list(range(0, L, 128)) if d == 0 else list(range(L - 128, -1, -128))
                return cb + lb_


            if "dn" in mixers:
                with contextlib.ExitStack() as ph:
                    osum = sbuf(ph, "dno", [128, 2, T], F32)
                    qT = sbuf(ph, "dnq", [128, 2, T], BF16)
                    kT = sbuf(ph, "dnk", [128, 2, T], BF16)
                    vT = sbuf(ph, "dnv", [128, 2, T], BF16)
                    rowsT = sbuf(ph, "dnrows", [16, T], F32)
                    coef = sbuf(ph, "dncoef", [16, 1], F32)
                    I16 = sbuf(ph, "dnI16", [128, 128], BF16)
                    CP(I16[:], CM(0, "I"), ["scm"], ["dnI16"])
                    ACT(coef[:], V(("dnalog", li))[0:16, :], AF.Exp, ["vecs"], ["dncoef"])
                    TS(coef[:], coef[:], -1.0, 0.0, ALU.mult, ALU.add, ["dncoef"], ["dncoef"])
                    DMA(rowsT[:], PF[8 * 128:8 * 128 + 16, :], [], ["dnrows"])
                    for c0 in range(0, T, 1088):
                        sl = rowsT[:, c0:c0 + 1088]
                        ACT(sl, sl, AF.Exp, ["dnrows", "vecs"], ["dnrows"], bias=V(("dndtb", li))[0:16, :], scale=V("dnsgn")[0:16, :])
                        ACT(sl, sl, AF.Ln, ["dnrows"], ["dnrows"], bias=1.0)
                        TS(sl, sl, coef[:, 0:1], 0.0, ALU.mult, ALU.add, ["dnrows", "dncoef"], ["dnrows"])
                    with contextlib.ExitStack() as ph2:
                        xs = [sbuf(ph2, "dnxs%d" % i, [128, 514], F32) for i in range(2)]
                        ca = sbuf(ph2, "dnca", [128, 512], F32)
                        cb2 = sbuf(ph2, "dncb", [128, 512], F32)
                        sq2 = sbuf(ph2, "dnsq", [128, 512], F32)
                        n = 0
                        cw = voff[("dncw", li)]
                        for (sname, s0, slen) in SEQS:
                            for t0 in range(s0, s0 + slen, 512):
                                W = min(512, s0 + slen - t0)
                                a0 = max(t0 - 1, s0)
                                a1_ = min(t0 + W + 1, s0 + slen)
                                for blk in range(6):
                                    b = n % 2
                                    n += 1
                                    off0 = a0 - t0 + 1
                                    DMA(xs[b][:, off0:off0 + (a1_ - a0)], PF[blk * 128:(blk + 1) * 128, a0:a1_], [], [("dnxs", b)])
                                    if t0 == s0:
                                        MSET(xs[b][:, 0:1], 0.0, [("dnxs", b)])
                                    if t0 + W == s0 + slen:
                                        MSET(xs[b][:, W + 1:W + 2], 0.0, [("dnxs", b)])
                                    TS(ca[:, 0:W], xs[b][:, 1:1 + W], vecs[:, cw + 6 + blk:cw + 7 + blk], 0.0, ALU.mult, ALU.add, [("dnxs", b), "vecs"], ["dnca"])
                                    STT(cb2[:, 0:W], xs[b][:, 0:W], vecs[:, cw + blk:cw + blk + 1], ca[:, 0:W], ALU.mult, ALU.add, [("dnxs", b), "vecs", "dnca"], ["dncb"])
                                    STT(ca[:, 0:W], xs[b][:, 2:2 + W], vecs[:, cw + 12 + blk:cw + 13 + blk], cb2[:, 0:W], ALU.mult, ALU.add, [("dnxs", b), "vecs", "dncb"], ["dnca"])
                                    ACT(cb2[:, 0:W], ca[:, 0:W], AF.Silu, ["dnca"], ["dncb"])
                                    if blk >= 4:
                                        CP(vT[:, blk - 4, t0:t0 + W], cb2[:, 0:W], ["dncb"], [("dnv", blk - 4)], eng="pool")
                                        continue
                                    ACT(sq2[:, 0:W], cb2[:, 0:W], AF.Square, ["dncb"], ["dnsq"])
                                    MM(PS[0][:, 0:W], BD64, sq2[:, 0:W], True, True, ["cmat", "dnsq"], ["ps0"])
                                    ACT(sq2[:, 0:W], PS[0][:, 0:W], AF.Ln, ["ps0"], ["dnsq"], bias=EPS / 64)
                                    ACT(sq2[:, 0:W], sq2[:, 0:W], AF.Exp, ["dnsq"], ["dnsq"], scale=-0.5)
                                    if blk < 2:
                                        STT(qT[:, blk, t0:t0 + W], cb2[:, 0:W], 1.0 / 64, sq2[:, 0:W], ALU.mult, ALU.mult, ["dncb", "dnsq"], [("dnq", blk)])
                                    else:
                                        STT(kT[:, blk - 2, t0:t0 + W], cb2[:, 0:W], 1.0 / 8, sq2[:, 0:W], ALU.mult, ALU.mult, ["dncb", "dnsq"], [("dnk", blk - 2)])
                        S.flush()
                    nxt = layer_list.index(li) + 1
                    if nxt < len(layer_list):
                        cast_layer(layer_list[nxt])
                    rt = sbuf(ph, "dnrt", [128, 16], F32)
                    gB4 = sbuf(ph, "dngB", [128, 4, 128], F32)
                    lB4 = sbuf(ph, "dnlB", [128, 4, 128], F32)
                    ekd = sbuf(ph, "dnekd", [128, 16], F32)
                    ebt = sbuf(ph, "dnebt", [128, 8], F32)
                    kv = sbuf(ph, "dnkv", [128, 512], F32)
                    E5 = sbuf(ph, "dnE5", [128, 5, 4, 128], F32)
                    Pb = [sbuf(ph, "dnP%d" % i, [128, 4, 128], F32) for i in range(2)]
                    Qb = [sbuf(ph, "dnQ%d" % i, [128, 4, 128], F32) for i in range(2)]
                    X = sbuf(ph, "dnX", [128, 4, 128], F32)
                    aqk = sbuf(ph, "dnaqk", [128, 4, 128], BF16)
                    kbe = sbuf(ph, "dnkbe", [128, 2, 128], BF16)
                    qd = sbuf(ph, "dnqd", [128, 2, 128], BF16)
                    vb = sbuf(ph, "dnvb", [128, 4, 64], F32)
                    kdc = sbuf(ph, "dnkdc", [128, 4, 64], BF16)
                    vnZ = [sbuf(ph, "dnvn%d" % i, [128, 4, 64], BF16) for i in range(2)]
                    for i in range(2):
                        MSET(vnZ[i][:], 0.0, [("dnvn", i)], eng="dve")
                    rF = [sbuf(ph, "dnrF%d" % i, [128, 4, 64], F32) for i in range(2)]
                    Sf = sbuf(ph, "dnS", [128, 4, 64], F32)
                    S16 = sbuf(ph, "dnS16", [128, 4, 64], BF16)
                    for i in range(2):
                        MSET(rF[i][:], 0.0, [("dnrF", i)], eng="dve")
                    Ibc = CM(0, "I").rearrange("p (o c) -> p o c", o=1).to_broadcast([128, 4, 128])

                    def flat(ap):
                        return ap.rearrange("p h c -> p (h c)")

                    for d in range(2):
                        MSET(Sf[:], 0.0, ["dnS"], eng="dve")
                        MSET(S16[:], 0.0, ["dnS16"], eng="dve")
                        for t0 in scan_blocks(d)[:debug.get("dn_nblk", 100)]:
                            MM(PS[0][:, 0:16], rowsT[:, t0:t0 + 128], CM(0, "I")[0:16, 0:16], True, True, ["dnrows", "scm"], ["ps0"])
                            CP(rt[:], PS[0][:, 0:16], ["ps0"], ["dnrt"])
                            MM(PS[0][:, 16:32], CM(d, "CKD"), rt[:], True, True, ["scm", "dnrt"], ["ps0"])
                            ACT(ekd[:], PS[0][:, 16:32], AF.Exp, ["ps0"], ["dnekd"])
                            ACT(ebt[:], rt[:, 0:8], AF.Exp, ["dnrt"], ["dnebt"])
                            for i2 in range(2):
                                MM(PS[1][:, i2 * 128:(i2 + 1) * 128], kT[:, i2, t0:t0 + 128], I16[:], True, True, [("dnk", i2), "dnI16"], ["ps1"])
                                MM(PS[1][:, 256 + i2 * 128:256 + (i2 + 1) * 128], vT[:, i2, t0:t0 + 128], I16[:], True, True, [("dnv", i2), "dnI16"], ["ps1"])
                            CP(kv[:], PS[1][:], ["ps1"], ["dnkv"])
                            HP = [0, 2, 1, 3]
                            for par in range(2):
                                gsrc = rt[:, 8 + d * 4:12 + d * 4].rearrange("p (j q) -> p q j", q=2)[:, par, :]
                                lsrc = rt[:, d * 4:d * 4 + 4].rearrange("p (j q) -> p q j", q=2)[:, par, :]
                                CP(gB4[:, 2 * par:2 * par + 2, :], gsrc.rearrange("p (j o) -> p j o", o=1).to_broadcast([128, 2, 128]), ["dnrt"], ["dngB"])
                                CP(lB4[:, 2 * par:2 * par + 2, :], lsrc.rearrange("p (j o) -> p j o", o=1).to_broadcast([128, 2, 128]), ["dnrt"], ["dnlB"])
                            kr = ["dngB", "dnlB", "scm"]
                            order = (0, 1) if d == 0 else (1, 0)
                            for ty in range(5):
                                pb = 2 + ty % 2
                                pk = "ps%d" % pb
                                for pos in range(4):
                                    o_ = PS[pb][:, pos * 128:(pos + 1) * 128]
                                    gB = gB4[:, pos, :]
                                    lB = lB4[:, pos, :]
                                    if ty == 0:
                                        seq = [(gB, CM(d, "U")), (CM(d, "NU"), gB), (CM(d, "I"), CM(d, "MI"))]
                                    elif ty == 1:
                                        seq = [(gB, CM(d, "U")), (CM(d, "NU"), gB), (lB, CM(d, "I")), (CM(d, "I"), CM(d, "MS"))]
                                    elif ty == 2:
                                        seq = [(CM(d, "U"), gB), (gB, CM(d, "NU")), (CM(d, "I"), lB), (CM(d, "I"), CM(d, "MST"))]
                                    elif ty == 3:
                                        seq = [(gB, CM(d, "U"))]
                                    else:
                                        seq = [(gB, CM(d, "U")), (lB, CM(d, "I"))]
                                    for si, (la_, ra_) in enumerate(seq):
                                        MM(o_, la_, ra_, si == 0, si == len(seq) - 1, kr, [pk])
                                ACT(flat(E5[:, ty]), PS[pb][:], AF.Exp, [pk], [("dnE5", ty)])
                            for par in range(2):
                                r0 = par * 64
                                for j in range(2):
                                    kTh = kT[r0:r0 + 64, j, t0:t0 + 128]
                                    MM(PS[par][:, j * 128:(j + 1) * 128], kTh, kTh, True, True, [("dnk", j)], ["ps%d" % par])
                                    MM(PS[par][:, 256 + j * 128:256 + (j + 1) * 128], kTh, qT[r0:r0 + 64, j, t0:t0 + 128], True, True,
                                       [("dnk", j), ("dnq", j)], ["ps%d" % par])
                            for par in range(2):
                                sl = slice(2 * par, 2 * par + 2)
                                pk = "ps%d" % par
                                STT(flat(Qb[0][:, sl, :]), PS[par][:, 0:256], -1.0, flat(E5[:, 1, sl, :]), ALU.mult, ALU.mult, [pk, ("dnE5", 1)], [("dnQ", 0)])
                                STT(flat(Pb[0][:, sl, :]), PS[par][:, 0:256], -1.0, flat(E5[:, 2, sl, :]), ALU.mult, ALU.mult, [pk, ("dnE5", 2)], [("dnP", 0)])
                                TT(flat(aqk[:, sl, :]), PS[par][:, 256:512], flat(E5[:, 0, sl, :]), ALU.mult, [pk, ("dnE5", 0)], ["dnaqk"])
                            TT(X[:], Qb[0][:], Ibc, ALU.add, [("dnQ", 0), "scm"], ["dnX"])
                            for lvl in range(5):
                                a_, bn = lvl % 2, (lvl + 1) % 2
                                for pos in range(4):
                                    MM(PS[4][:, pos * 128:(pos + 1) * 128], Qb[a_][:, pos, :], Pb[a_][:, pos, :], True, True, [("dnQ", a_), ("dnP", a_)], ["ps4"])
                                ACT(flat(Pb[bn][:]), PS[4][:], AF.Copy, ["ps4"], [("dnP", bn)])
                                if lvl < 4:
                                    for pos in range(4):
                                        MM(PS[0][:, pos * 128:(pos + 1) * 128], Pb[a_][:, pos, :], Qb[a_][:, pos, :], True, True, [("dnQ", a_), ("dnP", a_)], ["ps0"])
                                    CP(flat(Qb[bn][:]), PS[0][:], ["ps0"], [("dnQ", bn)])
                                for pos in range(4):
                                    MM(PS[1][:, pos * 128:(pos + 1) * 128], Pb[bn][:, pos, :], X[:, pos, :], True, True, [("dnP", bn), "dnX"], ["ps1"])
                                TT(flat(X[:]), flat(X[:]), PS[1][:], ALU.add, ["dnX", "ps1"], ["dnX"])
                            for pos in range(4):
                                par, j = pos // 2, pos % 2
                                r0 = par * 64
                                TT(kbe[r0:r0 + 64, j, :], kT[r0:r0 + 64, j, t0:t0 + 128], E5[r0:r0 + 64, 4, pos, :], ALU.mult,
                                   [("dnk", j), ("dnE5", 4)], ["dnkbe"])
                                TT(qd[r0:r0 + 64, j, :], qT[r0:r0 + 64, j, t0:t0 + 128], E5[r0:r0 + 64, 3, pos, :], ALU.mult,
                                   [("dnq", j), ("dnE5", 3)], ["dnqd"])
                            for par in range(2):
                                sl = slice(2 * par, 2 * par + 2)
                                bsrc = ebt[:, d * 4:d * 4 + 4].rearrange("p (j q) -> p q j", q=2)[:, par, :]
                                esrc = ekd[:, 8 + d * 4:12 + d * 4].rearrange("p (j q) -> p q j", q=2)[:, par, :]
                                vsrc = kv[:, 256:512].rearrange("p (j q v) -> p q j v", q=2, v=64)[:, par]
                                ksrc = kv[:, 0:256].rearrange("p (j q v) -> p q j v", q=2, v=64)[:, par]
                                TT(vb[:, sl, :], vsrc, bsrc.rearrange("p (j o) -> p j o", o=1).to_broadcast([128, 2, 64]), ALU.mult, ["dnkv", "dnebt"], ["dnvb"])
                                TT(kdc[:, sl, :], ksrc, esrc.rearrange("p (j o) -> p j o", o=1).to_broadcast([128, 2, 64]), ALU.mult, ["dnkv", "dnekd"], ["dnkdc"])
                            for i in order:
                                c0 = i * 64
                                for pos in range(4):
                                    par, j = pos // 2, pos % 2
                                    r0 = par * 64
                                    pbk = 5 - par
                                    MM(PS[pbk][c0:c0 + 64, j * 64:(j + 1) * 64], kbe[r0:r0 + 64, j, c0:c0 + 64], S16[r0:r0 + 64, pos, :], True, True,
                                       ["dnkbe", "dnS16"], ["ps%d" % pbk])
                                for par in range(2):
                                    sl = slice(2 * par, 2 * par + 2)
                                    pbk = 5 - par
                                    TT(rF[i][c0:c0 + 64, sl, :], vb[c0:c0 + 64, sl, :], PS[pbk][c0:c0 + 64, 0:128].rearrange("p (h v) -> p h v", h=2),
                                       ALU.subtract, ["dnvb", "ps%d" % pbk], [("dnrF", i)])
                                for pos in range(4):
                                    MM(PS[2][:, pos * 64:(pos + 1) * 64], X[:, pos, :], rF[i][:, pos, :], True, True, ["dnX", ("dnrF", i)], ["ps2"])
                                ACT(vnZ[i][c0:c0 + 64, :, :], PS[2][c0:c0 + 64, 0:256].rearrange("p (h v) -> p h v", h=4), AF.Copy, ["ps2"], [("dnvn", i)])
                                for pos in range(4):
                                    par, j = pos // 2, pos % 2
                                    r0 = par * 64
                                    pO = 6 + par
                                    MM(PS[pO][j * 64:(j + 1) * 64, c0:c0 + 64], S16[r0:r0 + 64, pos, :], qd[r0:r0 + 64, j, c0:c0 + 64], True, False,
                                       ["dnS16", "dnqd"], ["ps%d" % pO])
                                    MM(PS[pO][j * 64:(j + 1) * 64, c0:c0 + 64], vnZ[i][:, pos, :], aqk[:, pos, c0:c0 + 64], False, True,
                                       [("dnvn", i), "dnaqk"], ["ps%d" % pO])
                                for pos in range(4):
                                    r0 = (pos // 2) * 64
                                    MM(PS[3][r0:r0 + 64, pos * 64:(pos + 1) * 64], kdc[c0:c0 + 64, pos, :], vnZ[i][c0:c0 + 64, pos, :], True, True,
                                       ["dnkdc", ("dnvn", i)], ["ps3"])
                                last = c0 + 63 if d == 0 else c0
                                TT(Sf[:], Sf[:], E5[:, 3, :, last:last + 1].to_broadcast([128, 4, 64]), ALU.mult, ["dnS", ("dnE5", 3)], ["dnS"])
                                TT(Sf[:], Sf[:], PS[3][:, 0:256].rearrange("p (h v) -> p h v", h=4), ALU.add, ["dnS", "ps3"], ["dnS"])
                                ACT(S16[:], Sf[:], AF.Copy, ["dnS"], ["dnS16"])
                            for hp in range(2):
                                if d == 0:
                                    ACT(osum[:, hp, t0:t0 + 128], PS[6 + hp][:, 0:128], AF.Copy, ["ps%d" % (6 + hp)], ["dno"])
                                else:
                                    TT(osum[:, hp, t0:t0 + 128], osum[:, hp, t0:t0 + 128], PS[6 + hp][:, 0:128], ALU.add, ["dno", "ps%d" % (6 + hp)], ["dno"])
                    head_norm_gate(ph, osum, "dno", 6, V(("dnng", li)), 0, perm=True)
                    S.flush()

            if "gla" in mixers:
                with contextlib.ExitStack() as ph:
                    osum = sbuf(ph, "glo", [128, 2, T], F32)
                    qT = sbuf(ph, "glq", [128, T], F32)
                    kT = sbuf(ph, "glk", [128, T], F32)
                    a1 = [sbuf(ph, "gla1_%d" % i, [17, T], F32) for i in range(2)]
                    wa = sbuf(ph, "glwa", [17, 2, 128], F32)
                    DMA(qT[:], PF[15 * 128:16 * 128, :], [], ["glq"])
                    DMA(kT[:], PF[16 * 128:17 * 128, :], [], ["glk"])
                    for d in range(2):
                        MSET(a1[d][:], 1.0, [("gla1", d)])
                        DMA(a1[d][0:16, :], PF[(21 + d) * 128:(21 + d) * 128 + 16, :], [], [("gla1", d)])
                        DMA(wa[:, d, :], gla_w_in[li, d], [], ["glwa"])
                    ktok = [sbuf(ph, "glkt%d" % i, [128, 384], F32) for i in range(2)]
                    vb16 = [sbuf(ph, "glvb%d" % i, [128, 256], BF16) for i in range(2)]
                    ln_ = sbuf(ph, "glln", [128, 128], F32)
                    eq = sbuf(ph, "gleq", [128, 128], F32)
                    ek = sbuf(ph, "glek", [128, 128], F32)
                    eb = sbuf(ph, "gleb", [128, 128], F32)
                    ekd = sbuf(ph, "glekd", [128, 128], F32)
                    qt_ = sbuf(ph, "glqt", [128, 128], F32)
                    ktl = sbuf(ph, "glktl", [128, 128], BF16)
                    qb = sbuf(ph, "glqb", [128, 128], F32)
                    qth = sbuf(ph, "glqth", [128, 4, 128], BF16)
                    qbh = sbuf(ph, "glqbh", [128, 4, 128], BF16)
                    kdec = sbuf(ph, "glkdec", [128, 128], BF16)
                    S16 = sbuf(ph, "glS16", [128, 64], BF16)
                    Ah = sbuf(ph, "glA", [128, 4, 128], BF16)
                    dsm = sbuf(ph, "gldsm", [128, 4, 64], F32)
                    dsr = sbuf(ph, "gldsr", [128, 64], F32)
                    Sst = sbuf(ph, "glS", [128, 64], F32)
                    osb = sbuf(ph, "glosb", [128, 2, 128], F32)
                    sc = 32 ** -0.5
                    nb = 0
                    for d in range(2):
                        MSET(Sst[:], 0.0, ["glS"])
                        for t0 in scan_blocks(d)[:debug.get("gla_nblk", 100)]:
                            b = nb % 2
                            nb += 1
                            stage = debug.get("gla_stage", 99)
                            DMA(ktok[b][:], PT[t0:t0 + 128, 512:896], [], [("glkt", b)])
                            CP(vb16[b][:], ktok[b][:, 128:384], [("glkt", b)], [("glvb", b)], eng="pool")
                            MM(PS[0][:, 0:128], a1[d][:, t0:t0 + 128], wa[:, d, :], True, True, [("gla1", d), "glwa"], ["ps0"])
                            ACT(ln_[:], PS[0][:, 0:128], AF.Exp, ["ps0"], ["glln"], scale=-1.0)
                            ACT(ln_[:], ln_[:], AF.Ln, ["glln"], ["glln"], bias=1.0)
                            if stage < 1:
                                continue
                            MM(PS[1][:, 0:128], ln_[:], CM(d, "CQ"), True, True, ["glln", "scm"], ["ps1"])
                            MM(PS[1][:, 128:256], ln_[:], CM(d, "U"), True, True, ["glln", "scm"], ["ps1"])
                            MM(PS[1][:, 256:384], CM(d, "CKD"), ln_[:], True, True, ["glln", "scm"], ["ps1"])
                            ACT(eq[:], PS[1][:, 0:128], AF.Exp, ["ps1"], ["gleq"], scale=-1.0 / 16)
                            ACT(ek[:], PS[1][:, 0:128], AF.Exp, ["ps1"], ["glek"], scale=1.0 / 16)
                            ACT(eb[:], PS[1][:, 128:256], AF.Exp, ["ps1"], ["gleb"], scale=-1.0 / 16)
                            ACT(ekd[:], PS[1][:, 256:384], AF.Exp, ["ps1"], ["glekd"], scale=-1.0 / 16)
                            STT(qt_[:], qT[:, t0:t0 + 128], sc, eq[:], ALU.mult, ALU.mult, ["glq", "gleq"], ["glqt"])
                            TT(ktl[:], kT[:, t0:t0 + 128], ek[:], ALU.mult, ["glk", "glek"], ["glktl"])
                            STT(qb[:], qT[:, t0:t0 + 128], sc, eb[:], ALU.mult, ALU.mult, ["glq", "gleb"], ["glqb"])
                            TT(kdec[:], ktok[b][:, 0:128], ekd[:], ALU.mult, [("glkt", b), "glekd"], ["glkdec"], eng="pool")
                            if stage < 2:
                                continue
                            for h in range(4):
                                TS(qth[:, h, :], qt_[:], V("hm", h), 0.0, ALU.mult, ALU.add, ["glqt", "vecs"], ["glqth"])
                                TS(qbh[:, h, :], qb[:], V("hm", h), 0.0, ALU.mult, ALU.add, ["glqb", "vecs"], ["glqbh"])
                            if stage < 3:
                                continue
                            for h in range(4):
                                MM(PS[2][:, h * 128:(h + 1) * 128], ktl[:], qth[:, h, :], True, True, ["glktl", "glqth"], ["ps2"])
                            TT(Ah[:], PS[2][:].rearrange("p (h c) -> p h c", h=4),
                               CM(d, "M01").rearrange("p (o c) -> p o c", o=1).to_broadcast([128, 4, 128]), ALU.mult, ["ps2", "scm"], ["glA"])
                            order = (0, 1) if d == 0 else (1, 0)
                            if stage < 4:
                                continue
                            for h in range(4):
                                orow = (h % 2) * 64
                                pO = 3 + h // 2
                                MM(PS[pO][orow:orow + 64, 0:128], vb16[b][:, h * 64:(h + 1) * 64], Ah[:, h, :], True, False,
                                   [("glvb", b), "glA"], ["ps%d" % pO])
                            for ii, i in enumerate(order):
                                c0 = i * 64
                                CP(S16[:], Sst[:], ["glS"], ["glS16"], eng="pool")
                                for h in range(4):
                                    orow = (h % 2) * 64
                                    pO = 3 + h // 2
                                    MM(PS[pO][orow:orow + 64, c0:c0 + 64], S16[:, :], qbh[:, h, c0:c0 + 64], False, ii == 1,
                                       ["glS16", "glqbh"], ["ps%d" % pO])
                                MM(PS[5][:, 0:256], kdec[c0:c0 + 64, :], vb16[b][c0:c0 + 64, :], True, True, ["glkdec", ("glvb", b)], ["ps5"])
                                TT(dsm[:], PS[5][:, 0:256].rearrange("p (h v) -> p h v", h=4),
                                   V("hm", 0, 4).rearrange("p (h o) -> p h o", o=1).to_broadcast([128, 4, 64]), ALU.mult, ["ps5", "vecs"], ["gldsm"])
                                S.op("dve", lambda e: e.reduce_sum(out=dsr[:], in_=dsm[:].rearrange("p h v -> p v h"), axis=AX.X),
                                     reads=["gldsm"], writes=["gldsr"])
                                last = c0 + 63 if d == 0 else c0
                                STT(Sst[:], Sst[:], eb[:, last:last + 1], dsr[:], ALU.mult, ALU.add, ["glS", "gleb", "gldsr"], ["glS"])
                            for hp in range(2):
                                if d == 0:
                                    ACT(osum[:, hp, t0:t0 + 128], PS[3 + hp][:, 0:128], AF.Copy, ["ps%d" % (3 + hp)], ["glo"])
                                else:
                                    TT(osum[:, hp, t0:t0 + 128], osum[:, hp, t0:t0 + 128], PS[3 + hp][:, 0:128], ALU.add, ["glo", "ps%d" % (3 + hp)], ["glo"])
                    if debug.get("gla_stage", 99) >= 99:
                        head_norm_gate(ph, osum, "glo", 19, V(("glng", li)), 512)
                    S.flush()

            if "df" in mixers:
                with contextlib.ExitStack() as ph:
                    qr = sbuf(ph, "dfqr", [128, 2, T], BF16)
                    k1z = sbuf(ph, "dfk1", [128, 2, T], BF16)
                    k2z = sbuf(ph, "dfk2", [128, 2, T], BF16)
                    with contextlib.ExitStack() as ph2:
                        specs = []
                        for i in range(2):
                            specs.append((BLK["dfq"] + i, BD32, V(("dfqn", li)), True, [(qr[:, i, :], ("dfqr", i), V("one"))]))
                            specs.append((BLK["dfk"] + i, BD32, V(("dfkn", li)), True,
                                          [(k1z[:, i, :], ("dfk1", i), V("m1")), (k2z[:, i, :], ("dfk2", i), V("m2"))]))
                        qk_prep(ph2, specs)
                        S.flush()
                    va = load_vaug(ph, "dfva", 256)
                    lp = sbuf(ph, "lp", [128, 2, 2, 32], F32)
                    lpp = sbuf(ph, "lpp", [128, 2, 32], F32)
                    lps = sbuf(ph, "lps", [128, 2], F32)
                    nlam = sbuf(ph, "nlam", [128, 1], F32)
                    lam_init = 0.8 - 0.6 * math.exp(-0.3 * li)
                    DMA(lp[:].rearrange("p a b d -> p (a b d)"), dflam_in[li:li + 1, :].partition_broadcast(128), [], ["lp"])
                    TT(lpp[:], lp[:, :, 0, :], lp[:, :, 1, :], ALU.mult, ["lp"], ["lpp"])
                    S.op("dve", lambda e: e.reduce_sum(out=lps[:], in_=lpp[:], axis=AX.X), reads=["lpp"], writes=["lps"])
                    ACT(lps[:], lps[:], AF.Exp, ["lps"], ["lps"])
                    TT(nlam[:], lps[:, 1:2], lps[:, 0:1], ALU.subtract, ["lps"], ["nlam"])
                    TS(nlam[:], nlam[:], -lam_init, 0.0, ALU.add, ALU.add, ["nlam"], ["nlam"])
                    E = [sbuf(ph, "dfE%d" % i, [128, 512], BF16) for i in range(4)]
                    osb = sbuf(ph, "osb", [128, 512], F32)
                    rrow = sbuf(ph, "rrow", [128, 512], F32)
                    o1 = sbuf(ph, "o1n", [128, 512], F32)
                    o2 = sbuf(ph, "o2n", [128, 512], F32)
                    dsq = sbuf(ph, "dsq", [128, 512], F32)
                    yo = [sbuf(ph, "dfy%d" % i, [128, 512], BF16) for i in range(2)]
                    sc = 32 ** -0.5
                    ne = 0
                    ny = 0
                    qtiles = [(t0, 512, list(range(34))) for t0 in range(0, L, 512)]
                    if with_ctx:
                        qtiles.append((L, CL, [32, 33]))
                    steps = []
                    for (t0, W, kcs) in qtiles:
                        for h in range(4):
                            for ci, kc in enumerate(kcs):
                                for t in range(2):
                                    steps.append((t0, W, h, kc, t, ci == 0, ci == len(kcs) - 1))
                    PIPE = 2
                    kzs = (k1z, k2z)

                    def emit_score(i):
                        (t0, W, h, kc, t, first, last) = steps[i]
                        blk = h // 2
                        r0 = (h % 2) * 64
                        sb_ = 2 + (i % 4)
                        MM(PS[sb_][:, 0:W], kzs[t][r0:r0 + 64, blk, kc * 128:(kc + 1) * 128], qr[r0:r0 + 64, blk, t0:t0 + W],
                           True, True, [("dfk%d" % (t + 1), blk), ("dfqr", blk)], ["ps%d" % sb_])

                    def emit_rest(i):
                        nonlocal ny
                        (t0, W, h, kc, t, first, last) = steps[i]
                        sb_ = 2 + (i % 4)
                        eb = i % 4
                        ACT(E[eb][:, 0:W], PS[sb_][:, 0:W], AF.Exp, ["ps%d" % sb_], [("dfE", eb)], scale=sc)
                        MM(PS[t][0:65, 0:W], va[:, kc, h, :], E[eb][:, 0:W], first, last, ["dfva", ("dfE", eb)], ["ps%d" % t])
                        if not (last and t == 1):
                            return
                        attn_norm((osb, rrow, o1), 0, W, "o1n")
                        attn_norm((osb, rrow, o2), 1, W, "o2n")
                        STT(o1[0:64, 0:W], o2[0:64, 0:W], nlam[0:64, 0:1], o1[0:64, 0:W], ALU.mult, ALU.add,
                            ["o1n", "o2n", "nlam"], ["o1n"])
                        ACT(dsq[0:64, 0:W], o1[0:64, 0:W], AF.Square, ["o1n"], ["dsq"])
                        MM(PS[7][0:64, 0:W], BD64[0:64, 0:64], dsq[0:64, 0:W], True, True, ["cmat", "dsq"], ["ps7"])
                        ACT(dsq[0:64, 0:W], PS[7][0:64, 0:W], AF.Ln, ["ps7"], ["dsq"], bias=EPS)
                        ACT(dsq[0:64, 0:W], dsq[0:64, 0:W], AF.Exp, ["dsq"], ["dsq"], scale=-0.5)
                        yb_ = ny % 2
                        ny += 1
                        STT(dsq[0:64, 0:W], o1[0:64, 0:W], V(("dfng", li))[0:64, :], dsq[0:64, 0:W], ALU.mult, ALU.mult,
                            ["o1n", "vecs", "dsq"], ["dsq"])
                        TS(yo[yb_][0:64, 0:W], dsq[0:64, 0:W], 1.0 - lam_init, 0.0, ALU.mult, ALU.add, ["dsq"], [("dfy", yb_)])
                        DMA(ybuf[768 + h * 64:768 + (h + 1) * 64, t0:t0 + W], yo[yb_][0:64, 0:W], [("dfy", yb_)], [("ybuf", "df")])

                    for i in range(len(steps) + PIPE):
                        if i < len(steps):
                            emit_score(i)
                        if i >= PIPE:
                            emit_rest(i - PIPE)
                    S.flush()

            if "na" in mixers:
                with contextlib.ExitStack() as ph:
                    qn = sbuf(ph, "naq", [128, 2, T], BF16)
                    kn = sbuf(ph, "nak", [128, 2, T], BF16)
                    with contextlib.ExitStack() as ph2:
                        specs = []
                        for i in range(2):
                            specs.append((BLK["naq"] + i, BD64, V(("naqn", li)), False, [(qn[:, i, :], ("naq", i), V("one"))]))
                            specs.append((BLK["nak"] + i, BD64, V(("nakn", li)), False, [(kn[:, i, :], ("nak", i), V("one"))]))
                        qk_prep(ph2, specs)
                        S.flush()
                    va = load_vaug(ph, "nava", 0)
                    bias = [sbuf(ph, "nab%d" % i, [128, 21, 128], F32) for i in range(2)]
                    sbt = [sbuf(ph, "nasb%d" % i, [128, 640], F32) for i in range(2)]
                    E = [sbuf(ph, "naE%d" % i, [128, 896], BF16) for i in range(2)]
                    osb = sbuf(ph, "osb", [128, 512], F32)
                    rrow = sbuf(ph, "rrow", [128, 512], F32)
                    o1 = sbuf(ph, "o1n", [128, 512], F32)
                    yo = [sbuf(ph, "nay%d" % i, [128, 512], BF16) for i in range(2)]
                    sc = 64 ** -0.5
                    n = 0
                    ny = 0
                    for h in range(4):
                        blk = h // 2
                        r0 = (h % 2) * 64
                        hb = h % 2
                        DMA(bias[hb][:], nab_in[li, h], [], [("nab", hb)])
                        for rg in range(8):
                            for rr_ in range(4):
                                rp = rg * 4 + rr_
                                chunks = na_chunks(rp)
                                b = n % 2
                                n += 1
                                pA = 2 + 2 * b
                                pB = pA + 1
                                q_ap = qn[r0:r0 + 64, blk, rp * 128:(rp + 1) * 128]
                                nw = len(chunks)
                                for j, (kc, bi) in enumerate(chunks):
                                    pbk, pc = (pA, j * 128) if j < 4 else (pB, 0)
                                    MM(PS[pbk][:, pc:pc + 128], kn[r0:r0 + 64, blk, kc * 128:(kc + 1) * 128], q_ap, True, True,
                                       [("nak", blk), ("naq", blk)], ["ps%d" % pbk])
                                for j2 in range(2):
                                    pc = 128 + j2 * 128
                                    MM(PS[pB][:, pc:pc + 128], kn[r0:r0 + 64, blk, L + j2 * 128:L + (j2 + 1) * 128], q_ap, True, True,
                                       [("nak", blk), ("naq", blk)], ["ps%d" % pB])
                                bi0 = chunks[0][1]
                                STT(sbt[b][:, 0:512], PS[pA][:, 0:512], sc, bias[hb][:, bi0:bi0 + 4, :].rearrange("p a q -> p (a q)"),
                                    ALU.mult, ALU.add, ["ps%d" % pA, ("nab", hb)], [("nasb", b)])
                                if nw == 5:
                                    STT(sbt[b][:, 512:640], PS[pB][:, 0:128], sc, bias[hb][:, 4, :], ALU.mult, ALU.add,
                                        ["ps%d" % pB, ("nab", hb)], [("nasb", b)])
                                ACT(E[b][:, 0:nw * 128], sbt[b][:, 0:nw * 128], AF.Exp, [("nasb", b)], [("naE", b)])
                                ACT(E[b][:, 640:896], PS[pB][:, 128:384], AF.Exp, ["ps%d" % pB], [("naE", b)], scale=sc)
                                ecols = [(kc, j * 128) for j, (kc, bi) in enumerate(chunks)] + [(32, 640), (33, 768)]
                                for ci, (kc, ec) in enumerate(ecols):
                                    MM(PS[0][0:65, rr_ * 128:(rr_ + 1) * 128], va[:, kc, h, :], E[b][:, ec:ec + 128], ci == 0, ci == len(ecols) - 1,
                                       ["nava", ("naE", b)], ["ps0"])
                            attn_norm((osb, rrow, o1), 0, 512, "o1n")
                            yb_ = ny % 2
                            ny += 1
                            CP(yo[yb_][0:64, :], o1[0:64, :], ["o1n"], [("nay", yb_)], eng="pool")
                            DMA(ybuf[256 + h * 64:256 + (h + 1) * 64, rg * 512:(rg + 1) * 512], yo[yb_][0:64, :], [("nay", yb_)], [("ybuf", "na")])
                        if with_ctx:
                            q_ap = qn[r0:r0 + 64, blk, L:L + CL]
                            for j2 in range(2):
                                MM(PS[1][:, j2 * 256:(j2 + 1) * 256], kn[r0:r0 + 64, blk, L + j2 * 128:L + (j2 + 1) * 128], q_ap, True, True,
                                   [("nak", blk), ("naq", blk)], ["ps1"])
                            ACT(E[0][:, 0:512], PS[1][:, 0:512], AF.Exp, ["ps1"], [("naE", 0)], scale=sc)
                            for j2 in range(2):
                                MM(PS[0][0:65, 0:256], va[:, 32 + j2, h, :], E[0][:, j2 * 256:(j2 + 1) * 256], j2 == 0, j2 == 1,
                                   ["nava", ("naE", 0)], ["ps0"])
                            attn_norm((osb, rrow, o1), 0, 256, "o1n")
                            yb_ = ny % 2
                            ny += 1
                            CP(yo[yb_][0:64, 0:256], o1[0:64, 0:256], ["o1n"], [("nay", yb_)], eng="pool")
                            DMA(ybuf[256 + h * 64:256 + (h + 1) * 64, L:L + CL], yo[yb_][0:64, 0:256], [("nay", yb_)], [("ybuf", "na")])
                    S.flush()

            with contextlib.ExitStack() as ph:
                wg = sbuf(ph, "wg", [128, 8, 4096], BF16)
                wbr = sbuf(ph, "wbr", [128, 8, D], BF16)
                wo = sbuf(ph, "wo", [128, 8, D], BF16)
                xt = [sbuf(ph, "xt%d" % i, [128, 8, 512], F32) for i in range(2)]
                ht = [sbuf(ph, "ht%d" % i, [128, 8, 512], BF16) for i in range(2)]
                yt = [sbuf(ph, "yt%d" % i, [128, 8, 512], BF16) for i in range(2)]
                mg = sbuf(ph, "mg", [128, 8, 512], BF16)
                sig = [sbuf(ph, "sig%d" % i, [128, 512], F32) for i in range(2)]
                acc = sbuf(ph, "acc", [128, 512], F32)
                tmp = sbuf(ph, "tmp", [128, 512], F32)
                for k in range(8):
                    DMA(wg[:, k, :], wi_b[li][k * 128:(k + 1) * 128, NMIX:NIN], [("wi_b", li)], [("wg", k)])
                    DMA(wbr[:, k, :], wb_b[li][k * 128:(k + 1) * 128, :], [("wb_b", li)], [("wbr", k)])
                    DMA(wo[:, k, :], wo_b[li][k * 128:(k + 1) * 128, :], [("wo_b", li)], [("wo", k)])
                it = 0
                ng = 0
                for (sname, s0, slen) in SEQS:
                    s = 0 if sname == "lat" else 1
                    if s == 1 and not with_ctx:
                        continue
                    for t0 in range(s0, s0 + slen, 512):
                        W = min(512, s0 + slen - t0)
                        b = it % 2
                        it += 1
                        if li == layer_list[0]:
                            src = xT_in[:, t0:t0 + W] if s == 0 else cT_in[:, t0 - L:t0 - L + W]
                        else:
                            src = xbuf[:, t0:t0 + W]
                        DMA(xt[b][:, :, 0:W], src.rearrange("(k p) t -> p k t", p=128), ["xbuf"], [("xt", b)])
                        DMA(ht[b][:, :, 0:W], hbuf[:, t0:t0 + W].rearrange("(k p) t -> p k t", p=128), ["hbuf"], [("ht", b)])
                        DMA(yt[b][:, :, 0:W], ybuf[:, t0:t0 + W].rearrange("(k p) t -> p k t", p=128), ["ybuf"], [("yt", b)])
                        for dc in range(8):
                            for g in range(4):
                                pa = 2 * (ng % 2)
                                pbk = pa + 1
                                sgi = ng % 2
                                ng += 1
                                cg = g * D + dc * 128
                                for k in range(8):
                                    MM(PS[pa][:, 0:W], wg[:, k, cg:cg + 128], ht[b][:, k, 0:W], k == 0, k == 7,
                                       [("wg", k), ("ht", b)], ["ps%d" % pa])
                                for k2 in range(2):
                                    MM(PS[pbk][:, 0:W], wbr[:, 2 * g + k2, dc * 128:(dc + 1) * 128], yt[b][:, 2 * g + k2, 0:W],
                                       k2 == 0, k2 == 1, [("wbr", 2 * g + k2), ("yt", b)], ["ps%d" % pbk])
                                ACT(sig[sgi][:, 0:W], PS[pa][:, 0:W], AF.Sigmoid, ["ps%d" % pa, "vecs"], [("sig", sgi)],
                                    bias=V(("bgate", li), g * 8 + dc))
                                if g == 0:
                                    TT(acc[:, 0:W], sig[sgi][:, 0:W], PS[pbk][:, 0:W], ALU.mult, [("sig", sgi), "ps%d" % pbk], ["acc"])
                                else:
                                    TT(tmp[:, 0:W], sig[sgi][:, 0:W], PS[pbk][:, 0:W], ALU.mult, [("sig", sgi), "ps%d" % pbk], ["tmp"])
                                    if g < 3:
                                        TT(acc[:, 0:W], acc[:, 0:W], tmp[:, 0:W], ALU.add, ["acc", "tmp"], ["acc"], eng="pool")
                                    else:
                                        TT(mg[:, dc, 0:W], acc[:, 0:W], tmp[:, 0:W], ALU.add, ["acc", "tmp"], [("mg", dc)], eng="pool")
                        for dc in range(8):
                            pb = 4 + dc % 2
                            for k in range(8):
                                MM(PS[pb][:, 0:W], wo[:, k, dc * 128:(dc + 1) * 128], mg[:, k, 0:W], k == 0, k == 7,
                                   [("wo", k), ("mg", k)], ["ps%d" % pb])
                            STT(xt[b][:, dc, 0:W], PS[pb][:, 0:W], modv(li, "g1", dc, s), xt[b][:, dc, 0:W], ALU.mult, ALU.add,
                                ["ps%d" % pb, "modsb", ("xt", b)], [("xt", b)])
                        DMA(x1buf[:, t0:t0 + W].rearrange("(k p) t -> p k t", p=128), xt[b][:, :, 0:W], [("xt", b)], ["x1buf"])
                S.flush()

            with contextlib.ExitStack() as ph:
                FT = 456
                xt = [sbuf(ph, "xt%d" % i, [128, 8, 512], F32) for i in range(2)]
                ht = sbuf(ph, "ht", [128, 8, 512], BF16)
                gt = sbuf(ph, "gt", [128, 22, 512], BF16)
                sq = sbuf(ph, "sq", [128, 512], BF16)
                rr = sbuf(ph, "rr", [128, 512], F32)
                ff = sbuf(ph, "ff", [128, 512], F32)
                ust = [sbuf(ph, "ust%d" % i, [128, 516], F32) for i in range(2)]
                ca = sbuf(ph, "ca", [128, 512], F32)
                cb_ = sbuf(ph, "cb", [128, 512], F32)
                w1 = [sbuf(ph, "w1_%d" % i, [128, 8, 256], BF16) for i in range(3)]
                w2f = sbuf(ph, "w2f", [128, 22, D], BF16)
                for j in range(22):
                    DMA(w2f[:, j, :], f2_b[li][j * 128:(j + 1) * 128, :], [("f2_b", li)], [("w2f", j)])
                it = 0
                nw = 0
                for (sname, s0, slen) in SEQS:
                    s = 0 if sname == "lat" else 1
                    if s == 1 and not with_ctx:
                        continue
                    for t0 in range(s0, s0 + slen, FT):
                        t1 = min(t0 + FT, s0 + slen)
                        a0 = max(t0 - 1, s0)
                        a1 = min(t1 + 1, s0 + slen)
                        W = a1 - a0
                        WI = t1 - t0
                        io = t0 - a0
                        b = it % 2
                        it += 1
                        DMA(xt[b][:, :, 0:W], x1buf[:, a0:a1].rearrange("(k p) t -> p k t", p=128), ["x1buf"], [("xt", b)])
                        norm_tile(xt[b], ht, W, li, 1, s, sq, rr, ff, 0, ("xt", b), "ht", "n2")
                        for j in range(22):
                            wb = nw % 3
                            nw += 1
                            DMA(w1[wb][:, :, 0:128], f1_b[li][:, j * 128:(j + 1) * 128].rearrange("(k p) c -> p k c", p=128),
                                [("f1_b", li)], [("w1", wb)])
                            DMA(w1[wb][:, :, 128:256], f1_b[li][:, DFF + j * 128:DFF + (j + 1) * 128].rearrange("(k p) c -> p k c", p=128),
                                [("f1_b", li)], [("w1", wb)])
                            pu = 1 + 2 * (j % 2)
                            pv = pu + 1
                            ub = j % 2
                            for k in range(8):
                                MM(PS[pu][:, 0:W], w1[wb][:, k, 0:128], ht[:, k, 0:W], k == 0, k == 7, [("w1", wb), "ht"], ["ps%d" % pu])
                            for k in range(8):
                                MM(PS[pv][:, 0:W], w1[wb][:, k, 128:256], ht[:, k, 0:W], k == 0, k == 7, [("w1", wb), "ht"], ["ps%d" % pv])
                            c_in = 1 - io
                            ACT(ust[ub][:, c_in + 0:c_in + W], PS[pu][:, 0:W], AF.Copy, ["ps%d" % pu], [("ust", ub)])
                            if io == 0:
                                MSET(ust[ub][:, 0:1], 0.0, [("ust", ub)])
                            if a1 == t1:
                                MSET(ust[ub][:, WI + 1:WI + 2], 0.0, [("ust", ub)])
                            fo = voff[("fcw", li)]
                            TS(ca[:, 0:WI], ust[ub][:, 1:1 + WI], vecs[:, fo + 22 + j:fo + 23 + j], V(("fcb", li), j), ALU.mult, ALU.add,
                               [("ust", ub), "vecs"], ["ca"])
                            STT(cb_[:, 0:WI], ust[ub][:, 0:WI], vecs[:, fo + j:fo + j + 1], ca[:, 0:WI], ALU.mult, ALU.add,
                                [("ust", ub), "vecs", "ca"], ["cb"])
                            STT(ca[:, 0:WI], ust[ub][:, 2:2 + WI], vecs[:, fo + 44 + j:fo + 45 + j], cb_[:, 0:WI], ALU.mult, ALU.add,
                                [("ust", ub), "vecs", "cb"], ["ca"])
                            ACT(cb_[:, 0:WI], ca[:, 0:WI], AF.Silu, ["ca"], ["cb"])
                            TT(gt[:, j, 0:WI], cb_[:, 0:WI], PS[pv][:, io:io + WI], ALU.mult, ["cb", "ps%d" % pv], [("gt", j)])
                        for dc in range(8):
                            pb = 5 + dc % 2
                            for j in range(22):
                                MM(PS[pb][:, 0:WI], w2f[:, j, dc * 128:(dc + 1) * 128], gt[:, j, 0:WI], j == 0, j == 21,
                                   [("w2f", j), ("gt", j)], ["ps%d" % pb])
                            STT(xt[b][:, dc, io:io + WI], PS[pb][:, 0:WI], modv(li, "g2", dc, s), xt[b][:, dc, io:io + WI], ALU.mult, ALU.add,
                                ["ps%d" % pb, "modsb", ("xt", b)], [("xt", b)])
                        dst = outT[:, t0:t1] if (li == DEPTH - 1 and s == 0) else xbuf[:, t0:t1]
                        DMA(dst.rearrange("(k p) t -> p k t", p=128), xt[b][:, :, io:io + WI], [("xt", b)], ["xbuf"])
                        if xd is not None:
                            DMA(xd[li][:, t0:t1].rearrange("(k p) t -> p k t", p=128), xt[b][:, :, io:io + WI], [("xt", b)], ["xd"])
                S.flush()
    return nc


def host_inputs(inp, b):
    m = {}
    m["xT"] = np.ascontiguousarray(np.asarray(inp["x"][b], np.float32).T)
    m["ctxT"] = np.ascontiguousarray(np.asarray(inp["ctx"][b], np.float32).T)
    cs = np.stack([np.asarray(inp["c"][b], np.float32), np.asarray(inp["c_ctx"], np.float32)], axis=-1)
    m["cs"] = np.ascontiguousarray(cs.reshape(8, 128, 2).transpose(1, 0, 2).reshape(128, 16))
    m["vecs"] = pack_vecs(inp)
    m["cmat"] = const_mats()
    m["rope"] = rope_tables()
    m["nab"] = np.stack([na_bias_tables(np.asarray(inp["na_rpb"][li], np.float32)) for li in range(DEPTH)], 0)
    m["scm"] = scan_mats()
    m["gla_w"] = np.ascontiguousarray(np.concatenate([np.asarray(inp["gla_w_a2"], np.float32),
                                                      np.asarray(inp["gla_b_a"], np.float32)[:, :, None, :]], axis=2))
    m["dflam"] = np.ascontiguousarray(np.asarray(inp["df_lambda"], np.float32).reshape(DEPTH, 128))
    for k in ("w_mod", "w_in", "w_out", "ffn_w_in", "ffn_w_out"):
        m[k] = np.ascontiguousarray(np.asarray(inp[k], np.float32))
    m["w_branch"] = np.ascontiguousarray(np.asarray(inp["w_branch"], np.float32).reshape(DEPTH, D, D))
    return m


def kernel(**inp):
    nc = build()
    shared = None
    in_maps = []
    for b in range(8):
        m = host_inputs(inp, b)
        if shared is None:
            shared = {k: m[k] for k in ("vecs", "cmat", "rope", "nab", "dflam", "scm", "gla_w", "w_mod", "w_in", "w_out", "ffn_w_in", "ffn_w_out", "w_branch")}
        else:
            m.update(shared)
        in_maps.append(m)
    res = run_bass_kernel_spmd(nc, in_maps, core_ids=list(range(8)))
    out = np.stack([np.ascontiguousarray(r["outT"].T) for r in res.results], axis=0)
    return out.astype(np.float32)
```

```python
import contextlib
import math
import numpy as np
import ml_dtypes
import concourse.bass as bass
import concourse.mybir as mybir
from concourse.bass_utils import run_bass_kernel_spmd

F32 = mybir.dt.float32
BF16 = mybir.dt.bfloat16
AF = mybir.ActivationFunctionType
ALU = mybir.AluOpType
AX = mybir.AxisListType

D = 1024
L = 4096
CL = 256
T = L + CL
DEPTH = 4
NIN = 7472
NMIX = 3376
DFF = 2816
EPS = 1e-6
NDMA_SEM = 8
SEQS = (("lat", 0, L), ("ctx", L, CL))


class Sched:
    ENGS = ("pe", "act", "dve", "pool", "sp")

    def __init__(self, nc, st):
        self.nc = nc
        self.ops = []
        self.last_w = {}
        self.readers = {}
        self.dma_count = {"sp": 0, "pool": 0}
        self.dma_hist = {"sp": [], "pool": []}
        self.emitted = 0
        self.cnt = {e: 0 for e in self.ENGS}
        self.sems = {}
        for e in self.ENGS:
            self.sems[e] = st.enter_context(nc.semaphore("s_" + e))
        for q in ("sp", "pool"):
            for i in range(NDMA_SEM):
                self.sems[(q, i)] = st.enter_context(nc.semaphore("d_%s%d" % (q, i)))

    def op(self, eng, fn, reads=(), writes=(), dma=False):
        oid = len(self.ops)
        deps = {}
        lo = self.emitted
        for k in reads:
            w = self.last_w.get(k)
            if w is not None and w >= lo:
                deps[w] = 2
        for k in writes:
            w = self.last_w.get(k)
            if w is not None and w >= lo:
                deps[w] = max(deps.get(w, 0), 1)
            for r in self.readers.get(k, ()):
                if r >= lo:
                    deps.setdefault(r, 0)
        for d in list(deps):
            do = self.ops[d]
            if do["eng"] == eng and not do["dma"] and not dma:
                if deps[d] == 0 or (deps[d] == 1 and eng == "pe"):
                    del deps[d]
        o = dict(eng=eng, fn=fn, dma=dma, deps=deps)
        if dma:
            n = self.dma_count[eng]
            self.dma_count[eng] = n + 1
            o["dma_i"] = n
            h = self.dma_hist[eng]
            if n >= NDMA_SEM and h[n - NDMA_SEM] >= lo:
                deps[h[n - NDMA_SEM]] = 2
            h.append(oid)
        self.ops.append(o)
        for k in reads:
            self.readers.setdefault(k, []).append(oid)
        for k in writes:
            self.last_w[k] = oid
            self.readers[k] = []
        return oid

    def flush(self):
        nc = self.nc
        allops = self.ops
        lo = self.emitted
        ops = allops[lo:]
        self.emitted = len(allops)
        if not ops:
            return
        for o in ops:
            o["sig"] = o["dma"]
        for o in ops:
            for d in o["deps"]:
                if not allops[d]["dma"]:
                    allops[d]["sig"] = True
        for o in ops:
            if o["dma"]:
                i = o["dma_i"]
                o["semkey"] = (o["eng"], i % NDMA_SEM)
                o["semval"] = 16 * (i // NDMA_SEM + 1)
            elif o["sig"]:
                self.cnt[o["eng"]] += 1
                o["semkey"] = o["eng"]
                o["semval"] = self.cnt[o["eng"]]
        known = {e: {} for e in self.ENGS}
        for o in ops:
            kn = known[o["eng"]]
            waits = []
            for d in sorted(o["deps"]):
                do = allops[d]
                sk, sv = do["semkey"], do["semval"]
                if kn.get(sk, 0) >= sv:
                    continue
                waits.append((sk, sv))
                kn[sk] = sv
                for k2, v2 in do["clock"].items():
                    if kn.get(k2, 0) < v2:
                        kn[k2] = v2
            o["waits"] = waits
            o["clock"] = dict(kn)
            if "semkey" in o and not o["dma"]:
                o["clock"][o["semkey"]] = o["semval"]
        sems = self.sems
        dma_count = dict(self.dma_count)

        def replay(ename):
            def body(eng):
                for o in ops:
                    if o["eng"] != ename:
                        continue
                    for sk, sv in o["waits"]:
                        eng.wait_ge(sems[sk], sv)
                    ins = o["fn"](eng)
                    if o["dma"]:
                        ins.then_inc(sems[o["semkey"]], 16)
                    elif o["sig"]:
                        ins.then_inc(sems[o["semkey"]], 1)
                if ename in ("sp", "pool"):
                    n = dma_count[ename]
                    for i in range(NDMA_SEM):
                        c = (n - i + NDMA_SEM - 1) // NDMA_SEM
                        if c > 0:
                            eng.wait_ge(sems[(ename, i)], 16 * c)
            return body

        with nc.Block() as block:
            block.tensor(replay("pe"))
            block.scalar(replay("act"))
            block.vector(replay("dve"))
            block.gpsimd(replay("pool"))
            block.sync(replay("sp"))
        for o in ops:
            o["fn"] = None
            o["clock"] = None


def vec_layout():
    off = {}
    n = 0

    def add(name, cols):
        nonlocal n
        off[name] = n
        n += cols

    for li in range(DEPTH):
        add(("bmod", li), 48)
        add(("n1g", li), 8)
        add(("n2g", li), 8)
        add(("bgate", li), 32)
        add(("fcw", li), 66)
        add(("fcb", li), 22)
        for nm in ("dfqn", "dfkn", "dfng", "naqn", "nakn", "glng", "dnng", "dndtb", "dnalog"):
            add((nm, li), 1)
        add(("dncw", li), 18)
    add("m1", 1)
    add("m2", 1)
    add("one", 1)
    add("hm", 4)
    add("dnsgn", 1)
    return off, n


def pmajor(v):
    v = np.asarray(v, np.float32)
    lead = int(np.prod(v.shape[:-1])) if v.ndim > 1 else 1
    n = v.shape[-1] // 128
    return np.ascontiguousarray(v.reshape(lead, n, 128).transpose(2, 0, 1).reshape(128, lead * n))


def pack_vecs(inp):
    off, n = vec_layout()
    V = np.zeros((128, n), np.float32)

    def put(name, arr):
        V[:, off[name]:off[name] + arr.shape[1]] = arr

    for li in range(DEPTH):
        put(("bmod", li), pmajor(inp["b_mod"][li]))
        put(("n1g", li), pmajor(inp["norm1_g"][li]))
        put(("n2g", li), pmajor(inp["norm2_g"][li]))
        put(("bgate", li), pmajor(inp["b_gate"][li]))
        put(("fcw", li), pmajor(inp["ffn_conv_w"][li]))
        put(("fcb", li), pmajor(inp["ffn_conv_b"][li]))
        put(("dfqn", li), np.tile(inp["df_q_norm"][li], 4)[:, None])
        put(("dfkn", li), np.tile(inp["df_k_norm"][li], 4)[:, None])
        put(("dfng", li), np.tile(inp["df_norm_g"][li], 2)[:, None])
        put(("naqn", li), np.tile(inp["na_q_norm"][li], 2)[:, None])
        put(("nakn", li), np.tile(inp["na_k_norm"][li], 2)[:, None])
        put(("glng", li), np.tile(inp["gla_norm_g"][li], 2)[:, None])
        put(("dnng", li), np.tile(inp["dn_norm_g"][li], 2)[:, None])
        z8 = np.zeros(8, np.float32)
        put(("dndtb", li), np.concatenate([z8, np.asarray(inp["dn_dt_bias"][li], np.float32).reshape(8), np.zeros(112, np.float32)])[:, None])
        put(("dnalog", li), np.concatenate([z8, np.asarray(inp["dn_a_log"][li], np.float32).reshape(8), np.zeros(112, np.float32)])[:, None])
        put(("dncw", li), pmajor(inp["dn_conv"][li]))
    p = np.arange(128)
    put("m1", ((p // 32) % 2 == 0).astype(np.float32)[:, None])
    put("m2", ((p // 32) % 2 == 1).astype(np.float32)[:, None])
    put("one", np.ones((128, 1), np.float32))
    put("hm", (p[:, None] // 32 == np.arange(4)[None, :]).astype(np.float32))
    put("dnsgn", np.where(p < 8, -1.0, 1.0).astype(np.float32)[:, None])
    return V


NEG = -30000.0


def const_mats():
    C = np.zeros((4, 128, 128), np.float32)
    p = np.arange(128)
    C[0] = (p[:, None] // 32 == p[None, :] // 32) / 32.0
    C[1] = (p[:, None] // 64 == p[None, :] // 64) / 64.0
    for m in range(128):
        g, d = m // 32, m % 32
        q = d // 8
        if q == 0:
            C[2, g * 32 + d + 8, m] = -1.0
        elif q == 1:
            C[2, g * 32 + d - 8, m] = 1.0
        elif q == 2:
            C[2, g * 32 + d + 8, m] = -1.0
        else:
            C[2, g * 32 + d - 8, m] = 1.0
    C[3] = 1.0
    return np.ascontiguousarray(C.transpose(1, 0, 2).reshape(128, 512))


def rope_tables():
    t = np.arange(L)
    row = (t // 64).astype(np.float32)
    col = (t % 64).astype(np.float32)
    nf = 8
    inv = np.power(np.float32(10000.0), -np.arange(nf, dtype=np.float32) / nf).astype(np.float32)
    ar = row[:, None] * inv
    ac = col[:, None] * inv
    ang = np.concatenate([ar, ar, ac, ac], -1)
    cs = np.stack([np.cos(ang), np.sin(ang)], 0).astype(np.float32)
    return np.ascontiguousarray(np.tile(cs.transpose(0, 2, 1), (1, 4, 1)))


SCM = {}
for _i, _n in enumerate(("I", "U", "NU", "MI", "MS", "MST", "CKD", "CQ", "M01")):
    SCM[_n] = _i
NSCM = len(SCM)


def scan_mats():
    t = np.arange(128)
    same = (t[:, None] // 64) == (t[None, :] // 64)
    out = np.zeros((2, NSCM, 128, 128), np.float32)
    for d in range(2):
        before = (t[:, None] <= t[None, :]) if d == 0 else (t[:, None] >= t[None, :])
        strict = (t[:, None] < t[None, :]) if d == 0 else (t[:, None] > t[None, :])
        U = (same & before).astype(np.float32)
        out[d, SCM["I"]] = np.eye(128, dtype=np.float32)
        out[d, SCM["U"]] = U
        out[d, SCM["NU"]] = -U
        out[d, SCM["MI"]] = np.where(same & before, 0.0, NEG)
        out[d, SCM["MS"]] = np.where(same & strict, 0.0, NEG)
        out[d, SCM["MST"]] = np.where(same & strict, 0.0, NEG).T
        out[d, SCM["CKD"]] = (same & strict.T).astype(np.float32)
        pos = t % 64
        midpos = 31 if d == 0 else 32
        umid = (same & ((pos[:, None] <= midpos) if d == 0 else (pos[:, None] >= midpos))).astype(np.float32)
        out[d, SCM["CQ"]] = U - umid
        out[d, SCM["M01"]] = (same & before).astype(np.float32)
    return np.ascontiguousarray(out.transpose(2, 0, 1, 3).reshape(128, 2 * NSCM * 128))


def na_chunks(rp):
    if rp in (0, 1):
        return [(kc, 5 + rp * 4 + kc) for kc in range(4)]
    if rp in (30, 31):
        return [(28 + j, 13 + (rp - 30) * 4 + j) for j in range(4)]
    return [(rp - 2 + j, j) for j in range(5)]


def na_bias_tables(rpb):
    out = np.full((4, 128, 21, 128), NEG, np.float32)
    kk = np.arange(128)
    for rp in [2, 0, 1, 30, 31]:
        for (kc, bi) in na_chunks(rp):
            qrow = 2 * rp + kk // 64
            qcol = kk % 64
            krow = 2 * kc + kk // 64
            kcol = kk % 64
            rs = np.clip(qrow - 4, 0, 56)
            cst = np.clip(qcol - 8, 0, 48)
            dr = krow[:, None] - qrow[None, :]
            dc = kcol[:, None] - qcol[None, :]
            valid = ((krow[:, None] >= rs[None, :]) & (krow[:, None] < rs[None, :] + 8)
                     & (kcol[:, None] >= cst[None, :]) & (kcol[:, None] < cst[None, :] + 16))
            ri = np.clip(dr + 7, 0, 14)
            ci = np.clip(dc + 15, 0, 30)
            for h in range(4):
                g = rpb[h][ri, ci]
                out[h, :, bi, :] = np.where(valid, g, np.float32(NEG))
    return out


PF_BLOCKS = ([(i * 128, 128) for i in range(8)] + [(1024, 16)] + [(1040 + i * 128, 128) for i in range(6)]
             + [(1808, 128), (1936, 128), (2064, 128), (2192, 128), (2320, 128), (2448, 128), (2576, 16), (2592, 16)]
             + [(2608 + i * 128, 128) for i in range(6)])
NPF = len(PF_BLOCKS)
PT_GROUPS = ((1552, 256, 0), (3120, 256, 256), (1936, 384, 512))
NPT = 896


def build(debug=None, nlayers=DEPTH):
    debug = debug or {}
    nc = bass.Bass("TRN2", target_bir_lowering=False)
    voff, NV = vec_layout()

    def din(name, shape, dt=F32):
        return nc.dram_tensor(name, list(shape), dt, kind="ExternalInput").ap()

    def dscr(name, shape, dt):
        kind = "ExternalOutput" if name in debug.get("dump", ()) else "Internal"
        return nc.dram_tensor(name, list(shape), dt, kind=kind).ap()

    xT_in = din("xT", [D, L])
    cT_in = din("ctxT", [D, CL])
    cs_in = din("cs", [128, 16])
    vecs_in = din("vecs", [128, NV])
    w_mod = din("w_mod", [DEPTH, D, 6 * D])
    w_in = din("w_in", [DEPTH, D, NIN])
    w_branch = din("w_branch", [DEPTH, D, D])
    w_out = din("w_out", [DEPTH, D, D])
    f_w_in = din("ffn_w_in", [DEPTH, D, 2 * DFF])
    f_w_out = din("ffn_w_out", [DEPTH, DFF, D])
    outT = nc.dram_tensor("outT", [D, L], F32, kind="ExternalOutput").ap()
    y_dbg = din("y_dbg", [D, T], BF16) if debug.get("y_in") else None
    mixers = debug.get("mixers", ("dn", "na", "gla", "df"))
    cmat_in = din("cmat", [128, 512])
    rope_in = din("rope", [2, 128, L])
    nab_in = din("nab", [DEPTH, 4, 128, 21, 128])
    dflam_in = din("dflam", [DEPTH, 128])
    scm_in = din("scm", [128, 2 * NSCM * 128])
    gla_w_in = din("gla_w", [DEPTH, 2, 17, 128])

    wi_b = dscr("wi_b", [DEPTH, D, NIN], BF16)
    wb_b = dscr("wb_b", [DEPTH, D, D], BF16)
    wo_b = dscr("wo_b", [DEPTH, D, D], BF16)
    f1_b = dscr("f1_b", [DEPTH, D, 2 * DFF], BF16)
    f2_b = dscr("f2_b", [DEPTH, DFF, D], BF16)
    xbuf = dscr("xbuf", [D, T], F32)
    x1buf = dscr("x1buf", [D, T], F32)
    hbuf = dscr("hbuf", [D, T], BF16)
    ybuf = dscr("ybuf", [D, T], BF16)
    PF = dscr("PF", [NPF * 128, T], F32)
    PT = dscr("PT", [T, NPT], F32)
    xd = nc.dram_tensor("xd", [DEPTH, D, T], F32, kind="ExternalOutput").ap() if debug.get("xdump") else None

    with contextlib.ExitStack() as top:
        S = Sched(nc, top)

        uid = [0]

        def sbuf(st, name, shape, dt):
            uid[0] += 1
            return st.enter_context(nc.sbuf_tensor("sb%d_%s" % (uid[0], name), list(shape), dt))

        PS = [top.enter_context(nc.psum_tensor("ps%d" % i, [128, 512], F32)) for i in range(8)]

        def MM(out, lhsT, rhs, st, sp, r, w):
            S.op("pe", lambda e: e.matmul(out, lhsT=lhsT, rhs=rhs, start=st, stop=sp), reads=r, writes=w)

        def ACT(out, in_, func, r, w, bias=0.0, scale=1.0):
            S.op("act", lambda e: e.activation(out=out, in_=in_, func=func, bias=bias, scale=scale), reads=r, writes=w)

        def TT(out, in0, in1, op, r, w, eng="dve"):
            S.op(eng, lambda e: e.tensor_tensor(out=out, in0=in0, in1=in1, op=op), reads=r, writes=w)

        def TS(out, in0, s1, s2, op0, op1, r, w, eng="dve"):
            S.op(eng, lambda e: e.tensor_scalar(out=out, in0=in0, scalar1=s1, scalar2=s2, op0=op0, op1=op1), reads=r, writes=w)

        def STT(out, in0, scalar, in1, op0, op1, r, w, eng="dve"):
            S.op(eng, lambda e: e.scalar_tensor_tensor(out=out, in0=in0, scalar=scalar, in1=in1, op0=op0, op1=op1), reads=r, writes=w)

        def CP(out, in_, r, w, eng="dve"):
            S.op(eng, lambda e: e.tensor_copy(out=out, in_=in_), reads=r, writes=w)

        def MSET(ap, val, w, eng="pool"):
            S.op(eng, lambda e: e.memset(ap, val), writes=w)

        def DMA(out, in_, r, w, q="sp"):
            S.op(q, lambda e: e.dma_start(out=out, in_=in_), reads=r, writes=w, dma=True)

        vecs = sbuf(top, "vecs", [128, NV], F32)
        modsb = sbuf(top, "modsb", [128, DEPTH, 48, 2], F32)
        gsc = sbuf(top, "gsc", [128, DEPTH, 2, 8, 2], F32)
        ones_b = sbuf(top, "ones_b", [128, 128], BF16)
        DMA(vecs[:], vecs_in[:], [], ["vecs"])
        MSET(ones_b[:], 1.0, ["ones_b"])

        cmat = sbuf(top, "cmat", [128, 512], F32)
        DMA(cmat[:], cmat_in[:], [], ["cmat"])
        BD32 = cmat[:, 0:128]
        BD64 = cmat[:, 128:256]
        RM = cmat[:, 256:384]
        ONESF = cmat[:, 384:512]

        scm = sbuf(top, "scm", [128, 2, NSCM, 128], F32)
        DMA(scm[:].rearrange("p a b c -> p (a b c)"), scm_in[:], [], ["scm"])

        def CM(d, name):
            return scm[:, d, SCM[name], :]

        def V(name, j=0, n=1):
            o = voff[name] + j
            return vecs[:, o:o + n]

        def cast2d(dst, src, rows, cols, key):
            for r0 in range(0, rows, 1024):
                r1 = min(rows, r0 + 1024)
                for c0 in range(0, cols, 2048):
                    c1 = min(cols, c0 + 2048)
                    DMA(dst[r0:r1, c0:c1], src[r0:r1, c0:c1], [], [key], q="pool")

        layer_list = list(debug.get("layers", range(nlayers)))

        def cast_layer(li):
            cast2d(wi_b[li], w_in[li], D, NIN, ("wi_b", li))
            cast2d(wb_b[li], w_branch[li], D, D, ("wb_b", li))
            cast2d(wo_b[li], w_out[li], D, D, ("wo_b", li))
            cast2d(f1_b[li], f_w_in[li], D, 2 * DFF, ("f1_b", li))
            cast2d(f2_b[li], f_w_out[li], DFF, D, ("f2_b", li))

        cast_layer(layer_list[0])

        with contextlib.ExitStack() as ph:
            scs = sbuf(ph, "scs", [128, 16], F32)
            wm = [sbuf(ph, "wm%d" % i, [128, 8, 768], F32) for i in range(2)]
            DMA(scs[:], cs_in[:], [], ["scs"])
            ACT(scs[:], scs[:], AF.Silu, ["scs"], ["scs"])
            nb = 0
            for li in layer_list:
                wv = w_mod[li].rearrange("(k p) c -> p k c", p=128)
                for cb in range(8):
                    wt = wm[nb % 2]
                    wk = ("wm", nb % 2)
                    nb += 1
                    DMA(wt[:], wv[:, :, cb * 768:(cb + 1) * 768], [], [wk])
                    for jj in range(6):
                        j = cb * 6 + jj
                        for k in range(8):
                            MM(PS[0][:, 2 * j:2 * j + 2], wt[:, k, jj * 128:(jj + 1) * 128], scs[:, 2 * k:2 * k + 2],
                               k == 0, k == 7, [wk, "scs"], ["ps0"])
                TT(modsb[:, li], PS[0][:, 0:96].rearrange("p (j s) -> p j s", s=2),
                   V(("bmod", li), 0, 48).rearrange("p (j o) -> p j o", o=1).to_broadcast([128, 48, 2]), ALU.add,
                   ["ps0", "vecs"], ["modsb"])
                for which, (so, gname) in enumerate(((8, "n1g"), (32, "n2g"))):
                    TS(gsc[:, li, which], modsb[:, li, so:so + 8, :], 1.0, 0.0, ALU.add, ALU.add, ["modsb"], ["gsc"])
                    TT(gsc[:, li, which], gsc[:, li, which],
                       V((gname, li), 0, 8).rearrange("p (j o) -> p j o", o=1).to_broadcast([128, 8, 2]), ALU.mult,
                       ["gsc", "vecs"], ["gsc"])
            S.flush()

        def modv(li, what, k, s):
            base = {"sh1": 0, "sc1": 8, "g1": 16, "sh2": 24, "sc2": 32, "g2": 40}[what]
            return modsb[:, li, base + k, s:s + 1]

        def norm_tile(xt, ht, W, li, which, s, tmp_sq, tmp_r, tmp_f, psb, kx, kh, tag):
            shn = "sh1" if which == 0 else "sh2"
            for k in range(8):
                ACT(tmp_sq[:, 0:W], xt[:, k, 0:W], AF.Square, [kx], [tag + "sq"])
                MM(PS[psb][:, 0:W], ones_b[:], tmp_sq[:, 0:W], k == 0, k == 7, ["ones_b", tag + "sq"], ["ps%d" % psb])
            ACT(tmp_r[:, 0:W], PS[psb][:, 0:W], AF.Ln, ["ps%d" % psb], [tag + "r"], bias=EPS, scale=1.0 / D)
            ACT(tmp_r[:, 0:W], tmp_r[:, 0:W], AF.Exp, [tag + "r"], [tag + "r"], scale=-0.5)
            for k in range(8):
                STT(tmp_f[:, 0:W], xt[:, k, 0:W], gsc[:, li, which, k, s:s + 1], tmp_r[:, 0:W], ALU.mult, ALU.mult,
                    [kx, "gsc", tag + "r"], [tag + "f"])
                ACT(ht[:, k, 0:W], tmp_f[:, 0:W], AF.Identity, [tag + "f", "modsb"], [kh], bias=modv(li, shn, k, s))

        for li in layer_list:
            with_ctx = li < DEPTH - 1
            with contextlib.ExitStack() as ph:
                wi = sbuf(ph, "wi", [128, 8, NMIX], BF16)
                xt = [sbuf(ph, "xt%d" % i, [128, 8, 512], F32) for i in range(2)]
                ht = [sbuf(ph, "ht%d" % i, [128, 8, 512], BF16) for i in range(2)]
                sq = sbuf(ph, "sq", [128, 512], BF16)
                rr = sbuf(ph, "rr", [128, 512], F32)
                ff = sbuf(ph, "ff", [128, 512], F32)
                stg = [sbuf(ph, "stg%d" % i, [128, 512], F32) for i in range(4)]
                for k in range(8):
                    DMA(wi[:, k, :], wi_b[li][k * 128:(k + 1) * 128, 0:NMIX], [("wi_b", li)], [("wi", k)])
                wik = [("wi", k) for k in range(8)]
                nst = 0
                tiles1 = []
                for (sname, s0, slen) in SEQS:
                    for t0 in range(s0, s0 + slen, 512):
                        tiles1.append((0 if sname == "lat" else 1, t0, min(512, s0 + slen - t0)))

                def p1_norm(i):
                    (s, t0, W) = tiles1[i]
                    b = i % 2
                    if li == layer_list[0]:
                        src = xT_in[:, t0:t0 + W] if s == 0 else cT_in[:, t0 - L:t0 - L + W]
                    else:
                        src = xbuf[:, t0:t0 + W]
                    DMA(xt[b][:, :, 0:W], src.rearrange("(k p) t -> p k t", p=128), ["xbuf"], [("xt", b)])
                    norm_tile(xt[b], ht[b], W, li, 0, s, sq, rr, ff, 0, ("xt", b), ("ht", b), "n1")
                    DMA(hbuf[:, t0:t0 + W].rearrange("(k p) t -> p k t", p=128), ht[b][:, :, 0:W], [("ht", b)], ["hbuf"])

                p1_norm(0)
                for ti in range(len(tiles1)):
                    if True:
                        (s, t0, W) = tiles1[ti]
                        b = ti % 2
                        if ti + 1 < len(tiles1):
                            p1_norm(ti + 1)
                        for bi, (c0, ncol) in enumerate(PF_BLOCKS):
                            pb = 1 + (bi % 4)
                            for k in range(8):
                                MM(PS[pb][0:ncol, 0:W], wi[:, k, c0:c0 + ncol], ht[b][:, k, 0:W], k == 0, k == 7,
                                   [("wi", k), ("ht", b)], ["ps%d" % pb])
                            sg = nst % 4
                            nst += 1
                            if bi % 2 == 0:
                                ACT(stg[sg][0:ncol, 0:W], PS[pb][0:ncol, 0:W], AF.Copy, ["ps%d" % pb], [("stg", sg)])
                            else:
                                CP(stg[sg][0:ncol, 0:W], PS[pb][0:ncol, 0:W], ["ps%d" % pb], [("stg", sg)])
                            DMA(PF[bi * 128:bi * 128 + ncol, t0:t0 + W], stg[sg][0:ncol, 0:W], [("stg", sg)], [("PF", bi)])
                        for q in range(W // 128):
                            for gi, (c0, ncol, p0) in enumerate(PT_GROUPS):
                                pb = 5 if gi < 2 else 7
                                pcol = 256 if gi == 1 else 0
                                for k in range(8):
                                    MM(PS[pb][:, pcol:pcol + ncol], ht[b][:, k, q * 128:(q + 1) * 128], wi[:, k, c0:c0 + ncol],
                                       k == 0, k == 7, [("wi", k), ("ht", b)], ["ps%d" % pb])
                            for pb, p0, ncol in ((5, 0, 512), (7, 512, 384)):
                                sg = nst % 4
                                nst += 1
                                CP(stg[sg][:, 0:ncol], PS[pb][:, 0:ncol], ["ps%d" % pb], [("stg", sg)])
                                DMA(PT[t0 + q * 128:t0 + (q + 1) * 128, p0:p0 + ncol], stg[sg][:, 0:ncol], [("stg", sg)], [("PT", p0)])
                S.flush()

            if y_dbg is not None:
                with contextlib.ExitStack() as ph:
                    yt = sbuf(ph, "ycp", [128, 8, 512], BF16)
                    for t0 in range(0, T, 512):
                        W = min(512, T - t0)
                        DMA(yt[:, :, 0:W], y_dbg[:, t0:t0 + W].rearrange("(k p) t -> p k t", p=128), [], ["ycp"])
                        DMA(ybuf[:, t0:t0 + W].rearrange("(k p) t -> p k t", p=128), yt[:, :, 0:W], ["ycp"], ["ybuf"])
                    S.flush()


            BLK = {"dfq": 23, "dfk": 25, "naq": 9, "nak": 11}

            def qk_prep(ph, specs):
                px = [sbuf(ph, "px%d" % i, [128, 512], F32) for i in range(2)]
                psq = sbuf(ph, "psq", [128, 512], F32)
                prs = sbuf(ph, "prs", [128, 512], F32)
                pxn = sbuf(ph, "pxn", [128, 512], F32)
                pa = sbuf(ph, "ppa", [128, 512], F32)
                pb_ = sbuf(ph, "ppb", [128, 512], F32)
                rc = sbuf(ph, "rc", [128, 2, 512], F32)
                n = 0
                for t0 in range(0, T, 512):
                    W = min(512, T - t0)
                    lat = t0 < L
                    if lat and any(sp[3] for sp in specs):
                        DMA(rc[:, 0, :], rope_in[0, :, t0:t0 + 512], [], ["rc"])
                        DMA(rc[:, 1, :], rope_in[1, :, t0:t0 + 512], [], ["rc"])
                    for (blk, gm, gcol, rope, outs) in specs:
                        b = n % 2
                        n += 1
                        DMA(px[b][:, 0:W], PF[blk * 128:(blk + 1) * 128, t0:t0 + W], [], [("px", b)])
                        ACT(psq[:, 0:W], px[b][:, 0:W], AF.Square, [("px", b)], ["psq"])
                        MM(PS[0][:, 0:W], gm, psq[:, 0:W], True, True, ["cmat", "psq"], ["ps0"])
                        ACT(prs[:, 0:W], PS[0][:, 0:W], AF.Ln, ["ps0"], ["prs"], bias=EPS)
                        ACT(prs[:, 0:W], prs[:, 0:W], AF.Exp, ["prs"], ["prs"], scale=-0.5)
                        STT(pxn[:, 0:W], px[b][:, 0:W], gcol, prs[:, 0:W], ALU.mult, ALU.mult, [("px", b), "vecs", "prs"], ["pxn"])
                        val = pxn
                        vk = "pxn"
                        if rope and lat:
                            MM(PS[1][:, 0:W], RM, pxn[:, 0:W], True, True, ["cmat", "pxn"], ["ps1"])
                            TT(pa[:, 0:W], pxn[:, 0:W], rc[:, 0, 0:W], ALU.mult, ["pxn", "rc"], ["ppa"], eng="pool")
                            TT(pb_[:, 0:W], PS[1][:, 0:W], rc[:, 1, 0:W], ALU.mult, ["ps1", "rc"], ["ppb"])
                            TT(pa[:, 0:W], pa[:, 0:W], pb_[:, 0:W], ALU.add, ["ppa", "ppb"], ["ppa"], eng="pool")
                            val = pa
                            vk = "ppa"
                        for (dst, dk, mcol) in outs:
                            TS(dst[:, t0:t0 + W], val[:, 0:W], mcol, 0.0, ALU.mult, ALU.add, [vk, "vecs"], [dk])

            def load_vaug(ph, name, pcol):
                va = sbuf(ph, name, [128, 34, 4, 65], BF16)
                vst = [sbuf(ph, name + "st%d" % i, [128, 256], F32) for i in range(2)]
                MSET(va[:, :, :, 64:65], 1.0, [name])
                for kc in range(34):
                    b = kc % 2
                    DMA(vst[b][:], PT[kc * 128:(kc + 1) * 128, pcol:pcol + 256], [], [(name + "st", b)])
                    CP(va[:, kc, :, 0:64], vst[b][:].rearrange("p (h d) -> p h d", h=4), [(name + "st", b)], [name],
                       eng=("dve" if kc % 2 else "pool"))
                return va

            def attn_norm(ph_tiles, Obank, W, okey):
                osb, rrow, onrm = ph_tiles
                ACT(osb[0:65, 0:W], PS[Obank][0:65, 0:W], AF.Copy, ["ps%d" % Obank], ["osb"])
                S.op("dve", lambda e: e.reciprocal(out=rrow[64:65, 0:W], in_=osb[64:65, 0:W]), reads=["osb"], writes=["rrow"])
                MM(PS[7][0:64, 0:W], ONESF[64:65, 0:64], rrow[64:65, 0:W], True, True, ["cmat", "rrow"], ["ps7"])
                TT(onrm[0:64, 0:W], osb[0:64, 0:W], PS[7][0:64, 0:W], ALU.mult, ["osb", "ps7"], [okey])


            def head_norm_gate(ph, osum, okey, gate_blk, gcol, yrow0, perm=False):
                gx = [sbuf(ph, "hg_x%d" % i, [128, 512], F32) for i in range(2)]
                gs = sbuf(ph, "hg_s", [128, 512], F32)
                gr = sbuf(ph, "hg_r", [128, 512], F32)
                gy = [sbuf(ph, "hg_y%d" % i, [128, 512], BF16) for i in range(2)]
                n = 0
                for t0 in range(0, T, 512):
                    W = min(512, T - t0)
                    for hp in range(2):
                        b = n % 2
                        n += 1
                        if perm:
                            for j in range(2):
                                g0 = (gate_blk + j) * 128 + hp * 64
                                DMA(gx[b][j * 64:(j + 1) * 64, 0:W], PF[g0:g0 + 64, t0:t0 + W], [], [("hgx", b)])
                        else:
                            DMA(gx[b][:, 0:W], PF[(gate_blk + hp) * 128:(gate_blk + hp + 1) * 128, t0:t0 + W], [], [("hgx", b)])
                        ACT(gx[b][:, 0:W], gx[b][:, 0:W], AF.Silu, [("hgx", b)], [("hgx", b)])
                        ACT(gs[:, 0:W], osum[:, hp, t0:t0 + W], AF.Square, [okey], ["hgs"])
                        MM(PS[0][:, 0:W], BD64, gs[:, 0:W], True, True, ["cmat", "hgs"], ["ps0"])
                        ACT(gr[:, 0:W], PS[0][:, 0:W], AF.Ln, ["ps0"], ["hgr"], bias=EPS)
                        ACT(gr[:, 0:W], gr[:, 0:W], AF.Exp, ["hgr"], ["hgr"], scale=-0.5)
                        STT(gs[:, 0:W], osum[:, hp, t0:t0 + W], gcol, gr[:, 0:W], ALU.mult, ALU.mult, [okey, "vecs", "hgr"], ["hgs"])
                        TT(gy[b][:, 0:W], gs[:, 0:W], gx[b][:, 0:W], ALU.mult, ["hgs", ("hgx", b)], [("hgy", b)])
                        if perm:
                            for j in range(2):
                                y0 = yrow0 + (hp + 2 * j) * 64
                                DMA(ybuf[y0:y0 + 64, t0:t0 + W], gy[b][j * 64:(j + 1) * 64, 0:W], [("hgy", b)], [("ybuf", yrow0, j)])
                        else:
                            DMA(ybuf[yrow0 + hp * 128:yrow0 + (hp + 1) * 128, t0:t0 + W], gy[b][:, 0:W], [("hgy", b)], [("ybuf", yrow0)])

            def scan_blocks(d):
                cb = [L + 0, L + 128] if d == 0 else [L + 128, L + 0]
                lb_ = list(range(0, L, 128)) if d == 0 else list(range(L - 128, -1, -128))
                return cb + lb_


            if "dn" in mixers:
                with contextlib.ExitStack() as ph:
                    osum = sbuf(ph, "dno", [128, 2, T], F32)
                    qT = sbuf(ph, "dnq", [128, 2, T], BF16)
                    kT = sbuf(ph, "dnk", [128, 2, T], BF16)
                    vT = sbuf(ph, "dnv", [128, 2, T], BF16)
                    rowsT = sbuf(ph, "dnrows", [16, T], F32)
                    coef = sbuf(ph, "dncoef", [16, 1], F32)
                    I16 = sbuf(ph, "dnI16", [128, 128], BF16)
                    CP(I16[:], CM(0, "I"), ["scm"], ["dnI16"])
                    ACT(coef[:], V(("dnalog", li))[0:16, :], AF.Exp, ["vecs"], ["dncoef"])
                    TS(coef[:], coef[:], -1.0, 0.0, ALU.mult, ALU.add, ["dncoef"], ["dncoef"])
                    DMA(rowsT[:], PF[8 * 128:8 * 128 + 16, :], [], ["dnrows"])
                    for c0 in range(0, T, 1088):
                        sl = rowsT[:, c0:c0 + 1088]
                        ACT(sl, sl, AF.Exp, ["dnrows", "vecs"], ["dnrows"], bias=V(("dndtb", li))[0:16, :], scale=V("dnsgn")[0:16, :])
                        ACT(sl, sl, AF.Ln, ["dnrows"], ["dnrows"], bias=1.0)
                        TS(sl, sl, coef[:, 0:1], 0.0, ALU.mult, ALU.add, ["dnrows", "dncoef"], ["dnrows"])
                    with contextlib.ExitStack() as ph2:
                        xs = [sbuf(ph2, "dnxs%d" % i, [128, 514], F32) for i in range(2)]
                        ca = sbuf(ph2, "dnca", [128, 512], F32)
                        cb2 = sbuf(ph2, "dncb", [128, 512], F32)
                        sq2 = sbuf(ph2, "dnsq", [128, 512], F32)
                        n = 0
                        cw = voff[("dncw", li)]
                        for (sname, s0, slen) in SEQS:
                            for t0 in range(s0, s0 + slen, 512):
                                W = min(512, s0 + slen - t0)
                                a0 = max(t0 - 1, s0)
                                a1_ = min(t0 + W + 1, s0 + slen)
                                for blk in range(6):
                                    b = n % 2
                                    n += 1
                                    off0 = a0 - t0 + 1
                                    DMA(xs[b][:, off0:off0 + (a1_ - a0)], PF[blk * 128:(blk + 1) * 128, a0:a1_], [], [("dnxs", b)])
                                    if t0 == s0:
                                        MSET(xs[b][:, 0:1], 0.0, [("dnxs", b)])
                                    if t0 + W == s0 + slen:
                                        MSET(xs[b][:, W + 1:W + 2], 0.0, [("dnxs", b)])
                                    TS(ca[:, 0:W], xs[b][:, 1:1 + W], vecs[:, cw + 6 + blk:cw + 7 + blk], 0.0, ALU.mult, ALU.add, [("dnxs", b), "vecs"], ["dnca"])
                                    STT(cb2[:, 0:W], xs[b][:, 0:W], vecs[:, cw + blk:cw + blk + 1], ca[:, 0:W], ALU.mult, ALU.add, [("dnxs", b), "vecs", "dnca"], ["dncb"])
                                    STT(ca[:, 0:W], xs[b][:, 2:2 + W], vecs[:, cw + 12 + blk:cw + 13 + blk], cb2[:, 0:W], ALU.mult, ALU.add, [("dnxs", b), "vecs", "dncb"], ["dnca"])
                                    ACT(cb2[:, 0:W], ca[:, 0:W], AF.Silu, ["dnca"], ["dncb"])
                                    if blk >= 4:
                                        CP(vT[:, blk - 4, t0:t0 + W], cb2[:, 0:W], ["dncb"], [("dnv", blk - 4)], eng="pool")
                                        continue
                                    ACT(sq2[:, 0:W], cb2[:, 0:W], AF.Square, ["dncb"], ["dnsq"])
                                    MM(PS[0][:, 0:W], BD64, sq2[:, 0:W], True, True, ["cmat", "dnsq"], ["ps0"])
                                    ACT(sq2[:, 0:W], PS[0][:, 0:W], AF.Ln, ["ps0"], ["dnsq"], bias=EPS / 64)
                                    ACT(sq2[:, 0:W], sq2[:, 0:W], AF.Exp, ["dnsq"], ["dnsq"], scale=-0.5)
                                    if blk < 2:
                                        STT(qT[:, blk, t0:t0 + W], cb2[:, 0:W], 1.0 / 64, sq2[:, 0:W], ALU.mult, ALU.mult, ["dncb", "dnsq"], [("dnq", blk)])
                                    else:
                                        STT(kT[:, blk - 2, t0:t0 + W], cb2[:, 0:W], 1.0 / 8, sq2[:, 0:W], ALU.mult, ALU.mult, ["dncb", "dnsq"], [("dnk", blk - 2)])
                        S.flush()
                    nxt = layer_list.index(li) + 1
                    if nxt < len(layer_list):
                        cast_layer(layer_list[nxt])
                    rt = sbuf(ph, "dnrt", [128, 16], F32)
                    gB4 = sbuf(ph, "dngB", [128, 4, 128], F32)
                    lB4 = sbuf(ph, "dnlB", [128, 4, 128], F32)
                    ekd = sbuf(ph, "dnekd", [128, 16], F32)
                    ebt = sbuf(ph, "dnebt", [128, 8], F32)
                    kv = sbuf(ph, "dnkv", [128, 512], F32)
                    E5 = sbuf(ph, "dnE5", [128, 5, 4, 128], F32)
                    Pb = [sbuf(ph, "dnP%d" % i, [128, 4, 128], F32) for i in range(2)]
                    Qb = [sbuf(ph, "dnQ%d" % i, [128, 4, 128], F32) for i in range(2)]
                    X = sbuf(ph, "dnX", [128, 4, 128], F32)
                    aqk = sbuf(ph, "dnaqk", [128, 4, 128], BF16)
                    kbe = sbuf(ph, "dnkbe", [128, 2, 128], BF16)
                    qd = sbuf(ph, "dnqd", [128, 2, 128], BF16)
                    vb = sbuf(ph, "dnvb", [128, 4, 64], F32)
                    kdc = sbuf(ph, "dnkdc", [128, 4, 64], BF16)
                    vnZ = [sbuf(ph, "dnvn%d" % i, [128, 4, 64], BF16) for i in range(2)]
                    for i in range(2):
                        MSET(vnZ[i][:], 0.0, [("dnvn", i)], eng="dve")
                    rF = [sbuf(ph, "dnrF%d" % i, [128, 4, 64], F32) for i in range(2)]
                    Sf = sbuf(ph, "dnS", [128, 4, 64], F32)
                    S16 = sbuf(ph, "dnS16", [128, 4, 64], BF16)
                    for i in range(2):
                        MSET(rF[i][:], 0.0, [("dnrF", i)], eng="dve")
                    Ibc = CM(0, "I").rearrange("p (o c) -> p o c", o=1).to_broadcast([128, 4, 128])

                    def flat(ap):
                        return ap.rearrange("p h c -> p (h c)")

                    for d in range(2):
                        MSET(Sf[:], 0.0, ["dnS"], eng="dve")
                        MSET(S16[:], 0.0, ["dnS16"], eng="dve")
                        for t0 in scan_blocks(d)[:debug.get("dn_nblk", 100)]:
                            MM(PS[0][:, 0:16], rowsT[:, t0:t0 + 128], CM(0, "I")[0:16, 0:16], True, True, ["dnrows", "scm"], ["ps0"])
                            CP(rt[:], PS[0][:, 0:16], ["ps0"], ["dnrt"])
                            MM(PS[0][:, 16:32], CM(d, "CKD"), rt[:], True, True, ["scm", "dnrt"], ["ps0"])
                            ACT(ekd[:], PS[0][:, 16:32], AF.Exp, ["ps0"], ["dnekd"])
                            ACT(ebt[:], rt[:, 0:8], AF.Exp, ["dnrt"], ["dnebt"])
                            for i2 in range(2):
                                MM(PS[1][:, i2 * 128:(i2 + 1) * 128], kT[:, i2, t0:t0 + 128], I16[:], True, True, [("dnk", i2), "dnI16"], ["ps1"])
                                MM(PS[1][:, 256 + i2 * 128:256 + (i2 + 1) * 128], vT[:, i2, t0:t0 + 128], I16[:], True, True, [("dnv", i2), "dnI16"], ["ps1"])
                            CP(kv[:], PS[1][:], ["ps1"], ["dnkv"])
                            HP = [0, 2, 1, 3]
                            for par in range(2):
                                gsrc = rt[:, 8 + d * 4:12 + d * 4].rearrange("p (j q) -> p q j", q=2)[:, par, :]
                                lsrc = rt[:, d * 4:d * 4 + 4].rearrange("p (j q) -> p q j", q=2)[:, par, :]
                                CP(gB4[:, 2 * par:2 * par + 2, :], gsrc.rearrange("p (j o) -> p j o", o=1).to_broadcast([128, 2, 128]), ["dnrt"], ["dngB"])
                                CP(lB4[:, 2 * par:2 * par + 2, :], lsrc.rearrange("p (j o) -> p j o", o=1).to_broadcast([128, 2, 128]), ["dnrt"], ["dnlB"])
                            kr = ["dngB", "dnlB", "scm"]
                            order = (0, 1) if d == 0 else (1, 0)
                            for ty in range(5):
                                pb = 2 + ty % 2
                                pk = "ps%d" % pb
                                for pos in range(4):
                                    o_ = PS[pb][:, pos * 128:(pos + 1) * 128]
                                    gB = gB4[:, pos, :]
                                    lB = lB4[:, pos, :]
                                    if ty == 0:
                                        seq = [(gB, CM(d, "U")), (CM(d, "NU"), gB), (CM(d, "I"), CM(d, "MI"))]
                                    elif ty == 1:
                                        seq = [(gB, CM(d, "U")), (CM(d, "NU"), gB), (lB, CM(d, "I")), (CM(d, "I"), CM(d, "MS"))]
                                    elif ty == 2:
                                        seq = [(CM(d, "U"), gB), (gB, CM(d, "NU")), (CM(d, "I"), lB), (CM(d, "I"), CM(d, "MST"))]
                                    elif ty == 3:
                                        seq = [(gB, CM(d, "U"))]
                                    else:
                                        seq = [(gB, CM(d, "U")), (lB, CM(d, "I"))]
                                    for si, (la_, ra_) in enumerate(seq):
                                        MM(o_, la_, ra_, si == 0, si == len(seq) - 1, kr, [pk])
                                ACT(flat(E5[:, ty]), PS[pb][:], AF.Exp, [pk], [("dnE5", ty)])
                            for par in range(2):
                                r0 = par * 64
                                for j in range(2):
                                    kTh = kT[r0:r0 + 64, j, t0:t0 + 128]
                                    MM(PS[par][:, j * 128:(j + 1) * 128], kTh, kTh, True, True, [("dnk", j)], ["ps%d" % par])
                                    MM(PS[par][:, 256 + j * 128:256 + (j + 1) * 128], kTh, qT[r0:r0 + 64, j, t0:t0 + 128], True, True,
                                       [("dnk", j), ("dnq", j)], ["ps%d" % par])
                            for par in range(2):
                                sl = slice(2 * par, 2 * par + 2)
                                pk = "ps%d" % par
                                STT(flat(Qb[0][:, sl, :]), PS[par][:, 0:256], -1.0, flat(E5[:, 1, sl, :]), ALU.mult, ALU.mult, [pk, ("dnE5", 1)], [("dnQ", 0)])
                                STT(flat(Pb[0][:, sl, :]), PS[par][:, 0:256], -1.0, flat(E5[:, 2, sl, :]), ALU.mult, ALU.mult, [pk, ("dnE5", 2)], [("dnP", 0)])
                                TT(flat(aqk[:, sl, :]), PS[par][:, 256:512], flat(E5[:, 0, sl, :]), ALU.mult, [pk, ("dnE5", 0)], ["dnaqk"])
                            TT(X[:], Qb[0][:], Ibc, ALU.add, [("dnQ", 0), "scm"], ["dnX"])
                            for lvl in range(5):
                                a_, bn = lvl % 2, (lvl + 1) % 2
                                for pos in range(4):
                                    MM(PS[4][:, pos * 128:(pos + 1) * 128], Qb[a_][:, pos, :], Pb[a_][:, pos, :], True, True, [("dnQ", a_), ("dnP", a_)], ["ps4"])
                                ACT(flat(Pb[bn][:]), PS[4][:], AF.Copy, ["ps4"], [("dnP", bn)])
                                if lvl < 4:
                                    for pos in range(4):
                                        MM(PS[0][:, pos * 128:(pos + 1) * 128], Pb[a_][:, pos, :], Qb[a_][:, pos, :], True, True, [("dnQ", a_), ("dnP", a_)], ["ps0"])
                                    CP(flat(Qb[bn][:]), PS[0][:], ["ps0"], [("dnQ", bn)])
                                for pos in range(4):
                                    MM(PS[1][:, pos * 128:(pos + 1) * 128], Pb[bn][:, pos, :], X[:, pos, :], True, True, [("dnP", bn), "dnX"], ["ps1"])
                                TT(flat(X[:]), flat(X[:]), PS[1][:], ALU.add, ["dnX", "ps1"], ["dnX"])
                            for pos in range(4):
                                par, j = pos // 2, pos % 2
                                r0 = par * 64
                                TT(kbe[r0:r0 + 64, j, :], kT[r0:r0 + 64, j, t0:t0 + 128], E5[r0:r0 + 64, 4, pos, :], ALU.mult,
                                   [("dnk", j), ("dnE5", 4)], ["dnkbe"])
                                TT(qd[r0:r0 + 64, j, :], qT[r0:r0 + 64, j, t0:t0 + 128], E5[r0:r0 + 64, 3, pos, :], ALU.mult,
                                   [("dnq", j), ("dnE5", 3)], ["dnqd"])
                            for par in range(2):
                                sl = slice(2 * par, 2 * par + 2)
                                bsrc = ebt[:, d * 4:d * 4 + 4].rearrange("p (j q) -> p q j", q=2)[:, par, :]
                                esrc = ekd[:, 8 + d * 4:12 + d * 4].rearrange("p (j q) -> p q j", q=2)[:, par, :]
                                vsrc = kv[:, 256:512].rearrange("p (j q v) -> p q j v", q=2, v=64)[:, par]
                                ksrc = kv[:, 0:256].rearrange("p (j q v) -> p q j v", q=2, v=64)[:, par]
                                TT(vb[:, sl, :], vsrc, bsrc.rearrange("p (j o) -> p j o", o=1).to_broadcast([128, 2, 64]), ALU.mult, ["dnkv", "dnebt"], ["dnvb"])
                                TT(kdc[:, sl, :], ksrc, esrc.rearrange("p (j o) -> p j o", o=1).to_broadcast([128, 2, 64]), ALU.mult, ["dnkv", "dnekd"], ["dnkdc"])
                            for i in order:
                                c0 = i * 64
                                for pos in range(4):
                                    par, j = pos // 2, pos % 2
                                    r0 = par * 64
                                    pbk = 5 - par
                                    MM(PS[pbk][c0:c0 + 64, j * 64:(j + 1) * 64], kbe[r0:r0 + 64, j, c0:c0 + 64], S16[r0:r0 + 64, pos, :], True, True,
                                       ["dnkbe", "dnS16"], ["ps%d" % pbk])
                                for par in range(2):
                                    sl = slice(2 * par, 2 * par + 2)
                                    pbk = 5 - par
                                    TT(rF[i][c0:c0 + 64, sl, :], vb[c0:c0 + 64, sl, :], PS[pbk][c0:c0 + 64, 0:128].rearrange("p (h v) -> p h v", h=2),
                                       ALU.subtract, ["dnvb", "ps%d" % pbk], [("dnrF", i)])
                                for pos in range(4):
                                    MM(PS[2][:, pos * 64:(pos + 1) * 64], X[:, pos, :], rF[i][:, pos, :], True, True, ["dnX", ("dnrF", i)], ["ps2"])
                                ACT(vnZ[i][c0:c0 + 64, :, :], PS[2][c0:c0 + 64, 0:256].rearrange("p (h v) -> p h v", h=4), AF.Copy, ["ps2"], [("dnvn", i)])
                                for pos in range(4):
                                    par, j = pos // 2, pos % 2
                                    r0 = par * 64
                                    pO = 6 + par
                                    MM(PS[pO][j * 64:(j + 1) * 64, c0:c0 + 64], S16[r0:r0 + 64, pos, :], qd[r0:r0 + 64, j, c0:c0 + 64], True, False,
                                       ["dnS16", "dnqd"], ["ps%d" % pO])
                                    MM(PS[pO][j * 64:(j + 1) * 64, c0:c0 + 64], vnZ[i][:, pos, :], aqk[:, pos, c0:c0 + 64], False, True,
                                       [("dnvn", i), "dnaqk"], ["ps%d" % pO])
                                for pos in range(4):
                                    r0 = (pos // 2) * 64
                                    MM(PS[3][r0:r0 + 64, pos * 64:(pos + 1) * 64], kdc[c0:c0 + 64, pos, :], vnZ[i][c0:c0 + 64, pos, :], True, True,
                                       ["dnkdc", ("dnvn", i)], ["ps3"])
                                last = c0 + 63 if d == 0 else c0
                                TT(Sf[:], Sf[:], E5[:, 3, :, last:last + 1].to_broadcast([128, 4, 64]), ALU.mult, ["dnS", ("dnE5", 3)], ["dnS"])
                                TT(Sf[:], Sf[:], PS[3][:, 0:256].rearrange("p (h v) -> p h v", h=4), ALU.add, ["dnS", "ps3"], ["dnS"])
                                ACT(S16[:], Sf[:], AF.Copy, ["dnS"], ["dnS16"])
                            for hp in range(2):
                                if d == 0:
                                    ACT(osum[:, hp, t0:t0 + 128], PS[6 + hp][:, 0:128], AF.Copy, ["ps%d" % (6 + hp)], ["dno"])
                                else:
                                    TT(osum[:, hp, t0:t0 + 128], osum[:, hp, t0:t0 + 128], PS[6 + hp][:, 0:128], ALU.add, ["dno", "ps%d" % (6 + hp)], ["dno"])
                    head_norm_gate(ph, osum, "dno", 6, V(("dnng", li)), 0, perm=True)
                    S.flush()

            if "gla" in mixers:
                with contextlib.ExitStack() as ph:
                    osum = sbuf(ph, "glo", [128, 2, T], F32)
                    qT = sbuf(ph, "glq", [128, T], F32)
                    kT = sbuf(ph, "glk", [128, T], F32)
                    a1 = [sbuf(ph, "gla1_%d" % i, [17, T], F32) for i in range(2)]
                    wa = sbuf(ph, "glwa", [17, 2, 128], F32)
                    DMA(qT[:], PF[15 * 128:16 * 128, :], [], ["glq"])
                    DMA(kT[:], PF[16 * 128:17 * 128, :], [], ["glk"])
                    for d in range(2):
                        MSET(a1[d][:], 1.0, [("gla1", d)])
                        DMA(a1[d][0:16, :], PF[(21 + d) * 128:(21 + d) * 128 + 16, :], [], [("gla1", d)])
                        DMA(wa[:, d, :], gla_w_in[li, d], [], ["glwa"])
                    ktok = [sbuf(ph, "glkt%d" % i, [128, 384], F32) for i in range(2)]
                    vb16 = [sbuf(ph, "glvb%d" % i, [128, 256], BF16) for i in range(2)]
                    ln_ = sbuf(ph, "glln", [128, 128], F32)
                    eq = sbuf(ph, "gleq", [128, 128], F32)
                    ek = sbuf(ph, "glek", [128, 128], F32)
                    eb = sbuf(ph, "gleb", [128, 128], F32)
                    ekd = sbuf(ph, "glekd", [128, 128], F32)
                    qt_ = sbuf(ph, "glqt", [128, 128], F32)
                    ktl = sbuf(ph, "glktl", [128, 128], BF16)
                    qb = sbuf(ph, "glqb", [128, 128], F32)
                    qth = sbuf(ph, "glqth", [128, 4, 128], BF16)
                    qbh = sbuf(ph, "glqbh", [128, 4, 128], BF16)
                    kdec = sbuf(ph, "glkdec", [128, 128], BF16)
                    S16 = sbuf(ph, "glS16", [128, 64], BF16)
                    Ah = sbuf(ph, "glA", [128, 4, 128], BF16)
                    dsm = sbuf(ph, "gldsm", [128, 4, 64], F32)
                    dsr = sbuf(ph, "gldsr", [128, 64], F32)
                    Sst = sbuf(ph, "glS", [128, 64], F32)
                    osb = sbuf(ph, "glosb", [128, 2, 128], F32)
                    sc = 32 ** -0.5
                    nb = 0
                    for d in range(2):
                        MSET(Sst[:], 0.0, ["glS"])
                        for t0 in scan_blocks(d)[:debug.get("gla_nblk", 100)]:
                            b = nb % 2
                            nb += 1
                            stage = debug.get("gla_stage", 99)
                            DMA(ktok[b][:], PT[t0:t0 + 128, 512:896], [], [("glkt", b)])
                            CP(vb16[b][:], ktok[b][:, 128:384], [("glkt", b)], [("glvb", b)], eng="pool")
                            MM(PS[0][:, 0:128], a1[d][:, t0:t0 + 128], wa[:, d, :], True, True, [("gla1", d), "glwa"], ["ps0"])
                            ACT(ln_[:], PS[0][:, 0:128], AF.Exp, ["ps0"], ["glln"], scale=-1.0)
                            ACT(ln_[:], ln_[:], AF.Ln, ["glln"], ["glln"], bias=1.0)
                            if stage < 1:
                                continue
                            MM(PS[1][:, 0:128], ln_[:], CM(d, "CQ"), True, True, ["glln", "scm"], ["ps1"])
                            MM(PS[1][:, 128:256], ln_[:], CM(d, "U"), True, True, ["glln", "scm"], ["ps1"])
                            MM(PS[1][:, 256:384], CM(d, "CKD"), ln_[:], True, True, ["glln", "scm"], ["ps1"])
                            ACT(eq[:], PS[1][:, 0:128], AF.Exp, ["ps1"], ["gleq"], scale=-1.0 / 16)
                            ACT(ek[:], PS[1][:, 0:128], AF.Exp, ["ps1"], ["glek"], scale=1.0 / 16)
                            ACT(eb[:], PS[1][:, 128:256], AF.Exp, ["ps1"], ["gleb"], scale=-1.0 / 16)
                            ACT(ekd[:], PS[1][:, 256:384], AF.Exp, ["ps1"], ["glekd"], scale=-1.0 / 16)
                            STT(qt_[:], qT[:, t0:t0 + 128], sc, eq[:], ALU.mult, ALU.mult, ["glq", "gleq"], ["glqt"])
                            TT(ktl[:], kT[:, t0:t0 + 128], ek[:], ALU.mult, ["glk", "glek"], ["glktl"])
                            STT(qb[:], qT[:, t0:t0 + 128], sc, eb[:], ALU.mult, ALU.mult, ["glq", "gleb"], ["glqb"])
                            TT(kdec[:], ktok[b][:, 0:128], ekd[:], ALU.mult, [("glkt", b), "glekd"], ["glkdec"], eng="pool")
                            if stage < 2:
                                continue
                            for h in range(4):
                                TS(qth[:, h, :], qt_[:], V("hm", h), 0.0, ALU.mult, ALU.add, ["glqt", "vecs"], ["glqth"])
                                TS(qbh[:, h, :], qb[:], V("hm", h), 0.0, ALU.mult, ALU.add, ["glqb", "vecs"], ["glqbh"])
                            if stage < 3:
                                continue
                            for h in range(4):
                                MM(PS[2][:, h * 128:(h + 1) * 128], ktl[:], qth[:, h, :], True, True, ["glktl", "glqth"], ["ps2"])
                            TT(Ah[:], PS[2][:].rearrange("p (h c) -> p h c", h=4),
                               CM(d, "M01").rearrange("p (o c) -> p o c", o=1).to_broadcast([128, 4, 128]), ALU.mult, ["ps2", "scm"], ["glA"])
                            order = (0, 1) if d == 0 else (1, 0)
                            if stage < 4:
                                continue
                            for h in range(4):
                                orow = (h % 2) * 64
                                pO = 3 + h // 2
                                MM(PS[pO][orow:orow + 64, 0:128], vb16[b][:, h * 64:(h + 1) * 64], Ah[:, h, :], True, False,
                                   [("glvb", b), "glA"], ["ps%d" % pO])
                            for ii, i in enumerate(order):
                                c0 = i * 64
                                CP(S16[:], Sst[:], ["glS"], ["glS16"], eng="pool")
                                for h in range(4):
                                    orow = (h % 2) * 64
                                    pO = 3 + h // 2
                                    MM(PS[pO][orow:orow + 64, c0:c0 + 64], S16[:, :], qbh[:, h, c0:c0 + 64], False, ii == 1,
                                       ["glS16", "glqbh"], ["ps%d" % pO])
                                MM(PS[5][:, 0:256], kdec[c0:c0 + 64, :], vb16[b][c0:c0 + 64, :], True, True, ["glkdec", ("glvb", b)], ["ps5"])
                                TT(dsm[:], PS[5][:, 0:256].rearrange("p (h v) -> p h v", h=4),
                                   V("hm", 0, 4).rearrange("p (h o) -> p h o", o=1).to_broadcast([128, 4, 64]), ALU.mult, ["ps5", "vecs"], ["gldsm"])
                                S.op("dve", lambda e: e.reduce_sum(out=dsr[:], in_=dsm[:].rearrange("p h v -> p v h"), axis=AX.X),
                                     reads=["gldsm"], writes=["gldsr"])
                                last = c0 + 63 if d == 0 else c0
                                STT(Sst[:], Sst[:], eb[:, last:last + 1], dsr[:], ALU.mult, ALU.add, ["glS", "gleb", "gldsr"], ["glS"])
                            for hp in range(2):
                                if d == 0:
                                    ACT(osum[:, hp, t0:t0 + 128], PS[3 + hp][:, 0:128], AF.Copy, ["ps%d" % (3 + hp)], ["glo"])
                                else:
                                    TT(osum[:, hp, t0:t0 + 128], osum[:, hp, t0:t0 + 128], PS[3 + hp][:, 0:128], ALU.add, ["glo", "ps%d" % (3 + hp)], ["glo"])
                    if debug.get("gla_stage", 99) >= 99:
                        head_norm_gate(ph, osum, "glo", 19, V(("glng", li)), 512)
                    S.flush()

            if "df" in mixers:
                with contextlib.ExitStack() as ph:
                    qr = sbuf(ph, "dfqr", [128, 2, T], BF16)
                    k1z = sbuf(ph, "dfk1", [128, 2, T], BF16)
                    k2z = sbuf(ph, "dfk2", [128, 2, T], BF16)
                    with contextlib.ExitStack() as ph2:
                        specs = []
                        for i in range(2):
                            specs.append((BLK["dfq"] + i, BD32, V(("dfqn", li)), True, [(qr[:, i, :], ("dfqr", i), V("one"))]))
                            specs.append((BLK["dfk"] + i, BD32, V(("dfkn", li)), True,
                                          [(k1z[:, i, :], ("dfk1", i), V("m1")), (k2z[:, i, :], ("dfk2", i), V("m2"))]))
                        qk_prep(ph2, specs)
                        S.flush()
                    va = load_vaug(ph, "dfva", 256)
                    lp = sbuf(ph, "lp", [128, 2, 2, 32], F32)
                    lpp = sbuf(ph, "lpp", [128, 2, 32], F32)
                    lps = sbuf(ph, "lps", [128, 2], F32)
                    nlam = sbuf(ph, "nlam", [128, 1], F32)
                    lam_init = 0.8 - 0.6 * math.exp(-0.3 * li)
                    DMA(lp[:].rearrange("p a b d -> p (a b d)"), dflam_in[li:li + 1, :].partition_broadcast(128), [], ["lp"])
                    TT(lpp[:], lp[:, :, 0, :], lp[:, :, 1, :], ALU.mult, ["lp"], ["lpp"])
                    S.op("dve", lambda e: e.reduce_sum(out=lps[:], in_=lpp[:], axis=AX.X), reads=["lpp"], writes=["lps"])
                    ACT(lps[:], lps[:], AF.Exp, ["lps"], ["lps"])
                    TT(nlam[:], lps[:, 1:2], lps[:, 0:1], ALU.subtract, ["lps"], ["nlam"])
                    TS(nlam[:], nlam[:], -lam_init, 0.0, ALU.add, ALU.add, ["nlam"], ["nlam"])
                    E = [sbuf(ph, "dfE%d" % i, [128, 512], BF16) for i in range(4)]
                    osb = sbuf(ph, "osb", [128, 512], F32)
                    rrow = sbuf(ph, "rrow", [128, 512], F32)
                    o1 = sbuf(ph, "o1n", [128, 512], F32)
                    o2 = sbuf(ph, "o2n", [128, 512], F32)
                    dsq = sbuf(ph, "dsq", [128, 512], F32)
                    yo = [sbuf(ph, "dfy%d" % i, [128, 512], BF16) for i in range(2)]
                    sc = 32 ** -0.5
                    ne = 0
                    ny = 0
                    qtiles = [(t0, 512, list(range(34))) for t0 in range(0, L, 512)]
                    if with_ctx:
                        qtiles.append((L, CL, [32, 33]))
                    steps = []
                    for (t0, W, kcs) in qtiles:
                        for h in range(4):
                            for ci, kc in enumerate(kcs):
                                for t in range(2):
                                    steps.append((t0, W, h, kc, t, ci == 0, ci == len(kcs) - 1))
                    PIPE = 2
                    kzs = (k1z, k2z)

                    def emit_score(i):
                        (t0, W, h, kc, t, first, last) = steps[i]
                        blk = h // 2
                        r0 = (h % 2) * 64
                        sb_ = 2 + (i % 4)
                        MM(PS[sb_][:, 0:W], kzs[t][r0:r0 + 64, blk, kc * 128:(kc + 1) * 128], qr[r0:r0 + 64, blk, t0:t0 + W],
                           True, True, [("dfk%d" % (t + 1), blk), ("dfqr", blk)], ["ps%d" % sb_])

                    def emit_rest(i):
                        nonlocal ny
                        (t0, W, h, kc, t, first, last) = steps[i]
                        sb_ = 2 + (i % 4)
                        eb = i % 4
                        ACT(E[eb][:, 0:W], PS[sb_][:, 0:W], AF.Exp, ["ps%d" % sb_], [("dfE", eb)], scale=sc)
                        MM(PS[t][0:65, 0:W], va[:, kc, h, :], E[eb][:, 0:W], first, last, ["dfva", ("dfE", eb)], ["ps%d" % t])
                        if not (last and t == 1):
                            return
                        attn_norm((osb, rrow, o1), 0, W, "o1n")
                        attn_norm((osb, rrow, o2), 1, W, "o2n")
                        STT(o1[0:64, 0:W], o2[0:64, 0:W], nlam[0:64, 0:1], o1[0:64, 0:W], ALU.mult, ALU.add,
                            ["o1n", "o2n", "nlam"], ["o1n"])
                        ACT(dsq[0:64, 0:W], o1[0:64, 0:W], AF.Square, ["o1n"], ["dsq"])
                        MM(PS[7][0:64, 0:W], BD64[0:64, 0:64], dsq[0:64, 0:W], True, True, ["cmat", "dsq"], ["ps7"])
                        ACT(dsq[0:64, 0:W], PS[7][0:64, 0:W], AF.Ln, ["ps7"], ["dsq"], bias=EPS)
                        ACT(dsq[0:64, 0:W], dsq[0:64, 0:W], AF.Exp, ["dsq"], ["dsq"], scale=-0.5)
                        yb_ = ny % 2
                        ny += 1
                        STT(dsq[0:64, 0:W], o1[0:64, 0:W], V(("dfng", li))[0:64, :], dsq[0:64, 0:W], ALU.mult, ALU.mult,
                            ["o1n", "vecs", "dsq"], ["dsq"])
                        TS(yo[yb_][0:64, 0:W], dsq[0:64, 0:W], 1.0 - lam_init, 0.0, ALU.mult, ALU.add, ["dsq"], [("dfy", yb_)])
                        DMA(ybuf[768 + h * 64:768 + (h + 1) * 64, t0:t0 + W], yo[yb_][0:64, 0:W], [("dfy", yb_)], [("ybuf", "df")])

                    for i in range(len(steps) + PIPE):
                        if i < len(steps):
                            emit_score(i)
                        if i >= PIPE:
                            emit_rest(i - PIPE)
                    S.flush()

            if "na" in mixers:
                with contextlib.ExitStack() as ph:
                    qn = sbuf(ph, "naq", [128, 2, T], BF16)
                    kn = sbuf(ph, "nak", [128, 2, T], BF16)
                    with contextlib.ExitStack() as ph2:
                        specs = []
                        for i in range(2):
                            specs.append((BLK["naq"] + i, BD64, V(("naqn", li)), False, [(qn[:, i, :], ("naq", i), V("one"))]))
                            specs.append((BLK["nak"] + i, BD64, V(("nakn", li)), False, [(kn[:, i, :], ("nak", i), V("one"))]))
                        qk_prep(ph2, specs)
                        S.flush()
                    va = load_vaug(ph, "nava", 0)
                    bias = [sbuf(ph, "nab%d" % i, [128, 21, 128], F32) for i in range(2)]
                    sbt = [sbuf(ph, "nasb%d" % i, [128, 640], F32) for i in range(2)]
                    E = [sbuf(ph, "naE%d" % i, [128, 896], BF16) for i in range(2)]
                    osb = sbuf(ph, "osb", [128, 512], F32)
                    rrow = sbuf(ph, "rrow", [128, 512], F32)
                    o1 = sbuf(ph, "o1n", [128, 512], F32)
                    yo = [sbuf(ph, "nay%d" % i, [128, 512], BF16) for i in range(2)]
                    sc = 64 ** -0.5
                    n = 0
                    ny = 0
                    for h in range(4):
                        blk = h // 2
                        r0 = (h % 2) * 64
                        hb = h % 2
                        DMA(bias[hb][:], nab_in[li, h], [], [("nab", hb)])
                        def na_scores(rp, b):
                            chunks = na_chunks(rp)
                            pA = 2 + 2 * b
                            pB = pA + 1
                            q_ap = qn[r0:r0 + 64, blk, rp * 128:(rp + 1) * 128]
                            for j, (kc, bi) in enumerate(chunks):
                                pbk, pc = (pA, j * 128) if j < 4 else (pB, 0)
                                MM(PS[pbk][:, pc:pc + 128], kn[r0:r0 + 64, blk, kc * 128:(kc + 1) * 128], q_ap, True, True,
                                   [("nak", blk), ("naq", blk)], ["ps%d" % pbk])
                            for j2 in range(2):
                                pc = 128 + j2 * 128
                                MM(PS[pB][:, pc:pc + 128], kn[r0:r0 + 64, blk, L + j2 * 128:L + (j2 + 1) * 128], q_ap, True, True,
                                   [("nak", blk), ("naq", blk)], ["ps%d" % pB])

                        def na_rest(rp, b):
                            nonlocal ny
                            rg, rr_ = rp // 4, rp % 4
                            chunks = na_chunks(rp)
                            pA = 2 + 2 * b
                            pB = pA + 1
                            nw = len(chunks)
                            bi0 = chunks[0][1]
                            STT(sbt[b][:, 0:512], PS[pA][:, 0:512], sc, bias[hb][:, bi0:bi0 + 4, :].rearrange("p a q -> p (a q)"),
                                ALU.mult, ALU.add, ["ps%d" % pA, ("nab", hb)], [("nasb", b)])
                            if nw == 5:
                                STT(sbt[b][:, 512:640], PS[pB][:, 0:128], sc, bias[hb][:, 4, :], ALU.mult, ALU.add,
                                    ["ps%d" % pB, ("nab", hb)], [("nasb", b)])
                            ACT(E[b][:, 0:nw * 128], sbt[b][:, 0:nw * 128], AF.Exp, [("nasb", b)], [("naE", b)])
                            ACT(E[b][:, 640:896], PS[pB][:, 128:384], AF.Exp, ["ps%d" % pB], [("naE", b)], scale=sc)
                            ecols = [(kc, j * 128) for j, (kc, bi) in enumerate(chunks)] + [(32, 640), (33, 768)]
                            for ci, (kc, ec) in enumerate(ecols):
                                MM(PS[0][0:65, rr_ * 128:(rr_ + 1) * 128], va[:, kc, h, :], E[b][:, ec:ec + 128], ci == 0, ci == len(ecols) - 1,
                                   ["nava", ("naE", b)], ["ps0"])
                            if rr_ == 3:
                                attn_norm((osb, rrow, o1), 0, 512, "o1n")
                                yb_ = ny % 2
                                ny += 1
                                CP(yo[yb_][0:64, :], o1[0:64, :], ["o1n"], [("nay", yb_)], eng="pool")
                                DMA(ybuf[256 + h * 64:256 + (h + 1) * 64, rg * 512:(rg + 1) * 512], yo[yb_][0:64, :], [("nay", yb_)], [("ybuf", "na")])

                        na_scores(0, 0)
                        for rp in range(32):
                            if rp + 1 < 32:
                                na_scores(rp + 1, (rp + 1) % 2)
                            na_rest(rp, rp % 2)
                        if with_ctx:
                            q_ap = qn[r0:r0 + 64, blk, L:L + CL]
                            for j2 in range(2):
                                MM(PS[1][:, j2 * 256:(j2 + 1) * 256], kn[r0:r0 + 64, blk, L + j2 * 128:L + (j2 + 1) * 128], q_ap, True, True,
                                   [("nak", blk), ("naq", blk)], ["ps1"])
                            ACT(E[0][:, 0:512], PS[1][:, 0:512], AF.Exp, ["ps1"], [("naE", 0)], scale=sc)
                            for j2 in range(2):
                                MM(PS[0][0:65, 0:256], va[:, 32 + j2, h, :], E[0][:, j2 * 256:(j2 + 1) * 256], j2 == 0, j2 == 1,
                                   ["nava", ("naE", 0)], ["ps0"])
                            attn_norm((osb, rrow, o1), 0, 256, "o1n")
                            yb_ = ny % 2
                            ny += 1
                            CP(yo[yb_][0:64, 0:256], o1[0:64, 0:256], ["o1n"], [("nay", yb_)], eng="pool")
                            DMA(ybuf[256 + h * 64:256 + (h + 1) * 64, L:L + CL], yo[yb_][0:64, 0:256], [("nay", yb_)], [("ybuf", "na")])
                    S.flush()

            with contextlib.ExitStack() as ph:
                wg = sbuf(ph, "wg", [128, 8, 4096], BF16)
                wbr = sbuf(ph, "wbr", [128, 8, D], BF16)
                wo = sbuf(ph, "wo", [128, 8, D], BF16)
                xt = [sbuf(ph, "xt%d" % i, [128, 8, 512], F32) for i in range(2)]
                ht = [sbuf(ph, "ht%d" % i, [128, 8, 512], BF16) for i in range(2)]
                yt = [sbuf(ph, "yt%d" % i, [128, 8, 512], BF16) for i in range(2)]
                mg = sbuf(ph, "mg", [128, 8, 512], BF16)
                sig = [sbuf(ph, "sig%d" % i, [128, 512], F32) for i in range(2)]
                acc = sbuf(ph, "acc", [128, 512], F32)
                tmp = sbuf(ph, "tmp", [128, 512], F32)
                for k in range(8):
                    DMA(wg[:, k, :], wi_b[li][k * 128:(k + 1) * 128, NMIX:NIN], [("wi_b", li)], [("wg", k)])
                    DMA(wbr[:, k, :], wb_b[li][k * 128:(k + 1) * 128, :], [("wb_b", li)], [("wbr", k)])
                    DMA(wo[:, k, :], wo_b[li][k * 128:(k + 1) * 128, :], [("wo_b", li)], [("wo", k)])
                it = 0
                ng = 0
                for (sname, s0, slen) in SEQS:
                    s = 0 if sname == "lat" else 1
                    if s == 1 and not with_ctx:
                        continue
                    for t0 in range(s0, s0 + slen, 512):
                        W = min(512, s0 + slen - t0)
                        b = it % 2
                        it += 1
                        if li == layer_list[0]:
                            src = xT_in[:, t0:t0 + W] if s == 0 else cT_in[:, t0 - L:t0 - L + W]
                        else:
                            src = xbuf[:, t0:t0 + W]
                        DMA(xt[b][:, :, 0:W], src.rearrange("(k p) t -> p k t", p=128), ["xbuf"], [("xt", b)])
                        DMA(ht[b][:, :, 0:W], hbuf[:, t0:t0 + W].rearrange("(k p) t -> p k t", p=128), ["hbuf"], [("ht", b)])
                        DMA(yt[b][:, :, 0:W], ybuf[:, t0:t0 + W].rearrange("(k p) t -> p k t", p=128), ["ybuf"], [("yt", b)])
                        for dc in range(8):
                            for g in range(4):
                                pa = 2 * (ng % 2)
                                pbk = pa + 1
                                sgi = ng % 2
                                ng += 1
                                cg = g * D + dc * 128
                                for k in range(8):
                                    MM(PS[pa][:, 0:W], wg[:, k, cg:cg + 128], ht[b][:, k, 0:W], k == 0, k == 7,
                                       [("wg", k), ("ht", b)], ["ps%d" % pa])
                                for k2 in range(2):
                                    MM(PS[pbk][:, 0:W], wbr[:, 2 * g + k2, dc * 128:(dc + 1) * 128], yt[b][:, 2 * g + k2, 0:W],
                                       k2 == 0, k2 == 1, [("wbr", 2 * g + k2), ("yt", b)], ["ps%d" % pbk])
                                ACT(sig[sgi][:, 0:W], PS[pa][:, 0:W], AF.Sigmoid, ["ps%d" % pa, "vecs"], [("sig", sgi)],
                                    bias=V(("bgate", li), g * 8 + dc))
                                if g == 0:
                                    TT(acc[:, 0:W], sig[sgi][:, 0:W], PS[pbk][:, 0:W], ALU.mult, [("sig", sgi), "ps%d" % pbk], ["acc"])
                                else:
                                    TT(tmp[:, 0:W], sig[sgi][:, 0:W], PS[pbk][:, 0:W], ALU.mult, [("sig", sgi), "ps%d" % pbk], ["tmp"])
                                    if g < 3:
                                        TT(acc[:, 0:W], acc[:, 0:W], tmp[:, 0:W], ALU.add, ["acc", "tmp"], ["acc"], eng="pool")
                                    else:
                                        TT(mg[:, dc, 0:W], acc[:, 0:W], tmp[:, 0:W], ALU.add, ["acc", "tmp"], [("mg", dc)], eng="pool")
                        for dc in range(8):
                            pb = 4 + dc % 2
                            for k in range(8):
                                MM(PS[pb][:, 0:W], wo[:, k, dc * 128:(dc + 1) * 128], mg[:, k, 0:W], k == 0, k == 7,
                                   [("wo", k), ("mg", k)], ["ps%d" % pb])
                            STT(xt[b][:, dc, 0:W], PS[pb][:, 0:W], modv(li, "g1", dc, s), xt[b][:, dc, 0:W], ALU.mult, ALU.add,
                                ["ps%d" % pb, "modsb", ("xt", b)], [("xt", b)])
                        DMA(x1buf[:, t0:t0 + W].rearrange("(k p) t -> p k t", p=128), xt[b][:, :, 0:W], [("xt", b)], ["x1buf"])
                S.flush()

            with contextlib.ExitStack() as ph:
                FT = 456
                xt = [sbuf(ph, "xt%d" % i, [128, 8, 512], F32) for i in range(2)]
                ht2 = [sbuf(ph, "ht%d" % i, [128, 8, 512], BF16) for i in range(2)]
                gt = sbuf(ph, "gt", [128, 22, 512], BF16)
                sq = sbuf(ph, "sq", [128, 512], BF16)
                rr = sbuf(ph, "rr", [128, 512], F32)
                ff = sbuf(ph, "ff", [128, 512], F32)
                ust = [sbuf(ph, "ust%d" % i, [128, 516], F32) for i in range(2)]
                ca = sbuf(ph, "ca", [128, 512], F32)
                cb_ = sbuf(ph, "cb", [128, 512], F32)
                w1 = [sbuf(ph, "w1_%d" % i, [128, 8, 256], BF16) for i in range(3)]
                w2f = sbuf(ph, "w2f", [128, 22, D], BF16)
                for j in range(22):
                    DMA(w2f[:, j, :], f2_b[li][j * 128:(j + 1) * 128, :], [("f2_b", li)], [("w2f", j)])
                nw = 0
                tiles3 = []
                for (sname, s0, slen) in SEQS:
                    s = 0 if sname == "lat" else 1
                    if s == 1 and not with_ctx:
                        continue
                    for t0 in range(s0, s0 + slen, FT):
                        t1 = min(t0 + FT, s0 + slen)
                        a0 = max(t0 - 1, s0)
                        a1 = min(t1 + 1, s0 + slen)
                        tiles3.append((s, t0, t1, a0, a1))

                def p3_norm(i):
                    (s, t0, t1, a0, a1) = tiles3[i]
                    b = i % 2
                    DMA(xt[b][:, :, 0:a1 - a0], x1buf[:, a0:a1].rearrange("(k p) t -> p k t", p=128), ["x1buf"], [("xt", b)])
                    norm_tile(xt[b], ht2[b], a1 - a0, li, 1, s, sq, rr, ff, 0, ("xt", b), ("ht", b), "n2")

                p3_norm(0)
                for ti in range(len(tiles3)):
                    if True:
                        (s, t0, t1, a0, a1) = tiles3[ti]
                        W = a1 - a0
                        WI = t1 - t0
                        io = t0 - a0
                        b = ti % 2
                        ht = ht2[b]
                        if ti + 1 < len(tiles3):
                            p3_norm(ti + 1)
                        for j in range(22):
                            wb = nw % 3
                            nw += 1
                            DMA(w1[wb][:, :, 0:128], f1_b[li][:, j * 128:(j + 1) * 128].rearrange("(k p) c -> p k c", p=128),
                                [("f1_b", li)], [("w1", wb)])
                            DMA(w1[wb][:, :, 128:256], f1_b[li][:, DFF + j * 128:DFF + (j + 1) * 128].rearrange("(k p) c -> p k c", p=128),
                                [("f1_b", li)], [("w1", wb)])
                            pu = 1 + 2 * (j % 2)
                            pv = pu + 1
                            ub = j % 2
                            for k in range(8):
                                MM(PS[pu][:, 0:W], w1[wb][:, k, 0:128], ht[:, k, 0:W], k == 0, k == 7, [("w1", wb), ("ht", b)], ["ps%d" % pu])
                            for k in range(8):
                                MM(PS[pv][:, 0:W], w1[wb][:, k, 128:256], ht[:, k, 0:W], k == 0, k == 7, [("w1", wb), ("ht", b)], ["ps%d" % pv])
                            c_in = 1 - io
                            ACT(ust[ub][:, c_in + 0:c_in + W], PS[pu][:, 0:W], AF.Copy, ["ps%d" % pu], [("ust", ub)])
                            if io == 0:
                                MSET(ust[ub][:, 0:1], 0.0, [("ust", ub)])
                            if a1 == t1:
                                MSET(ust[ub][:, WI + 1:WI + 2], 0.0, [("ust", ub)])
                            fo = voff[("fcw", li)]
                            TS(ca[:, 0:WI], ust[ub][:, 1:1 + WI], vecs[:, fo + 22 + j:fo + 23 + j], V(("fcb", li), j), ALU.mult, ALU.add,
                               [("ust", ub), "vecs"], ["ca"])
                            STT(cb_[:, 0:WI], ust[ub][:, 0:WI], vecs[:, fo + j:fo + j + 1], ca[:, 0:WI], ALU.mult, ALU.add,
                                [("ust", ub), "vecs", "ca"], ["cb"])
                            STT(ca[:, 0:WI], ust[ub][:, 2:2 + WI], vecs[:, fo + 44 + j:fo + 45 + j], cb_[:, 0:WI], ALU.mult, ALU.add,
                                [("ust", ub), "vecs", "cb"], ["ca"])
                            ACT(cb_[:, 0:WI], ca[:, 0:WI], AF.Silu, ["ca"], ["cb"])
                            TT(gt[:, j, 0:WI], cb_[:, 0:WI], PS[pv][:, io:io + WI], ALU.mult, ["cb", "ps%d" % pv], [("gt", j)])
                        for dc in range(8):
                            pb = 5 + dc % 2
                            for j in range(22):
                                MM(PS[pb][:, 0:WI], w2f[:, j, dc * 128:(dc + 1) * 128], gt[:, j, 0:WI], j == 0, j == 21,
                                   [("w2f", j), ("gt", j)], ["ps%d" % pb])
                            STT(xt[b][:, dc, io:io + WI], PS[pb][:, 0:WI], modv(li, "g2", dc, s), xt[b][:, dc, io:io + WI], ALU.mult, ALU.add,
                                ["ps%d" % pb, "modsb", ("xt", b)], [("xt", b)])
                        dst = outT[:, t0:t1] if (li == DEPTH - 1 and s == 0) else xbuf[:, t0:t1]
                        DMA(dst.rearrange("(k p) t -> p k t", p=128), xt[b][:, :, io:io + WI], [("xt", b)], ["xbuf"])
                        if xd is not None:
                            DMA(xd[li][:, t0:t1].rearrange("(k p) t -> p k t", p=128), xt[b][:, :, io:io + WI], [("xt", b)], ["xd"])
                S.flush()
    return nc


def host_inputs(inp, b):
    m = {}
    m["xT"] = np.ascontiguousarray(np.asarray(inp["x"][b], np.float32).T)
    m["ctxT"] = np.ascontiguousarray(np.asarray(inp["ctx"][b], np.float32).T)
    cs = np.stack([np.asarray(inp["c"][b], np.float32), np.asarray(inp["c_ctx"], np.float32)], axis=-1)
    m["cs"] = np.ascontiguousarray(cs.reshape(8, 128, 2).transpose(1, 0, 2).reshape(128, 16))
    m["vecs"] = pack_vecs(inp)
    m["cmat"] = const_mats()
    m["rope"] = rope_tables()
    m["nab"] = np.stack([na_bias_tables(np.asarray(inp["na_rpb"][li], np.float32)) for li in range(DEPTH)], 0)
    m["scm"] = scan_mats()
    m["gla_w"] = np.ascontiguousarray(np.concatenate([np.asarray(inp["gla_w_a2"], np.float32),
                                                      np.asarray(inp["gla_b_a"], np.float32)[:, :, None, :]], axis=2))
    m["dflam"] = np.ascontiguousarray(np.asarray(inp["df_lambda"], np.float32).reshape(DEPTH, 128))
    for k in ("w_mod", "w_in", "w_out", "ffn_w_in", "ffn_w_out"):
        m[k] = np.ascontiguousarray(np.asarray(inp[k], np.float32))
    m["w_branch"] = np.ascontiguousarray(np.asarray(inp["w_branch"], np.float32).reshape(DEPTH, D, D))
    return m


def kernel(**inp):
    nc = build()
    shared = None
    in_maps = []
    for b in range(8):
        m = host_inputs(inp, b)
        if shared is None:
            shared = {k: m[k] for k in ("vecs", "cmat", "rope", "nab", "dflam", "scm", "gla_w", "w_mod", "w_in", "w_out", "ffn_w_in", "ffn_w_out", "w_branch")}
        else:
            m.update(shared)
        in_maps.append(m)
    res = run_bass_kernel_spmd(nc, in_maps, core_ids=list(range(8)))
    out = np.stack([np.ascontiguousarray(r["outT"].T) for r in res.results], axis=0)
    return out.astype(np.float32)
```

```python
import contextlib
import math
import numpy as np
import ml_dtypes
import concourse.bass as bass
import concourse.mybir as mybir
from concourse.bass_utils import run_bass_kernel_spmd

F32 = mybir.dt.float32
BF16 = mybir.dt.bfloat16
AF = mybir.ActivationFunctionType
ALU = mybir.AluOpType
AX = mybir.AxisListType

D = 1024
L = 4096
CL = 256
T = L + CL
DEPTH = 4
NIN = 7472
NMIX = 3376
DFF = 2816
EPS = 1e-6
NDMA_SEM = 8
SEQS = (("lat", 0, L), ("ctx", L, CL))


class Sched:
    ENGS = ("pe", "act", "dve", "pool", "sp")

    def __init__(self, nc, st):
        self.nc = nc
        self.ops = []
        self.last_w = {}
        self.readers = {}
        self.dma_count = {"sp": 0, "pool": 0}
        self.dma_hist = {"sp": [], "pool": []}
        self.emitted = 0
        self.cnt = {e: 0 for e in self.ENGS}
        self.sems = {}
        for e in self.ENGS:
            self.sems[e] = st.enter_context(nc.semaphore("s_" + e))
        for q in ("sp", "pool"):
            for i in range(NDMA_SEM):
                self.sems[(q, i)] = st.enter_context(nc.semaphore("d_%s%d" % (q, i)))

    def op(self, eng, fn, reads=(), writes=(), dma=False):
        oid = len(self.ops)
        deps = {}
        lo = self.emitted
        for k in reads:
            w = self.last_w.get(k)
            if w is not None and w >= lo:
                deps[w] = 2
        for k in writes:
            w = self.last_w.get(k)
            if w is not None and w >= lo:
                deps[w] = max(deps.get(w, 0), 1)
            for r in self.readers.get(k, ()):
                if r >= lo:
                    deps.setdefault(r, 0)
        for d in list(deps):
            do = self.ops[d]
            if do["eng"] == eng and not do["dma"] and not dma:
                if deps[d] == 0 or (deps[d] == 1 and eng == "pe"):
                    del deps[d]
        o = dict(eng=eng, fn=fn, dma=dma, deps=deps)
        if dma:
            n = self.dma_count[eng]
            self.dma_count[eng] = n + 1
            o["dma_i"] = n
            h = self.dma_hist[eng]
            if n >= NDMA_SEM and h[n - NDMA_SEM] >= lo:
                deps[h[n - NDMA_SEM]] = 2
            h.append(oid)
        self.ops.append(o)
        for k in reads:
            self.readers.setdefault(k, []).append(oid)
        for k in writes:
            self.last_w[k] = oid
            self.readers[k] = []
        return oid

    def flush(self):
        nc = self.nc
        allops = self.ops
        lo = self.emitted
        ops = allops[lo:]
        self.emitted = len(allops)
        if not ops:
            return
        for o in ops:
            o["sig"] = o["dma"]
        for o in ops:
            for d in o["deps"]:
                if not allops[d]["dma"]:
                    allops[d]["sig"] = True
        for o in ops:
            if o["dma"]:
                i = o["dma_i"]
                o["semkey"] = (o["eng"], i % NDMA_SEM)
                o["semval"] = 16 * (i // NDMA_SEM + 1)
            elif o["sig"]:
                self.cnt[o["eng"]] += 1
                o["semkey"] = o["eng"]
                o["semval"] = self.cnt[o["eng"]]
        known = {e: {} for e in self.ENGS}
        for o in ops:
            kn = known[o["eng"]]
            waits = []
            for d in sorted(o["deps"]):
                do = allops[d]
                sk, sv = do["semkey"], do["semval"]
                if kn.get(sk, 0) >= sv:
                    continue
                waits.append((sk, sv))
                kn[sk] = sv
                for k2, v2 in do["clock"].items():
                    if kn.get(k2, 0) < v2:
                        kn[k2] = v2
            o["waits"] = waits
            o["clock"] = dict(kn)
            if "semkey" in o and not o["dma"]:
                o["clock"][o["semkey"]] = o["semval"]
        sems = self.sems
        dma_count = dict(self.dma_count)

        def replay(ename):
            def body(eng):
                for o in ops:
                    if o["eng"] != ename:
                        continue
                    for sk, sv in o["waits"]:
                        eng.wait_ge(sems[sk], sv)
                    ins = o["fn"](eng)
                    if o["dma"]:
                        ins.then_inc(sems[o["semkey"]], 16)
                    elif o["sig"]:
                        ins.then_inc(sems[o["semkey"]], 1)
                if ename in ("sp", "pool"):
                    n = dma_count[ename]
                    for i in range(NDMA_SEM):
                        c = (n - i + NDMA_SEM - 1) // NDMA_SEM
                        if c > 0:
                            eng.wait_ge(sems[(ename, i)], 16 * c)
            return body

        with nc.Block() as block:
            block.tensor(replay("pe"))
            block.scalar(replay("act"))
            block.vector(replay("dve"))
            block.gpsimd(replay("pool"))
            block.sync(replay("sp"))
        for o in ops:
            o["fn"] = None
            o["clock"] = None


def vec_layout():
    off = {}
    n = 0

    def add(name, cols):
        nonlocal n
        off[name] = n
        n += cols

    for li in range(DEPTH):
        add(("bmod", li), 48)
        add(("n1g", li), 8)
        add(("n2g", li), 8)
        add(("bgate", li), 32)
        add(("fcw", li), 66)
        add(("fcb", li), 22)
        for nm in ("dfqn", "dfkn", "dfng", "naqn", "nakn", "glng", "dnng", "dndtb", "dnalog"):
            add((nm, li), 1)
        add(("dncw", li), 18)
    add("m1", 1)
    add("m2", 1)
    add("one", 1)
    add("hm", 4)
    add("dnsgn", 1)
    return off, n


def pmajor(v):
    v = np.asarray(v, np.float32)
    lead = int(np.prod(v.shape[:-1])) if v.ndim > 1 else 1
    n = v.shape[-1] // 128
    return np.ascontiguousarray(v.reshape(lead, n, 128).transpose(2, 0, 1).reshape(128, lead * n))


def pack_vecs(inp):
    off, n = vec_layout()
    V = np.zeros((128, n), np.float32)

    def put(name, arr):
        V[:, off[name]:off[name] + arr.shape[1]] = arr

    for li in range(DEPTH):
        put(("bmod", li), pmajor(inp["b_mod"][li]))
        put(("n1g", li), pmajor(inp["norm1_g"][li]))
        put(("n2g", li), pmajor(inp["norm2_g"][li]))
        put(("bgate", li), pmajor(inp["b_gate"][li]))
        put(("fcw", li), pmajor(inp["ffn_conv_w"][li]))
        put(("fcb", li), pmajor(inp["ffn_conv_b"][li]))
        put(("dfqn", li), np.tile(inp["df_q_norm"][li], 4)[:, None])
        put(("dfkn", li), np.tile(inp["df_k_norm"][li], 4)[:, None])
        put(("dfng", li), np.tile(inp["df_norm_g"][li], 2)[:, None])
        put(("naqn", li), np.tile(inp["na_q_norm"][li], 2)[:, None])
        put(("nakn", li), np.tile(inp["na_k_norm"][li], 2)[:, None])
        put(("glng", li), np.tile(inp["gla_norm_g"][li], 2)[:, None])
        put(("dnng", li), np.tile(inp["dn_norm_g"][li], 2)[:, None])
        z8 = np.zeros(8, np.float32)
        put(("dndtb", li), np.concatenate([z8, np.asarray(inp["dn_dt_bias"][li], np.float32).reshape(8), np.zeros(112, np.float32)])[:, None])
        put(("dnalog", li), np.concatenate([z8, np.asarray(inp["dn_a_log"][li], np.float32).reshape(8), np.zeros(112, np.float32)])[:, None])
        put(("dncw", li), pmajor(inp["dn_conv"][li]))
    p = np.arange(128)
    put("m1", ((p // 32) % 2 == 0).astype(np.float32)[:, None])
    put("m2", ((p // 32) % 2 == 1).astype(np.float32)[:, None])
    put("one", np.ones((128, 1), np.float32))
    put("hm", (p[:, None] // 32 == np.arange(4)[None, :]).astype(np.float32))
    put("dnsgn", np.where(p < 8, -1.0, 1.0).astype(np.float32)[:, None])
    return V


NEG = -30000.0


def const_mats():
    C = np.zeros((4, 128, 128), np.float32)
    p = np.arange(128)
    C[0] = (p[:, None] // 32 == p[None, :] // 32) / 32.0
    C[1] = (p[:, None] // 64 == p[None, :] // 64) / 64.0
    for m in range(128):
        g, d = m // 32, m % 32
        q = d // 8
        if q == 0:
            C[2, g * 32 + d + 8, m] = -1.0
        elif q == 1:
            C[2, g * 32 + d - 8, m] = 1.0
        elif q == 2:
            C[2, g * 32 + d + 8, m] = -1.0
        else:
            C[2, g * 32 + d - 8, m] = 1.0
    C[3] = 1.0
    return np.ascontiguousarray(C.transpose(1, 0, 2).reshape(128, 512))


def rope_tables():
    t = np.arange(L)
    row = (t // 64).astype(np.float32)
    col = (t % 64).astype(np.float32)
    nf = 8
    inv = np.power(np.float32(10000.0), -np.arange(nf, dtype=np.float32) / nf).astype(np.float32)
    ar = row[:, None] * inv
    ac = col[:, None] * inv
    ang = np.concatenate([ar, ar, ac, ac], -1)
    cs = np.stack([np.cos(ang), np.sin(ang)], 0).astype(np.float32)
    return np.ascontiguousarray(np.tile(cs.transpose(0, 2, 1), (1, 4, 1)))


SCM = {}
for _i, _n in enumerate(("I", "U", "NU", "MI", "MS", "MST", "CKD", "CQ", "M01")):
    SCM[_n] = _i
NSCM = len(SCM)


def scan_mats():
    t = np.arange(128)
    same = (t[:, None] // 64) == (t[None, :] // 64)
    out = np.zeros((2, NSCM, 128, 128), np.float32)
    for d in range(2):
        before = (t[:, None] <= t[None, :]) if d == 0 else (t[:, None] >= t[None, :])
        strict = (t[:, None] < t[None, :]) if d == 0 else (t[:, None] > t[None, :])
        U = (same & before).astype(np.float32)
        out[d, SCM["I"]] = np.eye(128, dtype=np.float32)
        out[d, SCM["U"]] = U
        out[d, SCM["NU"]] = -U
        out[d, SCM["MI"]] = np.where(same & before, 0.0, NEG)
        out[d, SCM["MS"]] = np.where(same & strict, 0.0, NEG)
        out[d, SCM["MST"]] = np.where(same & strict, 0.0, NEG).T
        out[d, SCM["CKD"]] = (same & strict.T).astype(np.float32)
        pos = t % 64
        midpos = 31 if d == 0 else 32
        umid = (same & ((pos[:, None] <= midpos) if d == 0 else (pos[:, None] >= midpos))).astype(np.float32)
        out[d, SCM["CQ"]] = U - umid
        out[d, SCM["M01"]] = (same & before).astype(np.float32)
    return np.ascontiguousarray(out.transpose(2, 0, 1, 3).reshape(128, 2 * NSCM * 128))


def na_chunks(rp):
    if rp in (0, 1):
        return [(kc, 5 + rp * 4 + kc) for kc in range(4)]
    if rp in (30, 31):
        return [(28 + j, 13 + (rp - 30) * 4 + j) for j in range(4)]
    return [(rp - 2 + j, j) for j in range(5)]


def na_bias_tables(rpb):
    out = np.full((4, 128, 21, 128), NEG, np.float32)
    kk = np.arange(128)
    for rp in [2, 0, 1, 30, 31]:
        for (kc, bi) in na_chunks(rp):
            qrow = 2 * rp + kk // 64
            qcol = kk % 64
            krow = 2 * kc + kk // 64
            kcol = kk % 64
            rs = np.clip(qrow - 4, 0, 56)
            cst = np.clip(qcol - 8, 0, 48)
            dr = krow[:, None] - qrow[None, :]
            dc = kcol[:, None] - qcol[None, :]
            valid = ((krow[:, None] >= rs[None, :]) & (krow[:, None] < rs[None, :] + 8)
                     & (kcol[:, None] >= cst[None, :]) & (kcol[:, None] < cst[None, :] + 16))
            ri = np.clip(dr + 7, 0, 14)
            ci = np.clip(dc + 15, 0, 30)
            for h in range(4):
                g = rpb[h][ri, ci]
                out[h, :, bi, :] = np.where(valid, g, np.float32(NEG))
    return out


PF_BLOCKS = ([(i * 128, 128) for i in range(8)] + [(1024, 16)] + [(1040 + i * 128, 128) for i in range(6)]
             + [(1808, 128), (1936, 128), (2064, 128), (2192, 128), (2320, 128), (2448, 128), (2576, 16), (2592, 16)]
             + [(2608 + i * 128, 128) for i in range(6)])
NPF = len(PF_BLOCKS)
PT_GROUPS = ((1552, 256, 0), (3120, 256, 256), (1936, 384, 512))
NPT = 896


def build(debug=None, nlayers=DEPTH):
    debug = debug or {}
    nc = bass.Bass("TRN2", target_bir_lowering=False)
    voff, NV = vec_layout()

    def din(name, shape, dt=F32):
        return nc.dram_tensor(name, list(shape), dt, kind="ExternalInput").ap()

    def dscr(name, shape, dt):
        kind = "ExternalOutput" if name in debug.get("dump", ()) else "Internal"
        return nc.dram_tensor(name, list(shape), dt, kind=kind).ap()

    xT_in = din("xT", [D, L])
    cT_in = din("ctxT", [D, CL])
    cs_in = din("cs", [128, 16])
    vecs_in = din("vecs", [128, NV])
    w_mod = din("w_mod", [DEPTH, D, 6 * D])
    w_in = din("w_in", [DEPTH, D, NIN])
    w_branch = din("w_branch", [DEPTH, D, D])
    w_out = din("w_out", [DEPTH, D, D])
    f_w_in = din("ffn_w_in", [DEPTH, D, 2 * DFF])
    f_w_out = din("ffn_w_out", [DEPTH, DFF, D])
    outT = nc.dram_tensor("outT", [D, L], F32, kind="ExternalOutput").ap()
    y_dbg = din("y_dbg", [D, T], BF16) if debug.get("y_in") else None
    mixers = debug.get("mixers", ("dn", "na", "gla", "df"))
    cmat_in = din("cmat", [128, 512])
    rope_in = din("rope", [2, 128, L])
    nab_in = din("nab", [DEPTH, 4, 128, 21, 128])
    dflam_in = din("dflam", [DEPTH, 128])
    scm_in = din("scm", [128, 2 * NSCM * 128])
    gla_w_in = din("gla_w", [DEPTH, 2, 17, 128])

    wi_b = dscr("wi_b", [DEPTH, D, NIN], BF16)
    wb_b = dscr("wb_b", [DEPTH, D, D], BF16)
    wo_b = dscr("wo_b", [DEPTH, D, D], BF16)
    f1_b = dscr("f1_b", [DEPTH, D, 2 * DFF], BF16)
    f2_b = dscr("f2_b", [DEPTH, DFF, D], BF16)
    xbuf = dscr("xbuf", [D, T], F32)
    x1buf = dscr("x1buf", [D, T], F32)
    hbuf = dscr("hbuf", [D, T], BF16)
    ybuf = dscr("ybuf", [D, T], BF16)
    PF = dscr("PF", [NPF * 128, T], F32)
    PT = dscr("PT", [T, NPT], F32)
    xd = nc.dram_tensor("xd", [DEPTH, D, T], F32, kind="ExternalOutput").ap() if debug.get("xdump") else None

    with contextlib.ExitStack() as top:
        S = Sched(nc, top)

        uid = [0]

        def sbuf(st, name, shape, dt):
            uid[0] += 1
            return st.enter_context(nc.sbuf_tensor("sb%d_%s" % (uid[0], name), list(shape), dt))

        PS = [top.enter_context(nc.psum_tensor("ps%d" % i, [128, 512], F32)) for i in range(8)]

        def MM(out, lhsT, rhs, st, sp, r, w):
            S.op("pe", lambda e: e.matmul(out, lhsT=lhsT, rhs=rhs, start=st, stop=sp), reads=r, writes=w)

        def ACT(out, in_, func, r, w, bias=0.0, scale=1.0):
            S.op("act", lambda e: e.activation(out=out, in_=in_, func=func, bias=bias, scale=scale), reads=r, writes=w)

        def TT(out, in0, in1, op, r, w, eng="dve"):
            S.op(eng, lambda e: e.tensor_tensor(out=out, in0=in0, in1=in1, op=op), reads=r, writes=w)

        def TS(out, in0, s1, s2, op0, op1, r, w, eng="dve"):
            S.op(eng, lambda e: e.tensor_scalar(out=out, in0=in0, scalar1=s1, scalar2=s2, op0=op0, op1=op1), reads=r, writes=w)

        def STT(out, in0, scalar, in1, op0, op1, r, w, eng="dve"):
            S.op(eng, lambda e: e.scalar_tensor_tensor(out=out, in0=in0, scalar=scalar, in1=in1, op0=op0, op1=op1), reads=r, writes=w)

        def CP(out, in_, r, w, eng="dve"):
            S.op(eng, lambda e: e.tensor_copy(out=out, in_=in_), reads=r, writes=w)

        def MSET(ap, val, w, eng="pool"):
            S.op(eng, lambda e: e.memset(ap, val), writes=w)

        def DMA(out, in_, r, w, q="sp"):
            S.op(q, lambda e: e.dma_start(out=out, in_=in_), reads=r, writes=w, dma=True)

        vecs = sbuf(top, "vecs", [128, NV], F32)
        modsb = sbuf(top, "modsb", [128, DEPTH, 48, 2], F32)
        gsc = sbuf(top, "gsc", [128, DEPTH, 2, 8, 2], F32)
        ones_b = sbuf(top, "ones_b", [128, 128], BF16)
        DMA(vecs[:], vecs_in[:], [], ["vecs"])
        MSET(ones_b[:], 1.0, ["ones_b"])

        cmat = sbuf(top, "cmat", [128, 512], F32)
        DMA(cmat[:], cmat_in[:], [], ["cmat"])
        BD32 = cmat[:, 0:128]
        BD64 = cmat[:, 128:256]
        RM = cmat[:, 256:384]
        ONESF = cmat[:, 384:512]

        scm = sbuf(top, "scm", [128, 2, NSCM, 128], F32)
        DMA(scm[:].rearrange("p a b c -> p (a b c)"), scm_in[:], [], ["scm"])

        def CM(d, name):
            return scm[:, d, SCM[name], :]

        def V(name, j=0, n=1):
            o = voff[name] + j
            return vecs[:, o:o + n]

        def cast2d(dst, src, rows, cols, key):
            for r0 in range(0, rows, 1024):
                r1 = min(rows, r0 + 1024)
                for c0 in range(0, cols, 2048):
                    c1 = min(cols, c0 + 2048)
                    DMA(dst[r0:r1, c0:c1], src[r0:r1, c0:c1], [], [key], q="pool")

        layer_list = list(debug.get("layers", range(nlayers)))

        def cast_layer(li):
            cast2d(wi_b[li], w_in[li], D, NIN, ("wi_b", li))
            cast2d(wb_b[li], w_branch[li], D, D, ("wb_b", li))
            cast2d(wo_b[li], w_out[li], D, D, ("wo_b", li))
            cast2d(f1_b[li], f_w_in[li], D, 2 * DFF, ("f1_b", li))
            cast2d(f2_b[li], f_w_out[li], DFF, D, ("f2_b", li))

        cast_layer(layer_list[0])

        with contextlib.ExitStack() as ph:
            scs = sbuf(ph, "scs", [128, 16], F32)
            wm = [sbuf(ph, "wm%d" % i, [128, 8, 768], F32) for i in range(2)]
            DMA(scs[:], cs_in[:], [], ["scs"])
            ACT(scs[:], scs[:], AF.Silu, ["scs"], ["scs"])
            nb = 0
            for li in layer_list:
                wv = w_mod[li].rearrange("(k p) c -> p k c", p=128)
                for cb in range(8):
                    wt = wm[nb % 2]
                    wk = ("wm", nb % 2)
                    nb += 1
                    DMA(wt[:], wv[:, :, cb * 768:(cb + 1) * 768], [], [wk])
                    for jj in range(6):
                        j = cb * 6 + jj
                        for k in range(8):
                            MM(PS[0][:, 2 * j:2 * j + 2], wt[:, k, jj * 128:(jj + 1) * 128], scs[:, 2 * k:2 * k + 2],
                               k == 0, k == 7, [wk, "scs"], ["ps0"])
                TT(modsb[:, li], PS[0][:, 0:96].rearrange("p (j s) -> p j s", s=2),
                   V(("bmod", li), 0, 48).rearrange("p (j o) -> p j o", o=1).to_broadcast([128, 48, 2]), ALU.add,
                   ["ps0", "vecs"], ["modsb"])
                for which, (so, gname) in enumerate(((8, "n1g"), (32, "n2g"))):
                    TS(gsc[:, li, which], modsb[:, li, so:so + 8, :], 1.0, 0.0, ALU.add, ALU.add, ["modsb"], ["gsc"])
                    TT(gsc[:, li, which], gsc[:, li, which],
                       V((gname, li), 0, 8).rearrange("p (j o) -> p j o", o=1).to_broadcast([128, 8, 2]), ALU.mult,
                       ["gsc", "vecs"], ["gsc"])
            S.flush()

        def modv(li, what, k, s):
            base = {"sh1": 0, "sc1": 8, "g1": 16, "sh2": 24, "sc2": 32, "g2": 40}[what]
            return modsb[:, li, base + k, s:s + 1]

        def norm_tile(xt, ht, W, li, which, s, tmp_sq, tmp_r, tmp_f, psb, kx, kh, tag):
            shn = "sh1" if which == 0 else "sh2"
            for k in range(8):
                ACT(tmp_sq[:, 0:W], xt[:, k, 0:W], AF.Square, [kx], [tag + "sq"])
                MM(PS[psb][:, 0:W], ones_b[:], tmp_sq[:, 0:W], k == 0, k == 7, ["ones_b", tag + "sq"], ["ps%d" % psb])
            ACT(tmp_r[:, 0:W], PS[psb][:, 0:W], AF.Ln, ["ps%d" % psb], [tag + "r"], bias=EPS, scale=1.0 / D)
            ACT(tmp_r[:, 0:W], tmp_r[:, 0:W], AF.Exp, [tag + "r"], [tag + "r"], scale=-0.5)
            for k in range(8):
                STT(tmp_f[:, 0:W], xt[:, k, 0:W], gsc[:, li, which, k, s:s + 1], tmp_r[:, 0:W], ALU.mult, ALU.mult,
                    [kx, "gsc", tag + "r"], [tag + "f"])
                ACT(ht[:, k, 0:W], tmp_f[:, 0:W], AF.Identity, [tag + "f", "modsb"], [kh], bias=modv(li, shn, k, s))

        for li in layer_list:
            with_ctx = li < DEPTH - 1
            with contextlib.ExitStack() as ph:
                wi = sbuf(ph, "wi", [128, 8, NMIX], BF16)
                xt = [sbuf(ph, "xt%d" % i, [128, 8, 512], F32) for i in range(2)]
                ht = [sbuf(ph, "ht%d" % i, [128, 8, 512], BF16) for i in range(2)]
                sq = sbuf(ph, "sq", [128, 512], BF16)
                rr = sbuf(ph, "rr", [128, 512], F32)
                ff = sbuf(ph, "ff", [128, 512], F32)
                stg = [sbuf(ph, "stg%d" % i, [128, 512], F32) for i in range(4)]
                for k in range(8):
                    DMA(wi[:, k, :], wi_b[li][k * 128:(k + 1) * 128, 0:NMIX], [("wi_b", li)], [("wi", k)])
                wik = [("wi", k) for k in range(8)]
                nst = 0
                tiles1 = []
                for (sname, s0, slen) in SEQS:
                    for t0 in range(s0, s0 + slen, 512):
                        tiles1.append((0 if sname == "lat" else 1, t0, min(512, s0 + slen - t0)))

                def p1_norm(i):
                    (s, t0, W) = tiles1[i]
                    b = i % 2
                    if li == layer_list[0]:
                        src = xT_in[:, t0:t0 + W] if s == 0 else cT_in[:, t0 - L:t0 - L + W]
                    else:
                        src = xbuf[:, t0:t0 + W]
                    DMA(xt[b][:, :, 0:W], src.rearrange("(k p) t -> p k t", p=128), ["xbuf"], [("xt", b)])
                    norm_tile(xt[b], ht[b], W, li, 0, s, sq, rr, ff, 0, ("xt", b), ("ht", b), "n1")
                    DMA(hbuf[:, t0:t0 + W].rearrange("(k p) t -> p k t", p=128), ht[b][:, :, 0:W], [("ht", b)], ["hbuf"])

                p1_norm(0)
                for ti in range(len(tiles1)):
                    if True:
                        (s, t0, W) = tiles1[ti]
                        b = ti % 2
                        if ti + 1 < len(tiles1):
                            p1_norm(ti + 1)
                        for bi, (c0, ncol) in enumerate(PF_BLOCKS):
                            pb = 1 + (bi % 4)
                            for k in range(8):
                                MM(PS[pb][0:ncol, 0:W], wi[:, k, c0:c0 + ncol], ht[b][:, k, 0:W], k == 0, k == 7,
                                   [("wi", k), ("ht", b)], ["ps%d" % pb])
                            sg = nst % 4
                            nst += 1
                            if bi % 2 == 0:
                                ACT(stg[sg][0:ncol, 0:W], PS[pb][0:ncol, 0:W], AF.Copy, ["ps%d" % pb], [("stg", sg)])
                            else:
                                CP(stg[sg][0:ncol, 0:W], PS[pb][0:ncol, 0:W], ["ps%d" % pb], [("stg", sg)])
                            DMA(PF[bi * 128:bi * 128 + ncol, t0:t0 + W], stg[sg][0:ncol, 0:W], [("stg", sg)], [("PF", bi)])
                        for q in range(W // 128):
                            for gi, (c0, ncol, p0) in enumerate(PT_GROUPS):
                                pb = 5 if gi < 2 else 7
                                pcol = 256 if gi == 1 else 0
                                for k in range(8):
                                    MM(PS[pb][:, pcol:pcol + ncol], ht[b][:, k, q * 128:(q + 1) * 128], wi[:, k, c0:c0 + ncol],
                                       k == 0, k == 7, [("wi", k), ("ht", b)], ["ps%d" % pb])
                            for pb, p0, ncol in ((5, 0, 512), (7, 512, 384)):
                                sg = nst % 4
                                nst += 1
                                CP(stg[sg][:, 0:ncol], PS[pb][:, 0:ncol], ["ps%d" % pb], [("stg", sg)])
                                DMA(PT[t0 + q * 128:t0 + (q + 1) * 128, p0:p0 + ncol], stg[sg][:, 0:ncol], [("stg", sg)], [("PT", p0)])
                S.flush()

            if y_dbg is not None:
                with contextlib.ExitStack() as ph:
                    yt = sbuf(ph, "ycp", [128, 8, 512], BF16)
                    for t0 in range(0, T, 512):
                        W = min(512, T - t0)
                        DMA(yt[:, :, 0:W], y_dbg[:, t0:t0 + W].rearrange("(k p) t -> p k t", p=128), [], ["ycp"])
                        DMA(ybuf[:, t0:t0 + W].rearrange("(k p) t -> p k t", p=128), yt[:, :, 0:W], ["ycp"], ["ybuf"])
                    S.flush()


            BLK = {"dfq": 23, "dfk": 25, "naq": 9, "nak": 11}

            def qk_prep(ph, specs):
                px = [sbuf(ph, "px%d" % i, [128, 512], F32) for i in range(2)]
                psq = sbuf(ph, "psq", [128, 512], F32)
                prs = sbuf(ph, "prs", [128, 512], F32)
                pxn = sbuf(ph, "pxn", [128, 512], F32)
                pa = sbuf(ph, "ppa", [128, 512], F32)
                pb_ = sbuf(ph, "ppb", [128, 512], F32)
                rc = sbuf(ph, "rc", [128, 2, 512], F32)
                n = 0
                for t0 in range(0, T, 512):
                    W = min(512, T - t0)
                    lat = t0 < L
                    if lat and any(sp[3] for sp in specs):
                        DMA(rc[:, 0, :], rope_in[0, :, t0:t0 + 512], [], ["rc"])
                        DMA(rc[:, 1, :], rope_in[1, :, t0:t0 + 512], [], ["rc"])
                    for (blk, gm, gcol, rope, outs) in specs:
                        b = n % 2
                        n += 1
                        DMA(px[b][:, 0:W], PF[blk * 128:(blk + 1) * 128, t0:t0 + W], [], [("px", b)])
                        ACT(psq[:, 0:W], px[b][:, 0:W], AF.Square, [("px", b)], ["psq"])
                        MM(PS[0][:, 0:W], gm, psq[:, 0:W], True, True, ["cmat", "psq"], ["ps0"])
                        ACT(prs[:, 0:W], PS[0][:, 0:W], AF.Ln, ["ps0"], ["prs"], bias=EPS)
                        ACT(prs[:, 0:W], prs[:, 0:W], AF.Exp, ["prs"], ["prs"], scale=-0.5)
                        STT(pxn[:, 0:W], px[b][:, 0:W], gcol, prs[:, 0:W], ALU.mult, ALU.mult, [("px", b), "vecs", "prs"], ["pxn"])
                        val = pxn
                        vk = "pxn"
                        if rope and lat:
                            MM(PS[1][:, 0:W], RM, pxn[:, 0:W], True, True, ["cmat", "pxn"], ["ps1"])
                            TT(pa[:, 0:W], pxn[:, 0:W], rc[:, 0, 0:W], ALU.mult, ["pxn", "rc"], ["ppa"], eng="pool")
                            TT(pb_[:, 0:W], PS[1][:, 0:W], rc[:, 1, 0:W], ALU.mult, ["ps1", "rc"], ["ppb"])
                            TT(pa[:, 0:W], pa[:, 0:W], pb_[:, 0:W], ALU.add, ["ppa", "ppb"], ["ppa"], eng="pool")
                            val = pa
                            vk = "ppa"
                        for (dst, dk, mcol) in outs:
                            TS(dst[:, t0:t0 + W], val[:, 0:W], mcol, 0.0, ALU.mult, ALU.add, [vk, "vecs"], [dk])

            def load_vaug(ph, name, pcol):
                va = sbuf(ph, name, [128, 34, 4, 65], BF16)
                vst = [sbuf(ph, name + "st%d" % i, [128, 256], F32) for i in range(2)]
                MSET(va[:, :, :, 64:65], 1.0, [name])
                for kc in range(34):
                    b = kc % 2
                    DMA(vst[b][:], PT[kc * 128:(kc + 1) * 128, pcol:pcol + 256], [], [(name + "st", b)])
                    CP(va[:, kc, :, 0:64], vst[b][:].rearrange("p (h d) -> p h d", h=4), [(name + "st", b)], [name],
                       eng=("dve" if kc % 2 else "pool"))
                return va

            def attn_norm(ph_tiles, Obank, W, okey):
                osb, rrow, onrm = ph_tiles
                ACT(osb[0:65, 0:W], PS[Obank][0:65, 0:W], AF.Copy, ["ps%d" % Obank], ["osb"])
                S.op("dve", lambda e: e.reciprocal(out=rrow[64:65, 0:W], in_=osb[64:65, 0:W]), reads=["osb"], writes=["rrow"])
                MM(PS[7][0:64, 0:W], ONESF[64:65, 0:64], rrow[64:65, 0:W], True, True, ["cmat", "rrow"], ["ps7"])
                TT(onrm[0:64, 0:W], osb[0:64, 0:W], PS[7][0:64, 0:W], ALU.mult, ["osb", "ps7"], [okey])


            def head_norm_gate(ph, osum, okey, gate_blk, gcol, yrow0, perm=False):
                gx = [sbuf(ph, "hg_x%d" % i, [128, 512], F32) for i in range(2)]
                gs = sbuf(ph, "hg_s", [128, 512], F32)
                gr = sbuf(ph, "hg_r", [128, 512], F32)
                gy = [sbuf(ph, "hg_y%d" % i, [128, 512], BF16) for i in range(2)]
                n = 0
                for t0 in range(0, T, 512):
                    W = min(512, T - t0)
                    for hp in range(2):
                        b = n % 2
                        n += 1
                        if perm:
                            for j in range(2):
                                g0 = (gate_blk + j) * 128 + hp * 64
                                DMA(gx[b][j * 64:(j + 1) * 64, 0:W], PF[g0:g0 + 64, t0:t0 + W], [], [("hgx", b)])
                        else:
                            DMA(gx[b][:, 0:W], PF[(gate_blk + hp) * 128:(gate_blk + hp + 1) * 128, t0:t0 + W], [], [("hgx", b)])
                        ACT(gx[b][:, 0:W], gx[b][:, 0:W], AF.Silu, [("hgx", b)], [("hgx", b)])
                        ACT(gs[:, 0:W], osum[:, hp, t0:t0 + W], AF.Square, [okey], ["hgs"])
                        MM(PS[0][:, 0:W], BD64, gs[:, 0:W], True, True, ["cmat", "hgs"], ["ps0"])
                        ACT(gr[:, 0:W], PS[0][:, 0:W], AF.Ln, ["ps0"], ["hgr"], bias=EPS)
                        ACT(gr[:, 0:W], gr[:, 0:W], AF.Exp, ["hgr"], ["hgr"], scale=-0.5)
                        STT(gs[:, 0:W], osum[:, hp, t0:t0 + W], gcol, gr[:, 0:W], ALU.mult, ALU.mult, [okey, "vecs", "hgr"], ["hgs"])
                        TT(gy[b][:, 0:W], gs[:, 0:W], gx[b][:, 0:W], ALU.mult, ["hgs", ("hgx", b)], [("hgy", b)])
                        if perm:
                            for j in range(2):
                                y0 = yrow0 + (hp + 2 * j) * 64
                                DMA(ybuf[y0:y0 + 64, t0:t0 + W], gy[b][j * 64:(j + 1) * 64, 0:W], [("hgy", b)], [("ybuf", yrow0, j)])
                        else:
                            DMA(ybuf[yrow0 + hp * 128:yrow0 + (hp + 1) * 128, t0:t0 + W], gy[b][:, 0:W], [("hgy", b)], [("ybuf", yrow0)])

            def scan_blocks(d):
                cb = [L + 0, L + 128] if d == 0 else [L + 128, L + 0]
                lb_ = list(range(0, L, 128)) if d == 0 else list(range(L - 128, -1, -128))
                return cb + lb_


            if "dn" in mixers:
                with contextlib.ExitStack() as ph:
                    osum = sbuf(ph, "dno", [128, 2, T], F32)
                    qT = sbuf(ph, "dnq", [128, 2, T], BF16)
                    kT = sbuf(ph, "dnk", [128, 2, T], BF16)
                    vT = sbuf(ph, "dnv", [128, 2, T], BF16)
                    rowsT = sbuf(ph, "dnrows", [16, T], F32)
                    coef = sbuf(ph, "dncoef", [16, 1], F32)
                    I16 = sbuf(ph, "dnI16", [128, 128], BF16)
                    CP(I16[:], CM(0, "I"), ["scm"], ["dnI16"])
                    ACT(coef[:], V(("dnalog", li))[0:16, :], AF.Exp, ["vecs"], ["dncoef"])
                    TS(coef[:], coef[:], -1.0, 0.0, ALU.mult, ALU.add, ["dncoef"], ["dncoef"])
                    DMA(rowsT[:], PF[8 * 128:8 * 128 + 16, :], [], ["dnrows"])
                    for c0 in range(0, T, 1088):
                        sl = rowsT[:, c0:c0 + 1088]
                        ACT(sl, sl, AF.Exp, ["dnrows", "vecs"], ["dnrows"], bias=V(("dndtb", li))[0:16, :], scale=V("dnsgn")[0:16, :])
                        ACT(sl, sl, AF.Ln, ["dnrows"], ["dnrows"], bias=1.0)
                        TS(sl, sl, coef[:, 0:1], 0.0, ALU.mult, ALU.add, ["dnrows", "dncoef"], ["dnrows"])
                    with contextlib.ExitStack() as ph2:
                        xs = [sbuf(ph2, "dnxs%d" % i, [128, 514], F32) for i in range(2)]
                        ca = sbuf(ph2, "dnca", [128, 512], F32)
                        cb2 = sbuf(ph2, "dncb", [128, 512], F32)
                        sq2 = sbuf(ph2, "dnsq", [128, 512], F32)
                        n = 0
                        cw = voff[("dncw", li)]
                        for (sname, s0, slen) in SEQS:
                            for t0 in range(s0, s0 + slen, 512):
                                W = min(512, s0 + slen - t0)
                                a0 = max(t0 - 1, s0)
                                a1_ = min(t0 + W + 1, s0 + slen)
                                for blk in range(6):
                                    b = n % 2
                                    n += 1
                                    off0 = a0 - t0 + 1
                                    DMA(xs[b][:, off0:off0 + (a1_ - a0)], PF[blk * 128:(blk + 1) * 128, a0:a1_], [], [("dnxs", b)])
                                    if t0 == s0:
                                        MSET(xs[b][:, 0:1], 0.0, [("dnxs", b)])
                                    if t0 + W == s0 + slen:
                                        MSET(xs[b][:, W + 1:W + 2], 0.0, [("dnxs", b)])
                                    TS(ca[:, 0:W], xs[b][:, 1:1 + W], vecs[:, cw + 6 + blk:cw + 7 + blk], 0.0, ALU.mult, ALU.add, [("dnxs", b), "vecs"], ["dnca"])
                                    STT(cb2[:, 0:W], xs[b][:, 0:W], vecs[:, cw + blk:cw + blk + 1], ca[:, 0:W], ALU.mult, ALU.add, [("dnxs", b), "vecs", "dnca"], ["dncb"])
                                    STT(ca[:, 0:W], xs[b][:, 2:2 + W], vecs[:, cw + 12 + blk:cw + 13 + blk], cb2[:, 0:W], ALU.mult, ALU.add, [("dnxs", b), "vecs", "dncb"], ["dnca"])
                                    ACT(cb2[:, 0:W], ca[:, 0:W], AF.Silu, ["dnca"], ["dncb"])
                                    if blk >= 4:
                                        CP(vT[:, blk - 4, t0:t0 + W], cb2[:, 0:W], ["dncb"], [("dnv", blk - 4)], eng="pool")
                                        continue
                                    ACT(sq2[:, 0:W], cb2[:, 0:W], AF.Square, ["dncb"], ["dnsq"])
                                    MM(PS[0][:, 0:W], BD64, sq2[:, 0:W], True, True, ["cmat", "dnsq"], ["ps0"])
                                    ACT(sq2[:, 0:W], PS[0][:, 0:W], AF.Ln, ["ps0"], ["dnsq"], bias=EPS / 64)
                                    ACT(sq2[:, 0:W], sq2[:, 0:W], AF.Exp, ["dnsq"], ["dnsq"], scale=-0.5)
                                    if blk < 2:
                                        STT(qT[:, blk, t0:t0 + W], cb2[:, 0:W], 1.0 / 64, sq2[:, 0:W], ALU.mult, ALU.mult, ["dncb", "dnsq"], [("dnq", blk)])
                                    else:
                                        STT(kT[:, blk - 2, t0:t0 + W], cb2[:, 0:W], 1.0 / 8, sq2[:, 0:W], ALU.mult, ALU.mult, ["dncb", "dnsq"], [("dnk", blk - 2)])
                        S.flush()
                    nxt = layer_list.index(li) + 1
                    if nxt < len(layer_list):
                        cast_layer(layer_list[nxt])
                    rt = sbuf(ph, "dnrt", [128, 16], F32)
                    gB4 = sbuf(ph, "dngB", [128, 4, 128], F32)
                    lB4 = sbuf(ph, "dnlB", [128, 4, 128], F32)
                    gH = sbuf(ph, "dngH", [128, 4, 128], BF16)
                    gL = sbuf(ph, "dngL", [128, 4, 128], BF16)
                    lH = sbuf(ph, "dnlH", [128, 4, 128], BF16)
                    lL = sbuf(ph, "dnlL", [128, 4, 128], BF16)
                    scm16 = sbuf(ph, "dnscm16", [128, 2, NSCM, 128], BF16)
                    CP(scm16[:].rearrange("p a b c -> p (a b c)"), scm[:].rearrange("p a b c -> p (a b c)"), ["scm"], ["scm16"])

                    def C16(d_, name):
                        return scm16[:, d_, SCM[name], :]
                    ekd = sbuf(ph, "dnekd", [128, 16], F32)
                    ebt = sbuf(ph, "dnebt", [128, 8], F32)
                    kv = sbuf(ph, "dnkv", [128, 512], F32)
                    E5 = sbuf(ph, "dnE5", [128, 5, 4, 128], F32)
                    Pb = [sbuf(ph, "dnP%d" % i, [128, 4, 128], F32) for i in range(2)]
                    Qb = [sbuf(ph, "dnQ%d" % i, [128, 4, 128], F32) for i in range(2)]
                    X = sbuf(ph, "dnX", [128, 4, 128], F32)
                    aqk = sbuf(ph, "dnaqk", [128, 4, 128], BF16)
                    kbe = sbuf(ph, "dnkbe", [128, 2, 128], BF16)
                    qd = sbuf(ph, "dnqd", [128, 2, 128], BF16)
                    vb = sbuf(ph, "dnvb", [128, 4, 64], F32)
                    kdc = sbuf(ph, "dnkdc", [128, 4, 64], BF16)
                    vnZ = [sbuf(ph, "dnvn%d" % i, [128, 4, 64], BF16) for i in range(2)]
                    for i in range(2):
                        MSET(vnZ[i][:], 0.0, [("dnvn", i)], eng="dve")
                    rF = [sbuf(ph, "dnrF%d" % i, [128, 4, 64], F32) for i in range(2)]
                    Sf = sbuf(ph, "dnS", [128, 4, 64], F32)
                    S16 = sbuf(ph, "dnS16", [128, 4, 64], BF16)
                    for i in range(2):
                        MSET(rF[i][:], 0.0, [("dnrF", i)], eng="dve")
                    Ibc = CM(0, "I").rearrange("p (o c) -> p o c", o=1).to_broadcast([128, 4, 128])

                    def flat(ap):
                        return ap.rearrange("p h c -> p (h c)")

                    for d in range(2):
                        MSET(Sf[:], 0.0, ["dnS"], eng="dve")
                        MSET(S16[:], 0.0, ["dnS16"], eng="dve")
                        for t0 in scan_blocks(d)[:debug.get("dn_nblk", 100)]:
                            MM(PS[0][:, 0:16], rowsT[:, t0:t0 + 128], CM(0, "I")[0:16, 0:16], True, True, ["dnrows", "scm"], ["ps0"])
                            CP(rt[:], PS[0][:, 0:16], ["ps0"], ["dnrt"])
                            MM(PS[0][:, 16:32], CM(d, "CKD"), rt[:], True, True, ["scm", "dnrt"], ["ps0"])
                            ACT(ekd[:], PS[0][:, 16:32], AF.Exp, ["ps0"], ["dnekd"])
                            ACT(ebt[:], rt[:, 0:8], AF.Exp, ["dnrt"], ["dnebt"])
                            for i2 in range(2):
                                MM(PS[1][:, i2 * 128:(i2 + 1) * 128], kT[:, i2, t0:t0 + 128], I16[:], True, True, [("dnk", i2), "dnI16"], ["ps1"])
                                MM(PS[1][:, 256 + i2 * 128:256 + (i2 + 1) * 128], vT[:, i2, t0:t0 + 128], I16[:], True, True, [("dnv", i2), "dnI16"], ["ps1"])
                            CP(kv[:], PS[1][:], ["ps1"], ["dnkv"])
                            HP = [0, 2, 1, 3]
                            for par in range(2):
                                gsrc = rt[:, 8 + d * 4:12 + d * 4].rearrange("p (j q) -> p q j", q=2)[:, par, :]
                                lsrc = rt[:, d * 4:d * 4 + 4].rearrange("p (j q) -> p q j", q=2)[:, par, :]
                                CP(gB4[:, 2 * par:2 * par + 2, :], gsrc.rearrange("p (j o) -> p j o", o=1).to_broadcast([128, 2, 128]), ["dnrt"], ["dngB"])
                                CP(lB4[:, 2 * par:2 * par + 2, :], lsrc.rearrange("p (j o) -> p j o", o=1).to_broadcast([128, 2, 128]), ["dnrt"], ["dnlB"])
                            CP(gH[:], gB4[:], ["dngB"], ["dngH"])
                            TT(gL[:], gB4[:], gH[:], ALU.subtract, ["dngB", "dngH"], ["dngL"])
                            CP(lH[:], lB4[:], ["dnlB"], ["dnlH"])
                            TT(lL[:], lB4[:], lH[:], ALU.subtract, ["dnlB", "dnlH"], ["dnlL"])
                            kr = ["dngH", "dngL", "dnlH", "dnlL", "scm16"]
                            order = (0, 1) if d == 0 else (1, 0)
                            for ty in range(5):
                                pb = 2 + ty % 2
                                pk = "ps%d" % pb
                                for pos in range(4):
                                    o_ = PS[pb][:, pos * 128:(pos + 1) * 128]
                                    U_, NU_, I_ = C16(d, "U"), C16(d, "NU"), C16(d, "I")
                                    seq = []
                                    for gx_ in (gH[:, pos, :], gL[:, pos, :]):
                                        if ty == 2:
                                            seq += [(U_, gx_), (gx_, NU_)]
                                        elif ty in (0, 1):
                                            seq += [(gx_, U_), (NU_, gx_)]
                                        else:
                                            seq += [(gx_, U_)]
                                    for lx_ in (lH[:, pos, :], lL[:, pos, :]):
                                        if ty in (1, 4):
                                            seq += [(lx_, I_)]
                                        elif ty == 2:
                                            seq += [(I_, lx_)]
                                    if ty == 0:
                                        seq += [(I_, C16(d, "MI"))]
                                    elif ty == 1:
                                        seq += [(I_, C16(d, "MS"))]
                                    elif ty == 2:
                                        seq += [(I_, C16(d, "MST"))]
                                    for si, (la_, ra_) in enumerate(seq):
                                        MM(o_, la_, ra_, si == 0, si == len(seq) - 1, kr, [pk])
                                ACT(flat(E5[:, ty]), PS[pb][:], AF.Exp, [pk], [("dnE5", ty)])
                            for par in range(2):
                                r0 = par * 64
                                for j in range(2):
                                    kTh = kT[r0:r0 + 64, j, t0:t0 + 128]
                                    MM(PS[par][:, j * 128:(j + 1) * 128], kTh, kTh, True, True, [("dnk", j)], ["ps%d" % par])
                                    MM(PS[par][:, 256 + j * 128:256 + (j + 1) * 128], kTh, qT[r0:r0 + 64, j, t0:t0 + 128], True, True,
                                       [("dnk", j), ("dnq", j)], ["ps%d" % par])
                            for par in range(2):
                                sl = slice(2 * par, 2 * par + 2)
                                pk = "ps%d" % par
                                STT(flat(Qb[0][:, sl, :]), PS[par][:, 0:256], -1.0, flat(E5[:, 1, sl, :]), ALU.mult, ALU.mult, [pk, ("dnE5", 1)], [("dnQ", 0)])
                                STT(flat(Pb[0][:, sl, :]), PS[par][:, 0:256], -1.0, flat(E5[:, 2, sl, :]), ALU.mult, ALU.mult, [pk, ("dnE5", 2)], [("dnP", 0)])
                                TT(flat(aqk[:, sl, :]), PS[par][:, 256:512], flat(E5[:, 0, sl, :]), ALU.mult, [pk, ("dnE5", 0)], ["dnaqk"])
                            TT(X[:], Qb[0][:], Ibc, ALU.add, [("dnQ", 0), "scm"], ["dnX"])
                            for lvl in range(5):
                                a_, bn = lvl % 2, (lvl + 1) % 2
                                for pos in range(4):
                                    MM(PS[4][:, pos * 128:(pos + 1) * 128], Qb[a_][:, pos, :], Pb[a_][:, pos, :], True, True, [("dnQ", a_), ("dnP", a_)], ["ps4"])
                                ACT(flat(Pb[bn][:]), PS[4][:], AF.Copy, ["ps4"], [("dnP", bn)])
                                if lvl < 4:
                                    for pos in range(4):
                                        MM(PS[0][:, pos * 128:(pos + 1) * 128], Pb[a_][:, pos, :], Qb[a_][:, pos, :], True, True, [("dnQ", a_), ("dnP", a_)], ["ps0"])
                                    CP(flat(Qb[bn][:]), PS[0][:], ["ps0"], [("dnQ", bn)])
                                for pos in range(4):
                                    MM(PS[1][:, pos * 128:(pos + 1) * 128], Pb[bn][:, pos, :], X[:, pos, :], True, True, [("dnP", bn), "dnX"], ["ps1"])
                                TT(flat(X[:]), flat(X[:]), PS[1][:], ALU.add, ["dnX", "ps1"], ["dnX"])
                            for pos in range(4):
                                par, j = pos // 2, pos % 2
                                r0 = par * 64
                                TT(kbe[r0:r0 + 64, j, :], kT[r0:r0 + 64, j, t0:t0 + 128], E5[r0:r0 + 64, 4, pos, :], ALU.mult,
                                   [("dnk", j), ("dnE5", 4)], ["dnkbe"])
                                TT(qd[r0:r0 + 64, j, :], qT[r0:r0 + 64, j, t0:t0 + 128], E5[r0:r0 + 64, 3, pos, :], ALU.mult,
                                   [("dnq", j), ("dnE5", 3)], ["dnqd"])
                            for par in range(2):
                                sl = slice(2 * par, 2 * par + 2)
                                bsrc = ebt[:, d * 4:d * 4 + 4].rearrange("p (j q) -> p q j", q=2)[:, par, :]
                                esrc = ekd[:, 8 + d * 4:12 + d * 4].rearrange("p (j q) -> p q j", q=2)[:, par, :]
                                vsrc = kv[:, 256:512].rearrange("p (j q v) -> p q j v", q=2, v=64)[:, par]
                                ksrc = kv[:, 0:256].rearrange("p (j q v) -> p q j v", q=2, v=64)[:, par]
                                TT(vb[:, sl, :], vsrc, bsrc.rearrange("p (j o) -> p j o", o=1).to_broadcast([128, 2, 64]), ALU.mult, ["dnkv", "dnebt"], ["dnvb"])
                                TT(kdc[:, sl, :], ksrc, esrc.rearrange("p (j o) -> p j o", o=1).to_broadcast([128, 2, 64]), ALU.mult, ["dnkv", "dnekd"], ["dnkdc"])
                            for i in order:
                                c0 = i * 64
                                for pos in range(4):
                                    par, j = pos // 2, pos % 2
                                    r0 = par * 64
                                    pbk = 5 - par
                                    MM(PS[pbk][c0:c0 + 64, j * 64:(j + 1) * 64], kbe[r0:r0 + 64, j, c0:c0 + 64], S16[r0:r0 + 64, pos, :], True, True,
                                       ["dnkbe", "dnS16"], ["ps%d" % pbk])
                                for par in range(2):
                                    sl = slice(2 * par, 2 * par + 2)
                                    pbk = 5 - par
                                    TT(rF[i][c0:c0 + 64, sl, :], vb[c0:c0 + 64, sl, :], PS[pbk][c0:c0 + 64, 0:128].rearrange("p (h v) -> p h v", h=2),
                                       ALU.subtract, ["dnvb", "ps%d" % pbk], [("dnrF", i)])
                                for pos in range(4):
                                    MM(PS[2][:, pos * 64:(pos + 1) * 64], X[:, pos, :], rF[i][:, pos, :], True, True, ["dnX", ("dnrF", i)], ["ps2"])
                                ACT(vnZ[i][c0:c0 + 64, :, :], PS[2][c0:c0 + 64, 0:256].rearrange("p (h v) -> p h v", h=4), AF.Copy, ["ps2"], [("dnvn", i)])
                                for pos in range(4):
                                    par, j = pos // 2, pos % 2
                                    r0 = par * 64
                                    pO = 6 + par
                                    MM(PS[pO][j * 64:(j + 1) * 64, c0:c0 + 64], S16[r0:r0 + 64, pos, :], qd[r0:r0 + 64, j, c0:c0 + 64], True, False,
                                       ["dnS16", "dnqd"], ["ps%d" % pO])
                                    MM(PS[pO][j * 64:(j + 1) * 64, c0:c0 + 64], vnZ[i][:, pos, :], aqk[:, pos, c0:c0 + 64], False, True,
                                       [("dnvn", i), "dnaqk"], ["ps%d" % pO])
                                for pos in range(4):
                                    r0 = (pos // 2) * 64
                                    MM(PS[3][r0:r0 + 64, pos * 64:(pos + 1) * 64], kdc[c0:c0 + 64, pos, :], vnZ[i][c0:c0 + 64, pos, :], True, True,
                                       ["dnkdc", ("dnvn", i)], ["ps3"])
                                last = c0 + 63 if d == 0 else c0
                                TT(Sf[:], Sf[:], E5[:, 3, :, last:last + 1].to_broadcast([128, 4, 64]), ALU.mult, ["dnS", ("dnE5", 3)], ["dnS"])
                                TT(Sf[:], Sf[:], PS[3][:, 0:256].rearrange("p (h v) -> p h v", h=4), ALU.add, ["dnS", "ps3"], ["dnS"])
                                ACT(S16[:], Sf[:], AF.Copy, ["dnS"], ["dnS16"])
                            for hp in range(2):
                                if d == 0:
                                    ACT(osum[:, hp, t0:t0 + 128], PS[6 + hp][:, 0:128], AF.Copy, ["ps%d" % (6 + hp)], ["dno"])
                                else:
                                    TT(osum[:, hp, t0:t0 + 128], osum[:, hp, t0:t0 + 128], PS[6 + hp][:, 0:128], ALU.add, ["dno", "ps%d" % (6 + hp)], ["dno"])
                    head_norm_gate(ph, osum, "dno", 6, V(("dnng", li)), 0, perm=True)
                    S.flush()

            if "gla" in mixers:
                with contextlib.ExitStack() as ph:
                    osum = sbuf(ph, "glo", [128, 2, T], F32)
                    qT = sbuf(ph, "glq", [128, T], F32)
                    kT = sbuf(ph, "glk", [128, T], F32)
                    a1 = [sbuf(ph, "gla1_%d" % i, [17, T], F32) for i in range(2)]
                    wa = sbuf(ph, "glwa", [17, 2, 128], F32)
                    DMA(qT[:], PF[15 * 128:16 * 128, :], [], ["glq"])
                    DMA(kT[:], PF[16 * 128:17 * 128, :], [], ["glk"])
                    for d in range(2):
                        MSET(a1[d][:], 1.0, [("gla1", d)])
                        DMA(a1[d][0:16, :], PF[(21 + d) * 128:(21 + d) * 128 + 16, :], [], [("gla1", d)])
                        DMA(wa[:, d, :], gla_w_in[li, d], [], ["glwa"])
                    ktok = [sbuf(ph, "glkt%d" % i, [128, 384], F32) for i in range(2)]
                    vb16 = [sbuf(ph, "glvb%d" % i, [128, 256], BF16) for i in range(2)]
                    ln_ = sbuf(ph, "glln", [128, 128], F32)
                    eq = sbuf(ph, "gleq", [128, 128], F32)
                    ek = sbuf(ph, "glek", [128, 128], F32)
                    eb = sbuf(ph, "gleb", [128, 128], F32)
                    ekd = sbuf(ph, "glekd", [128, 128], F32)
                    qt_ = sbuf(ph, "glqt", [128, 128], F32)
                    ktl = sbuf(ph, "glktl", [128, 128], BF16)
                    qb = sbuf(ph, "glqb", [128, 128], F32)
                    qth = sbuf(ph, "glqth", [128, 4, 128], BF16)
                    qbh = sbuf(ph, "glqbh", [128, 4, 128], BF16)
                    kdec = sbuf(ph, "glkdec", [128, 128], BF16)
                    S16 = sbuf(ph, "glS16", [128, 64], BF16)
                    Ah = sbuf(ph, "glA", [128, 4, 128], BF16)
                    dsm = sbuf(ph, "gldsm", [128, 4, 64], F32)
                    dsr = sbuf(ph, "gldsr", [128, 64], F32)
                    Sst = sbuf(ph, "glS", [128, 64], F32)
                    osb = sbuf(ph, "glosb", [128, 2, 128], F32)
                    sc = 32 ** -0.5
                    nb = 0
                    for d in range(2):
                        MSET(Sst[:], 0.0, ["glS"])
                        for t0 in scan_blocks(d)[:debug.get("gla_nblk", 100)]:
                            b = nb % 2
                            nb += 1
                            stage = debug.get("gla_stage", 99)
                            DMA(ktok[b][:], PT[t0:t0 + 128, 512:896], [], [("glkt", b)])
                            CP(vb16[b][:], ktok[b][:, 128:384], [("glkt", b)], [("glvb", b)], eng="pool")
                            MM(PS[0][:, 0:128], a1[d][:, t0:t0 + 128], wa[:, d, :], True, True, [("gla1", d), "glwa"], ["ps0"])
                            ACT(ln_[:], PS[0][:, 0:128], AF.Exp, ["ps0"], ["glln"], scale=-1.0)
                            ACT(ln_[:], ln_[:], AF.Ln, ["glln"], ["glln"], bias=1.0)
                            if stage < 1:
                                continue
                            MM(PS[1][:, 0:128], ln_[:], CM(d, "CQ"), True, True, ["glln", "scm"], ["ps1"])
                            MM(PS[1][:, 128:256], ln_[:], CM(d, "U"), True, True, ["glln", "scm"], ["ps1"])
                            MM(PS[1][:, 256:384], CM(d, "CKD"), ln_[:], True, True, ["glln", "scm"], ["ps1"])
                            ACT(eq[:], PS[1][:, 0:128], AF.Exp, ["ps1"], ["gleq"], scale=-1.0 / 16)
                            ACT(ek[:], PS[1][:, 0:128], AF.Exp, ["ps1"], ["glek"], scale=1.0 / 16)
                            ACT(eb[:], PS[1][:, 128:256], AF.Exp, ["ps1"], ["gleb"], scale=-1.0 / 16)
                            ACT(ekd[:], PS[1][:, 256:384], AF.Exp, ["ps1"], ["glekd"], scale=-1.0 / 16)
                            STT(qt_[:], qT[:, t0:t0 + 128], sc, eq[:], ALU.mult, ALU.mult, ["glq", "gleq"], ["glqt"])
                            TT(ktl[:], kT[:, t0:t0 + 128], ek[:], ALU.mult, ["glk", "glek"], ["glktl"])
                            STT(qb[:], qT[:, t0:t0 + 128], sc, eb[:], ALU.mult, ALU.mult, ["glq", "gleb"], ["glqb"])
                            TT(kdec[:], ktok[b][:, 0:128], ekd[:], ALU.mult, [("glkt", b), "glekd"], ["glkdec"], eng="pool")
                            if stage < 2:
                                continue
                            for h in range(4):
                                TS(qth[:, h, :], qt_[:], V("hm", h), 0.0, ALU.mult, ALU.add, ["glqt", "vecs"], ["glqth"])
                                TS(qbh[:, h, :], qb[:], V("hm", h), 0.0, ALU.mult, ALU.add, ["glqb", "vecs"], ["glqbh"])
                            if stage < 3:
                                continue
                            for h in range(4):
                                MM(PS[2][:, h * 128:(h + 1) * 128], ktl[:], qth[:, h, :], True, True, ["glktl", "glqth"], ["ps2"])
                            TT(Ah[:], PS[2][:].rearrange("p (h c) -> p h c", h=4),
                               CM(d, "M01").rearrange("p (o c) -> p o c", o=1).to_broadcast([128, 4, 128]), ALU.mult, ["ps2", "scm"], ["glA"])
                            order = (0, 1) if d == 0 else (1, 0)
                            if stage < 4:
                                continue
                            for h in range(4):
                                orow = (h % 2) * 64
                                pO = 3 + h // 2
                                MM(PS[pO][orow:orow + 64, 0:128], vb16[b][:, h * 64:(h + 1) * 64], Ah[:, h, :], True, False,
                                   [("glvb", b), "glA"], ["ps%d" % pO])
                            for ii, i in enumerate(order):
                                c0 = i * 64
                                CP(S16[:], Sst[:], ["glS"], ["glS16"], eng="pool")
                                for h in range(4):
                                    orow = (h % 2) * 64
                                    pO = 3 + h // 2
                                    MM(PS[pO][orow:orow + 64, c0:c0 + 64], S16[:, :], qbh[:, h, c0:c0 + 64], False, ii == 1,
                                       ["glS16", "glqbh"], ["ps%d" % pO])
                                MM(PS[5][:, 0:256], kdec[c0:c0 + 64, :], vb16[b][c0:c0 + 64, :], True, True, ["glkdec", ("glvb", b)], ["ps5"])
                                TT(dsm[:], PS[5][:, 0:256].rearrange("p (h v) -> p h v", h=4),
                                   V("hm", 0, 4).rearrange("p (h o) -> p h o", o=1).to_broadcast([128, 4, 64]), ALU.mult, ["ps5", "vecs"], ["gldsm"])
                                S.op("dve", lambda e: e.reduce_sum(out=dsr[:], in_=dsm[:].rearrange("p h v -> p v h"), axis=AX.X),
                                     reads=["gldsm"], writes=["gldsr"])
                                last = c0 + 63 if d == 0 else c0
                                STT(Sst[:], Sst[:], eb[:, last:last + 1], dsr[:], ALU.mult, ALU.add, ["glS", "gleb", "gldsr"], ["glS"])
                            for hp in range(2):
                                if d == 0:
                                    ACT(osum[:, hp, t0:t0 + 128], PS[3 + hp][:, 0:128], AF.Copy, ["ps%d" % (3 + hp)], ["glo"])
                                else:
                                    TT(osum[:, hp, t0:t0 + 128], osum[:, hp, t0:t0 + 128], PS[3 + hp][:, 0:128], ALU.add, ["glo", "ps%d" % (3 + hp)], ["glo"])
                    if debug.get("gla_stage", 99) >= 99:
                        head_norm_gate(ph, osum, "glo", 19, V(("glng", li)), 512)
                    S.flush()

            if "df" in mixers:
                with contextlib.ExitStack() as ph:
                    qr = sbuf(ph, "dfqr", [128, 2, T], BF16)
                    k1z = sbuf(ph, "dfk1", [128, 2, T], BF16)
                    k2z = sbuf(ph, "dfk2", [128, 2, T], BF16)
                    with contextlib.ExitStack() as ph2:
                        specs = []
                        for i in range(2):
                            specs.append((BLK["dfq"] + i, BD32, V(("dfqn", li)), True, [(qr[:, i, :], ("dfqr", i), V("one"))]))
                            specs.append((BLK["dfk"] + i, BD32, V(("dfkn", li)), True,
                                          [(k1z[:, i, :], ("dfk1", i), V("m1")), (k2z[:, i, :], ("dfk2", i), V("m2"))]))
                        qk_prep(ph2, specs)
                        S.flush()
                    va = load_vaug(ph, "dfva", 256)
                    lp = sbuf(ph, "lp", [128, 2, 2, 32], F32)
                    lpp = sbuf(ph, "lpp", [128, 2, 32], F32)
                    lps = sbuf(ph, "lps", [128, 2], F32)
                    nlam = sbuf(ph, "nlam", [128, 1], F32)
                    lam_init = 0.8 - 0.6 * math.exp(-0.3 * li)
                    DMA(lp[:].rearrange("p a b d -> p (a b d)"), dflam_in[li:li + 1, :].partition_broadcast(128), [], ["lp"])
                    TT(lpp[:], lp[:, :, 0, :], lp[:, :, 1, :], ALU.mult, ["lp"], ["lpp"])
                    S.op("dve", lambda e: e.reduce_sum(out=lps[:], in_=lpp[:], axis=AX.X), reads=["lpp"], writes=["lps"])
                    ACT(lps[:], lps[:], AF.Exp, ["lps"], ["lps"])
                    TT(nlam[:], lps[:, 1:2], lps[:, 0:1], ALU.subtract, ["lps"], ["nlam"])
                    TS(nlam[:], nlam[:], -lam_init, 0.0, ALU.add, ALU.add, ["nlam"], ["nlam"])
                    E = [sbuf(ph, "dfE%d" % i, [128, 512], BF16) for i in range(4)]
                    osb = sbuf(ph, "osb", [128, 512], F32)
                    rrow = sbuf(ph, "rrow", [128, 512], F32)
                    o1 = sbuf(ph, "o1n", [128, 512], F32)
                    o2 = sbuf(ph, "o2n", [128, 512], F32)
                    dsq = sbuf(ph, "dsq", [128, 512], F32)
                    yo = [sbuf(ph, "dfy%d" % i, [128, 512], BF16) for i in range(2)]
                    sc = 32 ** -0.5
                    ne = 0
                    ny = 0
                    qtiles = [(t0, 512, list(range(34))) for t0 in range(0, L, 512)]
                    if with_ctx:
                        qtiles.append((L, CL, [32, 33]))
                    steps = []
                    for (t0, W, kcs) in qtiles:
                        for h in range(4):
                            for ci, kc in enumerate(kcs):
                                for t in range(2):
                                    steps.append((t0, W, h, kc, t, ci == 0, ci == len(kcs) - 1))
                    PIPE = 2
                    kzs = (k1z, k2z)

                    def emit_score(i):
                        (t0, W, h, kc, t, first, last) = steps[i]
                        blk = h // 2
                        r0 = (h % 2) * 64
                        sb_ = 2 + (i % 4)
                        MM(PS[sb_][:, 0:W], kzs[t][r0:r0 + 64, blk, kc * 128:(kc + 1) * 128], qr[r0:r0 + 64, blk, t0:t0 + W],
                           True, True, [("dfk%d" % (t + 1), blk), ("dfqr", blk)], ["ps%d" % sb_])

                    def emit_rest(i):
                        nonlocal ny
                        (t0, W, h, kc, t, first, last) = steps[i]
                        sb_ = 2 + (i % 4)
                        eb = i % 4
                        ACT(E[eb][:, 0:W], PS[sb_][:, 0:W], AF.Exp, ["ps%d" % sb_], [("dfE", eb)], scale=sc)
                        MM(PS[t][0:65, 0:W], va[:, kc, h, :], E[eb][:, 0:W], first, last, ["dfva", ("dfE", eb)], ["ps%d" % t])
                        if not (last and t == 1):
                            return
                        attn_norm((osb, rrow, o1), 0, W, "o1n")
                        attn_norm((osb, rrow, o2), 1, W, "o2n")
                        STT(o1[0:64, 0:W], o2[0:64, 0:W], nlam[0:64, 0:1], o1[0:64, 0:W], ALU.mult, ALU.add,
                            ["o1n", "o2n", "nlam"], ["o1n"])
                        ACT(dsq[0:64, 0:W], o1[0:64, 0:W], AF.Square, ["o1n"], ["dsq"])
                        MM(PS[7][0:64, 0:W], BD64[0:64, 0:64], dsq[0:64, 0:W], True, True, ["cmat", "dsq"], ["ps7"])
                        ACT(dsq[0:64, 0:W], PS[7][0:64, 0:W], AF.Ln, ["ps7"], ["dsq"], bias=EPS)
                        ACT(dsq[0:64, 0:W], dsq[0:64, 0:W], AF.Exp, ["dsq"], ["dsq"], scale=-0.5)
                        yb_ = ny % 2
                        ny += 1
                        STT(dsq[0:64, 0:W], o1[0:64, 0:W], V(("dfng", li))[0:64, :], dsq[0:64, 0:W], ALU.mult, ALU.mult,
                            ["o1n", "vecs", "dsq"], ["dsq"])
                        TS(yo[yb_][0:64, 0:W], dsq[0:64, 0:W], 1.0 - lam_init, 0.0, ALU.mult, ALU.add, ["dsq"], [("dfy", yb_)])
                        DMA(ybuf[768 + h * 64:768 + (h + 1) * 64, t0:t0 + W], yo[yb_][0:64, 0:W], [("dfy", yb_)], [("ybuf", "df")])

                    for i in range(len(steps) + PIPE):
                        if i < len(steps):
                            emit_score(i)
                        if i >= PIPE:
                            emit_rest(i - PIPE)
                    S.flush()

            if "na" in mixers:
                with contextlib.ExitStack() as ph:
                    qn = sbuf(ph, "naq", [128, 2, T], BF16)
                    kn = sbuf(ph, "nak", [128, 2, T], BF16)
                    with contextlib.ExitStack() as ph2:
                        specs = []
                        for i in range(2):
                            specs.append((BLK["naq"] + i, BD64, V(("naqn", li)), False, [(qn[:, i, :], ("naq", i), V("one"))]))
                            specs.append((BLK["nak"] + i, BD64, V(("nakn", li)), False, [(kn[:, i, :], ("nak", i), V("one"))]))
                        qk_prep(ph2, specs)
                        S.flush()
                    va = load_vaug(ph, "nava", 0)
                    bias = [sbuf(ph, "nab%d" % i, [128, 21, 128], F32) for i in range(2)]
                    sbt = [sbuf(ph, "nasb%d" % i, [128, 640], F32) for i in range(2)]
                    E = [sbuf(ph, "naE%d" % i, [128, 896], BF16) for i in range(2)]
                    osb = sbuf(ph, "osb", [128, 512], F32)
                    rrow = sbuf(ph, "rrow", [128, 512], F32)
                    o1 = sbuf(ph, "o1n", [128, 512], F32)
                    yo = [sbuf(ph, "nay%d" % i, [128, 512], BF16) for i in range(2)]
                    sc = 64 ** -0.5
                    n = 0
                    ny = 0
                    for h in range(4):
                        blk = h // 2
                        r0 = (h % 2) * 64
                        hb = h % 2
                        DMA(bias[hb][:], nab_in[li, h], [], [("nab", hb)])
                        def na_scores(rp, b):
                            chunks = na_chunks(rp)
                            pA = 2 + 2 * b
                            pB = pA + 1
                            q_ap = qn[r0:r0 + 64, blk, rp * 128:(rp + 1) * 128]
                            for j, (kc, bi) in enumerate(chunks):
                                pbk, pc = (pA, j * 128) if j < 4 else (pB, 0)
                                MM(PS[pbk][:, pc:pc + 128], kn[r0:r0 + 64, blk, kc * 128:(kc + 1) * 128], q_ap, True, True,
                                   [("nak", blk), ("naq", blk)], ["ps%d" % pbk])
                            for j2 in range(2):
                                pc = 128 + j2 * 128
                                MM(PS[pB][:, pc:pc + 128], kn[r0:r0 + 64, blk, L + j2 * 128:L + (j2 + 1) * 128], q_ap, True, True,
                                   [("nak", blk), ("naq", blk)], ["ps%d" % pB])

                        def na_rest(rp, b):
                            nonlocal ny
                            rg, rr_ = rp // 4, rp % 4
                            chunks = na_chunks(rp)
                            pA = 2 + 2 * b
                            pB = pA + 1
                            nw = len(chunks)
                            bi0 = chunks[0][1]
                            STT(sbt[b][:, 0:512], PS[pA][:, 0:512], sc, bias[hb][:, bi0:bi0 + 4, :].rearrange("p a q -> p (a q)"),
                                ALU.mult, ALU.add, ["ps%d" % pA, ("nab", hb)], [("nasb", b)])
                            if nw == 5:
                                STT(sbt[b][:, 512:640], PS[pB][:, 0:128], sc, bias[hb][:, 4, :], ALU.mult, ALU.add,
                                    ["ps%d" % pB, ("nab", hb)], [("nasb", b)])
                            ACT(E[b][:, 0:nw * 128], sbt[b][:, 0:nw * 128], AF.Exp, [("nasb", b)], [("naE", b)])
                            ACT(E[b][:, 640:896], PS[pB][:, 128:384], AF.Exp, ["ps%d" % pB], [("naE", b)], scale=sc)
                            ecols = [(kc, j * 128) for j, (kc, bi) in enumerate(chunks)] + [(32, 640), (33, 768)]
                            for ci, (kc, ec) in enumerate(ecols):
                                MM(PS[0][0:65, rr_ * 128:(rr_ + 1) * 128], va[:, kc, h, :], E[b][:, ec:ec + 128], ci == 0, ci == len(ecols) - 1,
                                   ["nava", ("naE", b)], ["ps0"])
                            if rr_ == 3:
                                attn_norm((osb, rrow, o1), 0, 512, "o1n")
                                yb_ = ny % 2
                                ny += 1
                                CP(yo[yb_][0:64, :], o1[0:64, :], ["o1n"], [("nay", yb_)], eng="pool")
                                DMA(ybuf[256 + h * 64:256 + (h + 1) * 64, rg * 512:(rg + 1) * 512], yo[yb_][0:64, :], [("nay", yb_)], [("ybuf", "na")])

                        na_scores(0, 0)
                        for rp in range(32):
                            if rp + 1 < 32:
                                na_scores(rp + 1, (rp + 1) % 2)
                            na_rest(rp, rp % 2)
                        if with_ctx:
                            q_ap = qn[r0:r0 + 64, blk, L:L + CL]
                            for j2 in range(2):
                                MM(PS[1][:, j2 * 256:(j2 + 1) * 256], kn[r0:r0 + 64, blk, L + j2 * 128:L + (j2 + 1) * 128], q_ap, True, True,
                                   [("nak", blk), ("naq", blk)], ["ps1"])
                            ACT(E[0][:, 0:512], PS[1][:, 0:512], AF.Exp, ["ps1"], [("naE", 0)], scale=sc)
                            for j2 in range(2):
                                MM(PS[0][0:65, 0:256], va[:, 32 + j2, h, :], E[0][:, j2 * 256:(j2 + 1) * 256], j2 == 0, j2 == 1,
                                   ["nava", ("naE", 0)], ["ps0"])
                            attn_norm((osb, rrow, o1), 0, 256, "o1n")
                            yb_ = ny % 2
                            ny += 1
                            CP(yo[yb_][0:64, 0:256], o1[0:64, 0:256], ["o1n"], [("nay", yb_)], eng="pool")
                            DMA(ybuf[256 + h * 64:256 + (h + 1) * 64, L:L + CL], yo[yb_][0:64, 0:256], [("nay", yb_)], [("ybuf", "na")])
                    S.flush()

            with contextlib.ExitStack() as ph:
                wg = sbuf(ph, "wg", [128, 8, 4096], BF16)
                wbr = sbuf(ph, "wbr", [128, 8, D], BF16)
                wo = sbuf(ph, "wo", [128, 8, D], BF16)
                xt = [sbuf(ph, "xt%d" % i, [128, 8, 512], F32) for i in range(2)]
                ht = [sbuf(ph, "ht%d" % i, [128, 8, 512], BF16) for i in range(2)]
                yt = [sbuf(ph, "yt%d" % i, [128, 8, 512], BF16) for i in range(2)]
                mg = sbuf(ph, "mg", [128, 8, 512], BF16)
                sig = [sbuf(ph, "sig%d" % i, [128, 512], F32) for i in range(2)]
                acc = sbuf(ph, "acc", [128, 512], F32)
                tmp = sbuf(ph, "tmp", [128, 512], F32)
                for k in range(8):
                    DMA(wg[:, k, :], wi_b[li][k * 128:(k + 1) * 128, NMIX:NIN], [("wi_b", li)], [("wg", k)])
                    DMA(wbr[:, k, :], wb_b[li][k * 128:(k + 1) * 128, :], [("wb_b", li)], [("wbr", k)])
                    DMA(wo[:, k, :], wo_b[li][k * 128:(k + 1) * 128, :], [("wo_b", li)], [("wo", k)])
                it = 0
                ng = 0
                for (sname, s0, slen) in SEQS:
                    s = 0 if sname == "lat" else 1
                    if s == 1 and not with_ctx:
                        continue
                    for t0 in range(s0, s0 + slen, 512):
                        W = min(512, s0 + slen - t0)
                        b = it % 2
                        it += 1
                        if li == layer_list[0]:
                            src = xT_in[:, t0:t0 + W] if s == 0 else cT_in[:, t0 - L:t0 - L + W]
                        else:
                            src = xbuf[:, t0:t0 + W]
                        DMA(xt[b][:, :, 0:W], src.rearrange("(k p) t -> p k t", p=128), ["xbuf"], [("xt", b)])
                        DMA(ht[b][:, :, 0:W], hbuf[:, t0:t0 + W].rearrange("(k p) t -> p k t", p=128), ["hbuf"], [("ht", b)])
                        DMA(yt[b][:, :, 0:W], ybuf[:, t0:t0 + W].rearrange("(k p) t -> p k t", p=128), ["ybuf"], [("yt", b)])
                        for dc in range(8):
                            for g in range(4):
                                pa = 2 * (ng % 2)
                                pbk = pa + 1
                                sgi = ng % 2
                                ng += 1
                                cg = g * D + dc * 128
                                for k in range(8):
                                    MM(PS[pa][:, 0:W], wg[:, k, cg:cg + 128], ht[b][:, k, 0:W], k == 0, k == 7,
                                       [("wg", k), ("ht", b)], ["ps%d" % pa])
                                for k2 in range(2):
                                    MM(PS[pbk][:, 0:W], wbr[:, 2 * g + k2, dc * 128:(dc + 1) * 128], yt[b][:, 2 * g + k2, 0:W],
                                       k2 == 0, k2 == 1, [("wbr", 2 * g + k2), ("yt", b)], ["ps%d" % pbk])
                                ACT(sig[sgi][:, 0:W], PS[pa][:, 0:W], AF.Sigmoid, ["ps%d" % pa, "vecs"], [("sig", sgi)],
                                    bias=V(("bgate", li), g * 8 + dc))
                                if g == 0:
                                    TT(acc[:, 0:W], sig[sgi][:, 0:W], PS[pbk][:, 0:W], ALU.mult, [("sig", sgi), "ps%d" % pbk], ["acc"])
                                else:
                                    TT(tmp[:, 0:W], sig[sgi][:, 0:W], PS[pbk][:, 0:W], ALU.mult, [("sig", sgi), "ps%d" % pbk], ["tmp"])
                                    if g < 3:
                                        TT(acc[:, 0:W], acc[:, 0:W], tmp[:, 0:W], ALU.add, ["acc", "tmp"], ["acc"], eng="pool")
                                    else:
                                        TT(mg[:, dc, 0:W], acc[:, 0:W], tmp[:, 0:W], ALU.add, ["acc", "tmp"], [("mg", dc)], eng="pool")
                        for dc in range(8):
                            pb = 4 + dc % 2
                            for k in range(8):
                                MM(PS[pb][:, 0:W], wo[:, k, dc * 128:(dc + 1) * 128], mg[:, k, 0:W], k == 0, k == 7,
                                   [("wo", k), ("mg", k)], ["ps%d" % pb])
                            STT(xt[b][:, dc, 0:W], PS[pb][:, 0:W], modv(li, "g1", dc, s), xt[b][:, dc, 0:W], ALU.mult, ALU.add,
                                ["ps%d" % pb, "modsb", ("xt", b)], [("xt", b)])
                        DMA(x1buf[:, t0:t0 + W].rearrange("(k p) t -> p k t", p=128), xt[b][:, :, 0:W], [("xt", b)], ["x1buf"])
                S.flush()

            with contextlib.ExitStack() as ph:
                FT = 456
                xt = [sbuf(ph, "xt%d" % i, [128, 8, 512], F32) for i in range(2)]
                ht2 = [sbuf(ph, "ht%d" % i, [128, 8, 512], BF16) for i in range(2)]
                gt = sbuf(ph, "gt", [128, 22, 512], BF16)
                sq = sbuf(ph, "sq", [128, 512], BF16)
                rr = sbuf(ph, "rr", [128, 512], F32)
                ff = sbuf(ph, "ff", [128, 512], F32)
                ust = [sbuf(ph, "ust%d" % i, [128, 516], F32) for i in range(2)]
                ca = sbuf(ph, "ca", [128, 512], F32)
                cb_ = sbuf(ph, "cb", [128, 512], F32)
                w1 = [sbuf(ph, "w1_%d" % i, [128, 8, 256], BF16) for i in range(3)]
                w2f = sbuf(ph, "w2f", [128, 22, D], BF16)
                for j in range(22):
                    DMA(w2f[:, j, :], f2_b[li][j * 128:(j + 1) * 128, :], [("f2_b", li)], [("w2f", j)])
                nw = 0
                tiles3 = []
                for (sname, s0, slen) in SEQS:
                    s = 0 if sname == "lat" else 1
                    if s == 1 and not with_ctx:
                        continue
                    for t0 in range(s0, s0 + slen, FT):
                        t1 = min(t0 + FT, s0 + slen)
                        a0 = max(t0 - 1, s0)
                        a1 = min(t1 + 1, s0 + slen)
                        tiles3.append((s, t0, t1, a0, a1))

                def p3_norm(i):
                    (s, t0, t1, a0, a1) = tiles3[i]
                    b = i % 2
                    DMA(xt[b][:, :, 0:a1 - a0], x1buf[:, a0:a1].rearrange("(k p) t -> p k t", p=128), ["x1buf"], [("xt", b)])
                    norm_tile(xt[b], ht2[b], a1 - a0, li, 1, s, sq, rr, ff, 0, ("xt", b), ("ht", b), "n2")

                p3_norm(0)
                for ti in range(len(tiles3)):
                    if True:
                        (s, t0, t1, a0, a1) = tiles3[ti]
                        W = a1 - a0
                        WI = t1 - t0
                        io = t0 - a0
                        b = ti % 2
                        ht = ht2[b]
                        if ti + 1 < len(tiles3):
                            p3_norm(ti + 1)
                        for j in range(22):
                            wb = nw % 3
                            nw += 1
                            DMA(w1[wb][:, :, 0:128], f1_b[li][:, j * 128:(j + 1) * 128].rearrange("(k p) c -> p k c", p=128),
                                [("f1_b", li)], [("w1", wb)])
                            DMA(w1[wb][:, :, 128:256], f1_b[li][:, DFF + j * 128:DFF + (j + 1) * 128].rearrange("(k p) c -> p k c", p=128),
                                [("f1_b", li)], [("w1", wb)])
                            pu = 1 + 2 * (j % 2)
                            pv = pu + 1
                            ub = j % 2
                            for k in range(8):
                                MM(PS[pu][:, 0:W], w1[wb][:, k, 0:128], ht[:, k, 0:W], k == 0, k == 7, [("w1", wb), ("ht", b)], ["ps%d" % pu])
                            for k in range(8):
                                MM(PS[pv][:, 0:W], w1[wb][:, k, 128:256], ht[:, k, 0:W], k == 0, k == 7, [("w1", wb), ("ht", b)], ["ps%d" % pv])
                            c_in = 1 - io
                            ACT(ust[ub][:, c_in + 0:c_in + W], PS[pu][:, 0:W], AF.Copy, ["ps%d" % pu], [("ust", ub)])
                            if io == 0:
                                MSET(ust[ub][:, 0:1], 0.0, [("ust", ub)])
                            if a1 == t1:
                                MSET(ust[ub][:, WI + 1:WI + 2], 0.0, [("ust", ub)])
                            fo = voff[("fcw", li)]
                            TS(ca[:, 0:WI], ust[ub][:, 1:1 + WI], vecs[:, fo + 22 + j:fo + 23 + j], V(("fcb", li), j), ALU.mult, ALU.add,
                               [("ust", ub), "vecs"], ["ca"])
                            STT(cb_[:, 0:WI], ust[ub][:, 0:WI], vecs[:, fo + j:fo + j + 1], ca[:, 0:WI], ALU.mult, ALU.add,
                                [("ust", ub), "vecs", "ca"], ["cb"])
                            STT(ca[:, 0:WI], ust[ub][:, 2:2 + WI], vecs[:, fo + 44 + j:fo + 45 + j], cb_[:, 0:WI], ALU.mult, ALU.add,
                                [("ust", ub), "vecs", "cb"], ["ca"])
                            ACT(cb_[:, 0:WI], ca[:, 0:WI], AF.Silu, ["ca"], ["cb"])
                            TT(gt[:, j, 0:WI], cb_[:, 0:WI], PS[pv][:, io:io + WI], ALU.mult, ["cb", "ps%d" % pv], [("gt", j)])
                        for dc in range(8):
                            pb = 5 + dc % 2
                            for j in range(22):
                                MM(PS[pb][:, 0:WI], w2f[:, j, dc * 128:(dc + 1) * 128], gt[:, j, 0:WI], j == 0, j == 21,
                                   [("w2f", j), ("gt", j)], ["ps%d" % pb])
                            STT(xt[b][:, dc, io:io + WI], PS[pb][:, 0:WI], modv(li, "g2", dc, s), xt[b][:, dc, io:io + WI], ALU.mult, ALU.add,
                                ["ps%d" % pb, "modsb", ("xt", b)], [("xt", b)])
                        dst = outT[:, t0:t1] if (li == DEPTH - 1 and s == 0) else xbuf[:, t0:t1]
                        DMA(dst.rearrange("(k p) t -> p k t", p=128), xt[b][:, :, io:io + WI], [("xt", b)], ["xbuf"])
                        if xd is not None:
                            DMA(xd[li][:, t0:t1].rearrange("(k p) t -> p k t", p=128), xt[b][:, :, io:io + WI], [("xt", b)], ["xd"])
                S.flush()
    return nc


def host_inputs(inp, b):
    m = {}
    m["xT"] = np.ascontiguousarray(np.asarray(inp["x"][b], np.float32).T)
    m["ctxT"] = np.ascontiguousarray(np.asarray(inp["ctx"][b], np.float32).T)
    cs = np.stack([np.asarray(inp["c"][b], np.float32), np.asarray(inp["c_ctx"], np.float32)], axis=-1)
    m["cs"] = np.ascontiguousarray(cs.reshape(8, 128, 2).transpose(1, 0, 2).reshape(128, 16))
    m["vecs"] = pack_vecs(inp)
    m["cmat"] = const_mats()
    m["rope"] = rope_tables()
    m["nab"] = np.stack([na_bias_tables(np.asarray(inp["na_rpb"][li], np.float32)) for li in range(DEPTH)], 0)
    m["scm"] = scan_mats()
    m["gla_w"] = np.ascontiguousarray(np.concatenate([np.asarray(inp["gla_w_a2"], np.float32),
                                                      np.asarray(inp["gla_b_a"], np.float32)[:, :, None, :]], axis=2))
    m["dflam"] = np.ascontiguousarray(np.asarray(inp["df_lambda"], np.float32).reshape(DEPTH, 128))
    for k in ("w_mod", "w_in", "w_out", "ffn_w_in", "ffn_w_out"):
        m[k] = np.ascontiguousarray(np.asarray(inp[k], np.float32))
    m["w_branch"] = np.ascontiguousarray(np.asarray(inp["w_branch"], np.float32).reshape(DEPTH, D, D))
    return m


def kernel(**inp):
    nc = build()
    shared = None
    in_maps = []
    for b in range(8):
        m = host_inputs(inp, b)
        if shared is None:
            shared = {k: m[k] for k in ("vecs", "cmat", "rope", "nab", "dflam", "scm", "gla_w", "w_mod", "w_in", "w_out", "ffn_w_in", "ffn_w_out", "w_branch")}
        else:
            m.update(shared)
        in_maps.append(m)
    res = run_bass_kernel_spmd(nc, in_maps, core_ids=list(range(8)))
    out = np.stack([np.ascontiguousarray(r["outT"].T) for r in res.results], axis=0)
    return out.astype(np.float32)
```
